# Optimizing a Trainium2 kernel written in Bass

```python
import math
import jax, jax.numpy as jnp
from jax import lax
import numpy as np

D_MODEL = 1024
BATCH = 2
SEQ = 8192
DEPTH = 1

D_MIX = D_MODEL
N_HEADS_A = 8
HEAD_DIM_A = 64
DIL_CONFIGS = ((128, 1), (512, 4), (2048, 16))
Q_BLOCK = 128
N_HEADS_B = 8
QK_NOPE = 64
QK_ROPE = 32
V_DIM = 64
Q_LORA = 384
KV_LORA = 256
ROPE_BASE = 10000.0
D_FF = 2816
CONV_WIDTH = 3
EPS = 1e-6
NEG = -1e30

WIDTH_A = N_HEADS_A * HEAD_DIM_A
WIDTH_B = N_HEADS_B * V_DIM
IN_SIZES = (WIDTH_A, WIDTH_A, WIDTH_A, Q_LORA, KV_LORA, QK_ROPE)
D_IN = sum(IN_SIZES)
SPLIT_POINTS = tuple(int(v) for v in np.cumsum(IN_SIZES)[:-1])

kernel_name = "hybrid_dilated_swa_mla_convffn_sandwich"


def _rmsnorm(x, g):
    xf = x.astype(jnp.float32)
    y = xf * lax.rsqrt(jnp.mean(xf * xf, axis=-1, keepdims=True) + EPS)
    return (y * g.astype(jnp.float32)).astype(x.dtype)


def _rope(x, cos, sin):
    xf = x.astype(jnp.float32)
    half = xf.shape[-1] // 2
    x1, x2 = xf[..., :half], xf[..., half:]
    out = jnp.concatenate([x1 * cos - x2 * sin, x2 * cos + x1 * sin], axis=-1)
    return out.astype(x.dtype)


def _dilated_branch(q, k, v, slopes, window, dilation):
    b, h, s, d = q.shape
    r = dilation
    half = window // (2 * dilation)
    L = s // r
    qb = min(Q_BLOCK, L)
    nblk = -(-L // qb)
    Lp = nblk * qb
    slab = qb + 2 * half

    def to_dilated(t):
        return t.reshape(b, h, L, r, d).transpose(0, 1, 3, 2, 4)

    qd, kd, vd = to_dilated(q), to_dilated(k), to_dilated(v)
    qd = jnp.pad(qd, ((0, 0), (0, 0), (0, 0), (0, Lp - L), (0, 0)))
    pad_kv = ((0, 0), (0, 0), (0, 0), (half, half + Lp - L), (0, 0))
    kd = jnp.pad(kd, pad_kv)
    vd = jnp.pad(vd, pad_kv)
    idx = jnp.arange(nblk)[:, None] * qb + jnp.arange(slab)[None, :]
    k_s = kd[:, :, :, idx, :].astype(jnp.float32)
    v_s = vd[:, :, :, idx, :].astype(jnp.float32)
    q_s = qd.reshape(b, h, r, nblk, qb, d).astype(jnp.float32)

    scores = jnp.einsum('bhcnqd,bhcnkd->bhcnqk', q_s, k_s) * (d ** -0.5)
    off = jnp.arange(slab)[None, :] - half - jnp.arange(qb)[:, None]
    key_pos = idx - half
    valid = (jnp.abs(off) <= half)[None, :, :] & ((key_pos >= 0) & (key_pos < L))[:, None, :]
    dist = (jnp.abs(off) * r).astype(jnp.float32)
    alibi = -slopes[:, None, None] * dist[None]
    scores = scores + alibi[None, :, None, None]
    scores = jnp.where(valid[None, None, None], scores, NEG)
    lse = jax.nn.logsumexp(scores, axis=-1)
    p = jnp.exp(scores - lse[..., None])
    o = jnp.einsum('bhcnqk,bhcnkd->bhcnqd', p, v_s)

    o = o.reshape(b, h, r, Lp, d)[:, :, :, :L].transpose(0, 1, 3, 2, 4).reshape(b, h, s, d)
    lse = lse.reshape(b, h, r, Lp)[:, :, :, :L].transpose(0, 1, 3, 2).reshape(b, h, s)
    return o, lse


def _dilated_attention(qa, ka, va):
    b, s, _ = qa.shape
    def heads(t):
        return t.reshape(b, s, N_HEADS_A, HEAD_DIM_A).transpose(0, 2, 1, 3)
    q, k, v = heads(qa), heads(ka), heads(va)
    slopes = jnp.exp2(-8.0 * jnp.arange(1, N_HEADS_A + 1, dtype=jnp.float32) / N_HEADS_A)
    outs, lses = [], []
    for window, dilation in DIL_CONFIGS:
        o, lse = _dilated_branch(q, k, v, slopes, window, dilation)
        outs.append(o)
        lses.append(lse)
    w = jax.nn.softmax(jnp.stack(lses, axis=0), axis=0)
    o = jnp.einsum('gbhs,gbhsd->bhsd', w, jnp.stack(outs, axis=0))
    return o.transpose(0, 2, 1, 3).reshape(b, s, WIDTH_A).astype(qa.dtype)


def _mla(c_q, c_kv, k_r, q_lat_norm, w_uq, kv_lat_norm, w_ukv):
    b, s, _ = c_q.shape
    pos = jnp.arange(s, dtype=jnp.float32)
    inv_freq = jnp.exp(-math.log(ROPE_BASE) * jnp.arange(0, QK_ROPE, 2, dtype=jnp.float32) / QK_ROPE)
    ang = pos[:, None] * inv_freq[None, :]
    cos, sin = jnp.cos(ang), jnp.sin(ang)

    q = (_rmsnorm(c_q, q_lat_norm) @ w_uq).reshape(b, s, N_HEADS_B, QK_NOPE + QK_ROPE)
    q_nope = q[..., :QK_NOPE]
    q_rope = _rope(q[..., QK_NOPE:], cos[:, None, :], sin[:, None, :])
    kv = (_rmsnorm(c_kv, kv_lat_norm) @ w_ukv).reshape(b, s, N_HEADS_B, QK_NOPE + V_DIM)
    k_nope = kv[..., :QK_NOPE].transpose(0, 2, 1, 3).astype(jnp.float32)
    v = kv[..., QK_NOPE:].transpose(0, 2, 1, 3).astype(jnp.float32)
    k_rope = _rope(k_r, cos, sin).astype(jnp.float32)
    scale = (QK_NOPE + QK_ROPE) ** -0.5

    nq = s // Q_BLOCK
    qn_blk = q_nope.reshape(b, nq, Q_BLOCK, N_HEADS_B, QK_NOPE).transpose(1, 0, 3, 2, 4)
    qr_blk = q_rope.reshape(b, nq, Q_BLOCK, N_HEADS_B, QK_ROPE).transpose(1, 0, 3, 2, 4)

    def block(args):
        qn, qr = args
        sc = (jnp.einsum('bhqd,bhkd->bhqk', qn.astype(jnp.float32), k_nope)
              + jnp.einsum('bhqr,bkr->bhqk', qr.astype(jnp.float32), k_rope)) * scale
        p = jax.nn.softmax(sc, axis=-1)
        return jnp.einsum('bhqk,bhkd->bhqd', p, v)

    o = lax.map(block, (qn_blk, qr_blk))
    return o.transpose(1, 0, 3, 2, 4).reshape(b, s, WIDTH_B).astype(c_q.dtype)


def _dwconv(u, w, bias):
    c = u.shape[-1]
    y = lax.conv_general_dilated(
        u, w[:, None, :].astype(u.dtype), window_strides=(1,),
        padding=((CONV_WIDTH // 2, CONV_WIDTH // 2),),
        dimension_numbers=('NWC', 'WIO', 'NWC'), feature_group_count=c)
    return y + bias.astype(u.dtype)


def setup_inputs(seed: int = 0) -> dict:
    key = jax.random.key(seed)
    ks = jax.random.split(key, 17)
    f32 = jnp.float32

    def gain(k, n):
        return 1.0 + 0.1 * jax.random.normal(k, (DEPTH, n), f32)

    def dense(k, fan_in, fan_out):
        return jax.random.normal(k, (DEPTH, fan_in, fan_out), f32) * fan_in ** -0.5

    return {
        "x": jax.random.normal(ks[0], (BATCH, SEQ, D_MODEL), f32),
        "norm_mix_pre": gain(ks[1], D_MODEL),
        "w_in": dense(ks[2], D_MODEL, D_IN),
        "q_lat_norm": gain(ks[3], Q_LORA),
        "w_uq": dense(ks[4], Q_LORA, N_HEADS_B * (QK_NOPE + QK_ROPE)),
        "kv_lat_norm": gain(ks[5], KV_LORA),
        "w_ukv": dense(ks[6], KV_LORA, N_HEADS_B * (QK_NOPE + V_DIM)),
        "out_norm_a": gain(ks[7], WIDTH_A),
        "out_norm_b": gain(ks[8], WIDTH_B),
        "w_o": dense(ks[9], D_MIX, D_MODEL),
        "norm_mix_post": gain(ks[10], D_MODEL),
        "norm_ffn_pre": gain(ks[11], D_MODEL),
        "w_up": dense(ks[12], D_MODEL, 2 * D_FF),
        "conv_w": jax.random.normal(ks[13], (DEPTH, CONV_WIDTH, 2 * D_FF), f32) * CONV_WIDTH ** -0.5,
        "conv_b": 0.02 * jax.random.normal(ks[14], (DEPTH, 2 * D_FF), f32),
        "w_down": dense(ks[15], D_FF, D_MODEL),
        "norm_ffn_post": gain(ks[16], D_MODEL),
    }


def reference(x, norm_mix_pre, w_in, q_lat_norm, w_uq, kv_lat_norm, w_ukv, out_norm_a,
              out_norm_b, w_o, norm_mix_post, norm_ffn_pre, w_up, conv_w, conv_b, w_down,
              norm_ffn_post):
    for l in range(DEPTH):
        h = _rmsnorm(x, norm_mix_pre[l])
        proj = h @ w_in[l]
        qa, ka, va, c_q, c_kv, k_r = jnp.split(proj, SPLIT_POINTS, axis=-1)
        ya = _dilated_attention(qa, ka, va)
        yb = _mla(c_q, c_kv, k_r, q_lat_norm[l], w_uq[l], kv_lat_norm[l], w_ukv[l])
        y = jnp.concatenate([_rmsnorm(ya, out_norm_a[l]), _rmsnorm(yb, out_norm_b[l])], axis=-1)
        y = y @ w_o[l]
        x = x + _rmsnorm(y, norm_mix_post[l])
        h = _rmsnorm(x, norm_ffn_pre[l])
        u = _dwconv(h @ w_up[l], conv_w[l], conv_b[l])
        g, v = u[..., :D_FF], u[..., D_FF:]
        y = (jax.nn.gelu(g, approximate=True) * v) @ w_down[l]
        x = x + _rmsnorm(y, norm_ffn_post[l])
    return x
```

```python
import contextlib
import types
import numpy as np
import ml_dtypes
import concourse.bass as bass
import concourse.mybir as mybir
from concourse.bass_utils import run_bass_kernel_spmd

F32 = mybir.dt.float32
BF16 = mybir.dt.bfloat16
U8 = mybir.dt.uint8
AF = mybir.ActivationFunctionType
ALU = mybir.AluOpType

S_LEN = 8192
D = 1024
OWN = 2048
OWN0 = 1152
NW = 4352
NX = 2050
DFF = 2816
EPS = 1e-6
ENGS = ("sync", "scalar", "gpsimd", "vector", "tensor")
XBLK = [(i * 410, 410) for i in range(5)]


class Sched:
    def __init__(self, nc, ndma_sems=8):
        self.nc = nc
        self.ops = []
        self.ndma = ndma_sems

    @staticmethod
    def _freeze(fn):
        if fn.__closure__ is None:
            return fn
        cells = []
        for c in fn.__closure__:
            try:
                cells.append(types.CellType(c.cell_contents))
            except ValueError:
                cells.append(c)
        return types.FunctionType(fn.__code__, fn.__globals__, fn.__name__, fn.__defaults__, tuple(cells))

    def op(self, eng, fn, reads=(), writes=(), dma=False):
        fn = self._freeze(fn)
        self.ops.append(dict(eng=eng, fn=fn, reads=tuple(reads), writes=tuple(writes), dma=dma, bar=False))

    def barrier(self):
        self.ops.append(dict(eng=None, fn=None, reads=(), writes=(), dma=False, bar=True))

    def emit(self, final_wait_eng="sync"):
        nc = self.nc
        ops = self.ops
        n = len(ops)
        last_writer = {}
        readers = {}
        deps = [set() for _ in range(n)]
        since_bar = []
        pending_bar = {}
        for i, o in enumerate(ops):
            if o["bar"]:
                lastc = {}
                dl = set()
                for j in since_bar:
                    if ops[j]["dma"]:
                        dl.add(j)
                    else:
                        lastc[ops[j]["eng"]] = j
                dl.update(lastc.values())
                for e in ENGS:
                    pending_bar[e] = set(dl) | pending_bar.get(e, set())
                since_bar = []
                continue
            d = deps[i]
            if o["eng"] in pending_bar:
                d.update(pending_bar.pop(o["eng"]))
            for r in o["reads"]:
                if r in last_writer:
                    d.add(last_writer[r])
            for w in o["writes"]:
                if w in last_writer:
                    d.add(last_writer[w])
                d.update(readers.get(w, ()))
            d.discard(i)
            for w in o["writes"]:
                last_writer[w] = i
                readers[w] = []
            for r in o["reads"]:
                if r not in o["writes"]:
                    readers.setdefault(r, []).append(i)
            since_bar.append(i)
        needed = set()
        red = [None] * n
        for i, o in enumerate(ops):
            if o["bar"]:
                continue
            per_eng = {}
            dl = []
            for j in deps[i]:
                pj = ops[j]
                if pj["dma"]:
                    dl.append(j)
                    continue
                if pj["eng"] == o["eng"] and not o["dma"] and o["eng"] == "tensor":
                    continue
                e = pj["eng"]
                if e not in per_eng or per_eng[e] < j:
                    per_eng[e] = j
            dl.extend(per_eng.values())
            red[i] = dl
            needed.update(dl)
        cnt = {e: 0 for e in ENGS}
        dcnt = {}
        sig = [None] * n
        dma_idx = {e: 0 for e in ENGS}
        for i, o in enumerate(ops):
            if o["bar"]:
                continue
            if o["dma"]:
                k = dma_idx[o["eng"]] % self.ndma
                dma_idx[o["eng"]] += 1
                key = ("dma", o["eng"], k)
                prev = dcnt.get(key, 0)
                dcnt[key] = prev + 16
                sig[i] = (key, prev + 16)
                o["dma_prev"] = (key, prev) if prev > 0 else None
            elif i in needed:
                cnt[o["eng"]] += 1
                sig[i] = (("eng", o["eng"]), cnt[o["eng"]])
        semkeys = sorted({s[0] for s in sig if s is not None}, key=str)
        stack = contextlib.ExitStack()
        sems = {}
        for sk in semkeys:
            sems[sk] = stack.enter_context(nc.semaphore("s_" + "_".join(str(x) for x in sk)))
        by_eng = {e: [i for i, o in enumerate(ops) if o["eng"] == e] for e in ENGS}
        dma_final = list(dcnt.items())

        def run(engname, eng):
            waited = {}
            for i in by_eng[engname]:
                o = ops[i]
                wl = [sig[j] for j in red[i]]
                if o["dma"] and o.get("dma_prev"):
                    wl.append(o["dma_prev"])
                for (sk, v) in wl:
                    if waited.get(sk, 0) >= v:
                        continue
                    eng.wait_ge(sems[sk], v)
                    waited[sk] = v
                ins = o["fn"](eng)
                if sig[i] is not None:
                    ins.then_inc(sems[sig[i][0]], 16 if o["dma"] else 1)
            if engname == final_wait_eng:
                for sk, v in dma_final:
                    if waited.get(sk, 0) < v:
                        eng.wait_ge(sems[sk], v)
                for e in ENGS:
                    if cnt[e] > 0 and waited.get(("eng", e), 0) < cnt[e]:
                        eng.wait_ge(sems[("eng", e)], cnt[e])

        with stack:
            with nc.Block() as block:
                @block.sync
                def _(e):
                    run("sync", e)

                @block.scalar
                def _(e):
                    run("scalar", e)

                @block.gpsimd
                def _(e):
                    run("gpsimd", e)

                @block.vector
                def _(e):
                    run("vector", e)

                @block.tensor
                def _(e):
                    run("tensor", e)


def dil_tables():
    vt = {}

    def vtile(r, e0, nk):
        key = (r, e0, nk)
        if key not in vt:
            vt[key] = len(vt)
        return vt[key]

    groups = {1: [], 4: [], 16: []}
    for r in (1, 4, 16):
        def blk(x0, N):
            eq0 = OWN0 - 1 + x0
            t0 = vtile(r, eq0 - 64 * r, 128)
            t1 = vtile(r, eq0 + 64 * r, 128 if N > 1 else 1)
            return (x0, N, t0, t1)
        if r == 1:
            for g in range(4):
                groups[r].append(("reg", g, [blk(1 + 128 * (4 * g + b), 128) for b in range(4)]))
        elif r == 4:
            for g in range(4):
                groups[r].append(("reg", g, [blk(1 + c + 512 * g, 128) for c in range(4)]))
        else:
            for g in range(4):
                groups[r].append(("reg", g, [blk(1 + 4 * g + c, 128) for c in range(4)]))
        groups[r].append(("halo", 0, [blk(0, 1), blk(NX - 1, 1)]))
    vlist = [None] * len(vt)
    for k, i in vt.items():
        vlist[i] = k
    return vlist, groups


VLIST, DGROUPS = dil_tables()
NVT = len(VLIST)


def build(dbg=False):
    nc = bass.Bass("TRN2", target_bir_lowering=False)

    def din(name, shape, dt=F32):
        return nc.dram_tensor(name, list(shape), dt, kind="ExternalInput").ap()

    xw = din("xw", [NW, D])
    xf = din("xf", [S_LEN, D])
    w_in = din("w_in", [D, 2208])
    w_uq = din("w_uq", [384, 768])
    w_ukv = din("w_ukv", [256, 1024])
    w_o = din("w_o", [D, D])
    w_up = din("w_up", [D, 2 * DFF])
    w_down = din("w_down", [DFF, D])
    gp = din("gp", [128, 32])
    gpost = din("gpost", [128, 2, D])
    cwb = din("cwb", [128, 44, 4])
    masks = din("masks", [8, 128, 6, 128])
    kvf = din("kvf", [128, NVT])
    rq = din("rq", [32, 2, NX])
    rk = din("rk", [128, 2, 64, 16])
    ident = din("ident", [128, 128])
    uflag = din("uflag", [128, 2])
    yout = nc.dram_tensor("y", [OWN, D], F32, kind="ExternalOutput").ap()
    xmid = nc.dram_tensor("xmid", [OWN, D], F32, kind="Internal").ap()
    dbg_out = {}

    SB_BYTES = 212000
    big = nc.alloc_sbuf_tensor("big", [128, SB_BYTES], U8).ap()
    ps = nc.alloc_psum_tensor("ps", [128, 4096], F32).ap()
    psb = ps.bitcast(BF16)

    class Region:
        def __init__(self, lo, hi):
            assert hi <= SB_BYTES and lo <= hi, (lo, hi)
            self.lo, self.hi, self.off = lo, hi, lo

        def alloc(self, shape, dt, p0=0):
            esz = 4 if dt == F32 else 2
            nb = int(np.prod(shape[1:])) * esz
            nb_al = (nb + 63) // 64 * 64
            assert self.off + nb_al <= self.hi, ("SBUF region overflow", self.lo, self.hi, self.off, nb_al)
            v = big[p0:p0 + shape[0], self.off:self.off + nb].bitcast(dt)
            self.off += nb_al
            if len(shape) == 3:
                v = v.rearrange("p (a b) -> p a b", a=shape[1])
            elif len(shape) == 4:
                v = v.rearrange("p (a b c) -> p a b c", a=shape[1], b=shape[2])
            return v

    RP = Region(0, 4096)
    R1 = Region(RP.hi, RP.hi + 86080)
    RC = Region(R1.hi, R1.hi + 12352)
    RY = Region(RC.hi, RC.hi + 16448 + 4096 + 16448)
    RT = Region(RY.hi, SB_BYTES)
    A = RP
    S = Sched(nc)
    op = S.op

    def dump(name, ap, shape, dt, keys):
        if not dbg:
            return
        import os
        sel = os.environ.get("DBGSEL", "")
        if sel and name not in sel.split(","):
            return
        d = nc.dram_tensor("dbg_" + name, list(shape), dt, kind="ExternalOutput").ap()
        op("sync", lambda e: e.dma_start(out=d, in_=ap), reads=keys, dma=True)

    def bank(b, n=512, p=128, p0=0):
        return ps[p0:p0 + p, b * 512:b * 512 + n]

    def bankb(b, n=1024, p=128, p0=0):
        return psb[p0:p0 + p, b * 1024:b * 1024 + n]

    idf = A.alloc([128, 128], F32)
    idb = A.alloc([128, 128], BF16)
    gpt = A.alloc([128, 32], F32)
    stt = A.alloc([128, 256], F32)
    junk = A.alloc([128, 1024], BF16)
    rec = A.alloc([128, 8], F32)
    op("sync", lambda e: e.dma_start(out=idf, in_=ident), writes=["idf"], dma=True)
    op("sync", lambda e: e.dma_start(out=gpt, in_=gp), writes=["gpt"], dma=True)
    op("vector", lambda e: e.tensor_copy(out=idb, in_=idf), reads=["idf"], writes=["idb"])

    def rstd_from_ss(ss_ap, out_ap, n, key):
        tmp = ss_ap
        op("scalar", lambda e: e.activation(out=tmp, in_=ss_ap, func=AF.Sqrt, bias=EPS, scale=1.0 / n),
           reads=[key], writes=[key])
        op("vector", lambda e: e.reciprocal(out=out_ap, in_=tmp), reads=[key], writes=[key])

    def load_weight(dst, src_ap, kch, ncols, gcol, stage, wkey, eng="gpsimd", negate_cols=None):
        op("gpsimd", lambda e: e.dma_start(out=stage[:, 0:kch, 0:ncols], in_=src_ap.rearrange("(k p) n -> p k n", p=128)),
           writes=[("stage", id(stage))], dma=True)
        for k in range(kch):
            if gcol is None:
                op(eng, lambda e, k=k: e.tensor_copy(out=dst[:, k, :], in_=stage[:, k, 0:ncols]),
                   reads=[("stage", id(stage))], writes=[wkey])
            else:
                op(eng, lambda e, k=k: e.tensor_scalar(out=dst[:, k, :], in0=stage[:, k, 0:ncols],
                                                      scalar1=gpt[:, gcol + k:gcol + k + 1], scalar2=None, op0=ALU.mult),
                   reads=[("stage", id(stage)), "gpt"], writes=[wkey])

    def norm_tiles_to_T(src_dram_rows, ntile, xbuf, xnbuf, key):
        op("sync", lambda e: e.dma_start(out=xbuf[:, 0:ntile, :], in_=src_dram_rows.rearrange("(t p) f -> p t f", p=128)),
           writes=[("x", key)], dma=True)
        for t in range(ntile):
            op("scalar", lambda e, t=t: e.activation(out=junk, in_=xbuf[:, t, :], func=AF.Square, accum_out=stt[:, t:t + 1]),
               reads=[("x", key)], writes=["junk", "stt"])
        rstd_from_ss(stt[:, 0:ntile], stt[:, 8:8 + ntile], D, "stt")
        op("vector", lambda e: e.tensor_tensor(out=xnbuf[:, 0:ntile, :], in0=xbuf[:, 0:ntile, :],
                                               in1=stt[:, 8:8 + ntile].unsqueeze(2).to_broadcast([128, ntile, D]), op=ALU.mult),
           reads=[("x", key), "stt"], writes=[("xn", key)])

    def transpose_tile(xn_tile, hT_dst, pbank, rkeys, wkey):
        for k in range(8):
            op("tensor", lambda e, k=k: e.transpose(out=bankb(pbank)[:, k * 128:(k + 1) * 128], in_=xn_tile[:, k * 128:(k + 1) * 128], identity=idb),
               reads=list(rkeys) + ["idb"], writes=[("ps", pbank)])
        op("vector", lambda e: e.tensor_copy(out=hT_dst, in_=bankb(pbank).rearrange("p (k t) -> p k t", k=8)),
           reads=[("ps", pbank)], writes=[wkey])

    KAT = R1.alloc([128, 4, NW], BF16)
    VAT = R1.alloc([128, 4, NW], BF16)
    QAT = R1.alloc([128, 4, NX], BF16)
    cqT = RC.alloc([128, 3, NX], BF16)
    A = Region(RC.hi, SB_BYTES)
    WA = A.alloc([128, 8, 1920], BF16)
    stage = A.alloc([128, 8, 480], F32)
    xbuf = [A.alloc([128, 2, D], F32) for _ in range(2)]
    xnb = [A.alloc([128, 2, D], BF16) for _ in range(2)]
    hTw = [A.alloc([128, 8, 256], BF16) for _ in range(2)]
    cqn = A.alloc([128, 384], BF16)
    for c in range(4):
        load_weight(WA[:, :, c * 480:(c + 1) * 480], w_in[:, c * 480:(c + 1) * 480], 8, 480, 0, stage, "WA")

    def cq_tile(cols_ap, M, xcol_ap_fn, hkey, pb):
        for k in range(8):
            la = cols_ap(k)
            op("tensor", lambda e, k=k, la=la: e.matmul(bank(pb, 384, M), lhsT=la, rhs=WA[:, k, 1536:1920], start=(k == 0), stop=(k == 7)),
               reads=[hkey, "WA"], writes=[("ps", pb)])
        op("scalar", lambda e: e.activation(out=junk[0:M, 0:384], in_=bank(pb, 384, M), func=AF.Square, accum_out=stt[0:M, 16:17]),
           reads=[("ps", pb)], writes=["junk", "stt2"])
        op("scalar", lambda e: e.activation(out=stt[0:M, 16:17], in_=stt[0:M, 16:17], func=AF.Sqrt, bias=EPS, scale=1.0 / 384),
           reads=["stt2"], writes=["stt2"])
        op("vector", lambda e: e.reciprocal(out=stt[0:M, 17:18], in_=stt[0:M, 16:17]), reads=["stt2"], writes=["stt2"])
        op("vector", lambda e: e.tensor_scalar(out=cqn[0:M, :], in0=bank(pb, 384, M), scalar1=stt[0:M, 17:18], scalar2=None, op0=ALU.mult),
           reads=[("ps", pb), "stt2"], writes=["cqn"])
        for j in range(3):
            op("tensor", lambda e, j=j: e.transpose(out=bankb(pb)[:, j * 128:j * 128 + M], in_=cqn[0:M, j * 128:(j + 1) * 128], identity=idb[0:M, 0:M]),
               reads=["cqn", "idb"], writes=[("ps", pb)])
        xc = xcol_ap_fn()
        op("vector", lambda e: e.tensor_copy(out=xc, in_=bankb(pb)[:, 0:384].rearrange("p (j t) -> p j t", j=3)[:, :, 0:M]),
           reads=[("ps", pb)], writes=["cqT"])

    NGW = NW // 256
    for g in range(NGW):
        bi = g % 2
        e0 = g * 256
        norm_tiles_to_T(xw[e0:e0 + 256, :], 2, xbuf[bi], xnb[bi], ("w", bi))
        for t in range(2):
            transpose_tile(xnb[bi][:, t, :], hTw[bi][:, :, t * 128:(t + 1) * 128], t, [("xn", ("w", bi))], ("hTw", bi))
        xlo, xhi = max(e0, OWN0 - 1), min(e0 + 256, OWN0 - 1 + NX)
        jobs = [("K", 512 + 128 * c, c) for c in range(4)] + [("V", 1024 + 128 * c, c) for c in range(4)]
        if xhi > xlo:
            jobs += [("Q", 128 * c, c) for c in range(4)]
        for ji, (kind, col0, c) in enumerate(jobs):
            pb = 2 + (ji % 4)
            for k in range(8):
                op("tensor", lambda e, k=k, col0=col0, pb=pb: e.matmul(bank(pb, 256), lhsT=WA[:, k, col0:col0 + 128], rhs=hTw[bi][:, k, :], start=(k == 0), stop=(k == 7)),
                   reads=[("hTw", bi), "WA"], writes=[("ps", pb)])
            if kind == "K":
                op("vector", lambda e, c=c, pb=pb: e.tensor_copy(out=KAT[:, c, e0:e0 + 256], in_=bank(pb, 256)), reads=[("ps", pb)], writes=["KAT"])
            elif kind == "V":
                op("scalar", lambda e, c=c, pb=pb: e.copy(out=VAT[:, c, e0:e0 + 256], in_=bank(pb, 256)), reads=[("ps", pb)], writes=["VAT"])
            else:
                op("vector", lambda e, c=c, pb=pb: e.tensor_copy(out=QAT[:, c, xlo - (OWN0 - 1):xhi - (OWN0 - 1)], in_=bank(pb, 256)[:, xlo - e0:xhi - e0]),
                   reads=[("ps", pb)], writes=["QAT"])
        for t in range(2):
            et = e0 + t * 128
            if OWN0 <= et < OWN0 + OWN:
                x0 = et - (OWN0 - 1)
                cq_tile(lambda k, t=t: hTw[bi][:, k, t * 128:(t + 1) * 128], 128, lambda x0=x0: cqT[:, :, x0:x0 + 128], ("hTw", bi), 6 + t)
            if et == OWN0 - 128:
                cq_tile(lambda k, t=t: hTw[bi][:, k, t * 128 + 127:t * 128 + 128], 1, lambda: cqT[:, :, 0:1], ("hTw", bi), 6 + t)
            if et == OWN0 + OWN:
                cq_tile(lambda k, t=t: hTw[bi][:, k, t * 128:t * 128 + 1], 1, lambda: cqT[:, :, NX - 1:NX], ("hTw", bi), 6 + t)
    dump("KAT", KAT, [128, 4, NW], BF16, ["KAT"])
    dump("VAT", VAT, [128, 4, NW], BF16, ["VAT"])
    dump("QAT", QAT, [128, 4, NX], BF16, ["QAT"])
    dump("cqT", cqT, [128, 3, NX], BF16, ["cqT"])
    S.barrier()

    ynTa = RY.alloc([128, 4, NX], BF16)
    A = Region(RY.lo + 16448, SB_BYTES)
    mk = [A.alloc([128, 6, 128], F32) for _ in range(2)]
    kvft = A.alloc([128, NVT], F32)
    Vp = [A.alloc([128, NVT, 65], BF16) for _ in range(2)]
    Oacc = A.alloc([128, NX], F32)
    ya_tm = A.alloc([128, 17, 512], F32)
    Eb = [A.alloc([128, 1024], F32) for _ in range(2)]
    Pb = [A.alloc([128, 1024], BF16) for _ in range(2)]
    op("sync", lambda e: e.dma_start(out=kvft, in_=kvf), writes=["kvft"], dma=True)

    def oacc_to_tm(h, dst_tm, okey, Oacc):
        tiles = [(1 + 128 * m, 128, 1) for m in range(16)] + [(0, 2, NX - 1)]
        for g0 in range(0, 17, 4):
            tl = tiles[g0:g0 + 4]
            pb = 6 + (g0 // 4) % 2
            for i, (x0, M, step) in enumerate(tl):
                src = Oacc[0:65, x0:x0 + 128] if step == 1 else Oacc[0:65, 0:NX:NX - 1]
                op("tensor", lambda e, i=i, src=src, M=M, pb=pb: e.transpose(out=bank(pb)[0:M, i * 65:(i + 1) * 65], in_=src, identity=idf[0:65, 0:65]),
                   reads=[okey, "idf"], writes=[("ps", pb)])
            nt = len(tl)
            M = tl[0][1]
            pv = bank(pb)[0:M, 0:nt * 65].rearrange("p (t c) -> p t c", t=nt)
            op("vector", lambda e, pv=pv, nt=nt, M=M: e.reciprocal(out=rec[0:M, 0:nt], in_=pv[:, :, 64]), reads=[("ps", pb)], writes=["rec"])
            op("vector", lambda e, pv=pv, nt=nt, M=M, g0=g0: e.tensor_tensor(out=dst_tm[0:M, g0:g0 + nt, h * 64:(h + 1) * 64], in0=pv[:, :, 0:64],
                                                                         in1=rec[0:M, 0:nt].unsqueeze(2).to_broadcast([M, nt, 64]), op=ALU.mult),
               reads=[("ps", pb), "rec"], writes=["tm"])

    def tm_to_ynT(src_tm, dstT, nkey, Pb):
        for t in range(17):
            M = 128 if t < 16 else 2
            op("scalar", lambda e, t=t, M=M: e.activation(out=junk[0:M, 0:512], in_=src_tm[0:M, t, :], func=AF.Square, accum_out=stt[0:M, 32 + t:33 + t]),
               reads=["tm"], writes=["junk", "stt3"])
        rstd_from_ss(stt[:, 32:49], stt[:, 64:81], 512, "stt3")
        for t in range(17):
            M = 128 if t < 16 else 2
            pb = t % 2
            ynb = Pb[t % 2]
            op("vector", lambda e, t=t, M=M, ynb=ynb: e.tensor_scalar(out=ynb[0:M, 0:512], in0=src_tm[0:M, t, :], scalar1=stt[0:M, 64 + t:65 + t], scalar2=None, op0=ALU.mult),
               reads=["tm", "stt3"], writes=[("Pb", t % 2)])
            for j in range(4):
                op("tensor", lambda e, j=j, M=M, ynb=ynb, pb=pb: e.transpose(out=bankb(pb)[:, j * 128:j * 128 + M], in_=ynb[0:M, j * 128:(j + 1) * 128], identity=idb[0:M, 0:M]),
                   reads=[("Pb", t % 2), "idb"], writes=[("ps", pb)])
            srcv = bankb(pb)[:, 0:512].rearrange("p (j t) -> p j t", j=4)[:, :, 0:M]
            if t < 16:
                dstv = dstT[:, :, 1 + 128 * t:1 + 128 * (t + 1)]
            else:
                dstv = dstT[:, :, 0:NX:NX - 1]
            op("vector", lambda e, srcv=srcv, dstv=dstv: e.tensor_copy(out=dstv, in_=srcv), reads=[("ps", pb)], writes=[nkey])

    for h in range(8):
        pr, hs = h // 2, (h % 2) * 64
        vb = Vp[h % 2]
        mb = mk[h % 2]
        op("sync", lambda e, h=h, mb=mb: e.dma_start(out=mb, in_=masks[h]), writes=[("mk", h % 2)], dma=True)
        for v0 in range(0, NVT, 8):
            vts = VLIST[v0:v0 + 8]
            pb = 4 + (v0 // 8) % 2
            for i, (r, es, nk) in enumerate(vts):
                op("tensor", lambda e, i=i, r=r, es=es, nk=nk, pb=pb: e.transpose(out=bankb(pb)[0:nk, i * 64:(i + 1) * 64],
                                                                               in_=VAT[hs:hs + 64, pr, es:es + r * (nk - 1) + 1:r], identity=idb[hs:hs + 64, hs:hs + 64]),
                   reads=["VAT", "idb"], writes=[("ps", pb)])
            nfull = [i for i, v in enumerate(vts) if v[2] == 128]
            i = 0
            while i < len(vts):
                nk = vts[i][2]
                j = i
                while j + 1 < len(vts) and vts[j + 1][2] == nk:
                    j += 1
                cnt_ = j - i + 1
                srcv = bankb(pb)[0:nk, i * 64:(j + 1) * 64].rearrange("p (t c) -> p t c", t=cnt_)
                op("vector", lambda e, srcv=srcv, i=i, cnt_=cnt_, nk=nk, v0=v0: e.tensor_tensor(
                    out=vb[0:nk, v0 + i:v0 + i + cnt_, 0:64], in0=srcv,
                    in1=kvft[0:nk, v0 + i:v0 + i + cnt_].unsqueeze(2).to_broadcast([nk, cnt_, 64]), op=ALU.mult),
                   reads=[("ps", pb), "kvft"], writes=[("Vp", h % 2)])
                i = j + 1
        op("gpsimd", lambda e, vb=vb: e.tensor_copy(out=vb[:, :, 64], in_=kvft), reads=["kvft"], writes=[("Vp", h % 2)])
        first = True
        gi = 0
        for ri, r in enumerate((1, 4, 16)):
            for (kind, g, blks) in DGROUPS[r]:
                sb = 0 + 2 * (gi % 2)
                eb = Eb[gi % 2]
                pbuf = Pb[gi % 2]
                ob = 6 + gi % 2
                gi += 1
                nb = len(blks)
                N = blks[0][1]
                for bi_, (x0, Nq, t0, t1) in enumerate(blks):
                    qv = QAT[hs:hs + 64, pr, x0:x0 + r * (Nq - 1) + 1:r]
                    for role, tix in ((0, t0), (1, t1)):
                        (rr, es, nk) = VLIST[tix]
                        op("tensor", lambda e, role=role, es=es, nk=nk, qv=qv, bi_=bi_, Nq=Nq, sb=sb: e.matmul(
                            bank(sb + role)[0:nk, bi_ * 128:bi_ * 128 + Nq], lhsT=KAT[hs:hs + 64, pr, es:es + r * (nk - 1) + 1:r], rhs=qv, start=True, stop=True),
                           reads=["KAT", "QAT"], writes=[("ps", sb + role)])
                if kind == "reg":
                    sv = ps[:, sb * 512:(sb + 2) * 512]
                    op("scalar", lambda e, sv=sv, eb=eb: e.activation(out=eb, in_=sv, func=AF.Exp, scale=0.125),
                       reads=[("ps", sb), ("ps", sb + 1)], writes=[("Eb", id(eb))])
                    op("vector", lambda e, eb=eb, pbuf=pbuf, mb=mb, ri=ri: e.tensor_tensor(
                        out=pbuf.rearrange("p (r b q) -> p r b q", r=2, b=4), in0=eb.rearrange("p (r b q) -> p r b q", r=2, b=4),
                        in1=mb[:, 2 * ri:2 * ri + 2, :].unsqueeze(2).to_broadcast([128, 2, 4, 128]), op=ALU.mult),
                       reads=[("Eb", id(eb)), ("mk", h % 2)], writes=[("Pbuf", id(pbuf))])
                else:
                    op("scalar", lambda e, eb=eb, sb=sb: e.activation(out=eb[:, 0:256:128], in_=bank(sb)[:, 0:256:128], func=AF.Exp, scale=0.125),
                       reads=[("ps", sb)], writes=[("Eb", id(eb))])
                    op("scalar", lambda e, eb=eb, sb=sb: e.activation(out=eb[0:1, 512:768:128], in_=bank(sb + 1)[0:1, 0:256:128], func=AF.Exp, scale=0.125),
                       reads=[("ps", sb + 1)], writes=[("Eb", id(eb))])
                    op("vector", lambda e, eb=eb, pbuf=pbuf, mb=mb, ri=ri: e.tensor_tensor(
                        out=pbuf[:, 0:256:128], in0=eb[:, 0:256:128], in1=mb[:, 2 * ri, 0:1].to_broadcast([128, 2]), op=ALU.mult),
                       reads=[("Eb", id(eb)), ("mk", h % 2)], writes=[("Pbuf", id(pbuf))])
                    op("vector", lambda e, eb=eb, pbuf=pbuf, mb=mb, ri=ri: e.tensor_tensor(
                        out=pbuf[0:1, 512:768:128], in0=eb[0:1, 512:768:128], in1=mb[0:1, 2 * ri + 1, 0:1].to_broadcast([1, 2]), op=ALU.mult),
                       reads=[("Eb", id(eb)), ("mk", h % 2)], writes=[("Pbuf", id(pbuf))])
                for bi_, (x0, Nq, t0, t1) in enumerate(blks):
                    for role, tix in ((0, t0), (1, t1)):
                        nk = VLIST[tix][2]
                        op("tensor", lambda e, role=role, tix=tix, nk=nk, bi_=bi_, Nq=Nq, ob=ob, pbuf=pbuf, vb=vb: e.matmul(
                            bank(ob)[0:65, bi_ * 128:bi_ * 128 + Nq], lhsT=vb[0:nk, tix, :], rhs=pbuf[0:nk, role * 512 + bi_ * 128:role * 512 + bi_ * 128 + Nq],
                            start=(role == 0), stop=(role == 1)),
                           reads=[("Pbuf", id(pbuf)), ("Vp", h % 2)], writes=[("ps", ob)])
                if kind == "reg":
                    src = bank(ob)[0:65, :].rearrange("p (c q) -> p c q", c=4)
                    if r == 1:
                        dst = Oacc[0:65, 1 + 512 * g:1 + 512 * (g + 1)].rearrange("p (c q) -> p c q", c=4)
                    elif r == 4:
                        dst = Oacc[0:65, 1 + 512 * g:1 + 512 * (g + 1)].rearrange("p (q c) -> p c q", c=4)
                    else:
                        dst = Oacc[0:65, 1:1 + OWN].rearrange("p (q c) -> p c q", c=16)[:, 4 * g:4 * g + 4, :]
                else:
                    src = bank(ob)[0:65, 0:256:128]
                    dst = Oacc[0:65, 0:NX:NX - 1]
                if r == 1:
                    op("vector", lambda e, src=src, dst=dst: e.tensor_copy(out=dst, in_=src), reads=[("ps", ob)], writes=["Oacc"])
                else:
                    op("vector", lambda e, src=src, dst=dst: e.tensor_tensor(out=dst, in0=src, in1=dst, op=ALU.add), reads=[("ps", ob), "Oacc"], writes=["Oacc"])
        oacc_to_tm(h, ya_tm, "Oacc", Oacc)
    tm_to_ynT(ya_tm, ynTa, "ynTa", Pb)
    dump("ynTa", ynTa, [128, 4, NX], BF16, ["ynTa"])
    dump("ya_tm", ya_tm, [128, 17, 512], F32, ["tm"])
    S.barrier()

    R1.off = R1.lo
    ckvT = R1.alloc([128, 2, S_LEN], BF16)
    KT = R1.alloc([96, S_LEN], BF16)
    R1s_lo = R1.off
    RY.off = RY.lo + 16448
    Wukv = RY.alloc([128, 2, 1024], BF16)
    A = Region(RY.off, SB_BYTES)
    Wkvl = A.alloc([128, 8, 288], BF16)
    stage2 = A.alloc([128, 8, 512], F32)
    xbuf = [A.alloc([128, 2, D], F32) for _ in range(2)]
    xnb = [A.alloc([128, 2, D], BF16) for _ in range(2)]
    hTk = [A.alloc([128, 8, 128], BF16) for _ in range(2)]
    ckvn = [A.alloc([128, 256], BF16) for _ in range(2)]
    kr_tm = A.alloc([128, 64, 32], F32)
    rkt = A.alloc([128, 2, 64, 16], F32)
    kr_pad = A.alloc([128, 64, 96], BF16)
    rt = [A.alloc([128, 64, 16], F32) for _ in range(2)]
    load_weight(Wkvl, w_in[:, 1920:2208], 8, 288, 0, stage2, "Wkvl")
    load_weight(Wukv[:, :, 0:512], w_ukv[:, 0:512], 2, 512, 11, stage2, "Wukv")
    load_weight(Wukv[:, :, 512:1024], w_ukv[:, 512:1024], 2, 512, 11, stage2, "Wukv")
    op("sync", lambda e: e.dma_start(out=rkt, in_=rk), writes=["rkt"], dma=True)
    op("gpsimd", lambda e: e.memset(kr_pad, 0.0), writes=["kr_pad"])
    for g in range(S_LEN // 256):
        bi = g % 2
        norm_tiles_to_T(xf[g * 256:(g + 1) * 256, :], 2, xbuf[bi], xnb[bi], ("k", bi))
        for t in range(2):
            tt = g * 2 + t
            hb = hTk[tt % 2]
            transpose_tile(xnb[bi][:, t, :], hb, tt % 2, [("xn", ("k", bi))], ("hTk", tt % 2))
            pb = 2 + tt % 2
            for k in range(8):
                op("tensor", lambda e, k=k, hb=hb, pb=pb: e.matmul(bank(pb, 288), lhsT=hb[:, k, :], rhs=Wkvl[:, k, :], start=(k == 0), stop=(k == 7)),
                   reads=[("hTk", tt % 2), "Wkvl"], writes=[("ps", pb)])
            sc = 96 + (tt % 8)
            sk = ("stt4", tt % 8)
            op("scalar", lambda e, pb=pb, sc=sc: e.activation(out=junk[:, 0:256], in_=bank(pb, 256), func=AF.Square, accum_out=stt[:, sc:sc + 1]),
               reads=[("ps", pb)], writes=["junk", sk])
            op("scalar", lambda e, sc=sc: e.activation(out=stt[:, sc:sc + 1], in_=stt[:, sc:sc + 1], func=AF.Sqrt, bias=EPS, scale=1.0 / 256),
               reads=[sk], writes=[sk])
            op("vector", lambda e, sc=sc: e.reciprocal(out=stt[:, sc + 8:sc + 9], in_=stt[:, sc:sc + 1]), reads=[sk], writes=[sk])
            cb = ckvn[tt % 2]
            op("vector", lambda e, pb=pb, sc=sc, cb=cb: e.tensor_scalar(out=cb, in0=bank(pb, 256), scalar1=stt[:, sc + 8:sc + 9], scalar2=None, op0=ALU.mult),
               reads=[("ps", pb), sk], writes=[("ckvn", tt % 2)])
            op("vector", lambda e, pb=pb, tt=tt: e.tensor_copy(out=kr_tm[:, tt, :], in_=bank(pb, 288)[:, 256:288]), reads=[("ps", pb)], writes=["kr_tm"])
            pb2 = 4 + tt % 2
            for j in range(2):
                op("tensor", lambda e, j=j, cb=cb, pb2=pb2: e.transpose(out=bankb(pb2)[:, j * 128:(j + 1) * 128], in_=cb[:, j * 128:(j + 1) * 128], identity=idb),
                   reads=[("ckvn", tt % 2), "idb"], writes=[("ps", pb2)])
            op("scalar", lambda e, pb2=pb2, tt=tt: e.copy(out=ckvT[:, :, tt * 128:(tt + 1) * 128], in_=bankb(pb2)[:, 0:256].rearrange("p (j t) -> p j t", j=2)),
               reads=[("ps", pb2)], writes=["ckvT"])
    x1, x2 = kr_tm[:, :, 0:16], kr_tm[:, :, 16:32]
    cosk, sink = rkt[:, 0], rkt[:, 1]
    op("vector", lambda e: e.tensor_tensor(out=rt[0], in0=x1, in1=cosk, op=ALU.mult), reads=["kr_tm", "rkt"], writes=["rt0"])
    op("vector", lambda e: e.tensor_tensor(out=rt[1], in0=x2, in1=sink, op=ALU.mult), reads=["kr_tm", "rkt"], writes=["rt1"])
    op("vector", lambda e: e.tensor_tensor(out=kr_pad[:, :, 64:80], in0=rt[0], in1=rt[1], op=ALU.subtract), reads=["rt0", "rt1", "kr_pad"], writes=["kr_pad"])
    op("vector", lambda e: e.tensor_tensor(out=rt[0], in0=x2, in1=cosk, op=ALU.mult), reads=["kr_tm", "rkt", "kr_pad"], writes=["rt0"])
    op("vector", lambda e: e.tensor_tensor(out=rt[1], in0=x1, in1=sink, op=ALU.mult), reads=["kr_tm", "rkt", "kr_pad"], writes=["rt1"])
    op("vector", lambda e: e.tensor_tensor(out=kr_pad[:, :, 80:96], in0=rt[0], in1=rt[1], op=ALU.add), reads=["rt0", "rt1", "kr_pad"], writes=["kr_pad"])
    for g8 in range(8):
        pb = 6 + g8 % 2
        for i in range(8):
            tt = g8 * 8 + i
            op("tensor", lambda e, i=i, tt=tt, pb=pb: e.transpose(out=bankb(pb)[0:96, i * 128:(i + 1) * 128], in_=kr_pad[:, tt, :], identity=idb),
               reads=["kr_pad", "idb"], writes=[("ps", pb)])
        op("vector", lambda e, pb=pb, g8=g8: e.tensor_copy(out=KT[64:96, g8 * 1024:(g8 + 1) * 1024], in_=bankb(pb)[64:96, :]),
           reads=[("ps", pb)], writes=["KTr"])
    dump("ckvT", ckvT, [128, 2, S_LEN], BF16, ["ckvT"])
    dump("KTr", KT[64:96, :], [32, S_LEN], BF16, ["KTr"])
    S.barrier()

    ynTb = RY.alloc([128, 4, NX], BF16)
    RT.off = RT.lo
    QBT = RT.alloc([96, 8, NX], BF16)
    R1s = Region(R1s_lo, R1.hi)
    Wuq = R1s.alloc([128, 3, 768], BF16)
    Wrot = R1s.alloc([128, 3, 8, 96], BF16)
    stage3 = R1s.alloc([128, 3, 768], F32)
    rqt = RT.alloc([96, 2, NX], F32)
    tq = [RT.alloc([96, 410], F32) for _ in range(2)]
    load_weight(Wuq, w_uq, 3, 768, 8, stage3, "Wuq")
    op("gpsimd", lambda e: e.memset(Wrot, 0.0), writes=["Wrot"])
    Wuq4 = Wuq.rearrange("p k (h c) -> p k h c", h=8)
    for k in range(3):
        op("gpsimd", lambda e, k=k: e.tensor_scalar(out=Wrot[:, k, :, 64:80], in0=Wuq4[:, k, :, 80:96], scalar1=-1.0, scalar2=None, op0=ALU.mult),
           reads=["Wuq", "Wrot"], writes=["Wrot"])
        op("gpsimd", lambda e, k=k: e.tensor_copy(out=Wrot[:, k, :, 80:96], in_=Wuq4[:, k, :, 64:80]), reads=["Wuq", "Wrot"], writes=["Wrot"])
    op("sync", lambda e: e.dma_start(out=rqt[64:96], in_=rq), writes=["rqt"], dma=True)
    qi = 0
    for h in range(8):
        for (c0, cn) in XBLK:
            pa, pr_ = 2 * (qi % 2), 1 + 2 * (qi % 2)
            tb_ = tq[qi % 2]
            tk = ("tq", qi % 2)
            qi += 1
            for k in range(3):
                op("tensor", lambda e, k=k, h=h, c0=c0, cn=cn, pa=pa: e.matmul(bank(pa, cn, 96), lhsT=Wuq[:, k, h * 96:(h + 1) * 96], rhs=cqT[:, k, c0:c0 + cn], start=(k == 0), stop=(k == 2)),
                   reads=["Wuq", "cqT"], writes=[("ps", pa)])
            for k in range(3):
                op("tensor", lambda e, k=k, h=h, c0=c0, cn=cn, pr_=pr_: e.matmul(bank(pr_, cn, 96), lhsT=Wrot[:, k, h, :], rhs=cqT[:, k, c0:c0 + cn], start=(k == 0), stop=(k == 2)),
                   reads=["Wrot", "cqT"], writes=[("ps", pr_)])
            op("scalar", lambda e, h=h, c0=c0, cn=cn, pa=pa: e.copy(out=QBT[0:64, h, c0:c0 + cn], in_=bank(pa, cn, 64)), reads=[("ps", pa)], writes=["QBT"])
            op("vector", lambda e, c0=c0, cn=cn, pa=pa, tb_=tb_: e.tensor_tensor(out=tb_[64:96, 0:cn], in0=bank(pa, cn, 32, 64), in1=rqt[64:96, 0, c0:c0 + cn], op=ALU.mult),
               reads=[("ps", pa), "rqt"], writes=[tk])
            op("vector", lambda e, h=h, c0=c0, cn=cn, pr_=pr_: e.tensor_tensor(out=QBT[64:96, h, c0:c0 + cn], in0=bank(pr_, cn, 32, 64), in1=rqt[64:96, 1, c0:c0 + cn], op=ALU.mult),
               reads=[("ps", pr_), "rqt"], writes=["QBT"])
            op("vector", lambda e, h=h, c0=c0, cn=cn, tb_=tb_: e.tensor_tensor(out=QBT[64:96, h, c0:c0 + cn], in0=QBT[64:96, h, c0:c0 + cn], in1=tb_[64:96, 0:cn], op=ALU.add),
               reads=[tk, "QBT"], writes=["QBT"])
    dump("QBT", QBT, [96, 8, NX], BF16, ["QBT"])
    S.barrier()
    RT.off = RT.lo + 32832
    yb_tm = RT.alloc([128, 17, 512], F32)
    ynb_tmp = [RT.alloc([128, 512], BF16) for _ in range(2)]
    R1s = Region(R1s_lo, R1.hi)
    Vb = [R1s.alloc([128, 64, 65], BF16) for _ in range(2)]
    PT = [R1s.alloc([128, 2, 410], BF16) for _ in range(3)]
    OaccB = R1s.alloc([128, NX], F32)
    for b_ in range(2):
        op("gpsimd", lambda e, b_=b_: e.memset(Vb[b_][:, :, 64], 1.0), writes=[("Vb", b_)])
    scale_b = 96.0 ** -0.5
    QG = [(0, 2), (2, 2), (4, 1)]
    si = 0
    for h in range(8):
        vb = Vb[h % 2]
        for nb_ in range(16):
            pb = 6 + nb_ % 2
            for j in range(2):
                op("tensor", lambda e, j=j, nb_=nb_, pb=pb, h=h: e.matmul(bank(pb, 512, 64), lhsT=Wukv[:, j, h * 128:h * 128 + 64], rhs=ckvT[:, j, nb_ * 512:(nb_ + 1) * 512], start=(j == 0), stop=(j == 1)),
                   reads=["Wukv", "ckvT"], writes=[("ps", pb)])
            op("vector", lambda e, nb_=nb_, pb=pb: e.tensor_copy(out=KT[0:64, nb_ * 512:(nb_ + 1) * 512], in_=bank(pb, 512, 64)),
               reads=[("ps", pb)], writes=["KTn"])
        for k8 in range(8):
            pb = 6 + k8 % 2
            for i in range(8):
                kt = k8 * 8 + i
                for j in range(2):
                    op("tensor", lambda e, j=j, kt=kt, i=i, pb=pb, h=h: e.matmul(bank(pb)[:, i * 64:(i + 1) * 64], lhsT=ckvT[:, j, kt * 128:(kt + 1) * 128], rhs=Wukv[:, j, h * 128 + 64:h * 128 + 128], start=(j == 0), stop=(j == 1)),
                       reads=["Wukv", "ckvT"], writes=[("ps", pb)])
            op("vector", lambda e, k8=k8, pb=pb, vb=vb: e.tensor_copy(out=vb[:, k8 * 8:(k8 + 1) * 8, 0:64], in_=bank(pb).rearrange("p (t c) -> p t c", t=8)),
               reads=[("ps", pb)], writes=[("Vb", h % 2)])
        for (b0, nbk) in QG:
            ob = 4

            def emit_S(kt, si_):
                sbk = 2 * (si_ % 2)
                for bb in range(nbk):
                    c0, cn = XBLK[b0 + bb]
                    op("tensor", lambda e, bb=bb, c0=c0, cn=cn, kt=kt, sbk=sbk: e.matmul(bank(sbk + bb, cn), lhsT=KT[0:96, kt * 128:(kt + 1) * 128], rhs=QBT[0:96, h, c0:c0 + cn], start=True, stop=True),
                       reads=["KTn", "KTr", "QBT"], writes=[("ps", sbk + bb)])
            emit_S(0, si)
            for kt in range(64):
                sbk = 2 * (si % 2)
                pt = PT[si % 3]
                ptk = ("PT", si % 3)
                if kt + 1 < 64:
                    emit_S(kt + 1, si + 1)
                sv = ps[:, sbk * 512:(sbk + nbk) * 512].rearrange("p (a b) -> p a b", a=nbk)[:, :, 0:410]
                op("scalar", lambda e, sv=sv, pt=pt: e.activation(out=pt[:, 0:nbk, :], in_=sv, func=AF.Exp, scale=scale_b),
                   reads=[("ps", sbk + bb) for bb in range(nbk)], writes=[ptk])
                for bb in range(nbk):
                    op("tensor", lambda e, bb=bb, kt=kt, pt=pt: e.matmul(bank(ob + bb, 410, 65), lhsT=vb[:, kt, :], rhs=pt[:, bb, :], start=(kt == 0), stop=(kt == 63)),
                       reads=[ptk, ("Vb", h % 2)], writes=[("ps", ob + bb)])
                si += 1
            ov = ps[0:65, ob * 512:(ob + nbk) * 512].rearrange("p (a b) -> p a b", a=nbk)[:, :, 0:410]
            c0 = XBLK[b0][0]
            op("vector", lambda e, ov=ov, c0=c0, nbk=nbk: e.tensor_copy(out=OaccB[0:65, c0:c0 + nbk * 410].rearrange("p (a b) -> p a b", a=nbk), in_=ov),
               reads=[("ps", ob + bb) for bb in range(nbk)], writes=["OaccB"])
        oacc_to_tm(h, yb_tm, "OaccB", OaccB)
    tm_to_ynT(yb_tm, ynTb, "ynTb", ynb_tmp)
    dump("ynTb", ynTb, [128, 4, NX], BF16, ["ynTb"])
    dump("yb_tm", yb_tm, [128, 17, 512], F32, ["tm"])
    S.barrier()

    RT.off = RT.lo
    hTf = RT.alloc([128, 8, NX], BF16)
    A = Region(R1.lo, RC.hi)
    Wo = A.alloc([128, 8, D], BF16)
    stage4 = A.alloc([128, 8, 256], F32)
    gpo = A.alloc([128, 2, D], F32)
    xt_ = [A.alloc([128, D], F32) for _ in range(2)]
    xm_ = [A.alloc([128, D], F32) for _ in range(2)]
    hn_ = [A.alloc([128, D], BF16) for _ in range(2)]
    for c in range(4):
        load_weight(Wo[:, :, c * 256:(c + 1) * 256], w_o[:, c * 256:(c + 1) * 256], 8, 256, 13, stage4, "Wo")
    op("sync", lambda e: e.dma_start(out=gpo, in_=gpost), writes=["gpo"], dma=True)
    for t in range(17):
        M = 128 if t < 16 else 2
        bi = t % 2
        xt, xm, hn = xt_[bi], xm_[bi], hn_[bi]
        xk, mkk, hk = ("xt", bi), ("xm", bi), ("hn", bi)
        if t < 16:
            op("sync", lambda e, t=t, xt=xt: e.dma_start(out=xt, in_=xw[OWN0 + 128 * t:OWN0 + 128 * (t + 1), :]), writes=[xk], dma=True)
            cols = lambda k, t=t: (ynTa if k < 4 else ynTb)[:, k % 4, 1 + 128 * t:1 + 128 * (t + 1)]
        else:
            op("sync", lambda e, xt=xt: e.dma_start(out=xt[0:1, :], in_=xw[OWN0 - 1:OWN0, :]), writes=[xk], dma=True)
            op("sync", lambda e, xt=xt: e.dma_start(out=xt[1:2, :], in_=xw[OWN0 + OWN:OWN0 + OWN + 1, :]), writes=[xk], dma=True)
            cols = lambda k: (ynTa if k < 4 else ynTb)[:, k % 4, 0:NX:NX - 1]
        pb = 2 * (t % 2)
        for n2 in range(2):
            for k in range(8):
                la = cols(k)
                op("tensor", lambda e, k=k, n2=n2, M=M, pb=pb, la=la: e.matmul(bank(pb + n2, 512, M), lhsT=la, rhs=Wo[:, k, n2 * 512:(n2 + 1) * 512], start=(k == 0), stop=(k == 7)),
                   reads=["ynTa", "ynTb", "Wo"], writes=[("ps", pb + n2)])
        yv = ps[0:M, pb * 512:(pb + 2) * 512]
        sc = 128 + 2 * (t % 4)
        sk = ("stt5", t % 4)
        op("scalar", lambda e, yv=yv, M=M, sc=sc: e.activation(out=junk[0:M, :], in_=yv, func=AF.Square, accum_out=stt[0:M, sc:sc + 1]),
           reads=[("ps", pb), ("ps", pb + 1)], writes=["junk", sk])
        op("scalar", lambda e, M=M, sc=sc: e.activation(out=stt[0:M, sc:sc + 1], in_=stt[0:M, sc:sc + 1], func=AF.Sqrt, bias=EPS, scale=1.0 / D), reads=[sk], writes=[sk])
        op("vector", lambda e, M=M, sc=sc: e.reciprocal(out=stt[0:M, sc + 1:sc + 2], in_=stt[0:M, sc:sc + 1]), reads=[sk], writes=[sk])
        op("vector", lambda e, yv=yv, M=M, sc=sc, xm=xm: e.scalar_tensor_tensor(out=xm[0:M, :], in0=yv, scalar=stt[0:M, sc + 1:sc + 2], in1=gpo[0:M, 0, :], op0=ALU.mult, op1=ALU.mult),
           reads=[("ps", pb), ("ps", pb + 1), sk, "gpo"], writes=[mkk])
        op("gpsimd", lambda e, M=M, xm=xm, xt=xt: e.tensor_tensor(out=xm[0:M, :], in0=xm[0:M, :], in1=xt[0:M, :], op=ALU.add), reads=[mkk, xk], writes=[mkk])
        if t < 16:
            op("sync", lambda e, t=t, xm=xm: e.dma_start(out=xmid[128 * t:128 * (t + 1), :], in_=xm), reads=[mkk], writes=["xmid"], dma=True)
        sc2 = 136 + 2 * (t % 4)
        sk2 = ("stt6", t % 4)
        op("scalar", lambda e, M=M, sc2=sc2, xm=xm: e.activation(out=junk[0:M, :], in_=xm[0:M, :], func=AF.Square, accum_out=stt[0:M, sc2:sc2 + 1]),
           reads=[mkk], writes=["junk", sk2])
        op("scalar", lambda e, M=M, sc2=sc2: e.activation(out=stt[0:M, sc2:sc2 + 1], in_=stt[0:M, sc2:sc2 + 1], func=AF.Sqrt, bias=EPS, scale=1.0 / D), reads=[sk2], writes=[sk2])
        op("vector", lambda e, M=M, sc2=sc2: e.reciprocal(out=stt[0:M, sc2 + 1:sc2 + 2], in_=stt[0:M, sc2:sc2 + 1]), reads=[sk2], writes=[sk2])
        op("vector", lambda e, M=M, sc2=sc2, xm=xm, hn=hn: e.tensor_scalar(out=hn[0:M, :], in0=xm[0:M, :], scalar1=stt[0:M, sc2 + 1:sc2 + 2], scalar2=None, op0=ALU.mult),
           reads=[mkk, sk2], writes=[hk])
        pb2 = 4 + t % 2
        for k in range(8):
            op("tensor", lambda e, k=k, M=M, hn=hn, pb2=pb2: e.transpose(out=bankb(pb2)[:, k * 128:k * 128 + M], in_=hn[0:M, k * 128:(k + 1) * 128], identity=idb[0:M, 0:M]),
               reads=[hk, "idb"], writes=[("ps", pb2)])
        srcv = bankb(pb2).rearrange("p (k t) -> p k t", k=8)[:, :, 0:M]
        dstv = hTf[:, :, 1 + 128 * t:1 + 128 * (t + 1)] if t < 16 else hTf[:, :, 0:NX:NX - 1]
        op("vector", lambda e, srcv=srcv, dstv=dstv: e.tensor_copy(out=dstv, in_=srcv), reads=[("ps", pb2)], writes=["hTf"])
    dump("hTf", hTf, [128, 8, NX], BF16, ["hTf"])
    S.barrier()

    aT = Region(R1.lo, RC.hi).alloc([128, 22, OWN], BF16)
    A = Region(RY.lo, RY.hi)
    stg = [A.alloc([128, 8, 256], F32) for _ in range(2)]
    Wub = [A.alloc([128, 8, 256], BF16) for _ in range(2)]
    cwt = A.alloc([128, 44, 4], F32)
    ufl = A.alloc([128, 2], F32)
    ctmp = A.alloc([128, OWN], F32)
    A = Region(RT.lo + 32832, SB_BYTES)
    ug = A.alloc([128, NX], F32)
    uv = A.alloc([128, NX], F32)
    cg = A.alloc([128, OWN], F32)
    cv = A.alloc([128, OWN], F32)
    op("sync", lambda e: e.dma_start(out=cwt, in_=cwb), writes=["cwt"], dma=True)
    op("sync", lambda e: e.dma_start(out=ufl, in_=uflag), writes=["ufl"], dma=True)
    for j in range(22):
        bi = j % 2
        sg, wb = stg[bi], Wub[bi]
        sgk, wbk = ("stg", bi), ("Wub", bi)
        op("gpsimd", lambda e, j=j, sg=sg: e.dma_start(out=sg[:, :, 0:128], in_=w_up[:, j * 128:(j + 1) * 128].rearrange("(k p) n -> p k n", p=128)), writes=[sgk], dma=True)
        op("gpsimd", lambda e, j=j, sg=sg: e.dma_start(out=sg[:, :, 128:256], in_=w_up[:, DFF + j * 128:DFF + (j + 1) * 128].rearrange("(k p) n -> p k n", p=128)), writes=[sgk], dma=True)
        for k in range(8):
            op("gpsimd", lambda e, k=k, sg=sg, wb=wb: e.tensor_scalar(out=wb[:, k, :], in0=sg[:, k, :], scalar1=gpt[:, 21 + k:22 + k], scalar2=None, op0=ALU.mult),
               reads=[sgk, "gpt"], writes=[wbk])
        for half, (ub, ukey) in enumerate(((ug, "ug"), (uv, "uv"))):
            for bx, (c0, cn) in enumerate(XBLK):
                pb = (half * 5 + bx) % 6
                for k in range(8):
                    op("tensor", lambda e, k=k, half=half, c0=c0, cn=cn, pb=pb, wb=wb: e.matmul(bank(pb, cn), lhsT=wb[:, k, half * 128:(half + 1) * 128], rhs=hTf[:, k, c0:c0 + cn], start=(k == 0), stop=(k == 7)),
                       reads=[wbk, "hTf"], writes=[("ps", pb)])
                op("scalar", lambda e, ub=ub, c0=c0, cn=cn, pb=pb: e.copy(out=ub[:, c0:c0 + cn], in_=bank(pb, cn)), reads=[("ps", pb)], writes=[ukey])
            op("gpsimd", lambda e, ub=ub: e.tensor_tensor(out=ub[:, 0:NX:NX - 1], in0=ub[:, 0:NX:NX - 1], in1=ufl, op=ALU.mult), reads=[ukey, "ufl"], writes=[ukey])
        fg, fv = j, 22 + j
        op("scalar", lambda e, fg=fg: e.activation(out=cg, in_=ug[:, 1:1 + OWN], func=AF.Identity, bias=cwt[:, fg, 3:4], scale=cwt[:, fg, 1:2]), reads=["ug", "cwt"], writes=["cg"])
        op("vector", lambda e, fg=fg: e.scalar_tensor_tensor(out=cg, in0=ug[:, 0:OWN], scalar=cwt[:, fg, 0:1], in1=cg, op0=ALU.mult, op1=ALU.add), reads=["ug", "cwt", "cg"], writes=["cg"])
        op("vector", lambda e, fg=fg: e.scalar_tensor_tensor(out=cg, in0=ug[:, 2:2 + OWN], scalar=cwt[:, fg, 2:3], in1=cg, op0=ALU.mult, op1=ALU.add), reads=["ug", "cwt", "cg"], writes=["cg"])
        op("scalar", lambda e, fv=fv: e.activation(out=cv, in_=uv[:, 1:1 + OWN], func=AF.Identity, bias=cwt[:, fv, 3:4], scale=cwt[:, fv, 1:2]), reads=["uv", "cwt"], writes=["cv"])
        op("gpsimd", lambda e, fv=fv: e.tensor_scalar(out=ctmp, in0=uv[:, 0:OWN], scalar1=cwt[:, fv, 0:1], scalar2=None, op0=ALU.mult), reads=["uv", "cwt"], writes=["ctmp"])
        op("gpsimd", lambda e: e.tensor_tensor(out=cv, in0=cv, in1=ctmp, op=ALU.add), reads=["ctmp", "cv"], writes=["cv"])
        op("gpsimd", lambda e, fv=fv: e.tensor_scalar(out=ctmp, in0=uv[:, 2:2 + OWN], scalar1=cwt[:, fv, 2:3], scalar2=None, op0=ALU.mult), reads=["uv", "cwt"], writes=["ctmp"])
        op("gpsimd", lambda e: e.tensor_tensor(out=cv, in0=cv, in1=ctmp, op=ALU.add), reads=["ctmp", "cv"], writes=["cv"])
        op("scalar", lambda e: e.activation(out=cg, in_=cg, func=AF.Gelu_apprx_tanh), reads=["cg"], writes=["cg"])
        op("vector", lambda e, j=j: e.tensor_tensor(out=aT[:, j, :], in0=cg, in1=cv, op=ALU.mult), reads=["cg", "cv"], writes=["aT"])
    dump("aT", aT, [128, 22, OWN], BF16, ["aT"])
    S.barrier()

    A = Region(RY.lo, SB_BYTES)
    Wdn = A.alloc([128, 22, D], BF16)
    stage5 = A.alloc([128, 2, D], F32)
    gpo2 = A.alloc([128, D], F32)
    xm2 = [A.alloc([128, D], F32) for _ in range(2)]
    ot = [A.alloc([128, D], F32) for _ in range(2)]
    for c in range(11):
        load_weight(Wdn[:, 2 * c:2 * c + 2, :], w_down[256 * c:256 * (c + 1), :], 2, D, None, stage5, "Wdn")
    op("sync", lambda e: e.dma_start(out=gpo2, in_=gpost[:, 1, :]), writes=["gpo2"], dma=True)
    for t in range(16):
        bi = t % 2
        xk, ok = ("xm2", bi), ("ot", bi)
        op("sync", lambda e, t=t, bi=bi: e.dma_start(out=xm2[bi], in_=xmid[128 * t:128 * (t + 1), :]), reads=["xmid"], writes=[xk], dma=True)
        pb = 2 * (t % 2)
        for n2 in range(2):
            for j in range(22):
                op("tensor", lambda e, j=j, n2=n2, t=t, pb=pb: e.matmul(bank(pb + n2), lhsT=aT[:, j, 128 * t:128 * (t + 1)], rhs=Wdn[:, j, n2 * 512:(n2 + 1) * 512], start=(j == 0), stop=(j == 21)),
                   reads=["aT", "Wdn"], writes=[("ps", pb + n2)])
        yv = ps[:, pb * 512:(pb + 2) * 512]
        sc = 144 + 2 * (t % 4)
        sk = ("stt7", t % 4)
        op("scalar", lambda e, yv=yv, sc=sc: e.activation(out=junk, in_=yv, func=AF.Square, accum_out=stt[:, sc:sc + 1]),
           reads=[("ps", pb), ("ps", pb + 1)], writes=["junk", sk])
        op("scalar", lambda e, sc=sc: e.activation(out=stt[:, sc:sc + 1], in_=stt[:, sc:sc + 1], func=AF.Sqrt, bias=EPS, scale=1.0 / D), reads=[sk], writes=[sk])
        op("vector", lambda e, sc=sc: e.reciprocal(out=stt[:, sc + 1:sc + 2], in_=stt[:, sc:sc + 1]), reads=[sk], writes=[sk])
        op("vector", lambda e, yv=yv, sc=sc, bi=bi: e.scalar_tensor_tensor(out=ot[bi], in0=yv, scalar=stt[:, sc + 1:sc + 2], in1=gpo2, op0=ALU.mult, op1=ALU.mult),
           reads=[("ps", pb), ("ps", pb + 1), sk, "gpo2"], writes=[ok])
        op("gpsimd", lambda e, bi=bi: e.tensor_tensor(out=ot[bi], in0=ot[bi], in1=xm2[bi], op=ALU.add), reads=[ok, xk], writes=[ok])
        op("sync", lambda e, t=t, bi=bi: e.dma_start(out=yout[128 * t:128 * (t + 1), :], in_=ot[bi]), reads=[ok], dma=True)
    S.emit()
    return nc


_CACHE = {}


def _consts():
    if "c" in _CACHE:
        return _CACHE["c"]
    slopes = np.exp2(-8.0 * np.arange(1, 9, dtype=np.float32) / 8).astype(np.float32)
    k = np.arange(128)[:, None]
    q = np.arange(128)[None, :]
    masks = np.zeros((8, 128, 6, 128), np.float32)
    for h in range(8):
        for ri, r in enumerate((1, 4, 16)):
            d0 = k - 64 - q
            d1 = k + 64 - q
            masks[h, :, 2 * ri, :] = np.where(k >= q, np.exp(-slopes[h] * (np.abs(d0) * r).astype(np.float32)), 0.0)
            masks[h, :, 2 * ri + 1, :] = np.where(k <= q, np.exp(-slopes[h] * (np.abs(d1) * r).astype(np.float32)), 0.0)
    inv_freq = np.exp(-np.log(10000.0) * np.arange(0, 32, 2, dtype=np.float32) / 32).astype(np.float32)
    pos = np.arange(S_LEN, dtype=np.float32)
    ang = pos[:, None] * inv_freq[None, :]
    cosk = np.cos(ang).astype(np.float32).reshape(64, 128, 16).transpose(1, 0, 2)
    sink = np.sin(ang).astype(np.float32).reshape(64, 128, 16).transpose(1, 0, 2)
    rk = np.ascontiguousarray(np.stack([cosk, sink], axis=1))
    c = dict(masks=masks, inv_freq=inv_freq, rk=rk, ident=np.eye(128, dtype=np.float32))
    _CACHE["c"] = c
    return c


def _core_inputs(c, x, shared):
    cst = _consts()
    b, qc = c // 4, c % 4
    T0 = qc * OWN
    pos_w = T0 - OWN0 + np.arange(NW)
    valid = (pos_w >= 0) & (pos_w < S_LEN)
    xw = np.zeros((NW, D), np.float32)
    xw[valid] = x[b, pos_w[valid]]
    kvf = np.zeros((128, NVT), np.float32)
    for i, (r, es, nk) in enumerate(VLIST):
        kvf[:nk, i] = valid[es + r * np.arange(nk)].astype(np.float32)
    posq = (T0 - 1 + np.arange(NX)).astype(np.float32)
    ang = posq[None, :] * cst["inv_freq"][:, None]
    cq, sq = np.cos(ang).astype(np.float32), np.sin(ang).astype(np.float32)
    rq = np.ascontiguousarray(np.stack([np.concatenate([cq, cq], 0), np.concatenate([sq, sq], 0)], axis=1))
    uflag = np.zeros((128, 2), np.float32)
    uflag[:, 0] = 1.0 if T0 > 0 else 0.0
    uflag[:, 1] = 1.0 if T0 + OWN < S_LEN else 0.0
    d = dict(shared)
    d.update(xw=xw, xf=np.ascontiguousarray(x[b]), kvf=kvf, rq=rq, uflag=uflag, masks=cst["masks"], rk=cst["rk"], ident=cst["ident"])
    return d


def kernel(x, norm_mix_pre, w_in, q_lat_norm, w_uq, kv_lat_norm, w_ukv, out_norm_a, out_norm_b, w_o,
           norm_mix_post, norm_ffn_pre, w_up, conv_w, conv_b, w_down, norm_ffn_post):
    f = lambda a: np.ascontiguousarray(np.asarray(a, dtype=np.float32))
    x = f(x)
    gp = np.zeros((128, 32), np.float32)
    gp[:, 0:8] = f(norm_mix_pre)[0].reshape(8, 128).T
    gp[:, 8:11] = f(q_lat_norm)[0].reshape(3, 128).T
    gp[:, 11:13] = f(kv_lat_norm)[0].reshape(2, 128).T
    gp[:, 13:21] = np.concatenate([f(out_norm_a)[0], f(out_norm_b)[0]]).reshape(8, 128).T
    gp[:, 21:29] = f(norm_ffn_pre)[0].reshape(8, 128).T
    gpost = np.ascontiguousarray(np.broadcast_to(np.stack([f(norm_mix_post)[0], f(norm_ffn_post)[0]])[None], (128, 2, D)))
    cwb = np.zeros((128, 44, 4), np.float32)
    cwb[:, :, 0:3] = f(conv_w)[0].T.reshape(44, 128, 3).transpose(1, 0, 2)
    cwb[:, :, 3] = f(conv_b)[0].reshape(44, 128).T
    shared = dict(w_in=f(w_in)[0], w_uq=f(w_uq)[0], w_ukv=f(w_ukv)[0], w_o=f(w_o)[0], w_up=f(w_up)[0], w_down=f(w_down)[0],
                  gp=gp, gpost=gpost, cwb=cwb)
    if "nc" not in _CACHE:
        _CACHE["nc"] = build()
    nc = _CACHE["nc"]
    in_maps = [_core_inputs(c, x, shared) for c in range(8)]
    res = run_bass_kernel_spmd(nc, in_maps, core_ids=list(range(8)))
    out = np.zeros((2, S_LEN, D), np.float32)
    for c in range(8):
        b, qc = c // 4, c % 4
        out[b, qc * OWN:(qc + 1) * OWN] = res.results[c]["y"]
    return out
```

```python
import contextlib
import types
import numpy as np
import ml_dtypes
import concourse.bass as bass
import concourse.mybir as mybir
from concourse.bass_utils import run_bass_kernel_spmd

F32 = mybir.dt.float32
BF16 = mybir.dt.bfloat16
U8 = mybir.dt.uint8
AF = mybir.ActivationFunctionType
ALU = mybir.AluOpType

S_LEN = 8192
D = 1024
OWN = 2048
OWN0 = 1152
NW = 4352
NX = 2050
DFF = 2816
EPS = 1e-6
ENGS = ("sync", "scalar", "gpsimd", "vector", "tensor")
XBLK = [(i * 410, 410) for i in range(5)]


class Sched:
    def __init__(self, nc, ndma_sems=8):
        self.nc = nc
        self.ops = []
        self.ndma = ndma_sems

    @staticmethod
    def _freeze(fn):
        if fn.__closure__ is None:
            return fn
        cells = []
        for c in fn.__closure__:
            try:
                cells.append(types.CellType(c.cell_contents))
            except ValueError:
                cells.append(c)
        return types.FunctionType(fn.__code__, fn.__globals__, fn.__name__, fn.__defaults__, tuple(cells))

    def op(self, eng, fn, reads=(), writes=(), dma=False):
        fn = self._freeze(fn)
        self.ops.append(dict(eng=eng, fn=fn, reads=tuple(reads), writes=tuple(writes), dma=dma, bar=False))

    def barrier(self):
        self.ops.append(dict(eng=None, fn=None, reads=(), writes=(), dma=False, bar=True))

    def emit(self, final_wait_eng="sync"):
        nc = self.nc
        ops = self.ops
        n = len(ops)
        last_writer = {}
        readers = {}
        deps = [set() for _ in range(n)]
        since_bar = []
        pending_bar = {}
        for i, o in enumerate(ops):
            if o["bar"]:
                lastc = {}
                dl = set()
                for j in since_bar:
                    if ops[j]["dma"]:
                        dl.add(j)
                    else:
                        lastc[ops[j]["eng"]] = j
                dl.update(lastc.values())
                for e in ENGS:
                    pending_bar[e] = set(dl) | pending_bar.get(e, set())
                since_bar = []
                continue
            d = deps[i]
            if o["eng"] in pending_bar:
                d.update(pending_bar.pop(o["eng"]))
            for r in o["reads"]:
                if r in last_writer:
                    d.add(last_writer[r])
            for w in o["writes"]:
                if w in last_writer:
                    d.add(last_writer[w])
                d.update(readers.get(w, ()))
            d.discard(i)
            for w in o["writes"]:
                last_writer[w] = i
                readers[w] = []
            for r in o["reads"]:
                if r not in o["writes"]:
                    readers.setdefault(r, []).append(i)
            since_bar.append(i)
        needed = set()
        red = [None] * n
        for i, o in enumerate(ops):
            if o["bar"]:
                continue
            per_eng = {}
            dl = []
            for j in deps[i]:
                pj = ops[j]
                if pj["dma"]:
                    dl.append(j)
                    continue
                if pj["eng"] == o["eng"] and not o["dma"] and o["eng"] == "tensor":
                    continue
                e = pj["eng"]
                if e not in per_eng or per_eng[e] < j:
                    per_eng[e] = j
            dl.extend(per_eng.values())
            red[i] = dl
            needed.update(dl)
        cnt = {e: 0 for e in ENGS}
        dcnt = {}
        sig = [None] * n
        dma_idx = {e: 0 for e in ENGS}
        for i, o in enumerate(ops):
            if o["bar"]:
                continue
            if o["dma"]:
                k = dma_idx[o["eng"]] % self.ndma
                dma_idx[o["eng"]] += 1
                key = ("dma", o["eng"], k)
                prev = dcnt.get(key, 0)
                dcnt[key] = prev + 16
                sig[i] = (key, prev + 16)
                o["dma_prev"] = (key, prev) if prev > 0 else None
            elif i in needed:
                cnt[o["eng"]] += 1
                sig[i] = (("eng", o["eng"]), cnt[o["eng"]])
        semkeys = sorted({s[0] for s in sig if s is not None}, key=str)
        stack = contextlib.ExitStack()
        sems = {}
        for sk in semkeys:
            sems[sk] = stack.enter_context(nc.semaphore("s_" + "_".join(str(x) for x in sk)))
        by_eng = {e: [i for i, o in enumerate(ops) if o["eng"] == e] for e in ENGS}
        dma_final = list(dcnt.items())

        def run(engname, eng):
            waited = {}
            for i in by_eng[engname]:
                o = ops[i]
                wl = [sig[j] for j in red[i]]
                if o["dma"] and o.get("dma_prev"):
                    wl.append(o["dma_prev"])
                for (sk, v) in wl:
                    if waited.get(sk, 0) >= v:
                        continue
                    eng.wait_ge(sems[sk], v)
                    waited[sk] = v
                ins = o["fn"](eng)
                if sig[i] is not None:
                    ins.then_inc(sems[sig[i][0]], 16 if o["dma"] else 1)
            if engname == final_wait_eng:
                for sk, v in dma_final:
                    if waited.get(sk, 0) < v:
                        eng.wait_ge(sems[sk], v)
                for e in ENGS:
                    if cnt[e] > 0 and waited.get(("eng", e), 0) < cnt[e]:
                        eng.wait_ge(sems[("eng", e)], cnt[e])

        with stack:
            with nc.Block() as block:
                @block.sync
                def _(e):
                    run("sync", e)

                @block.scalar
                def _(e):
                    run("scalar", e)

                @block.gpsimd
                def _(e):
                    run("gpsimd", e)

                @block.vector
                def _(e):
                    run("vector", e)

                @block.tensor
                def _(e):
                    run("tensor", e)


def dil_tables():
    vt = {}

    def vtile(r, e0, nk):
        key = (r, e0, nk)
        if key not in vt:
            vt[key] = len(vt)
        return vt[key]

    groups = {1: [], 4: [], 16: []}
    for r in (1, 4, 16):
        def blk(x0, N):
            eq0 = OWN0 - 1 + x0
            t0 = vtile(r, eq0 - 64 * r, 128)
            t1 = vtile(r, eq0 + 64 * r, 128 if N > 1 else 1)
            return (x0, N, t0, t1)
        if r == 1:
            for g in range(4):
                groups[r].append(("reg", g, [blk(1 + 128 * (4 * g + b), 128) for b in range(4)]))
        elif r == 4:
            for g in range(4):
                groups[r].append(("reg", g, [blk(1 + c + 512 * g, 128) for c in range(4)]))
        else:
            for g in range(4):
                groups[r].append(("reg", g, [blk(1 + 4 * g + c, 128) for c in range(4)]))
        groups[r].append(("halo", 0, [blk(0, 1), blk(NX - 1, 1)]))
    vlist = [None] * len(vt)
    for k, i in vt.items():
        vlist[i] = k
    return vlist, groups


VLIST, DGROUPS = dil_tables()
NVT = len(VLIST)


def build(dbg=False):
    nc = bass.Bass("TRN2", target_bir_lowering=False)

    def din(name, shape, dt=F32):
        return nc.dram_tensor(name, list(shape), dt, kind="ExternalInput").ap()

    xw = din("xw", [NW, D])
    xf = din("xf", [S_LEN, D])
    w_in = din("w_in", [D, 2208])
    w_uq = din("w_uq", [384, 768])
    w_ukv = din("w_ukv", [256, 1024])
    w_o = din("w_o", [D, D])
    w_up = din("w_up", [D, 2 * DFF])
    w_down = din("w_down", [DFF, D])
    gp = din("gp", [128, 32])
    gpost = din("gpost", [128, 2, D])
    cwb = din("cwb", [128, 44, 4])
    masks = din("masks", [8, 128, 6, 128])
    kvf = din("kvf", [128, NVT])
    rq = din("rq", [32, 2, NX])
    rk = din("rk", [128, 2, 64, 16])
    ident = din("ident", [128, 128])
    uflag = din("uflag", [128, 2])
    yout = nc.dram_tensor("y", [OWN, D], F32, kind="ExternalOutput").ap()
    xmid = nc.dram_tensor("xmid", [OWN, D], F32, kind="Internal").ap()
    dbg_out = {}

    SB_BYTES = 212000
    big = nc.alloc_sbuf_tensor("big", [128, SB_BYTES], U8).ap()
    ps = nc.alloc_psum_tensor("ps", [128, 4096], F32).ap()
    psb = ps.bitcast(BF16)

    class Region:
        def __init__(self, lo, hi):
            assert hi <= SB_BYTES and lo <= hi, (lo, hi)
            self.lo, self.hi, self.off = lo, hi, lo

        def alloc(self, shape, dt, p0=0):
            esz = 4 if dt == F32 else 2
            nb = int(np.prod(shape[1:])) * esz
            nb_al = (nb + 63) // 64 * 64
            assert self.off + nb_al <= self.hi, ("SBUF region overflow", self.lo, self.hi, self.off, nb_al)
            v = big[p0:p0 + shape[0], self.off:self.off + nb].bitcast(dt)
            self.off += nb_al
            if len(shape) == 3:
                v = v.rearrange("p (a b) -> p a b", a=shape[1])
            elif len(shape) == 4:
                v = v.rearrange("p (a b c) -> p a b c", a=shape[1], b=shape[2])
            return v

    RP = Region(0, 4096)
    R1 = Region(RP.hi, RP.hi + 86080)
    RC = Region(R1.hi, R1.hi + 12352)
    RY = Region(RC.hi, RC.hi + 16448 + 4096 + 16448)
    RT = Region(RY.hi, SB_BYTES)
    A = RP
    S = Sched(nc)
    op = S.op

    def dump(name, ap, shape, dt, keys):
        if not dbg:
            return
        import os
        sel = os.environ.get("DBGSEL", "")
        if sel and name not in sel.split(","):
            return
        d = nc.dram_tensor("dbg_" + name, list(shape), dt, kind="ExternalOutput").ap()
        op("sync", lambda e: e.dma_start(out=d, in_=ap), reads=keys, dma=True)

    def bank(b, n=512, p=128, p0=0):
        return ps[p0:p0 + p, b * 512:b * 512 + n]

    def bankb(b, n=1024, p=128, p0=0):
        return psb[p0:p0 + p, b * 1024:b * 1024 + n]

    idf = A.alloc([128, 128], F32)
    idb = A.alloc([128, 128], BF16)
    gpt = A.alloc([128, 32], F32)
    stt = A.alloc([128, 256], F32)
    junk = A.alloc([128, 1024], BF16)
    rec = A.alloc([128, 8], F32)
    op("sync", lambda e: e.dma_start(out=idf, in_=ident), writes=["idf"], dma=True)
    op("sync", lambda e: e.dma_start(out=gpt, in_=gp), writes=["gpt"], dma=True)
    op("vector", lambda e: e.tensor_copy(out=idb, in_=idf), reads=["idf"], writes=["idb"])

    def rstd_from_ss(ss_ap, out_ap, n, key):
        tmp = ss_ap
        op("scalar", lambda e: e.activation(out=tmp, in_=ss_ap, func=AF.Sqrt, bias=EPS, scale=1.0 / n),
           reads=[key], writes=[key])
        op("vector", lambda e: e.reciprocal(out=out_ap, in_=tmp), reads=[key], writes=[key])

    def load_weight(dst, src_ap, kch, ncols, gcol, stage, wkey, negate_cols=None):
        op("gpsimd", lambda e: e.dma_start(out=stage[:, 0:kch, 0:ncols], in_=src_ap.rearrange("(k p) n -> p k n", p=128)),
           writes=[("stage", id(stage))], dma=True)
        for k in range(kch):
            eng = "vector" if k % 2 == 0 else "scalar"
            if gcol is None:
                if eng == "vector":
                    op(eng, lambda e, k=k: e.tensor_copy(out=dst[:, k, :], in_=stage[:, k, 0:ncols]),
                       reads=[("stage", id(stage))], writes=[wkey])
                else:
                    op(eng, lambda e, k=k: e.copy(out=dst[:, k, :], in_=stage[:, k, 0:ncols]),
                       reads=[("stage", id(stage))], writes=[wkey])
            else:
                if eng == "vector":
                    op(eng, lambda e, k=k: e.tensor_scalar(out=dst[:, k, :], in0=stage[:, k, 0:ncols],
                                                          scalar1=gpt[:, gcol + k:gcol + k + 1], scalar2=None, op0=ALU.mult),
                       reads=[("stage", id(stage)), "gpt"], writes=[wkey])
                else:
                    op(eng, lambda e, k=k: e.activation(out=dst[:, k, :], in_=stage[:, k, 0:ncols], func=AF.Identity,
                                                        scale=gpt[:, gcol + k:gcol + k + 1]),
                       reads=[("stage", id(stage)), "gpt"], writes=[wkey])

    def norm_tiles_to_T(src_dram_rows, ntile, xbuf, xnbuf, key):
        op("sync", lambda e: e.dma_start(out=xbuf[:, 0:ntile, :], in_=src_dram_rows.rearrange("(t p) f -> p t f", p=128)),
           writes=[("x", key)], dma=True)
        for t in range(ntile):
            op("scalar", lambda e, t=t: e.activation(out=junk, in_=xbuf[:, t, :], func=AF.Square, accum_out=stt[:, t:t + 1]),
               reads=[("x", key)], writes=["junk", "stt"])
        rstd_from_ss(stt[:, 0:ntile], stt[:, 8:8 + ntile], D, "stt")
        op("vector", lambda e: e.tensor_tensor(out=xnbuf[:, 0:ntile, :], in0=xbuf[:, 0:ntile, :],
                                               in1=stt[:, 8:8 + ntile].unsqueeze(2).to_broadcast([128, ntile, D]), op=ALU.mult),
           reads=[("x", key), "stt"], writes=[("xn", key)])

    def transpose_tile(xn_tile, hT_dst, pbank, rkeys, wkey):
        for k in range(8):
            op("tensor", lambda e, k=k: e.transpose(out=bankb(pbank)[:, k * 128:(k + 1) * 128], in_=xn_tile[:, k * 128:(k + 1) * 128], identity=idb),
               reads=list(rkeys) + ["idb"], writes=[("ps", pbank)])
        op("vector", lambda e: e.tensor_copy(out=hT_dst, in_=bankb(pbank).rearrange("p (k t) -> p k t", k=8)),
           reads=[("ps", pbank)], writes=[wkey])

    KAT = R1.alloc([128, 4, NW], BF16)
    VAT = R1.alloc([128, 4, NW], BF16)
    QAT = R1.alloc([128, 4, NX], BF16)
    cqT = RC.alloc([128, 3, NX], BF16)
    A = Region(RC.hi, SB_BYTES)
    WA = A.alloc([128, 8, 1920], BF16)
    stage = A.alloc([128, 8, 480], F32)
    xbuf = [A.alloc([128, 2, D], F32) for _ in range(2)]
    xnb = [A.alloc([128, 2, D], BF16) for _ in range(2)]
    hTw = [A.alloc([128, 8, 256], BF16) for _ in range(2)]
    cqn = A.alloc([128, 384], BF16)
    for c in range(4):
        load_weight(WA[:, :, c * 480:(c + 1) * 480], w_in[:, c * 480:(c + 1) * 480], 8, 480, 0, stage, "WA")

    def cq_tile(cols_ap, M, xcol_ap_fn, hkey, pb):
        for k in range(8):
            la = cols_ap(k)
            op("tensor", lambda e, k=k, la=la: e.matmul(bank(pb, 384, M), lhsT=la, rhs=WA[:, k, 1536:1920], start=(k == 0), stop=(k == 7)),
               reads=[hkey, "WA"], writes=[("ps", pb)])
        op("scalar", lambda e: e.activation(out=junk[0:M, 0:384], in_=bank(pb, 384, M), func=AF.Square, accum_out=stt[0:M, 16:17]),
           reads=[("ps", pb)], writes=["junk", "stt2"])
        op("scalar", lambda e: e.activation(out=stt[0:M, 16:17], in_=stt[0:M, 16:17], func=AF.Sqrt, bias=EPS, scale=1.0 / 384),
           reads=["stt2"], writes=["stt2"])
        op("vector", lambda e: e.reciprocal(out=stt[0:M, 17:18], in_=stt[0:M, 16:17]), reads=["stt2"], writes=["stt2"])
        op("vector", lambda e: e.tensor_scalar(out=cqn[0:M, :], in0=bank(pb, 384, M), scalar1=stt[0:M, 17:18], scalar2=None, op0=ALU.mult),
           reads=[("ps", pb), "stt2"], writes=["cqn"])
        for j in range(3):
            op("tensor", lambda e, j=j: e.transpose(out=bankb(pb)[:, j * 128:j * 128 + M], in_=cqn[0:M, j * 128:(j + 1) * 128], identity=idb[0:M, 0:M]),
               reads=["cqn", "idb"], writes=[("ps", pb)])
        xc = xcol_ap_fn()
        op("vector", lambda e: e.tensor_copy(out=xc, in_=bankb(pb)[:, 0:384].rearrange("p (j t) -> p j t", j=3)[:, :, 0:M]),
           reads=[("ps", pb)], writes=["cqT"])

    NGW = NW // 256
    for g in range(NGW):
        bi = g % 2
        e0 = g * 256
        norm_tiles_to_T(xw[e0:e0 + 256, :], 2, xbuf[bi], xnb[bi], ("w", bi))
        for t in range(2):
            transpose_tile(xnb[bi][:, t, :], hTw[bi][:, :, t * 128:(t + 1) * 128], t, [("xn", ("w", bi))], ("hTw", bi))
        xlo, xhi = max(e0, OWN0 - 1), min(e0 + 256, OWN0 - 1 + NX)
        jobs = [("K", 512 + 128 * c, c) for c in range(4)] + [("V", 1024 + 128 * c, c) for c in range(4)]
        if xhi > xlo:
            jobs += [("Q", 128 * c, c) for c in range(4)]
        for ji, (kind, col0, c) in enumerate(jobs):
            pb = 2 + (ji % 4)
            for k in range(8):
                op("tensor", lambda e, k=k, col0=col0, pb=pb: e.matmul(bank(pb, 256), lhsT=WA[:, k, col0:col0 + 128], rhs=hTw[bi][:, k, :], start=(k == 0), stop=(k == 7)),
                   reads=[("hTw", bi), "WA"], writes=[("ps", pb)])
            if kind == "K":
                op("vector", lambda e, c=c, pb=pb: e.tensor_copy(out=KAT[:, c, e0:e0 + 256], in_=bank(pb, 256)), reads=[("ps", pb)], writes=["KAT"])
            elif kind == "V":
                op("scalar", lambda e, c=c, pb=pb: e.copy(out=VAT[:, c, e0:e0 + 256], in_=bank(pb, 256)), reads=[("ps", pb)], writes=["VAT"])
            else:
                op("vector", lambda e, c=c, pb=pb: e.tensor_copy(out=QAT[:, c, xlo - (OWN0 - 1):xhi - (OWN0 - 1)], in_=bank(pb, 256)[:, xlo - e0:xhi - e0]),
                   reads=[("ps", pb)], writes=["QAT"])
        for t in range(2):
            et = e0 + t * 128
            if OWN0 <= et < OWN0 + OWN:
                x0 = et - (OWN0 - 1)
                cq_tile(lambda k, t=t: hTw[bi][:, k, t * 128:(t + 1) * 128], 128, lambda x0=x0: cqT[:, :, x0:x0 + 128], ("hTw", bi), 6 + t)
            if et == OWN0 - 128:
                cq_tile(lambda k, t=t: hTw[bi][:, k, t * 128 + 127:t * 128 + 128], 1, lambda: cqT[:, :, 0:1], ("hTw", bi), 6 + t)
            if et == OWN0 + OWN:
                cq_tile(lambda k, t=t: hTw[bi][:, k, t * 128:t * 128 + 1], 1, lambda: cqT[:, :, NX - 1:NX], ("hTw", bi), 6 + t)
    dump("KAT", KAT, [128, 4, NW], BF16, ["KAT"])
    dump("VAT", VAT, [128, 4, NW], BF16, ["VAT"])
    dump("QAT", QAT, [128, 4, NX], BF16, ["QAT"])
    dump("cqT", cqT, [128, 3, NX], BF16, ["cqT"])
    S.barrier()

    ynTa = RY.alloc([128, 4, NX], BF16)
    A = Region(RY.lo + 16448, SB_BYTES)
    mk = [A.alloc([128, 6, 128], F32) for _ in range(2)]
    kvft = A.alloc([128, NVT], F32)
    Vp = [A.alloc([128, NVT, 65], BF16) for _ in range(2)]
    Oacc = A.alloc([128, NX], F32)
    ya_tm = A.alloc([128, 17, 512], F32)
    Eb = [A.alloc([128, 1024], F32) for _ in range(2)]
    Pb = [A.alloc([128, 1024], BF16) for _ in range(2)]
    op("sync", lambda e: e.dma_start(out=kvft, in_=kvf), writes=["kvft"], dma=True)

    def oacc_to_tm(h, dst_tm, okey, Oacc):
        tiles = [(1 + 128 * m, 128, 1) for m in range(16)] + [(0, 2, NX - 1)]
        for g0 in range(0, 17, 4):
            tl = tiles[g0:g0 + 4]
            pb = 6 + (g0 // 4) % 2
            for i, (x0, M, step) in enumerate(tl):
                src = Oacc[0:65, x0:x0 + 128] if step == 1 else Oacc[0:65, 0:NX:NX - 1]
                op("tensor", lambda e, i=i, src=src, M=M, pb=pb: e.transpose(out=bank(pb)[0:M, i * 65:(i + 1) * 65], in_=src, identity=idf[0:65, 0:65]),
                   reads=[okey, "idf"], writes=[("ps", pb)])
            nt = len(tl)
            M = tl[0][1]
            pv = bank(pb)[0:M, 0:nt * 65].rearrange("p (t c) -> p t c", t=nt)
            op("vector", lambda e, pv=pv, nt=nt, M=M: e.reciprocal(out=rec[0:M, 0:nt], in_=pv[:, :, 64]), reads=[("ps", pb)], writes=["rec"])
            op("vector", lambda e, pv=pv, nt=nt, M=M, g0=g0: e.tensor_tensor(out=dst_tm[0:M, g0:g0 + nt, h * 64:(h + 1) * 64], in0=pv[:, :, 0:64],
                                                                         in1=rec[0:M, 0:nt].unsqueeze(2).to_broadcast([M, nt, 64]), op=ALU.mult),
               reads=[("ps", pb), "rec"], writes=["tm"])

    def tm_to_ynT(src_tm, dstT, nkey, Pb):
        for t in range(17):
            M = 128 if t < 16 else 2
            op("scalar", lambda e, t=t, M=M: e.activation(out=junk[0:M, 0:512], in_=src_tm[0:M, t, :], func=AF.Square, accum_out=stt[0:M, 32 + t:33 + t]),
               reads=["tm"], writes=["junk", "stt3"])
        rstd_from_ss(stt[:, 32:49], stt[:, 64:81], 512, "stt3")
        for t in range(17):
            M = 128 if t < 16 else 2
            pb = t % 2
            ynb = Pb[t % 2]
            op("vector", lambda e, t=t, M=M, ynb=ynb: e.tensor_scalar(out=ynb[0:M, 0:512], in0=src_tm[0:M, t, :], scalar1=stt[0:M, 64 + t:65 + t], scalar2=None, op0=ALU.mult),
               reads=["tm", "stt3"], writes=[("Pb", t % 2)])
            for j in range(4):
                op("tensor", lambda e, j=j, M=M, ynb=ynb, pb=pb: e.transpose(out=bankb(pb)[:, j * 128:j * 128 + M], in_=ynb[0:M, j * 128:(j + 1) * 128], identity=idb[0:M, 0:M]),
                   reads=[("Pb", t % 2), "idb"], writes=[("ps", pb)])
            srcv = bankb(pb)[:, 0:512].rearrange("p (j t) -> p j t", j=4)[:, :, 0:M]
            if t < 16:
                dstv = dstT[:, :, 1 + 128 * t:1 + 128 * (t + 1)]
            else:
                dstv = dstT[:, :, 0:NX:NX - 1]
            op("vector", lambda e, srcv=srcv, dstv=dstv: e.tensor_copy(out=dstv, in_=srcv), reads=[("ps", pb)], writes=[nkey])

    for h in range(8):
        pr, hs = h // 2, (h % 2) * 64
        vb = Vp[h % 2]
        mb = mk[h % 2]
        op("sync", lambda e, h=h, mb=mb: e.dma_start(out=mb, in_=masks[h]), writes=[("mk", h % 2)], dma=True)
        for v0 in range(0, NVT, 8):
            vts = VLIST[v0:v0 + 8]
            pb = 4 + (v0 // 8) % 2
            for i, (r, es, nk) in enumerate(vts):
                op("tensor", lambda e, i=i, r=r, es=es, nk=nk, pb=pb: e.transpose(out=bankb(pb)[0:nk, i * 64:(i + 1) * 64],
                                                                               in_=VAT[hs:hs + 64, pr, es:es + r * (nk - 1) + 1:r], identity=idb[hs:hs + 64, hs:hs + 64]),
                   reads=["VAT", "idb"], writes=[("ps", pb)])
            nfull = [i for i, v in enumerate(vts) if v[2] == 128]
            i = 0
            while i < len(vts):
                nk = vts[i][2]
                j = i
                while j + 1 < len(vts) and vts[j + 1][2] == nk:
                    j += 1
                cnt_ = j - i + 1
                srcv = bankb(pb)[0:nk, i * 64:(j + 1) * 64].rearrange("p (t c) -> p t c", t=cnt_)
                op("vector", lambda e, srcv=srcv, i=i, cnt_=cnt_, nk=nk, v0=v0: e.tensor_tensor(
                    out=vb[0:nk, v0 + i:v0 + i + cnt_, 0:64], in0=srcv,
                    in1=kvft[0:nk, v0 + i:v0 + i + cnt_].unsqueeze(2).to_broadcast([nk, cnt_, 64]), op=ALU.mult),
                   reads=[("ps", pb), "kvft"], writes=[("Vp", h % 2)])
                i = j + 1
        op("gpsimd", lambda e, vb=vb: e.tensor_copy(out=vb[:, :, 64], in_=kvft), reads=["kvft"], writes=[("Vp", h % 2)])
        first = True
        gi = 0
        for ri, r in enumerate((1, 4, 16)):
            for (kind, g, blks) in DGROUPS[r]:
                sb = 0 + 2 * (gi % 2)
                eb = Eb[gi % 2]
                pbuf = Pb[gi % 2]
                ob = 6 + gi % 2
                gi += 1
                nb = len(blks)
                N = blks[0][1]
                for bi_, (x0, Nq, t0, t1) in enumerate(blks):
                    qv = QAT[hs:hs + 64, pr, x0:x0 + r * (Nq - 1) + 1:r]
                    for role, tix in ((0, t0), (1, t1)):
                        (rr, es, nk) = VLIST[tix]
                        op("tensor", lambda e, role=role, es=es, nk=nk, qv=qv, bi_=bi_, Nq=Nq, sb=sb: e.matmul(
                            bank(sb + role)[0:nk, bi_ * 128:bi_ * 128 + Nq], lhsT=KAT[hs:hs + 64, pr, es:es + r * (nk - 1) + 1:r], rhs=qv, start=True, stop=True),
                           reads=["KAT", "QAT"], writes=[("ps", sb + role)])
                if kind == "reg":
                    sv = ps[:, sb * 512:(sb + 2) * 512]
                    op("scalar", lambda e, sv=sv, eb=eb: e.activation(out=eb, in_=sv, func=AF.Exp, scale=0.125),
                       reads=[("ps", sb), ("ps", sb + 1)], writes=[("Eb", id(eb))])
                    op("vector", lambda e, eb=eb, pbuf=pbuf, mb=mb, ri=ri: e.tensor_tensor(
                        out=pbuf.rearrange("p (r b q) -> p r b q", r=2, b=4), in0=eb.rearrange("p (r b q) -> p r b q", r=2, b=4),
                        in1=mb[:, 2 * ri:2 * ri + 2, :].unsqueeze(2).to_broadcast([128, 2, 4, 128]), op=ALU.mult),
                       reads=[("Eb", id(eb)), ("mk", h % 2)], writes=[("Pbuf", id(pbuf))])
                else:
                    op("scalar", lambda e, eb=eb, sb=sb: e.activation(out=eb[:, 0:256:128], in_=bank(sb)[:, 0:256:128], func=AF.Exp, scale=0.125),
                       reads=[("ps", sb)], writes=[("Eb", id(eb))])
                    op("scalar", lambda e, eb=eb, sb=sb: e.activation(out=eb[0:1, 512:768:128], in_=bank(sb + 1)[0:1, 0:256:128], func=AF.Exp, scale=0.125),
                       reads=[("ps", sb + 1)], writes=[("Eb", id(eb))])
                    op("vector", lambda e, eb=eb, pbuf=pbuf, mb=mb, ri=ri: e.tensor_tensor(
                        out=pbuf[:, 0:256:128], in0=eb[:, 0:256:128], in1=mb[:, 2 * ri, 0:1].to_broadcast([128, 2]), op=ALU.mult),
                       reads=[("Eb", id(eb)), ("mk", h % 2)], writes=[("Pbuf", id(pbuf))])
                    op("vector", lambda e, eb=eb, pbuf=pbuf, mb=mb, ri=ri: e.tensor_tensor(
                        out=pbuf[0:1, 512:768:128], in0=eb[0:1, 512:768:128], in1=mb[0:1, 2 * ri + 1, 0:1].to_broadcast([1, 2]), op=ALU.mult),
                       reads=[("Eb", id(eb)), ("mk", h % 2)], writes=[("Pbuf", id(pbuf))])
                for bi_, (x0, Nq, t0, t1) in enumerate(blks):
                    for role, tix in ((0, t0), (1, t1)):
                        nk = VLIST[tix][2]
                        op("tensor", lambda e, role=role, tix=tix, nk=nk, bi_=bi_, Nq=Nq, ob=ob, pbuf=pbuf, vb=vb: e.matmul(
                            bank(ob)[0:65, bi_ * 128:bi_ * 128 + Nq], lhsT=vb[0:nk, tix, :], rhs=pbuf[0:nk, role * 512 + bi_ * 128:role * 512 + bi_ * 128 + Nq],
                            start=(role == 0), stop=(role == 1)),
                           reads=[("Pbuf", id(pbuf)), ("Vp", h % 2)], writes=[("ps", ob)])
                if kind == "reg":
                    src = bank(ob)[0:65, :].rearrange("p (c q) -> p c q", c=4)
                    if r == 1:
                        dst = Oacc[0:65, 1 + 512 * g:1 + 512 * (g + 1)].rearrange("p (c q) -> p c q", c=4)
                    elif r == 4:
                        dst = Oacc[0:65, 1 + 512 * g:1 + 512 * (g + 1)].rearrange("p (q c) -> p c q", c=4)
                    else:
                        dst = Oacc[0:65, 1:1 + OWN].rearrange("p (q c) -> p c q", c=16)[:, 4 * g:4 * g + 4, :]
                else:
                    src = bank(ob)[0:65, 0:256:128]
                    dst = Oacc[0:65, 0:NX:NX - 1]
                if r == 1:
                    op("vector", lambda e, src=src, dst=dst: e.tensor_copy(out=dst, in_=src), reads=[("ps", ob)], writes=["Oacc"])
                else:
                    op("vector", lambda e, src=src, dst=dst: e.tensor_tensor(out=dst, in0=src, in1=dst, op=ALU.add), reads=[("ps", ob), "Oacc"], writes=["Oacc"])
        oacc_to_tm(h, ya_tm, "Oacc", Oacc)
    tm_to_ynT(ya_tm, ynTa, "ynTa", Pb)
    dump("ynTa", ynTa, [128, 4, NX], BF16, ["ynTa"])
    dump("ya_tm", ya_tm, [128, 17, 512], F32, ["tm"])
    S.barrier()

    R1.off = R1.lo
    ckvT = R1.alloc([128, 2, S_LEN], BF16)
    KT = R1.alloc([96, S_LEN], BF16)
    R1s_lo = R1.off
    RY.off = RY.lo + 16448
    Wukv = RY.alloc([128, 2, 1024], BF16)
    A = Region(RY.off, SB_BYTES)
    Wkvl = A.alloc([128, 8, 288], BF16)
    stage2 = A.alloc([128, 8, 512], F32)
    xbuf = [A.alloc([128, 2, D], F32) for _ in range(2)]
    xnb = [A.alloc([128, 2, D], BF16) for _ in range(2)]
    hTk = [A.alloc([128, 8, 128], BF16) for _ in range(2)]
    ckvn = [A.alloc([128, 256], BF16) for _ in range(2)]
    kr_tm = A.alloc([128, 64, 32], F32)
    rkt = A.alloc([128, 2, 64, 16], F32)
    kr_pad = A.alloc([128, 64, 96], BF16)
    rt = [A.alloc([128, 64, 16], F32) for _ in range(2)]
    load_weight(Wkvl, w_in[:, 1920:2208], 8, 288, 0, stage2, "Wkvl")
    load_weight(Wukv[:, :, 0:512], w_ukv[:, 0:512], 2, 512, 11, stage2, "Wukv")
    load_weight(Wukv[:, :, 512:1024], w_ukv[:, 512:1024], 2, 512, 11, stage2, "Wukv")
    op("sync", lambda e: e.dma_start(out=rkt, in_=rk), writes=["rkt"], dma=True)
    op("gpsimd", lambda e: e.memset(kr_pad, 0.0), writes=["kr_pad"])
    for g in range(S_LEN // 256):
        bi = g % 2
        norm_tiles_to_T(xf[g * 256:(g + 1) * 256, :], 2, xbuf[bi], xnb[bi], ("k", bi))
        for t in range(2):
            tt = g * 2 + t
            hb = hTk[tt % 2]
            transpose_tile(xnb[bi][:, t, :], hb, tt % 2, [("xn", ("k", bi))], ("hTk", tt % 2))
            pb = 2 + tt % 2
            for k in range(8):
                op("tensor", lambda e, k=k, hb=hb, pb=pb: e.matmul(bank(pb, 288), lhsT=hb[:, k, :], rhs=Wkvl[:, k, :], start=(k == 0), stop=(k == 7)),
                   reads=[("hTk", tt % 2), "Wkvl"], writes=[("ps", pb)])
            sc = 96 + (tt % 8)
            sk = ("stt4", tt % 8)
            op("scalar", lambda e, pb=pb, sc=sc: e.activation(out=junk[:, 0:256], in_=bank(pb, 256), func=AF.Square, accum_out=stt[:, sc:sc + 1]),
               reads=[("ps", pb)], writes=["junk", sk])
            op("scalar", lambda e, sc=sc: e.activation(out=stt[:, sc:sc + 1], in_=stt[:, sc:sc + 1], func=AF.Sqrt, bias=EPS, scale=1.0 / 256),
               reads=[sk], writes=[sk])
            op("vector", lambda e, sc=sc: e.reciprocal(out=stt[:, sc + 8:sc + 9], in_=stt[:, sc:sc + 1]), reads=[sk], writes=[sk])
            cb = ckvn[tt % 2]
            op("vector", lambda e, pb=pb, sc=sc, cb=cb: e.tensor_scalar(out=cb, in0=bank(pb, 256), scalar1=stt[:, sc + 8:sc + 9], scalar2=None, op0=ALU.mult),
               reads=[("ps", pb), sk], writes=[("ckvn", tt % 2)])
            op("vector", lambda e, pb=pb, tt=tt: e.tensor_copy(out=kr_tm[:, tt, :], in_=bank(pb, 288)[:, 256:288]), reads=[("ps", pb)], writes=["kr_tm"])
            pb2 = 4 + tt % 2
            for j in range(2):
                op("tensor", lambda e, j=j, cb=cb, pb2=pb2: e.transpose(out=bankb(pb2)[:, j * 128:(j + 1) * 128], in_=cb[:, j * 128:(j + 1) * 128], identity=idb),
                   reads=[("ckvn", tt % 2), "idb"], writes=[("ps", pb2)])
            op("scalar", lambda e, pb2=pb2, tt=tt: e.copy(out=ckvT[:, :, tt * 128:(tt + 1) * 128], in_=bankb(pb2)[:, 0:256].rearrange("p (j t) -> p j t", j=2)),
               reads=[("ps", pb2)], writes=["ckvT"])
    x1, x2 = kr_tm[:, :, 0:16], kr_tm[:, :, 16:32]
    cosk, sink = rkt[:, 0], rkt[:, 1]
    op("vector", lambda e: e.tensor_tensor(out=rt[0], in0=x1, in1=cosk, op=ALU.mult), reads=["kr_tm", "rkt"], writes=["rt0"])
    op("vector", lambda e: e.tensor_tensor(out=rt[1], in0=x2, in1=sink, op=ALU.mult), reads=["kr_tm", "rkt"], writes=["rt1"])
    op("vector", lambda e: e.tensor_tensor(out=kr_pad[:, :, 64:80], in0=rt[0], in1=rt[1], op=ALU.subtract), reads=["rt0", "rt1", "kr_pad"], writes=["kr_pad"])
    op("vector", lambda e: e.tensor_tensor(out=rt[0], in0=x2, in1=cosk, op=ALU.mult), reads=["kr_tm", "rkt", "kr_pad"], writes=["rt0"])
    op("vector", lambda e: e.tensor_tensor(out=rt[1], in0=x1, in1=sink, op=ALU.mult), reads=["kr_tm", "rkt", "kr_pad"], writes=["rt1"])
    op("vector", lambda e: e.tensor_tensor(out=kr_pad[:, :, 80:96], in0=rt[0], in1=rt[1], op=ALU.add), reads=["rt0", "rt1", "kr_pad"], writes=["kr_pad"])
    for g8 in range(8):
        pb = 6 + g8 % 2
        for i in range(8):
            tt = g8 * 8 + i
            op("tensor", lambda e, i=i, tt=tt, pb=pb: e.transpose(out=bankb(pb)[0:96, i * 128:(i + 1) * 128], in_=kr_pad[:, tt, :], identity=idb),
               reads=["kr_pad", "idb"], writes=[("ps", pb)])
        op("vector", lambda e, pb=pb, g8=g8: e.tensor_copy(out=KT[64:96, g8 * 1024:(g8 + 1) * 1024], in_=bankb(pb)[64:96, :]),
           reads=[("ps", pb)], writes=["KTr"])
    dump("ckvT", ckvT, [128, 2, S_LEN], BF16, ["ckvT"])
    dump("KTr", KT[64:96, :], [32, S_LEN], BF16, ["KTr"])
    S.barrier()

    ynTb = RY.alloc([128, 4, NX], BF16)
    RT.off = RT.lo
    QBT = RT.alloc([96, 8, NX], BF16)
    R1s = Region(R1s_lo, R1.hi)
    Wuq = R1s.alloc([128, 3, 768], BF16)
    Wrot = R1s.alloc([128, 3, 8, 96], BF16)
    stage3 = R1s.alloc([128, 3, 768], F32)
    rqt = RT.alloc([96, 2, NX], F32)
    tq = [RT.alloc([96, 410], F32) for _ in range(2)]
    load_weight(Wuq, w_uq, 3, 768, 8, stage3, "Wuq")
    op("gpsimd", lambda e: e.memset(Wrot, 0.0), writes=["Wrot"])
    Wuq4 = Wuq.rearrange("p k (h c) -> p k h c", h=8)
    for k in range(3):
        op("gpsimd", lambda e, k=k: e.tensor_scalar(out=Wrot[:, k, :, 64:80], in0=Wuq4[:, k, :, 80:96], scalar1=-1.0, scalar2=None, op0=ALU.mult),
           reads=["Wuq", "Wrot"], writes=["Wrot"])
        op("gpsimd", lambda e, k=k: e.tensor_copy(out=Wrot[:, k, :, 80:96], in_=Wuq4[:, k, :, 64:80]), reads=["Wuq", "Wrot"], writes=["Wrot"])
    op("sync", lambda e: e.dma_start(out=rqt[64:96], in_=rq), writes=["rqt"], dma=True)
    qi = 0
    for h in range(8):
        for (c0, cn) in XBLK:
            pa, pr_ = 2 * (qi % 2), 1 + 2 * (qi % 2)
            tb_ = tq[qi % 2]
            tk = ("tq", qi % 2)
            qi += 1
            for k in range(3):
                op("tensor", lambda e, k=k, h=h, c0=c0, cn=cn, pa=pa: e.matmul(bank(pa, cn, 96), lhsT=Wuq[:, k, h * 96:(h + 1) * 96], rhs=cqT[:, k, c0:c0 + cn], start=(k == 0), stop=(k == 2)),
                   reads=["Wuq", "cqT"], writes=[("ps", pa)])
            for k in range(3):
                op("tensor", lambda e, k=k, h=h, c0=c0, cn=cn, pr_=pr_: e.matmul(bank(pr_, cn, 96), lhsT=Wrot[:, k, h, :], rhs=cqT[:, k, c0:c0 + cn], start=(k == 0), stop=(k == 2)),
                   reads=["Wrot", "cqT"], writes=[("ps", pr_)])
            op("scalar", lambda e, h=h, c0=c0, cn=cn, pa=pa: e.copy(out=QBT[0:64, h, c0:c0 + cn], in_=bank(pa, cn, 64)), reads=[("ps", pa)], writes=["QBT"])
            op("vector", lambda e, c0=c0, cn=cn, pa=pa, tb_=tb_: e.tensor_tensor(out=tb_[64:96, 0:cn], in0=bank(pa, cn, 32, 64), in1=rqt[64:96, 0, c0:c0 + cn], op=ALU.mult),
               reads=[("ps", pa), "rqt"], writes=[tk])
            op("vector", lambda e, h=h, c0=c0, cn=cn, pr_=pr_: e.tensor_tensor(out=QBT[64:96, h, c0:c0 + cn], in0=bank(pr_, cn, 32, 64), in1=rqt[64:96, 1, c0:c0 + cn], op=ALU.mult),
               reads=[("ps", pr_), "rqt"], writes=["QBT"])
            op("vector", lambda e, h=h, c0=c0, cn=cn, tb_=tb_: e.tensor_tensor(out=QBT[64:96, h, c0:c0 + cn], in0=QBT[64:96, h, c0:c0 + cn], in1=tb_[64:96, 0:cn], op=ALU.add),
               reads=[tk, "QBT"], writes=["QBT"])
    dump("QBT", QBT, [96, 8, NX], BF16, ["QBT"])
    S.barrier()
    RT.off = RT.lo + 32832
    yb_tm = RT.alloc([128, 17, 512], F32)
    ynb_tmp = [RT.alloc([128, 512], BF16) for _ in range(2)]
    R1s = Region(R1s_lo, R1.hi)
    Vb = [R1s.alloc([128, 64, 65], BF16) for _ in range(2)]
    PT = [R1s.alloc([128, 2, 410], BF16) for _ in range(3)]
    OaccB = R1s.alloc([128, NX], F32)
    for b_ in range(2):
        op("gpsimd", lambda e, b_=b_: e.memset(Vb[b_][:, :, 64], 1.0), writes=[("Vb", b_)])
    scale_b = 96.0 ** -0.5
    QG = [(0, 2), (2, 2), (4, 1)]
    si = 0
    for h in range(8):
        vb = Vb[h % 2]
        for nb_ in range(16):
            pb = 6 + nb_ % 2
            for j in range(2):
                op("tensor", lambda e, j=j, nb_=nb_, pb=pb, h=h: e.matmul(bank(pb, 512, 64), lhsT=Wukv[:, j, h * 128:h * 128 + 64], rhs=ckvT[:, j, nb_ * 512:(nb_ + 1) * 512], start=(j == 0), stop=(j == 1)),
                   reads=["Wukv", "ckvT"], writes=[("ps", pb)])
            op("vector", lambda e, nb_=nb_, pb=pb: e.tensor_copy(out=KT[0:64, nb_ * 512:(nb_ + 1) * 512], in_=bank(pb, 512, 64)),
               reads=[("ps", pb)], writes=["KTn"])
        for k8 in range(8):
            pb = 6 + k8 % 2
            for i in range(8):
                kt = k8 * 8 + i
                for j in range(2):
                    op("tensor", lambda e, j=j, kt=kt, i=i, pb=pb, h=h: e.matmul(bank(pb)[:, i * 64:(i + 1) * 64], lhsT=ckvT[:, j, kt * 128:(kt + 1) * 128], rhs=Wukv[:, j, h * 128 + 64:h * 128 + 128], start=(j == 0), stop=(j == 1)),
                       reads=["Wukv", "ckvT"], writes=[("ps", pb)])
            op("vector", lambda e, k8=k8, pb=pb, vb=vb: e.tensor_copy(out=vb[:, k8 * 8:(k8 + 1) * 8, 0:64], in_=bank(pb).rearrange("p (t c) -> p t c", t=8)),
               reads=[("ps", pb)], writes=[("Vb", h % 2)])
        for (b0, nbk) in QG:
            ob = 4

            def emit_S(kt, si_):
                sbk = 2 * (si_ % 2)
                for bb in range(nbk):
                    c0, cn = XBLK[b0 + bb]
                    op("tensor", lambda e, bb=bb, c0=c0, cn=cn, kt=kt, sbk=sbk: e.matmul(bank(sbk + bb, cn), lhsT=KT[0:96, kt * 128:(kt + 1) * 128], rhs=QBT[0:96, h, c0:c0 + cn], start=True, stop=True),
                       reads=["KTn", "KTr", "QBT"], writes=[("ps", sbk + bb)])
            emit_S(0, si)
            for kt in range(64):
                sbk = 2 * (si % 2)
                pt = PT[si % 3]
                ptk = ("PT", si % 3)
                if kt + 1 < 64:
                    emit_S(kt + 1, si + 1)
                sv = ps[:, sbk * 512:(sbk + nbk) * 512].rearrange("p (a b) -> p a b", a=nbk)[:, :, 0:410]
                op("scalar", lambda e, sv=sv, pt=pt: e.activation(out=pt[:, 0:nbk, :], in_=sv, func=AF.Exp, scale=scale_b),
                   reads=[("ps", sbk + bb) for bb in range(nbk)], writes=[ptk])
                for bb in range(nbk):
                    op("tensor", lambda e, bb=bb, kt=kt, pt=pt: e.matmul(bank(ob + bb, 410, 65), lhsT=vb[:, kt, :], rhs=pt[:, bb, :], start=(kt == 0), stop=(kt == 63)),
                       reads=[ptk, ("Vb", h % 2)], writes=[("ps", ob + bb)])
                si += 1
            ov = ps[0:65, ob * 512:(ob + nbk) * 512].rearrange("p (a b) -> p a b", a=nbk)[:, :, 0:410]
            c0 = XBLK[b0][0]
            op("vector", lambda e, ov=ov, c0=c0, nbk=nbk: e.tensor_copy(out=OaccB[0:65, c0:c0 + nbk * 410].rearrange("p (a b) -> p a b", a=nbk), in_=ov),
               reads=[("ps", ob + bb) for bb in range(nbk)], writes=["OaccB"])
        oacc_to_tm(h, yb_tm, "OaccB", OaccB)
    tm_to_ynT(yb_tm, ynTb, "ynTb", ynb_tmp)
    dump("ynTb", ynTb, [128, 4, NX], BF16, ["ynTb"])
    dump("yb_tm", yb_tm, [128, 17, 512], F32, ["tm"])
    S.barrier()

    RT.off = RT.lo
    hTf = RT.alloc([128, 8, NX], BF16)
    A = Region(R1.lo, RC.hi)
    Wo = A.alloc([128, 8, D], BF16)
    stage4 = A.alloc([128, 8, 256], F32)
    gpo = A.alloc([128, 2, D], F32)
    xt_ = [A.alloc([128, D], F32) for _ in range(2)]
    xm_ = [A.alloc([128, D], F32) for _ in range(2)]
    hn_ = [A.alloc([128, D], BF16) for _ in range(2)]
    for c in range(4):
        load_weight(Wo[:, :, c * 256:(c + 1) * 256], w_o[:, c * 256:(c + 1) * 256], 8, 256, 13, stage4, "Wo")
    op("sync", lambda e: e.dma_start(out=gpo, in_=gpost), writes=["gpo"], dma=True)
    for t in range(17):
        M = 128 if t < 16 else 2
        bi = t % 2
        xt, xm, hn = xt_[bi], xm_[bi], hn_[bi]
        xk, mkk, hk = ("xt", bi), ("xm", bi), ("hn", bi)
        if t < 16:
            op("sync", lambda e, t=t, xt=xt: e.dma_start(out=xt, in_=xw[OWN0 + 128 * t:OWN0 + 128 * (t + 1), :]), writes=[xk], dma=True)
            cols = lambda k, t=t: (ynTa if k < 4 else ynTb)[:, k % 4, 1 + 128 * t:1 + 128 * (t + 1)]
        else:
            op("sync", lambda e, xt=xt: e.dma_start(out=xt[0:1, :], in_=xw[OWN0 - 1:OWN0, :]), writes=[xk], dma=True)
            op("sync", lambda e, xt=xt: e.dma_start(out=xt[1:2, :], in_=xw[OWN0 + OWN:OWN0 + OWN + 1, :]), writes=[xk], dma=True)
            cols = lambda k: (ynTa if k < 4 else ynTb)[:, k % 4, 0:NX:NX - 1]
        pb = 2 * (t % 2)
        for n2 in range(2):
            for k in range(8):
                la = cols(k)
                op("tensor", lambda e, k=k, n2=n2, M=M, pb=pb, la=la: e.matmul(bank(pb + n2, 512, M), lhsT=la, rhs=Wo[:, k, n2 * 512:(n2 + 1) * 512], start=(k == 0), stop=(k == 7)),
                   reads=["ynTa", "ynTb", "Wo"], writes=[("ps", pb + n2)])
        yv = ps[0:M, pb * 512:(pb + 2) * 512]
        sc = 128 + 2 * (t % 4)
        sk = ("stt5", t % 4)
        op("scalar", lambda e, yv=yv, M=M, sc=sc: e.activation(out=junk[0:M, :], in_=yv, func=AF.Square, accum_out=stt[0:M, sc:sc + 1]),
           reads=[("ps", pb), ("ps", pb + 1)], writes=["junk", sk])
        op("scalar", lambda e, M=M, sc=sc: e.activation(out=stt[0:M, sc:sc + 1], in_=stt[0:M, sc:sc + 1], func=AF.Sqrt, bias=EPS, scale=1.0 / D), reads=[sk], writes=[sk])
        op("vector", lambda e, M=M, sc=sc: e.reciprocal(out=stt[0:M, sc + 1:sc + 2], in_=stt[0:M, sc:sc + 1]), reads=[sk], writes=[sk])
        op("vector", lambda e, yv=yv, M=M, sc=sc, xm=xm: e.scalar_tensor_tensor(out=xm[0:M, :], in0=yv, scalar=stt[0:M, sc + 1:sc + 2], in1=gpo[0:M, 0, :], op0=ALU.mult, op1=ALU.mult),
           reads=[("ps", pb), ("ps", pb + 1), sk, "gpo"], writes=[mkk])
        op("vector", lambda e, M=M, xm=xm, xt=xt: e.tensor_tensor(out=xm[0:M, :], in0=xm[0:M, :], in1=xt[0:M, :], op=ALU.add), reads=[mkk, xk], writes=[mkk])
        if t < 16:
            op("sync", lambda e, t=t, xm=xm: e.dma_start(out=xmid[128 * t:128 * (t + 1), :], in_=xm), reads=[mkk], writes=["xmid"], dma=True)
        sc2 = 136 + 2 * (t % 4)
        sk2 = ("stt6", t % 4)
        op("scalar", lambda e, M=M, sc2=sc2, xm=xm: e.activation(out=junk[0:M, :], in_=xm[0:M, :], func=AF.Square, accum_out=stt[0:M, sc2:sc2 + 1]),
           reads=[mkk], writes=["junk", sk2])
        op("scalar", lambda e, M=M, sc2=sc2: e.activation(out=stt[0:M, sc2:sc2 + 1], in_=stt[0:M, sc2:sc2 + 1], func=AF.Sqrt, bias=EPS, scale=1.0 / D), reads=[sk2], writes=[sk2])
        op("vector", lambda e, M=M, sc2=sc2: e.reciprocal(out=stt[0:M, sc2 + 1:sc2 + 2], in_=stt[0:M, sc2:sc2 + 1]), reads=[sk2], writes=[sk2])
        op("vector", lambda e, M=M, sc2=sc2, xm=xm, hn=hn: e.tensor_scalar(out=hn[0:M, :], in0=xm[0:M, :], scalar1=stt[0:M, sc2 + 1:sc2 + 2], scalar2=None, op0=ALU.mult),
           reads=[mkk, sk2], writes=[hk])
        pb2 = 4 + t % 2
        for k in range(8):
            op("tensor", lambda e, k=k, M=M, hn=hn, pb2=pb2: e.transpose(out=bankb(pb2)[:, k * 128:k * 128 + M], in_=hn[0:M, k * 128:(k + 1) * 128], identity=idb[0:M, 0:M]),
               reads=[hk, "idb"], writes=[("ps", pb2)])
        srcv = bankb(pb2).rearrange("p (k t) -> p k t", k=8)[:, :, 0:M]
        dstv = hTf[:, :, 1 + 128 * t:1 + 128 * (t + 1)] if t < 16 else hTf[:, :, 0:NX:NX - 1]
        op("vector", lambda e, srcv=srcv, dstv=dstv: e.tensor_copy(out=dstv, in_=srcv), reads=[("ps", pb2)], writes=["hTf"])
    dump("hTf", hTf, [128, 8, NX], BF16, ["hTf"])
    S.barrier()

    aT = Region(R1.lo, RC.hi).alloc([128, 22, OWN], BF16)
    A = Region(RY.lo, RY.hi)
    stg = [A.alloc([128, 8, 256], F32) for _ in range(2)]
    Wub = [A.alloc([128, 8, 256], BF16) for _ in range(2)]
    cwt = A.alloc([128, 44, 4], F32)
    ufl = A.alloc([128, 2], F32)
    A = Region(RT.lo + 32832, SB_BYTES)
    cgb = [A.alloc([128, OWN], F32) for _ in range(2)]
    cvb = [A.alloc([128, OWN], F32) for _ in range(2)]
    op("sync", lambda e: e.dma_start(out=cwt, in_=cwb), writes=["cwt"], dma=True)
    op("sync", lambda e: e.dma_start(out=ufl, in_=uflag), writes=["ufl"], dma=True)
    OB = [(410 * i, min(410, OWN - 410 * i)) for i in range(5)]
    pbi = 0

    def f1_weights(j):
        bi = j % 2
        sg, wb = stg[bi], Wub[bi]
        sgk, wbk = ("stg", bi), ("Wub", bi)
        op("gpsimd", lambda e: e.dma_start(out=sg[:, :, 0:128], in_=w_up[:, j * 128:(j + 1) * 128].rearrange("(k p) n -> p k n", p=128)), writes=[sgk], dma=True)
        op("gpsimd", lambda e: e.dma_start(out=sg[:, :, 128:256], in_=w_up[:, DFF + j * 128:DFF + (j + 1) * 128].rearrange("(k p) n -> p k n", p=128)), writes=[sgk], dma=True)
        for k in range(8):
            if k % 2 == 0:
                op("vector", lambda e, k=k: e.tensor_scalar(out=wb[:, k, :], in0=sg[:, k, :], scalar1=gpt[:, 21 + k:22 + k], scalar2=None, op0=ALU.mult),
                   reads=[sgk, "gpt"], writes=[wbk])
            else:
                op("scalar", lambda e, k=k: e.activation(out=wb[:, k, :], in_=sg[:, k, :], func=AF.Identity, scale=gpt[:, 21 + k:22 + k]),
                   reads=[sgk, "gpt"], writes=[wbk])

    f1_weights(0)
    for j in range(22):
        bi = j % 2
        wb = Wub[bi]
        wbk = ("Wub", bi)
        if j + 1 < 22:
            f1_weights(j + 1)
        for half in range(2):
            cb = (cgb if half == 0 else cvb)[bi]
            ck = ("cg" if half == 0 else "cv", bi)
            ff = j + 22 * half
            for bx, (o0, n) in enumerate(OB):
                pb = pbi % 8
                pbi += 1
                for k in range(8):
                    op("tensor", lambda e, k=k: e.matmul(bank(pb, n + 2), lhsT=wb[:, k, half * 128:(half + 1) * 128], rhs=hTf[:, k, o0:o0 + n + 2], start=(k == 0), stop=(k == 7)),
                       reads=[wbk, "hTf"], writes=[("ps", pb)])
                if bx == 0:
                    op("vector", lambda e: e.tensor_tensor(out=bank(pb, 1), in0=bank(pb, 1), in1=ufl[:, 0:1], op=ALU.mult), reads=[("ps", pb), "ufl"], writes=[("ps", pb)])
                if bx == 4:
                    op("vector", lambda e: e.tensor_tensor(out=bank(pb, n + 2)[:, n + 1:n + 2], in0=bank(pb, n + 2)[:, n + 1:n + 2], in1=ufl[:, 1:2], op=ALU.mult), reads=[("ps", pb), "ufl"], writes=[("ps", pb)])
                op("scalar", lambda e: e.activation(out=cb[:, o0:o0 + n], in_=bank(pb, n + 2)[:, 1:n + 1], func=AF.Identity, bias=cwt[:, ff, 3:4], scale=cwt[:, ff, 1:2]),
                   reads=[("ps", pb), "cwt"], writes=[ck])
                op("vector", lambda e: e.scalar_tensor_tensor(out=cb[:, o0:o0 + n], in0=bank(pb, n + 2)[:, 0:n], scalar=cwt[:, ff, 0:1], in1=cb[:, o0:o0 + n], op0=ALU.mult, op1=ALU.add),
                   reads=[("ps", pb), "cwt", ck], writes=[ck])
                op("vector", lambda e: e.scalar_tensor_tensor(out=cb[:, o0:o0 + n], in0=bank(pb, n + 2)[:, 2:n + 2], scalar=cwt[:, ff, 2:3], in1=cb[:, o0:o0 + n], op0=ALU.mult, op1=ALU.add),
                   reads=[("ps", pb), "cwt", ck], writes=[ck])
        cg, cv = cgb[bi], cvb[bi]
        op("scalar", lambda e: e.activation(out=cg, in_=cg, func=AF.Gelu_apprx_tanh), reads=[("cg", bi)], writes=[("cg", bi)])
        op("vector", lambda e: e.tensor_tensor(out=aT[:, j, :], in0=cg, in1=cv, op=ALU.mult), reads=[("cg", bi), ("cv", bi)], writes=["aT"])
    dump("aT", aT, [128, 22, OWN], BF16, ["aT"])
    S.barrier()

    A = Region(RY.lo, SB_BYTES)
    Wdn = A.alloc([128, 22, D], BF16)
    stage5 = A.alloc([128, 2, D], F32)
    gpo2 = A.alloc([128, D], F32)
    xm2 = [A.alloc([128, D], F32) for _ in range(2)]
    ot = [A.alloc([128, D], F32) for _ in range(2)]
    for c in range(11):
        load_weight(Wdn[:, 2 * c:2 * c + 2, :], w_down[256 * c:256 * (c + 1), :], 2, D, None, stage5, "Wdn")
    op("sync", lambda e: e.dma_start(out=gpo2, in_=gpost[:, 1, :]), writes=["gpo2"], dma=True)
    for t in range(16):
        bi = t % 2
        xk, ok = ("xm2", bi), ("ot", bi)
        op("sync", lambda e, t=t, bi=bi: e.dma_start(out=xm2[bi], in_=xmid[128 * t:128 * (t + 1), :]), reads=["xmid"], writes=[xk], dma=True)
        pb = 2 * (t % 2)
        for n2 in range(2):
            for j in range(22):
                op("tensor", lambda e, j=j, n2=n2, t=t, pb=pb: e.matmul(bank(pb + n2), lhsT=aT[:, j, 128 * t:128 * (t + 1)], rhs=Wdn[:, j, n2 * 512:(n2 + 1) * 512], start=(j == 0), stop=(j == 21)),
                   reads=["aT", "Wdn"], writes=[("ps", pb + n2)])
        yv = ps[:, pb * 512:(pb + 2) * 512]
        sc = 144 + 2 * (t % 4)
        sk = ("stt7", t % 4)
        op("scalar", lambda e, yv=yv, sc=sc: e.activation(out=junk, in_=yv, func=AF.Square, accum_out=stt[:, sc:sc + 1]),
           reads=[("ps", pb), ("ps", pb + 1)], writes=["junk", sk])
        op("scalar", lambda e, sc=sc: e.activation(out=stt[:, sc:sc + 1], in_=stt[:, sc:sc + 1], func=AF.Sqrt, bias=EPS, scale=1.0 / D), reads=[sk], writes=[sk])
        op("vector", lambda e, sc=sc: e.reciprocal(out=stt[:, sc + 1:sc + 2], in_=stt[:, sc:sc + 1]), reads=[sk], writes=[sk])
        op("vector", lambda e, yv=yv, sc=sc, bi=bi: e.scalar_tensor_tensor(out=ot[bi], in0=yv, scalar=stt[:, sc + 1:sc + 2], in1=gpo2, op0=ALU.mult, op1=ALU.mult),
           reads=[("ps", pb), ("ps", pb + 1), sk, "gpo2"], writes=[ok])
        op("vector", lambda e, bi=bi: e.tensor_tensor(out=ot[bi], in0=ot[bi], in1=xm2[bi], op=ALU.add), reads=[ok, xk], writes=[ok])
        op("sync", lambda e, t=t, bi=bi: e.dma_start(out=yout[128 * t:128 * (t + 1), :], in_=ot[bi]), reads=[ok], dma=True)
    S.emit()
    return nc


_CACHE = {}


def _consts():
    if "c" in _CACHE:
        return _CACHE["c"]
    slopes = np.exp2(-8.0 * np.arange(1, 9, dtype=np.float32) / 8).astype(np.float32)
    k = np.arange(128)[:, None]
    q = np.arange(128)[None, :]
    masks = np.zeros((8, 128, 6, 128), np.float32)
    for h in range(8):
        for ri, r in enumerate((1, 4, 16)):
            d0 = k - 64 - q
            d1 = k + 64 - q
            masks[h, :, 2 * ri, :] = np.where(k >= q, np.exp(-slopes[h] * (np.abs(d0) * r).astype(np.float32)), 0.0)
            masks[h, :, 2 * ri + 1, :] = np.where(k <= q, np.exp(-slopes[h] * (np.abs(d1) * r).astype(np.float32)), 0.0)
    inv_freq = np.exp(-np.log(10000.0) * np.arange(0, 32, 2, dtype=np.float32) / 32).astype(np.float32)
    pos = np.arange(S_LEN, dtype=np.float32)
    ang = pos[:, None] * inv_freq[None, :]
    cosk = np.cos(ang).astype(np.float32).reshape(64, 128, 16).transpose(1, 0, 2)
    sink = np.sin(ang).astype(np.float32).reshape(64, 128, 16).transpose(1, 0, 2)
    rk = np.ascontiguousarray(np.stack([cosk, sink], axis=1))
    c = dict(masks=masks, inv_freq=inv_freq, rk=rk, ident=np.eye(128, dtype=np.float32))
    _CACHE["c"] = c
    return c


def _core_inputs(c, x, shared):
    cst = _consts()
    b, qc = c // 4, c % 4
    T0 = qc * OWN
    pos_w = T0 - OWN0 + np.arange(NW)
    valid = (pos_w >= 0) & (pos_w < S_LEN)
    xw = np.zeros((NW, D), np.float32)
    xw[valid] = x[b, pos_w[valid]]
    kvf = np.zeros((128, NVT), np.float32)
    for i, (r, es, nk) in enumerate(VLIST):
        kvf[:nk, i] = valid[es + r * np.arange(nk)].astype(np.float32)
    posq = (T0 - 1 + np.arange(NX)).astype(np.float32)
    ang = posq[None, :] * cst["inv_freq"][:, None]
    cq, sq = np.cos(ang).astype(np.float32), np.sin(ang).astype(np.float32)
    rq = np.ascontiguousarray(np.stack([np.concatenate([cq, cq], 0), np.concatenate([sq, sq], 0)], axis=1))
    uflag = np.zeros((128, 2), np.float32)
    uflag[:, 0] = 1.0 if T0 > 0 else 0.0
    uflag[:, 1] = 1.0 if T0 + OWN < S_LEN else 0.0
    d = dict(shared)
    d.update(xw=xw, xf=np.ascontiguousarray(x[b]), kvf=kvf, rq=rq, uflag=uflag, masks=cst["masks"], rk=cst["rk"], ident=cst["ident"])
    return d


def kernel(x, norm_mix_pre, w_in, q_lat_norm, w_uq, kv_lat_norm, w_ukv, out_norm_a, out_norm_b, w_o,
           norm_mix_post, norm_ffn_pre, w_up, conv_w, conv_b, w_down, norm_ffn_post):
    f = lambda a: np.ascontiguousarray(np.asarray(a, dtype=np.float32))
    x = f(x)
    gp = np.zeros((128, 32), np.float32)
    gp[:, 0:8] = f(norm_mix_pre)[0].reshape(8, 128).T
    gp[:, 8:11] = f(q_lat_norm)[0].reshape(3, 128).T
    gp[:, 11:13] = f(kv_lat_norm)[0].reshape(2, 128).T
    gp[:, 13:21] = np.concatenate([f(out_norm_a)[0], f(out_norm_b)[0]]).reshape(8, 128).T
    gp[:, 21:29] = f(norm_ffn_pre)[0].reshape(8, 128).T
    gpost = np.ascontiguousarray(np.broadcast_to(np.stack([f(norm_mix_post)[0], f(norm_ffn_post)[0]])[None], (128, 2, D)))
    cwb = np.zeros((128, 44, 4), np.float32)
    cwb[:, :, 0:3] = f(conv_w)[0].T.reshape(44, 128, 3).transpose(1, 0, 2)
    cwb[:, :, 3] = f(conv_b)[0].reshape(44, 128).T
    shared = dict(w_in=f(w_in)[0], w_uq=f(w_uq)[0], w_ukv=f(w_ukv)[0], w_o=f(w_o)[0], w_up=f(w_up)[0], w_down=f(w_down)[0],
                  gp=gp, gpost=gpost, cwb=cwb)
    if "nc" not in _CACHE:
        _CACHE["nc"] = build()
    nc = _CACHE["nc"]
    in_maps = [_core_inputs(c, x, shared) for c in range(8)]
    res = run_bass_kernel_spmd(nc, in_maps, core_ids=list(range(8)))
    out = np.zeros((2, S_LEN, D), np.float32)
    for c in range(8):
        b, qc = c // 4, c % 4
        out[b, qc * OWN:(qc + 1) * OWN] = res.results[c]["y"]
    return out
```

```python
import contextlib
import types
import numpy as np
import ml_dtypes
import concourse.bass as bass
import concourse.mybir as mybir
from concourse.bass_utils import run_bass_kernel_spmd

F32 = mybir.dt.float32
BF16 = mybir.dt.bfloat16
U8 = mybir.dt.uint8
AF = mybir.ActivationFunctionType
ALU = mybir.AluOpType

S_LEN = 8192
D = 1024
OWN = 2048
OWN0 = 1152
NW = 4352
NX = 2050
DFF = 2816
EPS = 1e-6
ENGS = ("sync", "scalar", "gpsimd", "vector", "tensor")
XBLK = [(i * 410, 410) for i in range(5)]


class Sched:
    def __init__(self, nc, ndma_sems=8):
        self.nc = nc
        self.ops = []
        self.ndma = ndma_sems

    @staticmethod
    def _freeze(fn):
        if fn.__closure__ is None:
            return fn
        cells = []
        for c in fn.__closure__:
            try:
                cells.append(types.CellType(c.cell_contents))
            except ValueError:
                cells.append(c)
        return types.FunctionType(fn.__code__, fn.__globals__, fn.__name__, fn.__defaults__, tuple(cells))

    def op(self, eng, fn, reads=(), writes=(), dma=False):
        fn = self._freeze(fn)
        self.ops.append(dict(eng=eng, fn=fn, reads=tuple(reads), writes=tuple(writes), dma=dma, bar=False))

    def barrier(self):
        self.ops.append(dict(eng=None, fn=None, reads=(), writes=(), dma=False, bar=True))

    def emit(self, final_wait_eng="sync"):
        nc = self.nc
        ops = self.ops
        n = len(ops)
        last_writer = {}
        readers = {}
        deps = [set() for _ in range(n)]
        since_bar = []
        pending_bar = {}
        for i, o in enumerate(ops):
            if o["bar"]:
                lastc = {}
                dl = set()
                for j in since_bar:
                    if ops[j]["dma"]:
                        dl.add(j)
                    else:
                        lastc[ops[j]["eng"]] = j
                dl.update(lastc.values())
                for e in ENGS:
                    pending_bar[e] = set(dl) | pending_bar.get(e, set())
                since_bar = []
                continue
            d = deps[i]
            if o["eng"] in pending_bar:
                d.update(pending_bar.pop(o["eng"]))
            for r in o["reads"]:
                if r in last_writer:
                    d.add(last_writer[r])
            for w in o["writes"]:
                if w in last_writer:
                    d.add(last_writer[w])
                d.update(readers.get(w, ()))
            d.discard(i)
            for w in o["writes"]:
                last_writer[w] = i
                readers[w] = []
            for r in o["reads"]:
                if r not in o["writes"]:
                    readers.setdefault(r, []).append(i)
            since_bar.append(i)
        needed = set()
        red = [None] * n
        for i, o in enumerate(ops):
            if o["bar"]:
                continue
            per_eng = {}
            dl = []
            for j in deps[i]:
                pj = ops[j]
                if pj["dma"]:
                    dl.append(j)
                    continue
                if pj["eng"] == o["eng"] and not o["dma"] and o["eng"] == "tensor":
                    continue
                e = pj["eng"]
                if e not in per_eng or per_eng[e] < j:
                    per_eng[e] = j
            dl.extend(per_eng.values())
            red[i] = dl
            needed.update(dl)
        cnt = {e: 0 for e in ENGS}
        dcnt = {}
        sig = [None] * n
        dma_idx = {e: 0 for e in ENGS}
        for i, o in enumerate(ops):
            if o["bar"]:
                continue
            if o["dma"]:
                k = dma_idx[o["eng"]] % self.ndma
                dma_idx[o["eng"]] += 1
                key = ("dma", o["eng"], k)
                prev = dcnt.get(key, 0)
                dcnt[key] = prev + 16
                sig[i] = (key, prev + 16)
                o["dma_prev"] = (key, prev) if prev > 0 else None
            elif i in needed:
                cnt[o["eng"]] += 1
                sig[i] = (("eng", o["eng"]), cnt[o["eng"]])
        semkeys = sorted({s[0] for s in sig if s is not None}, key=str)
        stack = contextlib.ExitStack()
        sems = {}
        for sk in semkeys:
            sems[sk] = stack.enter_context(nc.semaphore("s_" + "_".join(str(x) for x in sk)))
        by_eng = {e: [i for i, o in enumerate(ops) if o["eng"] == e] for e in ENGS}
        dma_final = list(dcnt.items())

        def run(engname, eng):
            waited = {}
            for i in by_eng[engname]:
                o = ops[i]
                wl = [sig[j] for j in red[i]]
                if o["dma"] and o.get("dma_prev"):
                    wl.append(o["dma_prev"])
                for (sk, v) in wl:
                    if waited.get(sk, 0) >= v:
                        continue
                    eng.wait_ge(sems[sk], v)
                    waited[sk] = v
                ins = o["fn"](eng)
                if sig[i] is not None:
                    ins.then_inc(sems[sig[i][0]], 16 if o["dma"] else 1)
            if engname == final_wait_eng:
                for sk, v in dma_final:
                    if waited.get(sk, 0) < v:
                        eng.wait_ge(sems[sk], v)
                for e in ENGS:
                    if cnt[e] > 0 and waited.get(("eng", e), 0) < cnt[e]:
                        eng.wait_ge(sems[("eng", e)], cnt[e])

        with stack:
            with nc.Block() as block:
                @block.sync
                def _(e):
                    run("sync", e)

                @block.scalar
                def _(e):
                    run("scalar", e)

                @block.gpsimd
                def _(e):
                    run("gpsimd", e)

                @block.vector
                def _(e):
                    run("vector", e)

                @block.tensor
                def _(e):
                    run("tensor", e)


def dil_tables():
    vt = {}

    def vtile(r, e0, nk):
        key = (r, e0, nk)
        if key not in vt:
            vt[key] = len(vt)
        return vt[key]

    groups = {1: [], 4: [], 16: []}
    for r in (1, 4, 16):
        def blk(x0, N):
            eq0 = OWN0 - 1 + x0
            t0 = vtile(r, eq0 - 64 * r, 128)
            t1 = vtile(r, eq0 + 64 * r, 128 if N > 1 else 1)
            return (x0, N, t0, t1)
        if r == 1:
            for g in range(4):
                groups[r].append(("reg", g, [blk(1 + 128 * (4 * g + b), 128) for b in range(4)]))
        elif r == 4:
            for g in range(4):
                groups[r].append(("reg", g, [blk(1 + c + 512 * g, 128) for c in range(4)]))
        else:
            for g in range(4):
                groups[r].append(("reg", g, [blk(1 + 4 * g + c, 128) for c in range(4)]))
        groups[r].append(("halo", 0, [blk(0, 1), blk(NX - 1, 1)]))
    vlist = [None] * len(vt)
    for k, i in vt.items():
        vlist[i] = k
    return vlist, groups


VLIST, DGROUPS = dil_tables()
NVT = len(VLIST)


def build(dbg=False):
    nc = bass.Bass("TRN2", target_bir_lowering=False)

    def din(name, shape, dt=F32):
        return nc.dram_tensor(name, list(shape), dt, kind="ExternalInput").ap()

    xw = din("xw", [NW, D])
    xf = din("xf", [S_LEN, D])
    w_in = din("w_in", [D, 2208])
    w_uq = din("w_uq", [384, 768])
    w_ukv = din("w_ukv", [256, 1024])
    w_o = din("w_o", [D, D])
    w_up = din("w_up", [D, 2 * DFF])
    w_down = din("w_down", [DFF, D])
    gp = din("gp", [128, 32])
    gpost = din("gpost", [128, 2, D])
    cwb = din("cwb", [128, 44, 4])
    masks = din("masks", [8, 128, 6, 128])
    kvf = din("kvf", [128, NVT])
    rq = din("rq", [32, 2, NX])
    rk = din("rk", [128, 2, 64, 16])
    ident = din("ident", [128, 128])
    uflag = din("uflag", [128, 2])
    yout = nc.dram_tensor("y", [OWN, D], F32, kind="ExternalOutput").ap()
    xmid = nc.dram_tensor("xmid", [OWN, D], F32, kind="Internal").ap()
    dbg_out = {}

    SB_BYTES = 212000
    big = nc.alloc_sbuf_tensor("big", [128, SB_BYTES], U8).ap()
    ps = nc.alloc_psum_tensor("ps", [128, 4096], F32).ap()
    psb = ps.bitcast(BF16)

    class Region:
        def __init__(self, lo, hi):
            assert hi <= SB_BYTES and lo <= hi, (lo, hi)
            self.lo, self.hi, self.off = lo, hi, lo

        def alloc(self, shape, dt, p0=0):
            esz = 4 if dt == F32 else 2
            nb = int(np.prod(shape[1:])) * esz
            nb_al = (nb + 63) // 64 * 64
            assert self.off + nb_al <= self.hi, ("SBUF region overflow", self.lo, self.hi, self.off, nb_al)
            v = big[p0:p0 + shape[0], self.off:self.off + nb].bitcast(dt)
            self.off += nb_al
            if len(shape) == 3:
                v = v.rearrange("p (a b) -> p a b", a=shape[1])
            elif len(shape) == 4:
                v = v.rearrange("p (a b c) -> p a b c", a=shape[1], b=shape[2])
            return v

    RP = Region(0, 4096)
    R1 = Region(RP.hi, RP.hi + 86080)
    RC = Region(R1.hi, R1.hi + 12352)
    RY = Region(RC.hi, RC.hi + 16448 + 4096 + 16448)
    RT = Region(RY.hi, SB_BYTES)
    A = RP
    S = Sched(nc)
    op = S.op

    def dump(name, ap, shape, dt, keys):
        if not dbg:
            return
        import os
        sel = os.environ.get("DBGSEL", "")
        if sel and name not in sel.split(","):
            return
        d = nc.dram_tensor("dbg_" + name, list(shape), dt, kind="ExternalOutput").ap()
        op("sync", lambda e: e.dma_start(out=d, in_=ap), reads=keys, dma=True)

    def bank(b, n=512, p=128, p0=0):
        return ps[p0:p0 + p, b * 512:b * 512 + n]

    def bankb(b, n=1024, p=128, p0=0):
        return psb[p0:p0 + p, b * 1024:b * 1024 + n]

    idf = A.alloc([128, 128], F32)
    idb = A.alloc([128, 128], BF16)
    gpt = A.alloc([128, 32], F32)
    stt = A.alloc([128, 256], F32)
    junk = A.alloc([128, 1024], BF16)
    rec = A.alloc([128, 8], F32)
    op("sync", lambda e: e.dma_start(out=idf, in_=ident), writes=["idf"], dma=True)
    op("sync", lambda e: e.dma_start(out=gpt, in_=gp), writes=["gpt"], dma=True)
    op("vector", lambda e: e.tensor_copy(out=idb, in_=idf), reads=["idf"], writes=["idb"])

    def rstd_from_ss(ss_ap, out_ap, n, key):
        tmp = ss_ap
        op("scalar", lambda e: e.activation(out=tmp, in_=ss_ap, func=AF.Sqrt, bias=EPS, scale=1.0 / n),
           reads=[key], writes=[key])
        op("vector", lambda e: e.reciprocal(out=out_ap, in_=tmp), reads=[key], writes=[key])

    def load_weight(dst, src_ap, kch, ncols, gcol, stage, wkey, negate_cols=None):
        op("gpsimd", lambda e: e.dma_start(out=stage[:, 0:kch, 0:ncols], in_=src_ap.rearrange("(k p) n -> p k n", p=128)),
           writes=[("stage", id(stage))], dma=True)
        for k in range(kch):
            eng = "vector" if k % 2 == 0 else "scalar"
            if gcol is None:
                if eng == "vector":
                    op(eng, lambda e, k=k: e.tensor_copy(out=dst[:, k, :], in_=stage[:, k, 0:ncols]),
                       reads=[("stage", id(stage))], writes=[wkey])
                else:
                    op(eng, lambda e, k=k: e.copy(out=dst[:, k, :], in_=stage[:, k, 0:ncols]),
                       reads=[("stage", id(stage))], writes=[wkey])
            else:
                if eng == "vector":
                    op(eng, lambda e, k=k: e.tensor_scalar(out=dst[:, k, :], in0=stage[:, k, 0:ncols],
                                                          scalar1=gpt[:, gcol + k:gcol + k + 1], scalar2=None, op0=ALU.mult),
                       reads=[("stage", id(stage)), "gpt"], writes=[wkey])
                else:
                    op(eng, lambda e, k=k: e.activation(out=dst[:, k, :], in_=stage[:, k, 0:ncols], func=AF.Identity,
                                                        scale=gpt[:, gcol + k:gcol + k + 1]),
                       reads=[("stage", id(stage)), "gpt"], writes=[wkey])

    def norm_tiles_to_T(src_dram_rows, ntile, xbuf, xnbuf, key):
        op("sync", lambda e: e.dma_start(out=xbuf[:, 0:ntile, :], in_=src_dram_rows.rearrange("(t p) f -> p t f", p=128)),
           writes=[("x", key)], dma=True)
        for t in range(ntile):
            op("scalar", lambda e, t=t: e.activation(out=junk, in_=xbuf[:, t, :], func=AF.Square, accum_out=stt[:, t:t + 1]),
               reads=[("x", key)], writes=["junk", "stt"])
        rstd_from_ss(stt[:, 0:ntile], stt[:, 8:8 + ntile], D, "stt")
        op("vector", lambda e: e.tensor_tensor(out=xnbuf[:, 0:ntile, :], in0=xbuf[:, 0:ntile, :],
                                               in1=stt[:, 8:8 + ntile].unsqueeze(2).to_broadcast([128, ntile, D]), op=ALU.mult),
           reads=[("x", key), "stt"], writes=[("xn", key)])

    def transpose_tile(xn_tile, hT_dst, pbank, rkeys, wkey):
        for k in range(8):
            op("tensor", lambda e, k=k: e.transpose(out=bankb(pbank)[:, k * 128:(k + 1) * 128], in_=xn_tile[:, k * 128:(k + 1) * 128], identity=idb),
               reads=list(rkeys) + ["idb"], writes=[("ps", pbank)])
        op("vector", lambda e: e.tensor_copy(out=hT_dst, in_=bankb(pbank).rearrange("p (k t) -> p k t", k=8)),
           reads=[("ps", pbank)], writes=[wkey])

    KAT = R1.alloc([128, 4, NW], BF16)
    VAT = R1.alloc([128, 4, NW], BF16)
    QAT = R1.alloc([128, 4, NX], BF16)
    cqT = RC.alloc([128, 3, NX], BF16)
    A = Region(RC.hi, SB_BYTES)
    WA = A.alloc([128, 8, 1920], BF16)
    stage = A.alloc([128, 8, 480], F32)
    xbuf = [A.alloc([128, 2, D], F32) for _ in range(2)]
    xnb = [A.alloc([128, 2, D], BF16) for _ in range(2)]
    hTw = [A.alloc([128, 8, 256], BF16) for _ in range(2)]
    cqn = A.alloc([128, 384], BF16)
    for c in range(4):
        load_weight(WA[:, :, c * 480:(c + 1) * 480], w_in[:, c * 480:(c + 1) * 480], 8, 480, 0, stage, "WA")

    def cq_tile(cols_ap, M, xcol_ap_fn, hkey, pb):
        for k in range(8):
            la = cols_ap(k)
            op("tensor", lambda e, k=k, la=la: e.matmul(bank(pb, 384, M), lhsT=la, rhs=WA[:, k, 1536:1920], start=(k == 0), stop=(k == 7)),
               reads=[hkey, "WA"], writes=[("ps", pb)])
        op("scalar", lambda e: e.activation(out=junk[0:M, 0:384], in_=bank(pb, 384, M), func=AF.Square, accum_out=stt[0:M, 16:17]),
           reads=[("ps", pb)], writes=["junk", "stt2"])
        op("scalar", lambda e: e.activation(out=stt[0:M, 16:17], in_=stt[0:M, 16:17], func=AF.Sqrt, bias=EPS, scale=1.0 / 384),
           reads=["stt2"], writes=["stt2"])
        op("vector", lambda e: e.reciprocal(out=stt[0:M, 17:18], in_=stt[0:M, 16:17]), reads=["stt2"], writes=["stt2"])
        op("vector", lambda e: e.tensor_scalar(out=cqn[0:M, :], in0=bank(pb, 384, M), scalar1=stt[0:M, 17:18], scalar2=None, op0=ALU.mult),
           reads=[("ps", pb), "stt2"], writes=["cqn"])
        for j in range(3):
            op("tensor", lambda e, j=j: e.transpose(out=bankb(pb)[:, j * 128:j * 128 + M], in_=cqn[0:M, j * 128:(j + 1) * 128], identity=idb[0:M, 0:M]),
               reads=["cqn", "idb"], writes=[("ps", pb)])
        xc = xcol_ap_fn()
        op("vector", lambda e: e.tensor_copy(out=xc, in_=bankb(pb)[:, 0:384].rearrange("p (j t) -> p j t", j=3)[:, :, 0:M]),
           reads=[("ps", pb)], writes=["cqT"])

    NGW = NW // 256
    for g in range(NGW):
        bi = g % 2
        e0 = g * 256
        norm_tiles_to_T(xw[e0:e0 + 256, :], 2, xbuf[bi], xnb[bi], ("w", bi))
        for t in range(2):
            transpose_tile(xnb[bi][:, t, :], hTw[bi][:, :, t * 128:(t + 1) * 128], t, [("xn", ("w", bi))], ("hTw", bi))
        xlo, xhi = max(e0, OWN0 - 1), min(e0 + 256, OWN0 - 1 + NX)
        jobs = [("K", 512 + 128 * c, c) for c in range(4)] + [("V", 1024 + 128 * c, c) for c in range(4)]
        if xhi > xlo:
            jobs += [("Q", 128 * c, c) for c in range(4)]
        for ji, (kind, col0, c) in enumerate(jobs):
            pb = 2 + (ji % 4)
            for k in range(8):
                op("tensor", lambda e, k=k, col0=col0, pb=pb: e.matmul(bank(pb, 256), lhsT=WA[:, k, col0:col0 + 128], rhs=hTw[bi][:, k, :], start=(k == 0), stop=(k == 7)),
                   reads=[("hTw", bi), "WA"], writes=[("ps", pb)])
            if kind == "K":
                op("vector", lambda e, c=c, pb=pb: e.tensor_copy(out=KAT[:, c, e0:e0 + 256], in_=bank(pb, 256)), reads=[("ps", pb)], writes=["KAT"])
            elif kind == "V":
                op("scalar", lambda e, c=c, pb=pb: e.copy(out=VAT[:, c, e0:e0 + 256], in_=bank(pb, 256)), reads=[("ps", pb)], writes=["VAT"])
            else:
                op("vector", lambda e, c=c, pb=pb: e.tensor_copy(out=QAT[:, c, xlo - (OWN0 - 1):xhi - (OWN0 - 1)], in_=bank(pb, 256)[:, xlo - e0:xhi - e0]),
                   reads=[("ps", pb)], writes=["QAT"])
        for t in range(2):
            et = e0 + t * 128
            if OWN0 <= et < OWN0 + OWN:
                x0 = et - (OWN0 - 1)
                cq_tile(lambda k, t=t: hTw[bi][:, k, t * 128:(t + 1) * 128], 128, lambda x0=x0: cqT[:, :, x0:x0 + 128], ("hTw", bi), 6 + t)
            if et == OWN0 - 128:
                cq_tile(lambda k, t=t: hTw[bi][:, k, t * 128 + 127:t * 128 + 128], 1, lambda: cqT[:, :, 0:1], ("hTw", bi), 6 + t)
            if et == OWN0 + OWN:
                cq_tile(lambda k, t=t: hTw[bi][:, k, t * 128:t * 128 + 1], 1, lambda: cqT[:, :, NX - 1:NX], ("hTw", bi), 6 + t)
    dump("KAT", KAT, [128, 4, NW], BF16, ["KAT"])
    dump("VAT", VAT, [128, 4, NW], BF16, ["VAT"])
    dump("QAT", QAT, [128, 4, NX], BF16, ["QAT"])
    dump("cqT", cqT, [128, 3, NX], BF16, ["cqT"])
    S.barrier()

    ynTa = RY.alloc([128, 4, NX], BF16)
    A = Region(RY.lo + 16448, SB_BYTES)
    mk = [A.alloc([128, 6, 128], F32) for _ in range(2)]
    kvft = A.alloc([128, NVT], F32)
    Vp = [A.alloc([128, NVT, 65], BF16) for _ in range(2)]
    Oacc = A.alloc([128, NX], F32)
    ya_tm = A.alloc([128, 17, 512], F32)
    Eb = [A.alloc([128, 1024], F32) for _ in range(2)]
    Pb = [A.alloc([128, 1024], BF16) for _ in range(2)]
    op("sync", lambda e: e.dma_start(out=kvft, in_=kvf), writes=["kvft"], dma=True)

    def oacc_to_tm(h, dst_tm, okey, Oacc):
        tiles = [(1 + 128 * m, 128, 1) for m in range(16)] + [(0, 2, NX - 1)]
        for g0 in range(0, 17, 4):
            tl = tiles[g0:g0 + 4]
            pb = 6 + (g0 // 4) % 2
            for i, (x0, M, step) in enumerate(tl):
                src = Oacc[0:65, x0:x0 + 128] if step == 1 else Oacc[0:65, 0:NX:NX - 1]
                op("tensor", lambda e, i=i, src=src, M=M, pb=pb: e.transpose(out=bank(pb)[0:M, i * 65:(i + 1) * 65], in_=src, identity=idf[0:65, 0:65]),
                   reads=[okey, "idf"], writes=[("ps", pb)])
            nt = len(tl)
            M = tl[0][1]
            pv = bank(pb)[0:M, 0:nt * 65].rearrange("p (t c) -> p t c", t=nt)
            op("vector", lambda e, pv=pv, nt=nt, M=M: e.reciprocal(out=rec[0:M, 0:nt], in_=pv[:, :, 64]), reads=[("ps", pb)], writes=["rec"])
            op("vector", lambda e, pv=pv, nt=nt, M=M, g0=g0: e.tensor_tensor(out=dst_tm[0:M, g0:g0 + nt, h * 64:(h + 1) * 64], in0=pv[:, :, 0:64],
                                                                         in1=rec[0:M, 0:nt].unsqueeze(2).to_broadcast([M, nt, 64]), op=ALU.mult),
               reads=[("ps", pb), "rec"], writes=["tm"])

    def tm_to_ynT(src_tm, dstT, nkey, Pb):
        for t in range(17):
            M = 128 if t < 16 else 2
            op("scalar", lambda e, t=t, M=M: e.activation(out=junk[0:M, 0:512], in_=src_tm[0:M, t, :], func=AF.Square, accum_out=stt[0:M, 32 + t:33 + t]),
               reads=["tm"], writes=["junk", "stt3"])
        rstd_from_ss(stt[:, 32:49], stt[:, 64:81], 512, "stt3")
        for t in range(17):
            M = 128 if t < 16 else 2
            pb = t % 2
            ynb = Pb[t % 2]
            op("vector", lambda e, t=t, M=M, ynb=ynb: e.tensor_scalar(out=ynb[0:M, 0:512], in0=src_tm[0:M, t, :], scalar1=stt[0:M, 64 + t:65 + t], scalar2=None, op0=ALU.mult),
               reads=["tm", "stt3"], writes=[("Pb", t % 2)])
            for j in range(4):
                op("tensor", lambda e, j=j, M=M, ynb=ynb, pb=pb: e.transpose(out=bankb(pb)[:, j * 128:j * 128 + M], in_=ynb[0:M, j * 128:(j + 1) * 128], identity=idb[0:M, 0:M]),
                   reads=[("Pb", t % 2), "idb"], writes=[("ps", pb)])
            srcv = bankb(pb)[:, 0:512].rearrange("p (j t) -> p j t", j=4)[:, :, 0:M]
            if t < 16:
                dstv = dstT[:, :, 1 + 128 * t:1 + 128 * (t + 1)]
            else:
                dstv = dstT[:, :, 0:NX:NX - 1]
            op("vector", lambda e, srcv=srcv, dstv=dstv: e.tensor_copy(out=dstv, in_=srcv), reads=[("ps", pb)], writes=[nkey])

    def d_prep(h):
        pr, hs = h // 2, (h % 2) * 64
        vb = Vp[h % 2]
        mb = mk[h % 2]
        op("sync", lambda e: e.dma_start(out=mb, in_=masks[h]), writes=[("mk", h % 2)], dma=True)
        for v0 in range(0, NVT, 8):
            vts = VLIST[v0:v0 + 8]
            pb = 4 + (v0 // 8) % 2
            for i, (r, es, nk) in enumerate(vts):
                op("tensor", lambda e, i=i, r=r, es=es, nk=nk: e.transpose(out=bankb(pb)[0:nk, i * 64:(i + 1) * 64],
                                                                        in_=VAT[hs:hs + 64, pr, es:es + r * (nk - 1) + 1:r], identity=idb[hs:hs + 64, hs:hs + 64]),
                   reads=["VAT", "idb"], writes=[("ps", pb)])
            i = 0
            while i < len(vts):
                nk = vts[i][2]
                j = i
                while j + 1 < len(vts) and vts[j + 1][2] == nk:
                    j += 1
                cnt_ = j - i + 1
                srcv = bankb(pb)[0:nk, i * 64:(j + 1) * 64].rearrange("p (t c) -> p t c", t=cnt_)
                op("vector", lambda e, srcv=srcv, i=i, cnt_=cnt_, nk=nk: e.tensor_tensor(
                    out=vb[0:nk, v0 + i:v0 + i + cnt_, 0:64], in0=srcv,
                    in1=kvft[0:nk, v0 + i:v0 + i + cnt_].unsqueeze(2).to_broadcast([nk, cnt_, 64]), op=ALU.mult),
                   reads=[("ps", pb), "kvft"], writes=[("Vp", h % 2)])
                i = j + 1
        op("gpsimd", lambda e: e.tensor_copy(out=vb[:, :, 64], in_=kvft), reads=["kvft"], writes=[("Vp", h % 2)])

    GL = [(ri, r, kind, g, blks) for ri, r in enumerate((1, 4, 16)) for (kind, g, blks) in DGROUPS[r]]
    gctr = [0]

    def d_scores(h, gi, grp):
        pr, hs = h // 2, (h % 2) * 64
        ri, r, kind, g, blks = grp
        sb = 2 * (gi % 2)
        for bi_, (x0, Nq, t0, t1) in enumerate(blks):
            qv = QAT[hs:hs + 64, pr, x0:x0 + r * (Nq - 1) + 1:r]
            for role, tix in ((0, t0), (1, t1)):
                (rr, es, nk) = VLIST[tix]
                op("tensor", lambda e, role=role, es=es, nk=nk, qv=qv, bi_=bi_, Nq=Nq: e.matmul(
                    bank(sb + role)[0:nk, bi_ * 128:bi_ * 128 + Nq], lhsT=KAT[hs:hs + 64, pr, es:es + r * (nk - 1) + 1:r], rhs=qv, start=True, stop=True),
                   reads=["KAT", "QAT"], writes=[("ps", sb + role)])

    def d_rest(h, gi, grp):
        ri, r, kind, g, blks = grp
        vb = Vp[h % 2]
        mb = mk[h % 2]
        sb = 2 * (gi % 2)
        eb = Eb[gi % 2]
        pbuf = Pb[gi % 2]
        ob = 6 + gi % 2
        ek, pk = ("Eb", gi % 2), ("Pbuf", gi % 2)
        if kind == "reg":
            sv = ps[:, sb * 512:(sb + 2) * 512]
            op("scalar", lambda e: e.activation(out=eb, in_=sv, func=AF.Exp, scale=0.125),
               reads=[("ps", sb), ("ps", sb + 1)], writes=[ek])
            op("vector", lambda e: e.tensor_tensor(
                out=pbuf.rearrange("p (r b q) -> p r b q", r=2, b=4), in0=eb.rearrange("p (r b q) -> p r b q", r=2, b=4),
                in1=mb[:, 2 * ri:2 * ri + 2, :].unsqueeze(2).to_broadcast([128, 2, 4, 128]), op=ALU.mult),
               reads=[ek, ("mk", h % 2)], writes=[pk])
        else:
            op("scalar", lambda e: e.activation(out=eb[:, 0:256:128], in_=bank(sb)[:, 0:256:128], func=AF.Exp, scale=0.125),
               reads=[("ps", sb)], writes=[ek])
            op("scalar", lambda e: e.activation(out=eb[0:1, 512:768:128], in_=bank(sb + 1)[0:1, 0:256:128], func=AF.Exp, scale=0.125),
               reads=[("ps", sb + 1)], writes=[ek])
            op("vector", lambda e: e.tensor_tensor(
                out=pbuf[:, 0:256:128], in0=eb[:, 0:256:128], in1=mb[:, 2 * ri, 0:1].to_broadcast([128, 2]), op=ALU.mult),
               reads=[ek, ("mk", h % 2)], writes=[pk])
            op("vector", lambda e: e.tensor_tensor(
                out=pbuf[0:1, 512:768:128], in0=eb[0:1, 512:768:128], in1=mb[0:1, 2 * ri + 1, 0:1].to_broadcast([1, 2]), op=ALU.mult),
               reads=[ek, ("mk", h % 2)], writes=[pk])
        for bi_, (x0, Nq, t0, t1) in enumerate(blks):
            for role, tix in ((0, t0), (1, t1)):
                nk = VLIST[tix][2]
                op("tensor", lambda e, role=role, tix=tix, nk=nk, bi_=bi_, Nq=Nq: e.matmul(
                    bank(ob)[0:65, bi_ * 128:bi_ * 128 + Nq], lhsT=vb[0:nk, tix, :], rhs=pbuf[0:nk, role * 512 + bi_ * 128:role * 512 + bi_ * 128 + Nq],
                    start=(role == 0), stop=(role == 1)),
                   reads=[pk, ("Vp", h % 2)], writes=[("ps", ob)])
        if kind == "reg":
            src = bank(ob)[0:65, :].rearrange("p (c q) -> p c q", c=4)
            if r == 1:
                dst = Oacc[0:65, 1 + 512 * g:1 + 512 * (g + 1)].rearrange("p (c q) -> p c q", c=4)
            elif r == 4:
                dst = Oacc[0:65, 1 + 512 * g:1 + 512 * (g + 1)].rearrange("p (q c) -> p c q", c=4)
            else:
                dst = Oacc[0:65, 1:1 + OWN].rearrange("p (q c) -> p c q", c=16)[:, 4 * g:4 * g + 4, :]
        else:
            src = bank(ob)[0:65, 0:256:128]
            dst = Oacc[0:65, 0:NX:NX - 1]
        if r == 1:
            op("vector", lambda e: e.tensor_copy(out=dst, in_=src), reads=[("ps", ob)], writes=["Oacc"])
        else:
            op("vector", lambda e: e.tensor_tensor(out=dst, in0=src, in1=dst, op=ALU.add), reads=[("ps", ob), "Oacc"], writes=["Oacc"])

    d_prep(0)
    for h in range(8):
        g0 = gctr[0]
        d_scores(h, g0, GL[0])
        if h + 1 < 8:
            d_prep(h + 1)
        for i, grp in enumerate(GL):
            if i + 1 < len(GL):
                d_scores(h, g0 + i + 1, GL[i + 1])
            d_rest(h, g0 + i, grp)
        gctr[0] += len(GL)
        oacc_to_tm(h, ya_tm, "Oacc", Oacc)
    tm_to_ynT(ya_tm, ynTa, "ynTa", Pb)
    dump("ynTa", ynTa, [128, 4, NX], BF16, ["ynTa"])
    dump("ya_tm", ya_tm, [128, 17, 512], F32, ["tm"])
    S.barrier()

    R1.off = R1.lo
    ckvT = R1.alloc([128, 2, S_LEN], BF16)
    KT = R1.alloc([96, S_LEN], BF16)
    R1s_lo = R1.off
    RY.off = RY.lo + 16448
    Wukv = RY.alloc([128, 2, 1024], BF16)
    A = Region(RY.off, SB_BYTES)
    Wkvl = A.alloc([128, 8, 288], BF16)
    stage2 = A.alloc([128, 8, 512], F32)
    xbuf = [A.alloc([128, 2, D], F32) for _ in range(2)]
    xnb = [A.alloc([128, 2, D], BF16) for _ in range(2)]
    hTk = [A.alloc([128, 8, 128], BF16) for _ in range(2)]
    ckvn = [A.alloc([128, 256], BF16) for _ in range(2)]
    kr_tm = A.alloc([128, 64, 32], F32)
    rkt = A.alloc([128, 2, 64, 16], F32)
    kr_pad = A.alloc([128, 64, 96], BF16)
    rt = [A.alloc([128, 64, 16], F32) for _ in range(2)]
    load_weight(Wkvl, w_in[:, 1920:2208], 8, 288, 0, stage2, "Wkvl")
    load_weight(Wukv[:, :, 0:512], w_ukv[:, 0:512], 2, 512, 11, stage2, "Wukv")
    load_weight(Wukv[:, :, 512:1024], w_ukv[:, 512:1024], 2, 512, 11, stage2, "Wukv")
    op("sync", lambda e: e.dma_start(out=rkt, in_=rk), writes=["rkt"], dma=True)
    op("gpsimd", lambda e: e.memset(kr_pad, 0.0), writes=["kr_pad"])
    for g in range(S_LEN // 256):
        bi = g % 2
        norm_tiles_to_T(xf[g * 256:(g + 1) * 256, :], 2, xbuf[bi], xnb[bi], ("k", bi))
        for t in range(2):
            tt = g * 2 + t
            hb = hTk[tt % 2]
            transpose_tile(xnb[bi][:, t, :], hb, tt % 2, [("xn", ("k", bi))], ("hTk", tt % 2))
            pb = 2 + tt % 2
            for k in range(8):
                op("tensor", lambda e, k=k, hb=hb, pb=pb: e.matmul(bank(pb, 288), lhsT=hb[:, k, :], rhs=Wkvl[:, k, :], start=(k == 0), stop=(k == 7)),
                   reads=[("hTk", tt % 2), "Wkvl"], writes=[("ps", pb)])
            sc = 96 + (tt % 8)
            sk = ("stt4", tt % 8)
            op("scalar", lambda e, pb=pb, sc=sc: e.activation(out=junk[:, 0:256], in_=bank(pb, 256), func=AF.Square, accum_out=stt[:, sc:sc + 1]),
               reads=[("ps", pb)], writes=["junk", sk])
            op("scalar", lambda e, sc=sc: e.activation(out=stt[:, sc:sc + 1], in_=stt[:, sc:sc + 1], func=AF.Sqrt, bias=EPS, scale=1.0 / 256),
               reads=[sk], writes=[sk])
            op("vector", lambda e, sc=sc: e.reciprocal(out=stt[:, sc + 8:sc + 9], in_=stt[:, sc:sc + 1]), reads=[sk], writes=[sk])
            cb = ckvn[tt % 2]
            op("vector", lambda e, pb=pb, sc=sc, cb=cb: e.tensor_scalar(out=cb, in0=bank(pb, 256), scalar1=stt[:, sc + 8:sc + 9], scalar2=None, op0=ALU.mult),
               reads=[("ps", pb), sk], writes=[("ckvn", tt % 2)])
            op("vector", lambda e, pb=pb, tt=tt: e.tensor_copy(out=kr_tm[:, tt, :], in_=bank(pb, 288)[:, 256:288]), reads=[("ps", pb)], writes=["kr_tm"])
            pb2 = 4 + tt % 2
            for j in range(2):
                op("tensor", lambda e, j=j, cb=cb, pb2=pb2: e.transpose(out=bankb(pb2)[:, j * 128:(j + 1) * 128], in_=cb[:, j * 128:(j + 1) * 128], identity=idb),
                   reads=[("ckvn", tt % 2), "idb"], writes=[("ps", pb2)])
            op("scalar", lambda e, pb2=pb2, tt=tt: e.copy(out=ckvT[:, :, tt * 128:(tt + 1) * 128], in_=bankb(pb2)[:, 0:256].rearrange("p (j t) -> p j t", j=2)),
               reads=[("ps", pb2)], writes=["ckvT"])
    x1, x2 = kr_tm[:, :, 0:16], kr_tm[:, :, 16:32]
    cosk, sink = rkt[:, 0], rkt[:, 1]
    op("vector", lambda e: e.tensor_tensor(out=rt[0], in0=x1, in1=cosk, op=ALU.mult), reads=["kr_tm", "rkt"], writes=["rt0"])
    op("vector", lambda e: e.tensor_tensor(out=rt[1], in0=x2, in1=sink, op=ALU.mult), reads=["kr_tm", "rkt"], writes=["rt1"])
    op("vector", lambda e: e.tensor_tensor(out=kr_pad[:, :, 64:80], in0=rt[0], in1=rt[1], op=ALU.subtract), reads=["rt0", "rt1", "kr_pad"], writes=["kr_pad"])
    op("vector", lambda e: e.tensor_tensor(out=rt[0], in0=x2, in1=cosk, op=ALU.mult), reads=["kr_tm", "rkt", "kr_pad"], writes=["rt0"])
    op("vector", lambda e: e.tensor_tensor(out=rt[1], in0=x1, in1=sink, op=ALU.mult), reads=["kr_tm", "rkt", "kr_pad"], writes=["rt1"])
    op("vector", lambda e: e.tensor_tensor(out=kr_pad[:, :, 80:96], in0=rt[0], in1=rt[1], op=ALU.add), reads=["rt0", "rt1", "kr_pad"], writes=["kr_pad"])
    for g8 in range(8):
        pb = 6 + g8 % 2
        for i in range(8):
            tt = g8 * 8 + i
            op("tensor", lambda e, i=i, tt=tt, pb=pb: e.transpose(out=bankb(pb)[0:96, i * 128:(i + 1) * 128], in_=kr_pad[:, tt, :], identity=idb),
               reads=["kr_pad", "idb"], writes=[("ps", pb)])
        op("vector", lambda e, pb=pb, g8=g8: e.tensor_copy(out=KT[64:96, g8 * 1024:(g8 + 1) * 1024], in_=bankb(pb)[64:96, :]),
           reads=[("ps", pb)], writes=["KTr"])
    dump("ckvT", ckvT, [128, 2, S_LEN], BF16, ["ckvT"])
    dump("KTr", KT[64:96, :], [32, S_LEN], BF16, ["KTr"])
    S.barrier()

    ynTb = RY.alloc([128, 4, NX], BF16)
    RT.off = RT.lo
    QBT = RT.alloc([96, 8, NX], BF16)
    R1s = Region(R1s_lo, R1.hi)
    Wuq = R1s.alloc([128, 3, 768], BF16)
    Wrot = R1s.alloc([128, 3, 8, 96], BF16)
    stage3 = R1s.alloc([128, 3, 768], F32)
    rqt = RT.alloc([96, 2, NX], F32)
    tq = [RT.alloc([96, 410], F32) for _ in range(2)]
    load_weight(Wuq, w_uq, 3, 768, 8, stage3, "Wuq")
    op("gpsimd", lambda e: e.memset(Wrot, 0.0), writes=["Wrot"])
    Wuq4 = Wuq.rearrange("p k (h c) -> p k h c", h=8)
    for k in range(3):
        op("gpsimd", lambda e, k=k: e.tensor_scalar(out=Wrot[:, k, :, 64:80], in0=Wuq4[:, k, :, 80:96], scalar1=-1.0, scalar2=None, op0=ALU.mult),
           reads=["Wuq", "Wrot"], writes=["Wrot"])
        op("gpsimd", lambda e, k=k: e.tensor_copy(out=Wrot[:, k, :, 80:96], in_=Wuq4[:, k, :, 64:80]), reads=["Wuq", "Wrot"], writes=["Wrot"])
    op("sync", lambda e: e.dma_start(out=rqt[64:96], in_=rq), writes=["rqt"], dma=True)
    qi = 0
    for h in range(8):
        for (c0, cn) in XBLK:
            pa, pr_ = 2 * (qi % 2), 1 + 2 * (qi % 2)
            tb_ = tq[qi % 2]
            tk = ("tq", qi % 2)
            qi += 1
            for k in range(3):
                op("tensor", lambda e, k=k, h=h, c0=c0, cn=cn, pa=pa: e.matmul(bank(pa, cn, 96), lhsT=Wuq[:, k, h * 96:(h + 1) * 96], rhs=cqT[:, k, c0:c0 + cn], start=(k == 0), stop=(k == 2)),
                   reads=["Wuq", "cqT"], writes=[("ps", pa)])
            for k in range(3):
                op("tensor", lambda e, k=k, h=h, c0=c0, cn=cn, pr_=pr_: e.matmul(bank(pr_, cn, 96), lhsT=Wrot[:, k, h, :], rhs=cqT[:, k, c0:c0 + cn], start=(k == 0), stop=(k == 2)),
                   reads=["Wrot", "cqT"], writes=[("ps", pr_)])
            op("scalar", lambda e, h=h, c0=c0, cn=cn, pa=pa: e.copy(out=QBT[0:64, h, c0:c0 + cn], in_=bank(pa, cn, 64)), reads=[("ps", pa)], writes=["QBT"])
            op("vector", lambda e, c0=c0, cn=cn, pa=pa, tb_=tb_: e.tensor_tensor(out=tb_[64:96, 0:cn], in0=bank(pa, cn, 32, 64), in1=rqt[64:96, 0, c0:c0 + cn], op=ALU.mult),
               reads=[("ps", pa), "rqt"], writes=[tk])
            op("vector", lambda e, h=h, c0=c0, cn=cn, pr_=pr_: e.tensor_tensor(out=QBT[64:96, h, c0:c0 + cn], in0=bank(pr_, cn, 32, 64), in1=rqt[64:96, 1, c0:c0 + cn], op=ALU.mult),
               reads=[("ps", pr_), "rqt"], writes=["QBT"])
            op("vector", lambda e, h=h, c0=c0, cn=cn, tb_=tb_: e.tensor_tensor(out=QBT[64:96, h, c0:c0 + cn], in0=QBT[64:96, h, c0:c0 + cn], in1=tb_[64:96, 0:cn], op=ALU.add),
               reads=[tk, "QBT"], writes=["QBT"])
    dump("QBT", QBT, [96, 8, NX], BF16, ["QBT"])
    S.barrier()
    RT.off = RT.lo + 32832
    yb_tm = RT.alloc([128, 17, 512], F32)
    ynb_tmp = [RT.alloc([128, 512], BF16) for _ in range(2)]
    R1s = Region(R1s_lo, R1.hi)
    Vb = [R1s.alloc([128, 64, 65], BF16) for _ in range(2)]
    PT = [R1s.alloc([128, 2, 512], BF16) for _ in range(3)]
    OaccB = R1s.alloc([128, NX], F32)
    for b_ in range(2):
        op("gpsimd", lambda e, b_=b_: e.memset(Vb[b_][:, :, 64], 1.0), writes=[("Vb", b_)])
    scale_b = 96.0 ** -0.5
    MB = [(1 + 512 * i, 512) for i in range(4)]
    QG = [(0, 2), (2, 2)]
    si = 0

    def m_prep_v(h):
        vb = Vb[h % 2]
        for k8 in range(8):
            pb = 6 + k8 % 2
            for i in range(8):
                kt = k8 * 8 + i
                for j in range(2):
                    op("tensor", lambda e, j=j, kt=kt, i=i: e.matmul(bank(pb)[:, i * 64:(i + 1) * 64], lhsT=ckvT[:, j, kt * 128:(kt + 1) * 128], rhs=Wukv[:, j, h * 128 + 64:h * 128 + 128], start=(j == 0), stop=(j == 1)),
                       reads=["Wukv", "ckvT"], writes=[("ps", pb)])
            op("vector", lambda e: e.tensor_copy(out=vb[:, k8 * 8:(k8 + 1) * 8, 0:64], in_=bank(pb).rearrange("p (t c) -> p t c", t=8)),
               reads=[("ps", pb)], writes=[("Vb", h % 2)])

    def m_prep_k(h):
        for nb_ in range(16):
            pb = 6 + nb_ % 2
            for j in range(2):
                op("tensor", lambda e, j=j: e.matmul(bank(pb, 512, 64), lhsT=Wukv[:, j, h * 128:h * 128 + 64], rhs=ckvT[:, j, nb_ * 512:(nb_ + 1) * 512], start=(j == 0), stop=(j == 1)),
                   reads=["Wukv", "ckvT"], writes=[("ps", pb)])
            op("vector", lambda e: e.tensor_copy(out=KT[0:64, nb_ * 512:(nb_ + 1) * 512], in_=bank(pb, 512, 64)),
               reads=[("ps", pb)], writes=["KTn"])

    m_prep_v(0)
    for h in range(8):
        vb = Vb[h % 2]
        m_prep_k(h)
        for gq, (b0, nbk) in enumerate(QG):
            ob = 4

            def emit_S(kt, si_):
                sbk = 2 * (si_ % 2)
                for bb in range(nbk):
                    c0, cn = MB[b0 + bb]
                    op("tensor", lambda e, bb=bb, c0=c0, cn=cn: e.matmul(bank(sbk + bb, cn), lhsT=KT[0:96, kt * 128:(kt + 1) * 128], rhs=QBT[0:96, h, c0:c0 + cn], start=True, stop=True),
                       reads=["KTn", "KTr", "QBT"], writes=[("ps", sbk + bb)])
            emit_S(0, si)
            for kt in range(64):
                sbk = 2 * (si % 2)
                pt = PT[si % 3]
                ptk = ("PT", si % 3)
                if kt + 1 < 64:
                    emit_S(kt + 1, si + 1)
                if gq == 0 and kt == 8 and h + 1 < 8:
                    m_prep_v(h + 1)
                sv = ps[:, sbk * 512:(sbk + nbk) * 512].rearrange("p (a b) -> p a b", a=nbk)
                op("scalar", lambda e: e.activation(out=pt[:, 0:nbk, :], in_=sv, func=AF.Exp, scale=scale_b),
                   reads=[("ps", sbk + bb) for bb in range(nbk)], writes=[ptk])
                for bb in range(nbk):
                    op("tensor", lambda e, bb=bb: e.matmul(bank(ob + bb, 512, 65), lhsT=vb[:, kt, :], rhs=pt[:, bb, :], start=(kt == 0), stop=(kt == 63)),
                       reads=[ptk, ("Vb", h % 2)], writes=[("ps", ob + bb)])
                si += 1
            ov = ps[0:65, ob * 512:(ob + nbk) * 512]
            c0 = MB[b0][0]
            op("vector", lambda e: e.tensor_copy(out=OaccB[0:65, c0:c0 + nbk * 512], in_=ov),
               reads=[("ps", ob + bb) for bb in range(nbk)], writes=["OaccB"])
        sbk = 2 * (si % 2)
        pt = PT[si % 3]
        ptk = ("PT", si % 3)
        qh = QBT[0:96, h, 0:NX:NX - 1]
        for kt in range(64):
            op("tensor", lambda e, kt=kt: e.matmul(bank(sbk)[:, 2 * kt:2 * kt + 2], lhsT=KT[0:96, kt * 128:(kt + 1) * 128], rhs=qh, start=True, stop=True),
               reads=["KTn", "KTr", "QBT"], writes=[("ps", sbk)])
        op("scalar", lambda e: e.activation(out=pt[:, 0, 0:128], in_=bank(sbk, 128), func=AF.Exp, scale=scale_b), reads=[("ps", sbk)], writes=[ptk])
        for kt in range(64):
            op("tensor", lambda e, kt=kt: e.matmul(bank(4, 2, 65), lhsT=vb[:, kt, :], rhs=pt[:, 0, 2 * kt:2 * kt + 2], start=(kt == 0), stop=(kt == 63)),
               reads=[ptk, ("Vb", h % 2)], writes=[("ps", 4)])
        si += 1
        op("vector", lambda e: e.tensor_copy(out=OaccB[0:65, 0:NX:NX - 1], in_=bank(4, 2, 65)), reads=[("ps", 4)], writes=["OaccB"])
        oacc_to_tm(h, yb_tm, "OaccB", OaccB)
    tm_to_ynT(yb_tm, ynTb, "ynTb", ynb_tmp)
    dump("ynTb", ynTb, [128, 4, NX], BF16, ["ynTb"])
    dump("yb_tm", yb_tm, [128, 17, 512], F32, ["tm"])
    S.barrier()

    RT.off = RT.lo
    hTf = RT.alloc([128, 8, NX], BF16)
    A = Region(R1.lo, RC.hi)
    Wo = A.alloc([128, 8, D], BF16)
    stage4 = A.alloc([128, 8, 256], F32)
    gpo = A.alloc([128, 2, D], F32)
    xt_ = [A.alloc([128, D], F32) for _ in range(2)]
    xm_ = [A.alloc([128, D], F32) for _ in range(2)]
    hn_ = [A.alloc([128, D], BF16) for _ in range(2)]
    for c in range(4):
        load_weight(Wo[:, :, c * 256:(c + 1) * 256], w_o[:, c * 256:(c + 1) * 256], 8, 256, 13, stage4, "Wo")
    op("sync", lambda e: e.dma_start(out=gpo, in_=gpost), writes=["gpo"], dma=True)
    for t in range(17):
        M = 128 if t < 16 else 2
        bi = t % 2
        xt, xm, hn = xt_[bi], xm_[bi], hn_[bi]
        xk, mkk, hk = ("xt", bi), ("xm", bi), ("hn", bi)
        if t < 16:
            op("sync", lambda e, t=t, xt=xt: e.dma_start(out=xt, in_=xw[OWN0 + 128 * t:OWN0 + 128 * (t + 1), :]), writes=[xk], dma=True)
            cols = lambda k, t=t: (ynTa if k < 4 else ynTb)[:, k % 4, 1 + 128 * t:1 + 128 * (t + 1)]
        else:
            op("sync", lambda e, xt=xt: e.dma_start(out=xt[0:1, :], in_=xw[OWN0 - 1:OWN0, :]), writes=[xk], dma=True)
            op("sync", lambda e, xt=xt: e.dma_start(out=xt[1:2, :], in_=xw[OWN0 + OWN:OWN0 + OWN + 1, :]), writes=[xk], dma=True)
            cols = lambda k: (ynTa if k < 4 else ynTb)[:, k % 4, 0:NX:NX - 1]
        pb = 2 * (t % 2)
        for n2 in range(2):
            for k in range(8):
                la = cols(k)
                op("tensor", lambda e, k=k, n2=n2, M=M, pb=pb, la=la: e.matmul(bank(pb + n2, 512, M), lhsT=la, rhs=Wo[:, k, n2 * 512:(n2 + 1) * 512], start=(k == 0), stop=(k == 7)),
                   reads=["ynTa", "ynTb", "Wo"], writes=[("ps", pb + n2)])
        yv = ps[0:M, pb * 512:(pb + 2) * 512]
        sc = 128 + 2 * (t % 4)
        sk = ("stt5", t % 4)
        op("scalar", lambda e, yv=yv, M=M, sc=sc: e.activation(out=junk[0:M, :], in_=yv, func=AF.Square, accum_out=stt[0:M, sc:sc + 1]),
           reads=[("ps", pb), ("ps", pb + 1)], writes=["junk", sk])
        op("scalar", lambda e, M=M, sc=sc: e.activation(out=stt[0:M, sc:sc + 1], in_=stt[0:M, sc:sc + 1], func=AF.Sqrt, bias=EPS, scale=1.0 / D), reads=[sk], writes=[sk])
        op("vector", lambda e, M=M, sc=sc: e.reciprocal(out=stt[0:M, sc + 1:sc + 2], in_=stt[0:M, sc:sc + 1]), reads=[sk], writes=[sk])
        op("vector", lambda e, yv=yv, M=M, sc=sc, xm=xm: e.scalar_tensor_tensor(out=xm[0:M, :], in0=yv, scalar=stt[0:M, sc + 1:sc + 2], in1=gpo[0:M, 0, :], op0=ALU.mult, op1=ALU.mult),
           reads=[("ps", pb), ("ps", pb + 1), sk, "gpo"], writes=[mkk])
        op("vector", lambda e, M=M, xm=xm, xt=xt: e.tensor_tensor(out=xm[0:M, :], in0=xm[0:M, :], in1=xt[0:M, :], op=ALU.add), reads=[mkk, xk], writes=[mkk])
        if t < 16:
            op("sync", lambda e, t=t, xm=xm: e.dma_start(out=xmid[128 * t:128 * (t + 1), :], in_=xm), reads=[mkk], writes=["xmid"], dma=True)
        sc2 = 136 + 2 * (t % 4)
        sk2 = ("stt6", t % 4)
        op("scalar", lambda e, M=M, sc2=sc2, xm=xm: e.activation(out=junk[0:M, :], in_=xm[0:M, :], func=AF.Square, accum_out=stt[0:M, sc2:sc2 + 1]),
           reads=[mkk], writes=["junk", sk2])
        op("scalar", lambda e, M=M, sc2=sc2: e.activation(out=stt[0:M, sc2:sc2 + 1], in_=stt[0:M, sc2:sc2 + 1], func=AF.Sqrt, bias=EPS, scale=1.0 / D), reads=[sk2], writes=[sk2])
        op("vector", lambda e, M=M, sc2=sc2: e.reciprocal(out=stt[0:M, sc2 + 1:sc2 + 2], in_=stt[0:M, sc2:sc2 + 1]), reads=[sk2], writes=[sk2])
        op("vector", lambda e, M=M, sc2=sc2, xm=xm, hn=hn: e.tensor_scalar(out=hn[0:M, :], in0=xm[0:M, :], scalar1=stt[0:M, sc2 + 1:sc2 + 2], scalar2=None, op0=ALU.mult),
           reads=[mkk, sk2], writes=[hk])
        pb2 = 4 + t % 2
        for k in range(8):
            op("tensor", lambda e, k=k, M=M, hn=hn, pb2=pb2: e.transpose(out=bankb(pb2)[:, k * 128:k * 128 + M], in_=hn[0:M, k * 128:(k + 1) * 128], identity=idb[0:M, 0:M]),
               reads=[hk, "idb"], writes=[("ps", pb2)])
        srcv = bankb(pb2).rearrange("p (k t) -> p k t", k=8)[:, :, 0:M]
        dstv = hTf[:, :, 1 + 128 * t:1 + 128 * (t + 1)] if t < 16 else hTf[:, :, 0:NX:NX - 1]
        op("vector", lambda e, srcv=srcv, dstv=dstv: e.tensor_copy(out=dstv, in_=srcv), reads=[("ps", pb2)], writes=["hTf"])
    dump("hTf", hTf, [128, 8, NX], BF16, ["hTf"])
    S.barrier()

    aT = Region(R1.lo, RC.hi).alloc([128, 22, OWN], BF16)
    A = Region(RY.lo, RY.hi)
    stg = [A.alloc([128, 8, 256], F32) for _ in range(2)]
    Wub = [A.alloc([128, 8, 256], BF16) for _ in range(2)]
    cwt = A.alloc([128, 44, 4], F32)
    ufl = A.alloc([128, 2], F32)
    A = Region(RT.lo + 32832, SB_BYTES)
    cgb = [A.alloc([128, OWN], F32) for _ in range(2)]
    cvb = [A.alloc([128, OWN], F32) for _ in range(2)]
    op("sync", lambda e: e.dma_start(out=cwt, in_=cwb), writes=["cwt"], dma=True)
    op("sync", lambda e: e.dma_start(out=ufl, in_=uflag), writes=["ufl"], dma=True)
    OB = [(410 * i, min(410, OWN - 410 * i)) for i in range(5)]
    pbi = 0

    def f1_weights(j):
        bi = j % 2
        sg, wb = stg[bi], Wub[bi]
        sgk, wbk = ("stg", bi), ("Wub", bi)
        op("gpsimd", lambda e: e.dma_start(out=sg[:, :, 0:128], in_=w_up[:, j * 128:(j + 1) * 128].rearrange("(k p) n -> p k n", p=128)), writes=[sgk], dma=True)
        op("gpsimd", lambda e: e.dma_start(out=sg[:, :, 128:256], in_=w_up[:, DFF + j * 128:DFF + (j + 1) * 128].rearrange("(k p) n -> p k n", p=128)), writes=[sgk], dma=True)
        for k in range(8):
            if k % 2 == 0:
                op("vector", lambda e, k=k: e.tensor_scalar(out=wb[:, k, :], in0=sg[:, k, :], scalar1=gpt[:, 21 + k:22 + k], scalar2=None, op0=ALU.mult),
                   reads=[sgk, "gpt"], writes=[wbk])
            else:
                op("scalar", lambda e, k=k: e.activation(out=wb[:, k, :], in_=sg[:, k, :], func=AF.Identity, scale=gpt[:, 21 + k:22 + k]),
                   reads=[sgk, "gpt"], writes=[wbk])

    f1_weights(0)
    for j in range(22):
        bi = j % 2
        wb = Wub[bi]
        wbk = ("Wub", bi)
        if j + 1 < 22:
            f1_weights(j + 1)
        for half in range(2):
            cb = (cgb if half == 0 else cvb)[bi]
            ff = j + 22 * half
            for bx, (o0, n) in enumerate(OB):
                ck = ("cg" if half == 0 else "cv", bi, bx)
                pb = pbi % 8
                pbi += 1
                for k in range(8):
                    op("tensor", lambda e, k=k: e.matmul(bank(pb, n + 2), lhsT=wb[:, k, half * 128:(half + 1) * 128], rhs=hTf[:, k, o0:o0 + n + 2], start=(k == 0), stop=(k == 7)),
                       reads=[wbk, "hTf"], writes=[("ps", pb)])
                if bx == 0:
                    op("vector", lambda e: e.tensor_tensor(out=bank(pb, 1), in0=bank(pb, 1), in1=ufl[:, 0:1], op=ALU.mult), reads=[("ps", pb), "ufl"], writes=[("ps", pb)])
                if bx == 4:
                    op("vector", lambda e: e.tensor_tensor(out=bank(pb, n + 2)[:, n + 1:n + 2], in0=bank(pb, n + 2)[:, n + 1:n + 2], in1=ufl[:, 1:2], op=ALU.mult), reads=[("ps", pb), "ufl"], writes=[("ps", pb)])
                op("scalar", lambda e: e.activation(out=cb[:, o0:o0 + n], in_=bank(pb, n + 2)[:, 1:n + 1], func=AF.Identity, bias=cwt[:, ff, 3:4], scale=cwt[:, ff, 1:2]),
                   reads=[("ps", pb), "cwt"], writes=[ck])
                op("vector", lambda e: e.scalar_tensor_tensor(out=cb[:, o0:o0 + n], in0=bank(pb, n + 2)[:, 0:n], scalar=cwt[:, ff, 0:1], in1=cb[:, o0:o0 + n], op0=ALU.mult, op1=ALU.add),
                   reads=[("ps", pb), "cwt", ck], writes=[ck])
                op("vector", lambda e: e.scalar_tensor_tensor(out=cb[:, o0:o0 + n], in0=bank(pb, n + 2)[:, 2:n + 2], scalar=cwt[:, ff, 2:3], in1=cb[:, o0:o0 + n], op0=ALU.mult, op1=ALU.add),
                   reads=[("ps", pb), "cwt", ck], writes=[ck])
        cg, cv = cgb[bi], cvb[bi]
        cgk = [("cg", bi, bx) for bx in range(5)]
        cvk = [("cv", bi, bx) for bx in range(5)]
        op("scalar", lambda e: e.activation(out=cg, in_=cg, func=AF.Gelu_apprx_tanh), reads=cgk, writes=cgk)
        op("vector", lambda e: e.tensor_tensor(out=aT[:, j, :], in0=cg, in1=cv, op=ALU.mult), reads=cgk + cvk, writes=[("aT", j)])
    dump("aT", aT, [128, 22, OWN], BF16, [("aT", j) for j in range(22)])
    S.barrier()

    A = Region(RY.lo, SB_BYTES)
    Wdn = A.alloc([128, 22, D], BF16)
    stage5 = A.alloc([128, 2, D], F32)
    gpo2 = A.alloc([128, D], F32)
    xm2 = [A.alloc([128, D], F32) for _ in range(2)]
    ot = [A.alloc([128, D], F32) for _ in range(2)]
    for c in range(11):
        load_weight(Wdn[:, 2 * c:2 * c + 2, :], w_down[256 * c:256 * (c + 1), :], 2, D, None, stage5, "Wdn")
    op("sync", lambda e: e.dma_start(out=gpo2, in_=gpost[:, 1, :]), writes=["gpo2"], dma=True)
    for t in range(16):
        bi = t % 2
        xk, ok = ("xm2", bi), ("ot", bi)
        op("sync", lambda e, t=t, bi=bi: e.dma_start(out=xm2[bi], in_=xmid[128 * t:128 * (t + 1), :]), reads=["xmid"], writes=[xk], dma=True)
        pb = 2 * (t % 2)
        for n2 in range(2):
            for j in range(22):
                op("tensor", lambda e, j=j, n2=n2, t=t, pb=pb: e.matmul(bank(pb + n2), lhsT=aT[:, j, 128 * t:128 * (t + 1)], rhs=Wdn[:, j, n2 * 512:(n2 + 1) * 512], start=(j == 0), stop=(j == 21)),
                   reads=[("aT", j), "Wdn"], writes=[("ps", pb + n2)])
        yv = ps[:, pb * 512:(pb + 2) * 512]
        sc = 144 + 2 * (t % 4)
        sk = ("stt7", t % 4)
        op("scalar", lambda e, yv=yv, sc=sc: e.activation(out=junk, in_=yv, func=AF.Square, accum_out=stt[:, sc:sc + 1]),
           reads=[("ps", pb), ("ps", pb + 1)], writes=["junk", sk])
        op("scalar", lambda e, sc=sc: e.activation(out=stt[:, sc:sc + 1], in_=stt[:, sc:sc + 1], func=AF.Sqrt, bias=EPS, scale=1.0 / D), reads=[sk], writes=[sk])
        op("vector", lambda e, sc=sc: e.reciprocal(out=stt[:, sc + 1:sc + 2], in_=stt[:, sc:sc + 1]), reads=[sk], writes=[sk])
        op("vector", lambda e, yv=yv, sc=sc, bi=bi: e.scalar_tensor_tensor(out=ot[bi], in0=yv, scalar=stt[:, sc + 1:sc + 2], in1=gpo2, op0=ALU.mult, op1=ALU.mult),
           reads=[("ps", pb), ("ps", pb + 1), sk, "gpo2"], writes=[ok])
        op("vector", lambda e, bi=bi: e.tensor_tensor(out=ot[bi], in0=ot[bi], in1=xm2[bi], op=ALU.add), reads=[ok, xk], writes=[ok])
        op("sync", lambda e, t=t, bi=bi: e.dma_start(out=yout[128 * t:128 * (t + 1), :], in_=ot[bi]), reads=[ok], dma=True)
    S.emit()
    return nc


_CACHE = {}


def _consts():
    if "c" in _CACHE:
        return _CACHE["c"]
    slopes = np.exp2(-8.0 * np.arange(1, 9, dtype=np.float32) / 8).astype(np.float32)
    k = np.arange(128)[:, None]
    q = np.arange(128)[None, :]
    masks = np.zeros((8, 128, 6, 128), np.float32)
    for h in range(8):
        for ri, r in enumerate((1, 4, 16)):
            d0 = k - 64 - q
            d1 = k + 64 - q
            masks[h, :, 2 * ri, :] = np.where(k >= q, np.exp(-slopes[h] * (np.abs(d0) * r).astype(np.float32)), 0.0)
            masks[h, :, 2 * ri + 1, :] = np.where(k <= q, np.exp(-slopes[h] * (np.abs(d1) * r).astype(np.float32)), 0.0)
    inv_freq = np.exp(-np.log(10000.0) * np.arange(0, 32, 2, dtype=np.float32) / 32).astype(np.float32)
    pos = np.arange(S_LEN, dtype=np.float32)
    ang = pos[:, None] * inv_freq[None, :]
    cosk = np.cos(ang).astype(np.float32).reshape(64, 128, 16).transpose(1, 0, 2)
    sink = np.sin(ang).astype(np.float32).reshape(64, 128, 16).transpose(1, 0, 2)
    rk = np.ascontiguousarray(np.stack([cosk, sink], axis=1))
    c = dict(masks=masks, inv_freq=inv_freq, rk=rk, ident=np.eye(128, dtype=np.float32))
    _CACHE["c"] = c
    return c


def _core_inputs(c, x, shared):
    cst = _consts()
    b, qc = c // 4, c % 4
    T0 = qc * OWN
    pos_w = T0 - OWN0 + np.arange(NW)
    valid = (pos_w >= 0) & (pos_w < S_LEN)
    xw = np.zeros((NW, D), np.float32)
    xw[valid] = x[b, pos_w[valid]]
    kvf = np.zeros((128, NVT), np.float32)
    for i, (r, es, nk) in enumerate(VLIST):
        kvf[:nk, i] = valid[es + r * np.arange(nk)].astype(np.float32)
    posq = (T0 - 1 + np.arange(NX)).astype(np.float32)
    ang = posq[None, :] * cst["inv_freq"][:, None]
    cq, sq = np.cos(ang).astype(np.float32), np.sin(ang).astype(np.float32)
    rq = np.ascontiguousarray(np.stack([np.concatenate([cq, cq], 0), np.concatenate([sq, sq], 0)], axis=1))
    uflag = np.zeros((128, 2), np.float32)
    uflag[:, 0] = 1.0 if T0 > 0 else 0.0
    uflag[:, 1] = 1.0 if T0 + OWN < S_LEN else 0.0
    d = dict(shared)
    d.update(xw=xw, xf=np.ascontiguousarray(x[b]), kvf=kvf, rq=rq, uflag=uflag, masks=cst["masks"], rk=cst["rk"], ident=cst["ident"])
    return d


def kernel(x, norm_mix_pre, w_in, q_lat_norm, w_uq, kv_lat_norm, w_ukv, out_norm_a, out_norm_b, w_o,
           norm_mix_post, norm_ffn_pre, w_up, conv_w, conv_b, w_down, norm_ffn_post):
    f = lambda a: np.ascontiguousarray(np.asarray(a, dtype=np.float32))
    x = f(x)
    gp = np.zeros((128, 32), np.float32)
    gp[:, 0:8] = f(norm_mix_pre)[0].reshape(8, 128).T
    gp[:, 8:11] = f(q_lat_norm)[0].reshape(3, 128).T
    gp[:, 11:13] = f(kv_lat_norm)[0].reshape(2, 128).T
    gp[:, 13:21] = np.concatenate([f(out_norm_a)[0], f(out_norm_b)[0]]).reshape(8, 128).T
    gp[:, 21:29] = f(norm_ffn_pre)[0].reshape(8, 128).T
    gpost = np.ascontiguousarray(np.broadcast_to(np.stack([f(norm_mix_post)[0], f(norm_ffn_post)[0]])[None], (128, 2, D)))
    cwb = np.zeros((128, 44, 4), np.float32)
    cwb[:, :, 0:3] = f(conv_w)[0].T.reshape(44, 128, 3).transpose(1, 0, 2)
    cwb[:, :, 3] = f(conv_b)[0].reshape(44, 128).T
    shared = dict(w_in=f(w_in)[0], w_uq=f(w_uq)[0], w_ukv=f(w_ukv)[0], w_o=f(w_o)[0], w_up=f(w_up)[0], w_down=f(w_down)[0],
                  gp=gp, gpost=gpost, cwb=cwb)
    if "nc" not in _CACHE:
        _CACHE["nc"] = build()
    nc = _CACHE["nc"]
    in_maps = [_core_inputs(c, x, shared) for c in range(8)]
    res = run_bass_kernel_spmd(nc, in_maps, core_ids=list(range(8)))
    out = np.zeros((2, S_LEN, D), np.float32)
    for c in range(8):
        b, qc = c // 4, c % 4
        out[b, qc * OWN:(qc + 1) * OWN] = res.results[c]["y"]
    return out
```

```python
import contextlib
import types
import numpy as np
import ml_dtypes
import concourse.bass as bass
import concourse.mybir as mybir
from concourse.bass_utils import run_bass_kernel_spmd

F32 = mybir.dt.float32
BF16 = mybir.dt.bfloat16
U8 = mybir.dt.uint8
AF = mybir.ActivationFunctionType
ALU = mybir.AluOpType

S_LEN = 8192
D = 1024
OWN = 2048
OWN0 = 1152
NW = 4352
NX = 2050
DFF = 2816
EPS = 1e-6
ENGS = ("sync", "scalar", "gpsimd", "vector", "tensor")
XBLK = [(i * 410, 410) for i in range(5)]


class Sched:
    def __init__(self, nc, ndma_sems=8):
        self.nc = nc
        self.ops = []
        self.ndma = ndma_sems

    @staticmethod
    def _freeze(fn):
        if fn.__closure__ is None:
            return fn
        cells = []
        for c in fn.__closure__:
            try:
                cells.append(types.CellType(c.cell_contents))
            except ValueError:
                cells.append(c)
        return types.FunctionType(fn.__code__, fn.__globals__, fn.__name__, fn.__defaults__, tuple(cells))

    def op(self, eng, fn, reads=(), writes=(), dma=False):
        fn = self._freeze(fn)
        self.ops.append(dict(eng=eng, fn=fn, reads=tuple(reads), writes=tuple(writes), dma=dma, bar=False))

    def barrier(self):
        self.ops.append(dict(eng=None, fn=None, reads=(), writes=(), dma=False, bar=True))

    def emit(self, final_wait_eng="sync"):
        nc = self.nc
        ops = self.ops
        n = len(ops)
        groups = {}
        for o in ops:
            for k in o["reads"] + o["writes"]:
                if isinstance(k, tuple) and len(k) == 3 and k[0] == "grp":
                    groups.setdefault(k[1], set()).add(k)

        def expand(keys):
            out = []
            for k in keys:
                out.append(k)
                if k in groups:
                    out.extend(groups[k])
            return out

        last_writer = {}
        readers = {}
        deps = [dict() for _ in range(n)]
        since_bar = []
        pending_bar = {}
        for i, o in enumerate(ops):
            if o["bar"]:
                lastc = {}
                dl = set()
                for j in since_bar:
                    if ops[j]["dma"]:
                        dl.add(j)
                    else:
                        lastc[ops[j]["eng"]] = j
                dl.update(lastc.values())
                for e in ENGS:
                    pending_bar[e] = set(dl) | pending_bar.get(e, set())
                since_bar = []
                continue
            d = deps[i]
            if o["eng"] in pending_bar:
                for j in pending_bar.pop(o["eng"]):
                    d[j] = True
            rk, wk = expand(o["reads"]), expand(o["writes"])
            for r in rk:
                if r in last_writer:
                    d[last_writer[r]] = True
            for w in wk:
                if w in last_writer:
                    d.setdefault(last_writer[w], False)
                for j in readers.get(w, ()):
                    d.setdefault(j, False)
            d.pop(i, None)
            for w in wk:
                last_writer[w] = i
                readers[w] = []
            for r in rk:
                if r not in wk:
                    readers.setdefault(r, []).append(i)
            since_bar.append(i)
        needed = set()
        red = [None] * n
        for i, o in enumerate(ops):
            if o["bar"]:
                continue
            per_eng = {}
            dl = []
            for j, is_raw in deps[i].items():
                pj = ops[j]
                if pj["dma"]:
                    dl.append(j)
                    continue
                if pj["eng"] == o["eng"] and not o["dma"] and (o["eng"] == "tensor" or not is_raw):
                    continue
                e = pj["eng"]
                if e not in per_eng or per_eng[e] < j:
                    per_eng[e] = j
            dl.extend(per_eng.values())
            red[i] = dl
            needed.update(dl)
        cnt = {e: 0 for e in ENGS}
        dcnt = {}
        sig = [None] * n
        dma_idx = {e: 0 for e in ENGS}
        for i, o in enumerate(ops):
            if o["bar"]:
                continue
            if o["dma"]:
                k = dma_idx[o["eng"]] % self.ndma
                dma_idx[o["eng"]] += 1
                key = ("dma", o["eng"], k)
                prev = dcnt.get(key, 0)
                dcnt[key] = prev + 16
                sig[i] = (key, prev + 16)
                o["dma_prev"] = (key, prev) if prev > 0 else None
            elif i in needed:
                cnt[o["eng"]] += 1
                sig[i] = (("eng", o["eng"]), cnt[o["eng"]])
        semkeys = sorted({s[0] for s in sig if s is not None}, key=str)
        stack = contextlib.ExitStack()
        sems = {}
        for sk in semkeys:
            sems[sk] = stack.enter_context(nc.semaphore("s_" + "_".join(str(x) for x in sk)))
        by_eng = {e: [i for i, o in enumerate(ops) if o["eng"] == e] for e in ENGS}
        dma_final = list(dcnt.items())

        def run(engname, eng):
            waited = {}
            for i in by_eng[engname]:
                o = ops[i]
                wl = [sig[j] for j in red[i]]
                if o["dma"] and o.get("dma_prev"):
                    wl.append(o["dma_prev"])
                for (sk, v) in wl:
                    if waited.get(sk, 0) >= v:
                        continue
                    eng.wait_ge(sems[sk], v)
                    waited[sk] = v
                ins = o["fn"](eng)
                if sig[i] is not None:
                    ins.then_inc(sems[sig[i][0]], 16 if o["dma"] else 1)
            if engname == final_wait_eng:
                for sk, v in dma_final:
                    if waited.get(sk, 0) < v:
                        eng.wait_ge(sems[sk], v)
                for e in ENGS:
                    if cnt[e] > 0 and waited.get(("eng", e), 0) < cnt[e]:
                        eng.wait_ge(sems[("eng", e)], cnt[e])

        with stack:
            with nc.Block() as block:
                @block.sync
                def _(e):
                    run("sync", e)

                @block.scalar
                def _(e):
                    run("scalar", e)

                @block.gpsimd
                def _(e):
                    run("gpsimd", e)

                @block.vector
                def _(e):
                    run("vector", e)

                @block.tensor
                def _(e):
                    run("tensor", e)


def dil_tables():
    vt = {}

    def vtile(r, e0, nk):
        key = (r, e0, nk)
        if key not in vt:
            vt[key] = len(vt)
        return vt[key]

    groups = {1: [], 4: [], 16: []}
    for r in (1, 4, 16):
        def blk(x0, N):
            eq0 = OWN0 - 1 + x0
            t0 = vtile(r, eq0 - 64 * r, 128)
            t1 = vtile(r, eq0 + 64 * r, 128 if N > 1 else 1)
            return (x0, N, t0, t1)
        if r == 1:
            for g in range(4):
                groups[r].append(("reg", g, [blk(1 + 128 * (4 * g + b), 128) for b in range(4)]))
        elif r == 4:
            for g in range(4):
                groups[r].append(("reg", g, [blk(1 + c + 512 * g, 128) for c in range(4)]))
        else:
            for g in range(4):
                groups[r].append(("reg", g, [blk(1 + 4 * g + c, 128) for c in range(4)]))
        groups[r].append(("halo", 0, [blk(0, 1), blk(NX - 1, 1)]))
    vlist = [None] * len(vt)
    for k, i in vt.items():
        vlist[i] = k
    return vlist, groups


VLIST, DGROUPS = dil_tables()
NVT = len(VLIST)


def build(dbg=False):
    nc = bass.Bass("TRN2", target_bir_lowering=False)

    def din(name, shape, dt=F32):
        return nc.dram_tensor(name, list(shape), dt, kind="ExternalInput").ap()

    xw = din("xw", [NW, D])
    xf = din("xf", [S_LEN, D])
    w_in = din("w_in", [D, 2208])
    w_uq = din("w_uq", [384, 768])
    w_ukv = din("w_ukv", [256, 1024])
    w_o = din("w_o", [D, D])
    w_up = din("w_up", [D, 2 * DFF])
    w_down = din("w_down", [DFF, D])
    gp = din("gp", [128, 32])
    gpost = din("gpost", [128, 2, D])
    cwb = din("cwb", [128, 44, 4])
    masks = din("masks", [8, 128, 6, 128])
    kvf = din("kvf", [128, NVT])
    rq = din("rq", [32, 2, NX])
    rk = din("rk", [128, 2, 64, 16])
    ident = din("ident", [128, 128])
    uflag = din("uflag", [128, 2])
    yout = nc.dram_tensor("y", [OWN, D], F32, kind="ExternalOutput").ap()
    xmid = nc.dram_tensor("xmid", [OWN, D], F32, kind="Internal").ap()
    dbg_out = {}

    SB_BYTES = 212000
    big = nc.alloc_sbuf_tensor("big", [128, SB_BYTES], U8).ap()
    ps = nc.alloc_psum_tensor("ps", [128, 4096], F32).ap()
    psb = ps.bitcast(BF16)

    class Region:
        def __init__(self, lo, hi):
            assert hi <= SB_BYTES and lo <= hi, (lo, hi)
            self.lo, self.hi, self.off = lo, hi, lo

        def alloc(self, shape, dt, p0=0):
            esz = 4 if dt == F32 else 2
            nb = int(np.prod(shape[1:])) * esz
            nb_al = (nb + 63) // 64 * 64
            assert self.off + nb_al <= self.hi, ("SBUF region overflow", self.lo, self.hi, self.off, nb_al)
            v = big[p0:p0 + shape[0], self.off:self.off + nb].bitcast(dt)
            self.off += nb_al
            if len(shape) == 3:
                v = v.rearrange("p (a b) -> p a b", a=shape[1])
            elif len(shape) == 4:
                v = v.rearrange("p (a b c) -> p a b c", a=shape[1], b=shape[2])
            return v

    RP = Region(0, 4096)
    R1 = Region(RP.hi, RP.hi + 86080)
    RC = Region(R1.hi, R1.hi + 12352)
    RY = Region(RC.hi, RC.hi + 16448 + 4096 + 16448)
    RT = Region(RY.hi, SB_BYTES)
    A = RP
    S = Sched(nc)
    op = S.op

    def dump(name, ap, shape, dt, keys):
        if not dbg:
            return
        import os
        sel = os.environ.get("DBGSEL", "")
        if sel and name not in sel.split(","):
            return
        d = nc.dram_tensor("dbg_" + name, list(shape), dt, kind="ExternalOutput").ap()
        op("sync", lambda e: e.dma_start(out=d, in_=ap), reads=keys, dma=True)

    def bank(b, n=512, p=128, p0=0):
        return ps[p0:p0 + p, b * 512:b * 512 + n]

    def bankb(b, n=1024, p=128, p0=0):
        return psb[p0:p0 + p, b * 1024:b * 1024 + n]

    idf = A.alloc([128, 128], F32)
    idb = A.alloc([128, 128], BF16)
    gpt = A.alloc([128, 32], F32)
    stt = A.alloc([128, 256], F32)
    junk = A.alloc([128, 1024], BF16)
    rec = A.alloc([128, 8], F32)
    op("sync", lambda e: e.dma_start(out=idf, in_=ident), writes=["idf"], dma=True)
    op("sync", lambda e: e.dma_start(out=gpt, in_=gp), writes=["gpt"], dma=True)
    op("vector", lambda e: e.tensor_copy(out=idb, in_=idf), reads=["idf"], writes=["idb"])

    def rstd_from_ss(ss_ap, out_ap, n, key):
        tmp = ss_ap
        op("scalar", lambda e: e.activation(out=tmp, in_=ss_ap, func=AF.Sqrt, bias=EPS, scale=1.0 / n),
           reads=[key], writes=[key])
        op("vector", lambda e: e.reciprocal(out=out_ap, in_=tmp), reads=[key], writes=[key])

    def load_weight(dst, src_ap, kch, ncols, gcol, stage, wkey, negate_cols=None):
        op("gpsimd", lambda e: e.dma_start(out=stage[:, 0:kch, 0:ncols], in_=src_ap.rearrange("(k p) n -> p k n", p=128)),
           writes=[("stage", id(stage))], dma=True)
        for k in range(kch):
            eng = "vector" if k % 2 == 0 else "scalar"
            if gcol is None:
                if eng == "vector":
                    op(eng, lambda e, k=k: e.tensor_copy(out=dst[:, k, :], in_=stage[:, k, 0:ncols]),
                       reads=[("stage", id(stage))], writes=[("grp", wkey, (id(dst), k))])
                else:
                    op(eng, lambda e, k=k: e.copy(out=dst[:, k, :], in_=stage[:, k, 0:ncols]),
                       reads=[("stage", id(stage))], writes=[("grp", wkey, (id(dst), k))])
            else:
                if eng == "vector":
                    op(eng, lambda e, k=k: e.tensor_scalar(out=dst[:, k, :], in0=stage[:, k, 0:ncols],
                                                          scalar1=gpt[:, gcol + k:gcol + k + 1], scalar2=None, op0=ALU.mult),
                       reads=[("stage", id(stage)), "gpt"], writes=[("grp", wkey, (id(dst), k))])
                else:
                    op(eng, lambda e, k=k: e.activation(out=dst[:, k, :], in_=stage[:, k, 0:ncols], func=AF.Identity,
                                                        scale=gpt[:, gcol + k:gcol + k + 1]),
                       reads=[("stage", id(stage)), "gpt"], writes=[("grp", wkey, (id(dst), k))])

    def norm_tiles_to_T(src_dram_rows, ntile, xbuf, xnbuf, key):
        op("sync", lambda e: e.dma_start(out=xbuf[:, 0:ntile, :], in_=src_dram_rows.rearrange("(t p) f -> p t f", p=128)),
           writes=[("x", key)], dma=True)
        for t in range(ntile):
            op("scalar", lambda e, t=t: e.activation(out=junk, in_=xbuf[:, t, :], func=AF.Square, accum_out=stt[:, t:t + 1]),
               reads=[("x", key)], writes=["junk", "stt"])
        rstd_from_ss(stt[:, 0:ntile], stt[:, 8:8 + ntile], D, "stt")
        op("vector", lambda e: e.tensor_tensor(out=xnbuf[:, 0:ntile, :], in0=xbuf[:, 0:ntile, :],
                                               in1=stt[:, 8:8 + ntile].unsqueeze(2).to_broadcast([128, ntile, D]), op=ALU.mult),
           reads=[("x", key), "stt"], writes=[("xn", key)])

    def transpose_tile(xn_tile, hT_dst, pbank, rkeys, wkey):
        for k in range(8):
            op("tensor", lambda e, k=k: e.transpose(out=bankb(pbank)[:, k * 128:(k + 1) * 128], in_=xn_tile[:, k * 128:(k + 1) * 128], identity=idb),
               reads=list(rkeys) + ["idb"], writes=[("ps", pbank)])
        op("vector", lambda e: e.tensor_copy(out=hT_dst, in_=bankb(pbank).rearrange("p (k t) -> p k t", k=8)),
           reads=[("ps", pbank)], writes=[wkey])

    KAT = R1.alloc([128, 4, NW], BF16)
    VAT = R1.alloc([128, 4, NW], BF16)
    QAT = R1.alloc([128, 4, NX], BF16)
    cqT = RC.alloc([128, 3, NX], BF16)
    A = Region(RC.hi, SB_BYTES)
    WA = A.alloc([128, 8, 1920], BF16)
    stage = A.alloc([128, 8, 480], F32)
    xbuf = [A.alloc([128, 2, D], F32) for _ in range(2)]
    xnb = [A.alloc([128, 2, D], BF16) for _ in range(2)]
    hTw = [A.alloc([128, 8, 256], BF16) for _ in range(2)]
    cqn = A.alloc([128, 384], BF16)
    for c in range(4):
        load_weight(WA[:, :, c * 480:(c + 1) * 480], w_in[:, c * 480:(c + 1) * 480], 8, 480, 0, stage, "WA")

    def cq_tile(cols_ap, M, xcol_ap_fn, hkey, pb):
        for k in range(8):
            la = cols_ap(k)
            op("tensor", lambda e, k=k, la=la: e.matmul(bank(pb, 384, M), lhsT=la, rhs=WA[:, k, 1536:1920], start=(k == 0), stop=(k == 7)),
               reads=[hkey, "WA"], writes=[("ps", pb)])
        op("scalar", lambda e: e.activation(out=junk[0:M, 0:384], in_=bank(pb, 384, M), func=AF.Square, accum_out=stt[0:M, 16:17]),
           reads=[("ps", pb)], writes=["junk", "stt2"])
        op("scalar", lambda e: e.activation(out=stt[0:M, 16:17], in_=stt[0:M, 16:17], func=AF.Sqrt, bias=EPS, scale=1.0 / 384),
           reads=["stt2"], writes=["stt2"])
        op("vector", lambda e: e.reciprocal(out=stt[0:M, 17:18], in_=stt[0:M, 16:17]), reads=["stt2"], writes=["stt2"])
        op("vector", lambda e: e.tensor_scalar(out=cqn[0:M, :], in0=bank(pb, 384, M), scalar1=stt[0:M, 17:18], scalar2=None, op0=ALU.mult),
           reads=[("ps", pb), "stt2"], writes=["cqn"])
        for j in range(3):
            op("tensor", lambda e, j=j: e.transpose(out=bankb(pb)[:, j * 128:j * 128 + M], in_=cqn[0:M, j * 128:(j + 1) * 128], identity=idb[0:M, 0:M]),
               reads=["cqn", "idb"], writes=[("ps", pb)])
        xc = xcol_ap_fn()
        op("vector", lambda e: e.tensor_copy(out=xc, in_=bankb(pb)[:, 0:384].rearrange("p (j t) -> p j t", j=3)[:, :, 0:M]),
           reads=[("ps", pb)], writes=["cqT"])

    NGW = NW // 256

    def w_stageA(g):
        bi = g % 2
        e0 = g * 256
        norm_tiles_to_T(xw[e0:e0 + 256, :], 2, xbuf[bi], xnb[bi], ("w", bi))
        for t in range(2):
            transpose_tile(xnb[bi][:, t, :], hTw[bi][:, :, t * 128:(t + 1) * 128], t, [("xn", ("w", bi))], ("hTw", bi))

    def w_stageB(g):
        bi = g % 2
        e0 = g * 256
        xlo, xhi = max(e0, OWN0 - 1), min(e0 + 256, OWN0 - 1 + NX)
        jobs = [("K", 512 + 128 * c, c) for c in range(4)] + [("V", 1024 + 128 * c, c) for c in range(4)]
        if xhi > xlo:
            jobs += [("Q", 128 * c, c) for c in range(4)]
        for ji, (kind, col0, c) in enumerate(jobs):
            pb = 2 + (ji % 4)
            for k in range(8):
                op("tensor", lambda e, k=k: e.matmul(bank(pb, 256), lhsT=WA[:, k, col0:col0 + 128], rhs=hTw[bi][:, k, :], start=(k == 0), stop=(k == 7)),
                   reads=[("hTw", bi), "WA"], writes=[("ps", pb)])
            if kind == "K":
                op("vector", lambda e: e.tensor_copy(out=KAT[:, c, e0:e0 + 256], in_=bank(pb, 256)), reads=[("ps", pb)], writes=["KAT"])
            elif kind == "V":
                op("scalar", lambda e: e.copy(out=VAT[:, c, e0:e0 + 256], in_=bank(pb, 256)), reads=[("ps", pb)], writes=["VAT"])
            else:
                op("vector", lambda e: e.tensor_copy(out=QAT[:, c, xlo - (OWN0 - 1):xhi - (OWN0 - 1)], in_=bank(pb, 256)[:, xlo - e0:xhi - e0]),
                   reads=[("ps", pb)], writes=["QAT"])
        for t in range(2):
            et = e0 + t * 128
            if OWN0 <= et < OWN0 + OWN:
                x0 = et - (OWN0 - 1)
                cq_tile(lambda k, t=t, bi=bi: hTw[bi][:, k, t * 128:(t + 1) * 128], 128, lambda x0=x0: cqT[:, :, x0:x0 + 128], ("hTw", bi), 6 + t)
            if et == OWN0 - 128:
                cq_tile(lambda k, t=t, bi=bi: hTw[bi][:, k, t * 128 + 127:t * 128 + 128], 1, lambda: cqT[:, :, 0:1], ("hTw", bi), 6 + t)
            if et == OWN0 + OWN:
                cq_tile(lambda k, t=t, bi=bi: hTw[bi][:, k, t * 128:t * 128 + 1], 1, lambda: cqT[:, :, NX - 1:NX], ("hTw", bi), 6 + t)

    w_stageA(0)
    for g in range(NGW):
        if g + 1 < NGW:
            w_stageA(g + 1)
        w_stageB(g)
    dump("KAT", KAT, [128, 4, NW], BF16, ["KAT"])
    dump("VAT", VAT, [128, 4, NW], BF16, ["VAT"])
    dump("QAT", QAT, [128, 4, NX], BF16, ["QAT"])
    dump("cqT", cqT, [128, 3, NX], BF16, ["cqT"])
    S.barrier()

    ynTa = RY.alloc([128, 4, NX], BF16)
    A = Region(RY.lo + 16448, SB_BYTES)
    mk = [A.alloc([128, 6, 128], F32) for _ in range(2)]
    kvft = A.alloc([128, NVT], F32)
    Vp = [A.alloc([128, NVT, 65], BF16) for _ in range(2)]
    Oacc = A.alloc([128, NX], F32)
    ya_tm = A.alloc([128, 17, 512], F32)
    Eb = [A.alloc([128, 1024], F32) for _ in range(2)]
    Pb = [A.alloc([128, 1024], BF16) for _ in range(2)]
    op("sync", lambda e: e.dma_start(out=kvft, in_=kvf), writes=["kvft"], dma=True)

    def oacc_to_tm(h, dst_tm, okey, Oacc):
        tiles = [(1 + 128 * m, 128, 1) for m in range(16)] + [(0, 2, NX - 1)]
        for g0 in range(0, 17, 4):
            tl = tiles[g0:g0 + 4]
            pb = 6 + (g0 // 4) % 2
            for i, (x0, M, step) in enumerate(tl):
                src = Oacc[0:65, x0:x0 + 128] if step == 1 else Oacc[0:65, 0:NX:NX - 1]
                op("tensor", lambda e, i=i, src=src, M=M, pb=pb: e.transpose(out=bank(pb)[0:M, i * 65:(i + 1) * 65], in_=src, identity=idf[0:65, 0:65]),
                   reads=[okey, "idf"], writes=[("ps", pb)])
            nt = len(tl)
            M = tl[0][1]
            pv = bank(pb)[0:M, 0:nt * 65].rearrange("p (t c) -> p t c", t=nt)
            op("vector", lambda e, pv=pv, nt=nt, M=M: e.reciprocal(out=rec[0:M, 0:nt], in_=pv[:, :, 64]), reads=[("ps", pb)], writes=["rec"])
            op("vector", lambda e, pv=pv, nt=nt, M=M, g0=g0: e.tensor_tensor(out=dst_tm[0:M, g0:g0 + nt, h * 64:(h + 1) * 64], in0=pv[:, :, 0:64],
                                                                         in1=rec[0:M, 0:nt].unsqueeze(2).to_broadcast([M, nt, 64]), op=ALU.mult),
               reads=[("ps", pb), "rec"], writes=["tm"])

    def tm_to_ynT(src_tm, dstT, nkey, Pb):
        for t in range(17):
            M = 128 if t < 16 else 2
            op("scalar", lambda e, t=t, M=M: e.activation(out=junk[0:M, 0:512], in_=src_tm[0:M, t, :], func=AF.Square, accum_out=stt[0:M, 32 + t:33 + t]),
               reads=["tm"], writes=["junk", "stt3"])
        rstd_from_ss(stt[:, 32:49], stt[:, 64:81], 512, "stt3")
        for t in range(17):
            M = 128 if t < 16 else 2
            pb = t % 2
            ynb = Pb[t % 2]
            op("vector", lambda e, t=t, M=M, ynb=ynb: e.tensor_scalar(out=ynb[0:M, 0:512], in0=src_tm[0:M, t, :], scalar1=stt[0:M, 64 + t:65 + t], scalar2=None, op0=ALU.mult),
               reads=["tm", "stt3"], writes=[("Pb", t % 2)])
            for j in range(4):
                op("tensor", lambda e, j=j, M=M, ynb=ynb, pb=pb: e.transpose(out=bankb(pb)[:, j * 128:j * 128 + M], in_=ynb[0:M, j * 128:(j + 1) * 128], identity=idb[0:M, 0:M]),
                   reads=[("Pb", t % 2), "idb"], writes=[("ps", pb)])
            srcv = bankb(pb)[:, 0:512].rearrange("p (j t) -> p j t", j=4)[:, :, 0:M]
            if t < 16:
                dstv = dstT[:, :, 1 + 128 * t:1 + 128 * (t + 1)]
            else:
                dstv = dstT[:, :, 0:NX:NX - 1]
            op("vector", lambda e, srcv=srcv, dstv=dstv: e.tensor_copy(out=dstv, in_=srcv), reads=[("ps", pb)], writes=[nkey])

    def d_prep(h):
        pr, hs = h // 2, (h % 2) * 64
        vb = Vp[h % 2]
        mb = mk[h % 2]
        op("sync", lambda e: e.dma_start(out=mb, in_=masks[h]), writes=[("mk", h % 2)], dma=True)
        for v0 in range(0, NVT, 8):
            vts = VLIST[v0:v0 + 8]
            pb = 4 + (v0 // 8) % 2
            for i, (r, es, nk) in enumerate(vts):
                op("tensor", lambda e, i=i, r=r, es=es, nk=nk: e.transpose(out=bankb(pb)[0:nk, i * 64:(i + 1) * 64],
                                                                        in_=VAT[hs:hs + 64, pr, es:es + r * (nk - 1) + 1:r], identity=idb[hs:hs + 64, hs:hs + 64]),
                   reads=["VAT", "idb"], writes=[("ps", pb)])
            i = 0
            while i < len(vts):
                nk = vts[i][2]
                j = i
                while j + 1 < len(vts) and vts[j + 1][2] == nk:
                    j += 1
                cnt_ = j - i + 1
                srcv = bankb(pb)[0:nk, i * 64:(j + 1) * 64].rearrange("p (t c) -> p t c", t=cnt_)
                op("vector", lambda e, srcv=srcv, i=i, cnt_=cnt_, nk=nk: e.tensor_tensor(
                    out=vb[0:nk, v0 + i:v0 + i + cnt_, 0:64], in0=srcv,
                    in1=kvft[0:nk, v0 + i:v0 + i + cnt_].unsqueeze(2).to_broadcast([nk, cnt_, 64]), op=ALU.mult),
                   reads=[("ps", pb), "kvft"], writes=[("Vp", h % 2)])
                i = j + 1
        op("gpsimd", lambda e: e.tensor_copy(out=vb[:, :, 64], in_=kvft), reads=["kvft"], writes=[("Vp", h % 2)])

    GL = [(ri, r, kind, g, blks) for ri, r in enumerate((1, 4, 16)) for (kind, g, blks) in DGROUPS[r]]
    gctr = [0]

    def d_scores(h, gi, grp):
        pr, hs = h // 2, (h % 2) * 64
        ri, r, kind, g, blks = grp
        sb = 2 * (gi % 2)
        for bi_, (x0, Nq, t0, t1) in enumerate(blks):
            qv = QAT[hs:hs + 64, pr, x0:x0 + r * (Nq - 1) + 1:r]
            for role, tix in ((0, t0), (1, t1)):
                (rr, es, nk) = VLIST[tix]
                op("tensor", lambda e, role=role, es=es, nk=nk, qv=qv, bi_=bi_, Nq=Nq: e.matmul(
                    bank(sb + role)[0:nk, bi_ * 128:bi_ * 128 + Nq], lhsT=KAT[hs:hs + 64, pr, es:es + r * (nk - 1) + 1:r], rhs=qv, start=True, stop=True),
                   reads=["KAT", "QAT"], writes=[("ps", sb + role)])

    def d_rest(h, gi, grp):
        ri, r, kind, g, blks = grp
        vb = Vp[h % 2]
        mb = mk[h % 2]
        sb = 2 * (gi % 2)
        eb = Eb[gi % 2]
        pbuf = Pb[gi % 2]
        ob = 6 + gi % 2
        ek, pk = ("Eb", gi % 2), ("Pbuf", gi % 2)
        if kind == "reg":
            sv = ps[:, sb * 512:(sb + 2) * 512]
            op("scalar", lambda e: e.activation(out=eb, in_=sv, func=AF.Exp, scale=0.125),
               reads=[("ps", sb), ("ps", sb + 1)], writes=[ek])
            op("vector", lambda e: e.tensor_tensor(
                out=pbuf.rearrange("p (r b q) -> p r b q", r=2, b=4), in0=eb.rearrange("p (r b q) -> p r b q", r=2, b=4),
                in1=mb[:, 2 * ri:2 * ri + 2, :].unsqueeze(2).to_broadcast([128, 2, 4, 128]), op=ALU.mult),
               reads=[ek, ("mk", h % 2)], writes=[pk])
        else:
            op("scalar", lambda e: e.activation(out=eb[:, 0:256:128], in_=bank(sb)[:, 0:256:128], func=AF.Exp, scale=0.125),
               reads=[("ps", sb)], writes=[ek])
            op("scalar", lambda e: e.activation(out=eb[0:1, 512:768:128], in_=bank(sb + 1)[0:1, 0:256:128], func=AF.Exp, scale=0.125),
               reads=[("ps", sb + 1)], writes=[ek])
            op("vector", lambda e: e.tensor_tensor(
                out=pbuf[:, 0:256:128], in0=eb[:, 0:256:128], in1=mb[:, 2 * ri, 0:1].to_broadcast([128, 2]), op=ALU.mult),
               reads=[ek, ("mk", h % 2)], writes=[pk])
            op("vector", lambda e: e.tensor_tensor(
                out=pbuf[0:1, 512:768:128], in0=eb[0:1, 512:768:128], in1=mb[0:1, 2 * ri + 1, 0:1].to_broadcast([1, 2]), op=ALU.mult),
               reads=[ek, ("mk", h % 2)], writes=[pk])
        for bi_, (x0, Nq, t0, t1) in enumerate(blks):
            for role, tix in ((0, t0), (1, t1)):
                nk = VLIST[tix][2]
                op("tensor", lambda e, role=role, tix=tix, nk=nk, bi_=bi_, Nq=Nq: e.matmul(
                    bank(ob)[0:65, bi_ * 128:bi_ * 128 + Nq], lhsT=vb[0:nk, tix, :], rhs=pbuf[0:nk, role * 512 + bi_ * 128:role * 512 + bi_ * 128 + Nq],
                    start=(role == 0), stop=(role == 1)),
                   reads=[pk, ("Vp", h % 2)], writes=[("ps", ob)])
        if kind == "reg":
            src = bank(ob)[0:65, :].rearrange("p (c q) -> p c q", c=4)
            if r == 1:
                dst = Oacc[0:65, 1 + 512 * g:1 + 512 * (g + 1)].rearrange("p (c q) -> p c q", c=4)
            elif r == 4:
                dst = Oacc[0:65, 1 + 512 * g:1 + 512 * (g + 1)].rearrange("p (q c) -> p c q", c=4)
            else:
                dst = Oacc[0:65, 1:1 + OWN].rearrange("p (q c) -> p c q", c=16)[:, 4 * g:4 * g + 4, :]
        else:
            src = bank(ob)[0:65, 0:256:128]
            dst = Oacc[0:65, 0:NX:NX - 1]
        if r == 1:
            op("vector", lambda e: e.tensor_copy(out=dst, in_=src), reads=[("ps", ob)], writes=["Oacc"])
        else:
            op("vector", lambda e: e.tensor_tensor(out=dst, in0=src, in1=dst, op=ALU.add), reads=[("ps", ob), "Oacc"], writes=["Oacc"])

    d_prep(0)
    for h in range(8):
        g0 = gctr[0]
        d_scores(h, g0, GL[0])
        if h + 1 < 8:
            d_prep(h + 1)
        for i, grp in enumerate(GL):
            if i + 1 < len(GL):
                d_scores(h, g0 + i + 1, GL[i + 1])
            d_rest(h, g0 + i, grp)
        gctr[0] += len(GL)
        oacc_to_tm(h, ya_tm, "Oacc", Oacc)
    tm_to_ynT(ya_tm, ynTa, "ynTa", Pb)
    dump("ynTa", ynTa, [128, 4, NX], BF16, ["ynTa"])
    dump("ya_tm", ya_tm, [128, 17, 512], F32, ["tm"])
    S.barrier()

    R1.off = R1.lo
    ckvT = R1.alloc([128, 2, S_LEN], BF16)
    KT = R1.alloc([96, S_LEN], BF16)
    R1s_lo = R1.off
    RY.off = RY.lo + 16448
    Wukv = RY.alloc([128, 2, 1024], BF16)
    A = Region(RY.off, SB_BYTES)
    Wkvl = A.alloc([128, 8, 288], BF16)
    stage2 = A.alloc([128, 8, 512], F32)
    xbuf = [A.alloc([128, 2, D], F32) for _ in range(2)]
    xnb = [A.alloc([128, 2, D], BF16) for _ in range(2)]
    hTk = [A.alloc([128, 8, 128], BF16) for _ in range(2)]
    ckvn = [A.alloc([128, 256], BF16) for _ in range(2)]
    kr_tm = A.alloc([128, 64, 32], F32)
    rkt = A.alloc([128, 2, 64, 16], F32)
    kr_pad = A.alloc([128, 64, 96], BF16)
    rt = [A.alloc([128, 64, 16], F32) for _ in range(2)]
    load_weight(Wkvl, w_in[:, 1920:2208], 8, 288, 0, stage2, "Wkvl")
    load_weight(Wukv[:, :, 0:512], w_ukv[:, 0:512], 2, 512, 11, stage2, "Wukv")
    load_weight(Wukv[:, :, 512:1024], w_ukv[:, 512:1024], 2, 512, 11, stage2, "Wukv")
    op("sync", lambda e: e.dma_start(out=rkt, in_=rk), writes=["rkt"], dma=True)
    op("gpsimd", lambda e: e.memset(kr_pad, 0.0), writes=["kr_pad"])
    def k_stageA(g):
        bi = g % 2
        norm_tiles_to_T(xf[g * 256:(g + 1) * 256, :], 2, xbuf[bi], xnb[bi], ("k", bi))
        for t in range(2):
            tt = g * 2 + t
            hb = hTk[tt % 2]
            transpose_tile(xnb[bi][:, t, :], hb, tt % 2, [("xn", ("k", bi))], ("hTk", tt % 2))
            pb = 2 + tt % 4
            for k in range(8):
                op("tensor", lambda e, k=k: e.matmul(bank(pb, 288), lhsT=hb[:, k, :], rhs=Wkvl[:, k, :], start=(k == 0), stop=(k == 7)),
                   reads=[("hTk", tt % 2), "Wkvl"], writes=[("ps", pb)])

    def k_stageB(g):
        for t in range(2):
            tt = g * 2 + t
            pb = 2 + tt % 4
            sc = 96 + (tt % 8)
            sk = ("stt4", tt % 8)
            op("scalar", lambda e: e.activation(out=junk[:, 0:256], in_=bank(pb, 256), func=AF.Square, accum_out=stt[:, sc:sc + 1]),
               reads=[("ps", pb)], writes=["junk", sk])
            op("scalar", lambda e: e.activation(out=stt[:, sc:sc + 1], in_=stt[:, sc:sc + 1], func=AF.Sqrt, bias=EPS, scale=1.0 / 256),
               reads=[sk], writes=[sk])
            op("vector", lambda e: e.reciprocal(out=stt[:, sc + 8:sc + 9], in_=stt[:, sc:sc + 1]), reads=[sk], writes=[sk])
            cb = ckvn[tt % 2]
            op("vector", lambda e: e.tensor_scalar(out=cb, in0=bank(pb, 256), scalar1=stt[:, sc + 8:sc + 9], scalar2=None, op0=ALU.mult),
               reads=[("ps", pb), sk], writes=[("ckvn", tt % 2)])
            op("vector", lambda e: e.tensor_copy(out=kr_tm[:, tt, :], in_=bank(pb, 288)[:, 256:288]), reads=[("ps", pb)], writes=["kr_tm"])
            pb2 = 6 + tt % 2
            for j in range(2):
                op("tensor", lambda e, j=j: e.transpose(out=bankb(pb2)[:, j * 128:(j + 1) * 128], in_=cb[:, j * 128:(j + 1) * 128], identity=idb),
                   reads=[("ckvn", tt % 2), "idb"], writes=[("ps", pb2)])
            op("scalar", lambda e: e.copy(out=ckvT[:, :, tt * 128:(tt + 1) * 128], in_=bankb(pb2)[:, 0:256].rearrange("p (j t) -> p j t", j=2)),
               reads=[("ps", pb2)], writes=["ckvT"])

    k_stageA(0)
    for g in range(S_LEN // 256):
        if g + 1 < S_LEN // 256:
            k_stageA(g + 1)
        k_stageB(g)
    x1, x2 = kr_tm[:, :, 0:16], kr_tm[:, :, 16:32]
    cosk, sink = rkt[:, 0], rkt[:, 1]
    op("vector", lambda e: e.tensor_tensor(out=rt[0], in0=x1, in1=cosk, op=ALU.mult), reads=["kr_tm", "rkt"], writes=["rt0"])
    op("vector", lambda e: e.tensor_tensor(out=rt[1], in0=x2, in1=sink, op=ALU.mult), reads=["kr_tm", "rkt"], writes=["rt1"])
    op("vector", lambda e: e.tensor_tensor(out=kr_pad[:, :, 64:80], in0=rt[0], in1=rt[1], op=ALU.subtract), reads=["rt0", "rt1", "kr_pad"], writes=["kr_pad"])
    op("vector", lambda e: e.tensor_tensor(out=rt[0], in0=x2, in1=cosk, op=ALU.mult), reads=["kr_tm", "rkt", "kr_pad"], writes=["rt0"])
    op("vector", lambda e: e.tensor_tensor(out=rt[1], in0=x1, in1=sink, op=ALU.mult), reads=["kr_tm", "rkt", "kr_pad"], writes=["rt1"])
    op("vector", lambda e: e.tensor_tensor(out=kr_pad[:, :, 80:96], in0=rt[0], in1=rt[1], op=ALU.add), reads=["rt0", "rt1", "kr_pad"], writes=["kr_pad"])
    for g8 in range(8):
        pb = 6 + g8 % 2
        for i in range(8):
            tt = g8 * 8 + i
            op("tensor", lambda e, i=i, tt=tt, pb=pb: e.transpose(out=bankb(pb)[0:96, i * 128:(i + 1) * 128], in_=kr_pad[:, tt, :], identity=idb),
               reads=["kr_pad", "idb"], writes=[("ps", pb)])
        op("vector", lambda e, pb=pb, g8=g8: e.tensor_copy(out=KT[64:96, g8 * 1024:(g8 + 1) * 1024], in_=bankb(pb)[64:96, :]),
           reads=[("ps", pb)], writes=["KTr"])
    dump("ckvT", ckvT, [128, 2, S_LEN], BF16, ["ckvT"])
    dump("KTr", KT[64:96, :], [32, S_LEN], BF16, ["KTr"])
    S.barrier()

    ynTb = RY.alloc([128, 4, NX], BF16)
    RT.off = RT.lo
    QBT = RT.alloc([96, 8, NX], BF16)
    R1s = Region(R1s_lo, R1.hi)
    Wuq = R1s.alloc([128, 3, 768], BF16)
    Wrot = R1s.alloc([128, 3, 8, 96], BF16)
    stage3 = R1s.alloc([128, 3, 768], F32)
    rqt = RT.alloc([96, 2, NX], F32)
    tq = [RT.alloc([96, 410], F32) for _ in range(2)]
    load_weight(Wuq, w_uq, 3, 768, 8, stage3, "Wuq")
    op("gpsimd", lambda e: e.memset(Wrot, 0.0), writes=["Wrot"])
    Wuq4 = Wuq.rearrange("p k (h c) -> p k h c", h=8)
    for k in range(3):
        op("gpsimd", lambda e, k=k: e.tensor_scalar(out=Wrot[:, k, :, 64:80], in0=Wuq4[:, k, :, 80:96], scalar1=-1.0, scalar2=None, op0=ALU.mult),
           reads=["Wuq", "Wrot"], writes=["Wrot"])
        op("gpsimd", lambda e, k=k: e.tensor_copy(out=Wrot[:, k, :, 80:96], in_=Wuq4[:, k, :, 64:80]), reads=["Wuq", "Wrot"], writes=["Wrot"])
    op("sync", lambda e: e.dma_start(out=rqt[64:96], in_=rq), writes=["rqt"], dma=True)
    qi = 0
    for h in range(8):
        for (c0, cn) in XBLK:
            pa, pr_ = 2 * (qi % 2), 1 + 2 * (qi % 2)
            tb_ = tq[qi % 2]
            tk = ("tq", qi % 2)
            qi += 1
            for k in range(3):
                op("tensor", lambda e, k=k, h=h, c0=c0, cn=cn, pa=pa: e.matmul(bank(pa, cn, 96), lhsT=Wuq[:, k, h * 96:(h + 1) * 96], rhs=cqT[:, k, c0:c0 + cn], start=(k == 0), stop=(k == 2)),
                   reads=["Wuq", "cqT"], writes=[("ps", pa)])
            for k in range(3):
                op("tensor", lambda e, k=k, h=h, c0=c0, cn=cn, pr_=pr_: e.matmul(bank(pr_, cn, 96), lhsT=Wrot[:, k, h, :], rhs=cqT[:, k, c0:c0 + cn], start=(k == 0), stop=(k == 2)),
                   reads=["Wrot", "cqT"], writes=[("ps", pr_)])
            op("scalar", lambda e, h=h, c0=c0, cn=cn, pa=pa: e.copy(out=QBT[0:64, h, c0:c0 + cn], in_=bank(pa, cn, 64)), reads=[("ps", pa)], writes=["QBT"])
            op("vector", lambda e, c0=c0, cn=cn, pa=pa, tb_=tb_: e.tensor_tensor(out=tb_[64:96, 0:cn], in0=bank(pa, cn, 32, 64), in1=rqt[64:96, 0, c0:c0 + cn], op=ALU.mult),
               reads=[("ps", pa), "rqt"], writes=[tk])
            op("vector", lambda e, h=h, c0=c0, cn=cn, pr_=pr_: e.tensor_tensor(out=QBT[64:96, h, c0:c0 + cn], in0=bank(pr_, cn, 32, 64), in1=rqt[64:96, 1, c0:c0 + cn], op=ALU.mult),
               reads=[("ps", pr_), "rqt"], writes=["QBT"])
            op("vector", lambda e, h=h, c0=c0, cn=cn, tb_=tb_: e.tensor_tensor(out=QBT[64:96, h, c0:c0 + cn], in0=QBT[64:96, h, c0:c0 + cn], in1=tb_[64:96, 0:cn], op=ALU.add),
               reads=[tk, "QBT"], writes=["QBT"])
    dump("QBT", QBT, [96, 8, NX], BF16, ["QBT"])
    S.barrier()
    RT.off = RT.lo + 32832
    yb_tm = RT.alloc([128, 17, 512], F32)
    ynb_tmp = [RT.alloc([128, 512], BF16) for _ in range(2)]
    R1s = Region(R1s_lo, R1.hi)
    Vb = [R1s.alloc([128, 64, 65], BF16) for _ in range(2)]
    PT = [R1s.alloc([128, 2, 512], BF16) for _ in range(3)]
    OaccB = R1s.alloc([128, NX], F32)
    for b_ in range(2):
        op("gpsimd", lambda e, b_=b_: e.memset(Vb[b_][:, :, 64], 1.0), writes=[("Vb", b_)])
    scale_b = 96.0 ** -0.5
    MB = [(1 + 512 * i, 512) for i in range(4)]
    QG = [(0, 2), (2, 2)]
    si = 0

    def m_prep_v(h):
        vb = Vb[h % 2]
        for k8 in range(8):
            pb = 6 + k8 % 2
            for i in range(8):
                kt = k8 * 8 + i
                for j in range(2):
                    op("tensor", lambda e, j=j, kt=kt, i=i: e.matmul(bank(pb)[:, i * 64:(i + 1) * 64], lhsT=ckvT[:, j, kt * 128:(kt + 1) * 128], rhs=Wukv[:, j, h * 128 + 64:h * 128 + 128], start=(j == 0), stop=(j == 1)),
                       reads=["Wukv", "ckvT"], writes=[("ps", pb)])
            op("vector", lambda e: e.tensor_copy(out=vb[:, k8 * 8:(k8 + 1) * 8, 0:64], in_=bank(pb).rearrange("p (t c) -> p t c", t=8)),
               reads=[("ps", pb)], writes=[("Vb", h % 2)])

    def m_prep_k(h):
        for nb_ in range(16):
            pb = 6 + nb_ % 2
            for j in range(2):
                op("tensor", lambda e, j=j: e.matmul(bank(pb, 512, 64), lhsT=Wukv[:, j, h * 128:h * 128 + 64], rhs=ckvT[:, j, nb_ * 512:(nb_ + 1) * 512], start=(j == 0), stop=(j == 1)),
                   reads=["Wukv", "ckvT"], writes=[("ps", pb)])
            op("vector", lambda e: e.tensor_copy(out=KT[0:64, nb_ * 512:(nb_ + 1) * 512], in_=bank(pb, 512, 64)),
               reads=[("ps", pb)], writes=["KTn"])

    m_prep_v(0)
    for h in range(8):
        vb = Vb[h % 2]
        m_prep_k(h)
        for gq, (b0, nbk) in enumerate(QG):
            ob = 4

            def emit_S(kt, si_):
                sbk = 2 * (si_ % 2)
                for bb in range(nbk):
                    c0, cn = MB[b0 + bb]
                    op("tensor", lambda e, bb=bb, c0=c0, cn=cn: e.matmul(bank(sbk + bb, cn), lhsT=KT[0:96, kt * 128:(kt + 1) * 128], rhs=QBT[0:96, h, c0:c0 + cn], start=True, stop=True),
                       reads=["KTn", "KTr", "QBT"], writes=[("ps", sbk + bb)])
            emit_S(0, si)
            for kt in range(64):
                sbk = 2 * (si % 2)
                pt = PT[si % 3]
                ptk = ("PT", si % 3)
                if kt + 1 < 64:
                    emit_S(kt + 1, si + 1)
                if gq == 0 and kt == 8 and h + 1 < 8:
                    m_prep_v(h + 1)
                sv = ps[:, sbk * 512:(sbk + nbk) * 512].rearrange("p (a b) -> p a b", a=nbk)
                op("scalar", lambda e: e.activation(out=pt[:, 0:nbk, :], in_=sv, func=AF.Exp, scale=scale_b),
                   reads=[("ps", sbk + bb) for bb in range(nbk)], writes=[ptk])
                for bb in range(nbk):
                    op("tensor", lambda e, bb=bb: e.matmul(bank(ob + bb, 512, 65), lhsT=vb[:, kt, :], rhs=pt[:, bb, :], start=(kt == 0), stop=(kt == 63)),
                       reads=[ptk, ("Vb", h % 2)], writes=[("ps", ob + bb)])
                si += 1
            ov = ps[0:65, ob * 512:(ob + nbk) * 512]
            c0 = MB[b0][0]
            op("vector", lambda e: e.tensor_copy(out=OaccB[0:65, c0:c0 + nbk * 512], in_=ov),
               reads=[("ps", ob + bb) for bb in range(nbk)], writes=["OaccB"])
        sbk = 2 * (si % 2)
        pt = PT[si % 3]
        ptk = ("PT", si % 3)
        qh = QBT[0:96, h, 0:NX:NX - 1]
        for kt in range(64):
            op("tensor", lambda e, kt=kt: e.matmul(bank(sbk)[:, 2 * kt:2 * kt + 2], lhsT=KT[0:96, kt * 128:(kt + 1) * 128], rhs=qh, start=True, stop=True),
               reads=["KTn", "KTr", "QBT"], writes=[("ps", sbk)])
        op("scalar", lambda e: e.activation(out=pt[:, 0, 0:128], in_=bank(sbk, 128), func=AF.Exp, scale=scale_b), reads=[("ps", sbk)], writes=[ptk])
        for kt in range(64):
            op("tensor", lambda e, kt=kt: e.matmul(bank(4, 2, 65), lhsT=vb[:, kt, :], rhs=pt[:, 0, 2 * kt:2 * kt + 2], start=(kt == 0), stop=(kt == 63)),
               reads=[ptk, ("Vb", h % 2)], writes=[("ps", 4)])
        si += 1
        op("vector", lambda e: e.tensor_copy(out=OaccB[0:65, 0:NX:NX - 1], in_=bank(4, 2, 65)), reads=[("ps", 4)], writes=["OaccB"])
        oacc_to_tm(h, yb_tm, "OaccB", OaccB)
    tm_to_ynT(yb_tm, ynTb, "ynTb", ynb_tmp)
    dump("ynTb", ynTb, [128, 4, NX], BF16, ["ynTb"])
    dump("yb_tm", yb_tm, [128, 17, 512], F32, ["tm"])
    S.barrier()

    RT.off = RT.lo
    hTf = RT.alloc([128, 8, NX], BF16)
    A = Region(R1.lo, RC.hi)
    Wo = A.alloc([128, 8, D], BF16)
    stage4 = A.alloc([128, 8, 256], F32)
    gpo = A.alloc([128, 2, D], F32)
    xt_ = [A.alloc([128, D], F32) for _ in range(2)]
    xm_ = [A.alloc([128, D], F32) for _ in range(2)]
    hn_ = [A.alloc([128, D], BF16) for _ in range(2)]
    for c in range(4):
        load_weight(Wo[:, :, c * 256:(c + 1) * 256], w_o[:, c * 256:(c + 1) * 256], 8, 256, 13, stage4, "Wo")
    op("sync", lambda e: e.dma_start(out=gpo, in_=gpost), writes=["gpo"], dma=True)
    for t in range(17):
        M = 128 if t < 16 else 2
        bi = t % 2
        xt, xm, hn = xt_[bi], xm_[bi], hn_[bi]
        xk, mkk, hk = ("xt", bi), ("xm", bi), ("hn", bi)
        if t < 16:
            op("sync", lambda e, t=t, xt=xt: e.dma_start(out=xt, in_=xw[OWN0 + 128 * t:OWN0 + 128 * (t + 1), :]), writes=[xk], dma=True)
            cols = lambda k, t=t: (ynTa if k < 4 else ynTb)[:, k % 4, 1 + 128 * t:1 + 128 * (t + 1)]
        else:
            op("sync", lambda e, xt=xt: e.dma_start(out=xt[0:1, :], in_=xw[OWN0 - 1:OWN0, :]), writes=[xk], dma=True)
            op("sync", lambda e, xt=xt: e.dma_start(out=xt[1:2, :], in_=xw[OWN0 + OWN:OWN0 + OWN + 1, :]), writes=[xk], dma=True)
            cols = lambda k: (ynTa if k < 4 else ynTb)[:, k % 4, 0:NX:NX - 1]
        pb = 2 * (t % 2)
        for n2 in range(2):
            for k in range(8):
                la = cols(k)
                op("tensor", lambda e, k=k, n2=n2, M=M, pb=pb, la=la: e.matmul(bank(pb + n2, 512, M), lhsT=la, rhs=Wo[:, k, n2 * 512:(n2 + 1) * 512], start=(k == 0), stop=(k == 7)),
                   reads=["ynTa", "ynTb", "Wo"], writes=[("ps", pb + n2)])
        yv = ps[0:M, pb * 512:(pb + 2) * 512]
        sc = 128 + 2 * (t % 4)
        sk = ("stt5", t % 4)
        op("scalar", lambda e, yv=yv, M=M, sc=sc: e.activation(out=junk[0:M, :], in_=yv, func=AF.Square, accum_out=stt[0:M, sc:sc + 1]),
           reads=[("ps", pb), ("ps", pb + 1)], writes=["junk", sk])
        op("scalar", lambda e, M=M, sc=sc: e.activation(out=stt[0:M, sc:sc + 1], in_=stt[0:M, sc:sc + 1], func=AF.Sqrt, bias=EPS, scale=1.0 / D), reads=[sk], writes=[sk])
        op("vector", lambda e, M=M, sc=sc: e.reciprocal(out=stt[0:M, sc + 1:sc + 2], in_=stt[0:M, sc:sc + 1]), reads=[sk], writes=[sk])
        op("vector", lambda e, yv=yv, M=M, sc=sc, xm=xm: e.scalar_tensor_tensor(out=xm[0:M, :], in0=yv, scalar=stt[0:M, sc + 1:sc + 2], in1=gpo[0:M, 0, :], op0=ALU.mult, op1=ALU.mult),
           reads=[("ps", pb), ("ps", pb + 1), sk, "gpo"], writes=[mkk])
        op("vector", lambda e, M=M, xm=xm, xt=xt: e.tensor_tensor(out=xm[0:M, :], in0=xm[0:M, :], in1=xt[0:M, :], op=ALU.add), reads=[mkk, xk], writes=[mkk])
        if t < 16:
            op("sync", lambda e, t=t, xm=xm: e.dma_start(out=xmid[128 * t:128 * (t + 1), :], in_=xm), reads=[mkk], writes=["xmid"], dma=True)
        sc2 = 136 + 2 * (t % 4)
        sk2 = ("stt6", t % 4)
        op("scalar", lambda e, M=M, sc2=sc2, xm=xm: e.activation(out=junk[0:M, :], in_=xm[0:M, :], func=AF.Square, accum_out=stt[0:M, sc2:sc2 + 1]),
           reads=[mkk], writes=["junk", sk2])
        op("scalar", lambda e, M=M, sc2=sc2: e.activation(out=stt[0:M, sc2:sc2 + 1], in_=stt[0:M, sc2:sc2 + 1], func=AF.Sqrt, bias=EPS, scale=1.0 / D), reads=[sk2], writes=[sk2])
        op("vector", lambda e, M=M, sc2=sc2: e.reciprocal(out=stt[0:M, sc2 + 1:sc2 + 2], in_=stt[0:M, sc2:sc2 + 1]), reads=[sk2], writes=[sk2])
        op("vector", lambda e, M=M, sc2=sc2, xm=xm, hn=hn: e.tensor_scalar(out=hn[0:M, :], in0=xm[0:M, :], scalar1=stt[0:M, sc2 + 1:sc2 + 2], scalar2=None, op0=ALU.mult),
           reads=[mkk, sk2], writes=[hk])
        pb2 = 4 + t % 2
        for k in range(8):
            op("tensor", lambda e, k=k, M=M, hn=hn, pb2=pb2: e.transpose(out=bankb(pb2)[:, k * 128:k * 128 + M], in_=hn[0:M, k * 128:(k + 1) * 128], identity=idb[0:M, 0:M]),
               reads=[hk, "idb"], writes=[("ps", pb2)])
        srcv = bankb(pb2).rearrange("p (k t) -> p k t", k=8)[:, :, 0:M]
        dstv = hTf[:, :, 1 + 128 * t:1 + 128 * (t + 1)] if t < 16 else hTf[:, :, 0:NX:NX - 1]
        op("vector", lambda e, srcv=srcv, dstv=dstv: e.tensor_copy(out=dstv, in_=srcv), reads=[("ps", pb2)], writes=["hTf"])
    dump("hTf", hTf, [128, 8, NX], BF16, ["hTf"])
    S.barrier()

    aT = Region(R1.lo, RC.hi).alloc([128, 22, OWN], BF16)
    A = Region(RY.lo, RY.hi)
    stg = [A.alloc([128, 8, 256], F32) for _ in range(2)]
    Wub = [A.alloc([128, 8, 256], BF16) for _ in range(2)]
    cwt = A.alloc([128, 44, 4], F32)
    ufl = A.alloc([128, 2], F32)
    A = Region(RT.lo + 32832, SB_BYTES)
    cgb = [A.alloc([128, OWN], F32) for _ in range(2)]
    cvb = [A.alloc([128, OWN], F32) for _ in range(2)]
    op("sync", lambda e: e.dma_start(out=cwt, in_=cwb), writes=["cwt"], dma=True)
    op("sync", lambda e: e.dma_start(out=ufl, in_=uflag), writes=["ufl"], dma=True)
    OB = [(410 * i, min(410, OWN - 410 * i)) for i in range(5)]
    pbi = 0

    def f1_weights(j):
        bi = j % 2
        sg, wb = stg[bi], Wub[bi]
        sgk, wbk = ("stg", bi), ("Wub", bi)
        op("gpsimd", lambda e: e.dma_start(out=sg[:, :, 0:128], in_=w_up[:, j * 128:(j + 1) * 128].rearrange("(k p) n -> p k n", p=128)), writes=[sgk], dma=True)
        op("gpsimd", lambda e: e.dma_start(out=sg[:, :, 128:256], in_=w_up[:, DFF + j * 128:DFF + (j + 1) * 128].rearrange("(k p) n -> p k n", p=128)), writes=[sgk], dma=True)
        for k in range(8):
            if k % 2 == 0:
                op("vector", lambda e, k=k: e.tensor_scalar(out=wb[:, k, :], in0=sg[:, k, :], scalar1=gpt[:, 21 + k:22 + k], scalar2=None, op0=ALU.mult),
                   reads=[sgk, "gpt"], writes=[("grp", wbk, k)])
            else:
                op("scalar", lambda e, k=k: e.activation(out=wb[:, k, :], in_=sg[:, k, :], func=AF.Identity, scale=gpt[:, 21 + k:22 + k]),
                   reads=[sgk, "gpt"], writes=[("grp", wbk, k)])

    f1_weights(0)
    for j in range(22):
        bi = j % 2
        wb = Wub[bi]
        wbk = ("Wub", bi)
        if j + 1 < 22:
            f1_weights(j + 1)
        for half in range(2):
            cb = (cgb if half == 0 else cvb)[bi]
            ff = j + 22 * half
            for bx, (o0, n) in enumerate(OB):
                ck = ("cg" if half == 0 else "cv", bi, bx)
                pb = pbi % 8
                pbi += 1
                for k in range(8):
                    op("tensor", lambda e, k=k: e.matmul(bank(pb, n + 2), lhsT=wb[:, k, half * 128:(half + 1) * 128], rhs=hTf[:, k, o0:o0 + n + 2], start=(k == 0), stop=(k == 7)),
                       reads=[wbk, "hTf"], writes=[("ps", pb)])
                if bx == 0:
                    op("vector", lambda e: e.tensor_tensor(out=bank(pb, 1), in0=bank(pb, 1), in1=ufl[:, 0:1], op=ALU.mult), reads=[("ps", pb), "ufl"], writes=[("ps", pb)])
                if bx == 4:
                    op("vector", lambda e: e.tensor_tensor(out=bank(pb, n + 2)[:, n + 1:n + 2], in0=bank(pb, n + 2)[:, n + 1:n + 2], in1=ufl[:, 1:2], op=ALU.mult), reads=[("ps", pb), "ufl"], writes=[("ps", pb)])
                op("scalar", lambda e: e.activation(out=cb[:, o0:o0 + n], in_=bank(pb, n + 2)[:, 1:n + 1], func=AF.Identity, bias=cwt[:, ff, 3:4], scale=cwt[:, ff, 1:2]),
                   reads=[("ps", pb), "cwt"], writes=[ck])
                op("vector", lambda e: e.scalar_tensor_tensor(out=cb[:, o0:o0 + n], in0=bank(pb, n + 2)[:, 0:n], scalar=cwt[:, ff, 0:1], in1=cb[:, o0:o0 + n], op0=ALU.mult, op1=ALU.add),
                   reads=[("ps", pb), "cwt", ck], writes=[ck])
                op("vector", lambda e: e.scalar_tensor_tensor(out=cb[:, o0:o0 + n], in0=bank(pb, n + 2)[:, 2:n + 2], scalar=cwt[:, ff, 2:3], in1=cb[:, o0:o0 + n], op0=ALU.mult, op1=ALU.add),
                   reads=[("ps", pb), "cwt", ck], writes=[ck])
        cg, cv = cgb[bi], cvb[bi]
        cgk = [("cg", bi, bx) for bx in range(5)]
        cvk = [("cv", bi, bx) for bx in range(5)]
        op("scalar", lambda e: e.activation(out=cg, in_=cg, func=AF.Gelu_apprx_tanh), reads=cgk, writes=cgk)
        op("vector", lambda e: e.tensor_tensor(out=aT[:, j, :], in0=cg, in1=cv, op=ALU.mult), reads=cgk + cvk, writes=[("aT", j)])
    dump("aT", aT, [128, 22, OWN], BF16, [("aT", j) for j in range(22)])
    S.barrier()

    A = Region(RY.lo, SB_BYTES)
    Wdn = A.alloc([128, 22, D], BF16)
    stage5 = A.alloc([128, 2, D], F32)
    gpo2 = A.alloc([128, D], F32)
    xm2 = [A.alloc([128, D], F32) for _ in range(2)]
    ot = [A.alloc([128, D], F32) for _ in range(2)]
    for c in range(11):
        load_weight(Wdn[:, 2 * c:2 * c + 2, :], w_down[256 * c:256 * (c + 1), :], 2, D, None, stage5, "Wdn")
    op("sync", lambda e: e.dma_start(out=gpo2, in_=gpost[:, 1, :]), writes=["gpo2"], dma=True)
    for t in range(16):
        bi = t % 2
        xk, ok = ("xm2", bi), ("ot", bi)
        op("sync", lambda e, t=t, bi=bi: e.dma_start(out=xm2[bi], in_=xmid[128 * t:128 * (t + 1), :]), reads=["xmid"], writes=[xk], dma=True)
        pb = 2 * (t % 2)
        for n2 in range(2):
            for j in range(22):
                op("tensor", lambda e, j=j, n2=n2, t=t, pb=pb: e.matmul(bank(pb + n2), lhsT=aT[:, j, 128 * t:128 * (t + 1)], rhs=Wdn[:, j, n2 * 512:(n2 + 1) * 512], start=(j == 0), stop=(j == 21)),
                   reads=[("aT", j), "Wdn"], writes=[("ps", pb + n2)])
        yv = ps[:, pb * 512:(pb + 2) * 512]
        sc = 144 + 2 * (t % 4)
        sk = ("stt7", t % 4)
        op("scalar", lambda e, yv=yv, sc=sc: e.activation(out=junk, in_=yv, func=AF.Square, accum_out=stt[:, sc:sc + 1]),
           reads=[("ps", pb), ("ps", pb + 1)], writes=["junk", sk])
        op("scalar", lambda e, sc=sc: e.activation(out=stt[:, sc:sc + 1], in_=stt[:, sc:sc + 1], func=AF.Sqrt, bias=EPS, scale=1.0 / D), reads=[sk], writes=[sk])
        op("vector", lambda e, sc=sc: e.reciprocal(out=stt[:, sc + 1:sc + 2], in_=stt[:, sc:sc + 1]), reads=[sk], writes=[sk])
        op("vector", lambda e, yv=yv, sc=sc, bi=bi: e.scalar_tensor_tensor(out=ot[bi], in0=yv, scalar=stt[:, sc + 1:sc + 2], in1=gpo2, op0=ALU.mult, op1=ALU.mult),
           reads=[("ps", pb), ("ps", pb + 1), sk, "gpo2"], writes=[ok])
        op("vector", lambda e, bi=bi: e.tensor_tensor(out=ot[bi], in0=ot[bi], in1=xm2[bi], op=ALU.add), reads=[ok, xk], writes=[ok])
        op("sync", lambda e, t=t, bi=bi: e.dma_start(out=yout[128 * t:128 * (t + 1), :], in_=ot[bi]), reads=[ok], dma=True)
    S.emit()
    return nc


_CACHE = {}


def _consts():
    if "c" in _CACHE:
        return _CACHE["c"]
    slopes = np.exp2(-8.0 * np.arange(1, 9, dtype=np.float32) / 8).astype(np.float32)
    k = np.arange(128)[:, None]
    q = np.arange(128)[None, :]
    masks = np.zeros((8, 128, 6, 128), np.float32)
    for h in range(8):
        for ri, r in enumerate((1, 4, 16)):
            d0 = k - 64 - q
            d1 = k + 64 - q
            masks[h, :, 2 * ri, :] = np.where(k >= q, np.exp(-slopes[h] * (np.abs(d0) * r).astype(np.float32)), 0.0)
            masks[h, :, 2 * ri + 1, :] = np.where(k <= q, np.exp(-slopes[h] * (np.abs(d1) * r).astype(np.float32)), 0.0)
    inv_freq = np.exp(-np.log(10000.0) * np.arange(0, 32, 2, dtype=np.float32) / 32).astype(np.float32)
    pos = np.arange(S_LEN, dtype=np.float32)
    ang = pos[:, None] * inv_freq[None, :]
    cosk = np.cos(ang).astype(np.float32).reshape(64, 128, 16).transpose(1, 0, 2)
    sink = np.sin(ang).astype(np.float32).reshape(64, 128, 16).transpose(1, 0, 2)
    rk = np.ascontiguousarray(np.stack([cosk, sink], axis=1))
    c = dict(masks=masks, inv_freq=inv_freq, rk=rk, ident=np.eye(128, dtype=np.float32))
    _CACHE["c"] = c
    return c


def _core_inputs(c, x, shared):
    cst = _consts()
    b, qc = c // 4, c % 4
    T0 = qc * OWN
    pos_w = T0 - OWN0 + np.arange(NW)
    valid = (pos_w >= 0) & (pos_w < S_LEN)
    xw = np.zeros((NW, D), np.float32)
    xw[valid] = x[b, pos_w[valid]]
    kvf = np.zeros((128, NVT), np.float32)
    for i, (r, es, nk) in enumerate(VLIST):
        kvf[:nk, i] = valid[es + r * np.arange(nk)].astype(np.float32)
    posq = (T0 - 1 + np.arange(NX)).astype(np.float32)
    ang = posq[None, :] * cst["inv_freq"][:, None]
    cq, sq = np.cos(ang).astype(np.float32), np.sin(ang).astype(np.float32)
    rq = np.ascontiguousarray(np.stack([np.concatenate([cq, cq], 0), np.concatenate([sq, sq], 0)], axis=1))
    uflag = np.zeros((128, 2), np.float32)
    uflag[:, 0] = 1.0 if T0 > 0 else 0.0
    uflag[:, 1] = 1.0 if T0 + OWN < S_LEN else 0.0
    d = dict(shared)
    d.update(xw=xw, xf=np.ascontiguousarray(x[b]), kvf=kvf, rq=rq, uflag=uflag, masks=cst["masks"], rk=cst["rk"], ident=cst["ident"])
    return d


def kernel(x, norm_mix_pre, w_in, q_lat_norm, w_uq, kv_lat_norm, w_ukv, out_norm_a, out_norm_b, w_o,
           norm_mix_post, norm_ffn_pre, w_up, conv_w, conv_b, w_down, norm_ffn_post):
    f = lambda a: np.ascontiguousarray(np.asarray(a, dtype=np.float32))
    x = f(x)
    gp = np.zeros((128, 32), np.float32)
    gp[:, 0:8] = f(norm_mix_pre)[0].reshape(8, 128).T
    gp[:, 8:11] = f(q_lat_norm)[0].reshape(3, 128).T
    gp[:, 11:13] = f(kv_lat_norm)[0].reshape(2, 128).T
    gp[:, 13:21] = np.concatenate([f(out_norm_a)[0], f(out_norm_b)[0]]).reshape(8, 128).T
    gp[:, 21:29] = f(norm_ffn_pre)[0].reshape(8, 128).T
    gpost = np.ascontiguousarray(np.broadcast_to(np.stack([f(norm_mix_post)[0], f(norm_ffn_post)[0]])[None], (128, 2, D)))
    cwb = np.zeros((128, 44, 4), np.float32)
    cwb[:, :, 0:3] = f(conv_w)[0].T.reshape(44, 128, 3).transpose(1, 0, 2)
    cwb[:, :, 3] = f(conv_b)[0].reshape(44, 128).T
    shared = dict(w_in=f(w_in)[0], w_uq=f(w_uq)[0], w_ukv=f(w_ukv)[0], w_o=f(w_o)[0], w_up=f(w_up)[0], w_down=f(w_down)[0],
                  gp=gp, gpost=gpost, cwb=cwb)
    if "nc" not in _CACHE:
        _CACHE["nc"] = build()
    nc = _CACHE["nc"]
    in_maps = [_core_inputs(c, x, shared) for c in range(8)]
    res = run_bass_kernel_spmd(nc, in_maps, core_ids=list(range(8)))
    out = np.zeros((2, S_LEN, D), np.float32)
    for c in range(8):
        b, qc = c // 4, c % 4
        out[b, qc * OWN:(qc + 1) * OWN] = res.results[c]["y"]
    return out
```

```python
import contextlib
import types
import numpy as np
import ml_dtypes
import concourse.bass as bass
import concourse.mybir as mybir
from concourse.bass_utils import run_bass_kernel_spmd

F32 = mybir.dt.float32
BF16 = mybir.dt.bfloat16
U8 = mybir.dt.uint8
AF = mybir.ActivationFunctionType
ALU = mybir.AluOpType

S_LEN = 8192
D = 1024
OWN = 2048
OWN0 = 1152
NW = 4352
NX = 2050
DFF = 2816
EPS = 1e-6
ENGS = ("sync", "scalar", "gpsimd", "vector", "tensor")
XBLK = [(i * 410, 410) for i in range(5)]


class Sched:
    def __init__(self, nc, ndma_sems=8):
        self.nc = nc
        self.ops = []
        self.ndma = ndma_sems

    @staticmethod
    def _freeze(fn):
        if fn.__closure__ is None:
            return fn
        cells = []
        for c in fn.__closure__:
            try:
                cells.append(types.CellType(c.cell_contents))
            except ValueError:
                cells.append(c)
        return types.FunctionType(fn.__code__, fn.__globals__, fn.__name__, fn.__defaults__, tuple(cells))

    def op(self, eng, fn, reads=(), writes=(), dma=False):
        fn = self._freeze(fn)
        self.ops.append(dict(eng=eng, fn=fn, reads=tuple(reads), writes=tuple(writes), dma=dma, bar=False))

    def barrier(self):
        self.ops.append(dict(eng=None, fn=None, reads=(), writes=(), dma=False, bar=True))

    def emit(self, final_wait_eng="sync"):
        nc = self.nc
        ops = self.ops
        n = len(ops)
        groups = {}
        for o in ops:
            for k in o["reads"] + o["writes"]:
                if isinstance(k, tuple) and len(k) == 3 and k[0] == "grp":
                    groups.setdefault(k[1], set()).add(k)

        def expand(keys):
            out = []
            for k in keys:
                out.append(k)
                if k in groups:
                    out.extend(groups[k])
            return out

        last_writer = {}
        readers = {}
        deps = [dict() for _ in range(n)]
        since_bar = []
        pending_bar = {}
        for i, o in enumerate(ops):
            if o["bar"]:
                lastc = {}
                dl = set()
                for j in since_bar:
                    if ops[j]["dma"]:
                        dl.add(j)
                    else:
                        lastc[ops[j]["eng"]] = j
                dl.update(lastc.values())
                for e in ENGS:
                    pending_bar[e] = set(dl) | pending_bar.get(e, set())
                since_bar = []
                continue
            d = deps[i]
            if o["eng"] in pending_bar:
                for j in pending_bar.pop(o["eng"]):
                    d[j] = True
            rk, wk = expand(o["reads"]), expand(o["writes"])
            for r in rk:
                if r in last_writer:
                    d[last_writer[r]] = True
            for w in wk:
                if w in last_writer:
                    d.setdefault(last_writer[w], False)
                for j in readers.get(w, ()):
                    d.setdefault(j, False)
            d.pop(i, None)
            for w in wk:
                last_writer[w] = i
                readers[w] = []
            for r in rk:
                if r not in wk:
                    readers.setdefault(r, []).append(i)
            since_bar.append(i)
        needed = set()
        red = [None] * n
        for i, o in enumerate(ops):
            if o["bar"]:
                continue
            per_eng = {}
            dl = []
            for j, is_raw in deps[i].items():
                pj = ops[j]
                if pj["dma"]:
                    dl.append(j)
                    continue
                if pj["eng"] == o["eng"] and not o["dma"] and (o["eng"] == "tensor" or not is_raw):
                    continue
                e = pj["eng"]
                if e not in per_eng or per_eng[e] < j:
                    per_eng[e] = j
            dl.extend(per_eng.values())
            red[i] = dl
            needed.update(dl)
        cnt = {e: 0 for e in ENGS}
        dcnt = {}
        sig = [None] * n
        dma_idx = {e: 0 for e in ENGS}
        for i, o in enumerate(ops):
            if o["bar"]:
                continue
            if o["dma"]:
                k = dma_idx[o["eng"]] % self.ndma
                dma_idx[o["eng"]] += 1
                key = ("dma", o["eng"], k)
                prev = dcnt.get(key, 0)
                dcnt[key] = prev + 16
                sig[i] = (key, prev + 16)
                o["dma_prev"] = (key, prev) if prev > 0 else None
            elif i in needed:
                cnt[o["eng"]] += 1
                sig[i] = (("eng", o["eng"]), cnt[o["eng"]])
        semkeys = sorted({s[0] for s in sig if s is not None}, key=str)
        stack = contextlib.ExitStack()
        sems = {}
        for sk in semkeys:
            sems[sk] = stack.enter_context(nc.semaphore("s_" + "_".join(str(x) for x in sk)))
        by_eng = {e: [i for i, o in enumerate(ops) if o["eng"] == e] for e in ENGS}
        dma_final = list(dcnt.items())

        def run(engname, eng):
            waited = {}
            for i in by_eng[engname]:
                o = ops[i]
                wl = [sig[j] for j in red[i]]
                if o["dma"] and o.get("dma_prev"):
                    wl.append(o["dma_prev"])
                for (sk, v) in wl:
                    if waited.get(sk, 0) >= v:
                        continue
                    eng.wait_ge(sems[sk], v)
                    waited[sk] = v
                ins = o["fn"](eng)
                if sig[i] is not None:
                    ins.then_inc(sems[sig[i][0]], 16 if o["dma"] else 1)
            if engname == final_wait_eng:
                for sk, v in dma_final:
                    if waited.get(sk, 0) < v:
                        eng.wait_ge(sems[sk], v)
                for e in ENGS:
                    if cnt[e] > 0 and waited.get(("eng", e), 0) < cnt[e]:
                        eng.wait_ge(sems[("eng", e)], cnt[e])

        with stack:
            with nc.Block() as block:
                @block.sync
                def _(e):
                    run("sync", e)

                @block.scalar
                def _(e):
                    run("scalar", e)

                @block.gpsimd
                def _(e):
                    run("gpsimd", e)

                @block.vector
                def _(e):
                    run("vector", e)

                @block.tensor
                def _(e):
                    run("tensor", e)


def dil_tables():
    vt = {}

    def vtile(r, e0, nk):
        key = (r, e0, nk)
        if key not in vt:
            vt[key] = len(vt)
        return vt[key]

    groups = {1: [], 4: [], 16: []}
    for r in (1, 4, 16):
        def blk(x0, N):
            eq0 = OWN0 - 1 + x0
            t0 = vtile(r, eq0 - 64 * r, 128)
            t1 = vtile(r, eq0 + 64 * r, 128 if N > 1 else 1)
            return (x0, N, t0, t1)
        if r == 1:
            for g in range(4):
                groups[r].append(("reg", g, [blk(1 + 128 * (4 * g + b), 128) for b in range(4)]))
        elif r == 4:
            for g in range(4):
                groups[r].append(("reg", g, [blk(1 + c + 512 * g, 128) for c in range(4)]))
        else:
            for g in range(4):
                groups[r].append(("reg", g, [blk(1 + 4 * g + c, 128) for c in range(4)]))
        groups[r].append(("halo", 0, [blk(0, 1), blk(NX - 1, 1)]))
    vlist = [None] * len(vt)
    for k, i in vt.items():
        vlist[i] = k
    return vlist, groups


VLIST, DGROUPS = dil_tables()
NVT = len(VLIST)


def build(dbg=False):
    nc = bass.Bass("TRN2", target_bir_lowering=False)

    def din(name, shape, dt=F32):
        return nc.dram_tensor(name, list(shape), dt, kind="ExternalInput").ap()

    xw = din("xw", [NW, D])
    xf = din("xf", [S_LEN, D])
    w_in = din("w_in", [D, 2208])
    w_uq = din("w_uq", [384, 768])
    w_ukv = din("w_ukv", [256, 1024])
    w_o = din("w_o", [D, D])
    w_up = din("w_up", [D, 2 * DFF])
    w_down = din("w_down", [DFF, D])
    gp = din("gp", [128, 32])
    gpost = din("gpost", [128, 2, D])
    cwb = din("cwb", [128, 44, 4])
    masks = din("masks", [8, 128, 6, 128])
    kvf = din("kvf", [128, NVT])
    rq = din("rq", [32, 2, NX])
    rk = din("rk", [128, 2, 64, 16])
    ident = din("ident", [128, 128])
    uflag = din("uflag", [128, 2])
    yout = nc.dram_tensor("y", [OWN, D], F32, kind="ExternalOutput").ap()
    xmid = nc.dram_tensor("xmid", [OWN, D], F32, kind="Internal").ap()
    dbg_out = {}

    SB_BYTES = 212000
    big = nc.alloc_sbuf_tensor("big", [128, SB_BYTES], U8).ap()
    ps = nc.alloc_psum_tensor("ps", [128, 4096], F32).ap()
    psb = ps.bitcast(BF16)

    class Region:
        def __init__(self, lo, hi):
            assert hi <= SB_BYTES and lo <= hi, (lo, hi)
            self.lo, self.hi, self.off = lo, hi, lo

        def alloc(self, shape, dt, p0=0):
            esz = 4 if dt == F32 else 2
            nb = int(np.prod(shape[1:])) * esz
            nb_al = (nb + 63) // 64 * 64
            assert self.off + nb_al <= self.hi, ("SBUF region overflow", self.lo, self.hi, self.off, nb_al)
            v = big[p0:p0 + shape[0], self.off:self.off + nb].bitcast(dt)
            self.off += nb_al
            if len(shape) == 3:
                v = v.rearrange("p (a b) -> p a b", a=shape[1])
            elif len(shape) == 4:
                v = v.rearrange("p (a b c) -> p a b c", a=shape[1], b=shape[2])
            return v

    RP = Region(0, 4096)
    R1 = Region(RP.hi, RP.hi + 86080)
    RC = Region(R1.hi, R1.hi + 12352)
    RY = Region(RC.hi, RC.hi + 16448 + 4096 + 16448)
    RT = Region(RY.hi, SB_BYTES)
    A = RP
    S = Sched(nc)
    op = S.op

    def dump(name, ap, shape, dt, keys):
        if not dbg:
            return
        import os
        sel = os.environ.get("DBGSEL", "")
        if sel and name not in sel.split(","):
            return
        d = nc.dram_tensor("dbg_" + name, list(shape), dt, kind="ExternalOutput").ap()
        op("sync", lambda e: e.dma_start(out=d, in_=ap), reads=keys, dma=True)

    def bank(b, n=512, p=128, p0=0):
        return ps[p0:p0 + p, b * 512:b * 512 + n]

    def bankb(b, n=1024, p=128, p0=0):
        return psb[p0:p0 + p, b * 1024:b * 1024 + n]

    idf = A.alloc([128, 128], F32)
    idb = A.alloc([128, 128], BF16)
    gpt = A.alloc([128, 32], F32)
    stt = A.alloc([128, 256], F32)
    junk = A.alloc([128, 1024], BF16)
    rec = A.alloc([128, 8], F32)
    op("sync", lambda e: e.dma_start(out=idf, in_=ident), writes=["idf"], dma=True)
    op("sync", lambda e: e.dma_start(out=gpt, in_=gp), writes=["gpt"], dma=True)
    op("vector", lambda e: e.tensor_copy(out=idb, in_=idf), reads=["idf"], writes=["idb"])

    def rstd_from_ss(ss_ap, out_ap, n, key):
        tmp = ss_ap
        op("scalar", lambda e: e.activation(out=tmp, in_=ss_ap, func=AF.Sqrt, bias=EPS, scale=1.0 / n),
           reads=[key], writes=[key])
        op("vector", lambda e: e.reciprocal(out=out_ap, in_=tmp), reads=[key], writes=[key])

    def load_weight(dst, src_ap, kch, ncols, gcol, stage, wkey, negate_cols=None):
        op("gpsimd", lambda e: e.dma_start(out=stage[:, 0:kch, 0:ncols], in_=src_ap.rearrange("(k p) n -> p k n", p=128)),
           writes=[("stage", id(stage))], dma=True)
        for k in range(kch):
            eng = "vector" if k % 2 == 0 else "scalar"
            if gcol is None:
                if eng == "vector":
                    op(eng, lambda e, k=k: e.tensor_copy(out=dst[:, k, :], in_=stage[:, k, 0:ncols]),
                       reads=[("stage", id(stage))], writes=[("grp", wkey, (id(dst), k))])
                else:
                    op(eng, lambda e, k=k: e.copy(out=dst[:, k, :], in_=stage[:, k, 0:ncols]),
                       reads=[("stage", id(stage))], writes=[("grp", wkey, (id(dst), k))])
            else:
                if eng == "vector":
                    op(eng, lambda e, k=k: e.tensor_scalar(out=dst[:, k, :], in0=stage[:, k, 0:ncols],
                                                          scalar1=gpt[:, gcol + k:gcol + k + 1], scalar2=None, op0=ALU.mult),
                       reads=[("stage", id(stage)), "gpt"], writes=[("grp", wkey, (id(dst), k))])
                else:
                    op(eng, lambda e, k=k: e.activation(out=dst[:, k, :], in_=stage[:, k, 0:ncols], func=AF.Identity,
                                                        scale=gpt[:, gcol + k:gcol + k + 1]),
                       reads=[("stage", id(stage)), "gpt"], writes=[("grp", wkey, (id(dst), k))])

    def norm_tiles_to_T(src_dram_rows, ntile, xbuf, xnbuf, key):
        op("sync", lambda e: e.dma_start(out=xbuf[:, 0:ntile, :], in_=src_dram_rows.rearrange("(t p) f -> p t f", p=128)),
           writes=[("x", key)], dma=True)
        for t in range(ntile):
            op("scalar", lambda e, t=t: e.activation(out=junk, in_=xbuf[:, t, :], func=AF.Square, accum_out=stt[:, t:t + 1]),
               reads=[("x", key)], writes=["junk", "stt"])
        rstd_from_ss(stt[:, 0:ntile], stt[:, 8:8 + ntile], D, "stt")
        op("vector", lambda e: e.tensor_tensor(out=xnbuf[:, 0:ntile, :], in0=xbuf[:, 0:ntile, :],
                                               in1=stt[:, 8:8 + ntile].unsqueeze(2).to_broadcast([128, ntile, D]), op=ALU.mult),
           reads=[("x", key), "stt"], writes=[("xn", key)])

    def transpose_tile(xn_tile, hT_dst, pbank, rkeys, wkey):
        for k in range(8):
            op("tensor", lambda e, k=k: e.transpose(out=bankb(pbank)[:, k * 128:(k + 1) * 128], in_=xn_tile[:, k * 128:(k + 1) * 128], identity=idb),
               reads=list(rkeys) + ["idb"], writes=[("ps", pbank)])
        op("vector", lambda e: e.tensor_copy(out=hT_dst, in_=bankb(pbank).rearrange("p (k t) -> p k t", k=8)),
           reads=[("ps", pbank)], writes=[wkey])

    KAT = R1.alloc([128, 4, NW], BF16)
    VAT = R1.alloc([128, 4, NW], BF16)
    QAT = R1.alloc([128, 4, NX], BF16)
    cqT = RC.alloc([128, 3, NX], BF16)
    A = Region(RC.hi, SB_BYTES)
    WA = A.alloc([128, 8, 1920], BF16)
    stage = A.alloc([128, 8, 480], F32)
    xbuf = [A.alloc([128, 2, D], F32) for _ in range(2)]
    xnb = [A.alloc([128, 2, D], BF16) for _ in range(2)]
    hTw = [A.alloc([128, 8, 256], BF16) for _ in range(2)]
    cqn = A.alloc([128, 384], BF16)
    for c in range(4):
        load_weight(WA[:, :, c * 480:(c + 1) * 480], w_in[:, c * 480:(c + 1) * 480], 8, 480, 0, stage, "WA")

    def cq_tile(cols_ap, M, xcol_ap_fn, hkey, pb):
        for k in range(8):
            la = cols_ap(k)
            op("tensor", lambda e, k=k, la=la: e.matmul(bank(pb, 384, M), lhsT=la, rhs=WA[:, k, 1536:1920], start=(k == 0), stop=(k == 7)),
               reads=[hkey, "WA"], writes=[("ps", pb)])
        op("scalar", lambda e: e.activation(out=junk[0:M, 0:384], in_=bank(pb, 384, M), func=AF.Square, accum_out=stt[0:M, 16:17]),
           reads=[("ps", pb)], writes=["junk", "stt2"])
        op("scalar", lambda e: e.activation(out=stt[0:M, 16:17], in_=stt[0:M, 16:17], func=AF.Sqrt, bias=EPS, scale=1.0 / 384),
           reads=["stt2"], writes=["stt2"])
        op("vector", lambda e: e.reciprocal(out=stt[0:M, 17:18], in_=stt[0:M, 16:17]), reads=["stt2"], writes=["stt2"])
        op("vector", lambda e: e.tensor_scalar(out=cqn[0:M, :], in0=bank(pb, 384, M), scalar1=stt[0:M, 17:18], scalar2=None, op0=ALU.mult),
           reads=[("ps", pb), "stt2"], writes=["cqn"])
        for j in range(3):
            op("tensor", lambda e, j=j: e.transpose(out=bankb(pb)[:, j * 128:j * 128 + M], in_=cqn[0:M, j * 128:(j + 1) * 128], identity=idb[0:M, 0:M]),
               reads=["cqn", "idb"], writes=[("ps", pb)])
        xc = xcol_ap_fn()
        op("vector", lambda e: e.tensor_copy(out=xc, in_=bankb(pb)[:, 0:384].rearrange("p (j t) -> p j t", j=3)[:, :, 0:M]),
           reads=[("ps", pb)], writes=["cqT"])

    NGW = NW // 256

    def w_stageA1(g):
        bi = g % 2
        e0 = g * 256
        norm_tiles_to_T(xw[e0:e0 + 256, :], 2, xbuf[bi], xnb[bi], ("w", bi))

    def w_stageA2(g):
        bi = g % 2
        for t in range(2):
            for k in range(8):
                op("tensor", lambda e, k=k: e.transpose(out=bankb(t)[:, k * 128:(k + 1) * 128], in_=xnb[bi][:, t, k * 128:(k + 1) * 128], identity=idb),
                   reads=[("xn", ("w", bi)), "idb"], writes=[("ps", t)])
        for t in range(2):
            op("vector" if t == 0 else "scalar",
               (lambda e: e.tensor_copy(out=hTw[bi][:, :, t * 128:(t + 1) * 128], in_=bankb(t).rearrange("p (k t) -> p k t", k=8))) if t == 0 else
               (lambda e: e.copy(out=hTw[bi][:, :, t * 128:(t + 1) * 128], in_=bankb(t).rearrange("p (k t) -> p k t", k=8))),
               reads=[("ps", t)], writes=[("grp", ("hTw", bi), t)])

    def w_stageB(g):
        bi = g % 2
        e0 = g * 256
        xlo, xhi = max(e0, OWN0 - 1), min(e0 + 256, OWN0 - 1 + NX)
        jobs = [("K", 512 + 128 * c, c) for c in range(4)] + [("V", 1024 + 128 * c, c) for c in range(4)]
        if xhi > xlo:
            jobs += [("Q", 128 * c, c) for c in range(4)]
        for ji, (kind, col0, c) in enumerate(jobs):
            pb = 2 + (ji % 4)
            for k in range(8):
                op("tensor", lambda e, k=k: e.matmul(bank(pb, 256), lhsT=WA[:, k, col0:col0 + 128], rhs=hTw[bi][:, k, :], start=(k == 0), stop=(k == 7)),
                   reads=[("hTw", bi), "WA"], writes=[("ps", pb)])
            if kind == "K":
                op("vector", lambda e: e.tensor_copy(out=KAT[:, c, e0:e0 + 256], in_=bank(pb, 256)), reads=[("ps", pb)], writes=["KAT"])
            elif kind == "V":
                op("scalar", lambda e: e.copy(out=VAT[:, c, e0:e0 + 256], in_=bank(pb, 256)), reads=[("ps", pb)], writes=["VAT"])
            else:
                op("vector", lambda e: e.tensor_copy(out=QAT[:, c, xlo - (OWN0 - 1):xhi - (OWN0 - 1)], in_=bank(pb, 256)[:, xlo - e0:xhi - e0]),
                   reads=[("ps", pb)], writes=["QAT"])
        for t in range(2):
            et = e0 + t * 128
            if OWN0 <= et < OWN0 + OWN:
                x0 = et - (OWN0 - 1)
                cq_tile(lambda k, t=t, bi=bi: hTw[bi][:, k, t * 128:(t + 1) * 128], 128, lambda x0=x0: cqT[:, :, x0:x0 + 128], ("hTw", bi), 6 + t)
            if et == OWN0 - 128:
                cq_tile(lambda k, t=t, bi=bi: hTw[bi][:, k, t * 128 + 127:t * 128 + 128], 1, lambda: cqT[:, :, 0:1], ("hTw", bi), 6 + t)
            if et == OWN0 + OWN:
                cq_tile(lambda k, t=t, bi=bi: hTw[bi][:, k, t * 128:t * 128 + 1], 1, lambda: cqT[:, :, NX - 1:NX], ("hTw", bi), 6 + t)

    for g in range(NGW + 2):
        if g < NGW:
            w_stageA1(g)
        if 0 <= g - 1 < NGW:
            w_stageA2(g - 1)
        if 0 <= g - 2 < NGW:
            w_stageB(g - 2)
    dump("KAT", KAT, [128, 4, NW], BF16, ["KAT"])
    dump("VAT", VAT, [128, 4, NW], BF16, ["VAT"])
    dump("QAT", QAT, [128, 4, NX], BF16, ["QAT"])
    dump("cqT", cqT, [128, 3, NX], BF16, ["cqT"])
    S.barrier()

    ynTa = RY.alloc([128, 4, NX], BF16)
    A = Region(RY.lo + 16448, SB_BYTES)
    mk = [A.alloc([128, 6, 128], F32) for _ in range(2)]
    kvft = A.alloc([128, NVT], F32)
    Vp = [A.alloc([128, NVT, 65], BF16) for _ in range(2)]
    Oacc = A.alloc([128, NX], F32)
    ya_tm = A.alloc([128, 17, 512], F32)
    Eb = [A.alloc([128, 1024], F32) for _ in range(2)]
    Pb = [A.alloc([128, 1024], BF16) for _ in range(2)]
    op("sync", lambda e: e.dma_start(out=kvft, in_=kvf), writes=["kvft"], dma=True)

    def oacc_to_tm(h, dst_tm, okey, Oacc):
        tiles = [(1 + 128 * m, 128, 1) for m in range(16)] + [(0, 2, NX - 1)]
        for g0 in range(0, 17, 4):
            tl = tiles[g0:g0 + 4]
            pb = 6 + (g0 // 4) % 2
            for i, (x0, M, step) in enumerate(tl):
                src = Oacc[0:65, x0:x0 + 128] if step == 1 else Oacc[0:65, 0:NX:NX - 1]
                op("tensor", lambda e, i=i, src=src, M=M, pb=pb: e.transpose(out=bank(pb)[0:M, i * 65:(i + 1) * 65], in_=src, identity=idf[0:65, 0:65]),
                   reads=[okey, "idf"], writes=[("ps", pb)])
            nt = len(tl)
            M = tl[0][1]
            pv = bank(pb)[0:M, 0:nt * 65].rearrange("p (t c) -> p t c", t=nt)
            op("vector", lambda e, pv=pv, nt=nt, M=M: e.reciprocal(out=rec[0:M, 0:nt], in_=pv[:, :, 64]), reads=[("ps", pb)], writes=["rec"])
            op("vector", lambda e, pv=pv, nt=nt, M=M, g0=g0: e.tensor_tensor(out=dst_tm[0:M, g0:g0 + nt, h * 64:(h + 1) * 64], in0=pv[:, :, 0:64],
                                                                         in1=rec[0:M, 0:nt].unsqueeze(2).to_broadcast([M, nt, 64]), op=ALU.mult),
               reads=[("ps", pb), "rec"], writes=["tm"])

    def tm_to_ynT(src_tm, dstT, nkey, Pb):
        for t in range(17):
            M = 128 if t < 16 else 2
            op("scalar", lambda e, t=t, M=M: e.activation(out=junk[0:M, 0:512], in_=src_tm[0:M, t, :], func=AF.Square, accum_out=stt[0:M, 32 + t:33 + t]),
               reads=["tm"], writes=["junk", "stt3"])
        rstd_from_ss(stt[:, 32:49], stt[:, 64:81], 512, "stt3")
        for t in range(17):
            M = 128 if t < 16 else 2
            pb = t % 2
            ynb = Pb[t % 2]
            op("vector", lambda e, t=t, M=M, ynb=ynb: e.tensor_scalar(out=ynb[0:M, 0:512], in0=src_tm[0:M, t, :], scalar1=stt[0:M, 64 + t:65 + t], scalar2=None, op0=ALU.mult),
               reads=["tm", "stt3"], writes=[("Pb", t % 2)])
            for j in range(4):
                op("tensor", lambda e, j=j, M=M, ynb=ynb, pb=pb: e.transpose(out=bankb(pb)[:, j * 128:j * 128 + M], in_=ynb[0:M, j * 128:(j + 1) * 128], identity=idb[0:M, 0:M]),
                   reads=[("Pb", t % 2), "idb"], writes=[("ps", pb)])
            srcv = bankb(pb)[:, 0:512].rearrange("p (j t) -> p j t", j=4)[:, :, 0:M]
            if t < 16:
                dstv = dstT[:, :, 1 + 128 * t:1 + 128 * (t + 1)]
            else:
                dstv = dstT[:, :, 0:NX:NX - 1]
            op("vector", lambda e, srcv=srcv, dstv=dstv: e.tensor_copy(out=dstv, in_=srcv), reads=[("ps", pb)], writes=[nkey])

    def d_prep(h):
        pr, hs = h // 2, (h % 2) * 64
        vb = Vp[h % 2]
        mb = mk[h % 2]
        op("sync", lambda e: e.dma_start(out=mb, in_=masks[h]), writes=[("mk", h % 2)], dma=True)
        for v0 in range(0, NVT, 8):
            vts = VLIST[v0:v0 + 8]
            pb = 4 + (v0 // 8) % 2
            for i, (r, es, nk) in enumerate(vts):
                op("tensor", lambda e, i=i, r=r, es=es, nk=nk: e.transpose(out=bankb(pb)[0:nk, i * 64:(i + 1) * 64],
                                                                        in_=VAT[hs:hs + 64, pr, es:es + r * (nk - 1) + 1:r], identity=idb[hs:hs + 64, hs:hs + 64]),
                   reads=["VAT", "idb"], writes=[("ps", pb)])
            i = 0
            while i < len(vts):
                nk = vts[i][2]
                j = i
                while j + 1 < len(vts) and vts[j + 1][2] == nk:
                    j += 1
                cnt_ = j - i + 1
                srcv = bankb(pb)[0:nk, i * 64:(j + 1) * 64].rearrange("p (t c) -> p t c", t=cnt_)
                op("vector", lambda e, srcv=srcv, i=i, cnt_=cnt_, nk=nk: e.tensor_tensor(
                    out=vb[0:nk, v0 + i:v0 + i + cnt_, 0:64], in0=srcv,
                    in1=kvft[0:nk, v0 + i:v0 + i + cnt_].unsqueeze(2).to_broadcast([nk, cnt_, 64]), op=ALU.mult),
                   reads=[("ps", pb), "kvft"], writes=[("Vp", h % 2)])
                i = j + 1
        op("gpsimd", lambda e: e.tensor_copy(out=vb[:, :, 64], in_=kvft), reads=["kvft"], writes=[("Vp", h % 2)])

    GL = [(ri, r, kind, g, blks) for ri, r in enumerate((1, 4, 16)) for (kind, g, blks) in DGROUPS[r]]
    gctr = [0]

    def d_scores(h, gi, grp):
        pr, hs = h // 2, (h % 2) * 64
        ri, r, kind, g, blks = grp
        sb = 2 * (gi % 2)
        for bi_, (x0, Nq, t0, t1) in enumerate(blks):
            qv = QAT[hs:hs + 64, pr, x0:x0 + r * (Nq - 1) + 1:r]
            for role, tix in ((0, t0), (1, t1)):
                (rr, es, nk) = VLIST[tix]
                op("tensor", lambda e, role=role, es=es, nk=nk, qv=qv, bi_=bi_, Nq=Nq: e.matmul(
                    bank(sb + role)[0:nk, bi_ * 128:bi_ * 128 + Nq], lhsT=KAT[hs:hs + 64, pr, es:es + r * (nk - 1) + 1:r], rhs=qv, start=True, stop=True),
                   reads=["KAT", "QAT"], writes=[("ps", sb + role)])

    def d_rest(h, gi, grp):
        ri, r, kind, g, blks = grp
        vb = Vp[h % 2]
        mb = mk[h % 2]
        sb = 2 * (gi % 2)
        eb = Eb[gi % 2]
        pbuf = Pb[gi % 2]
        ob = 6 + gi % 2
        ek, pk = ("Eb", gi % 2), ("Pbuf", gi % 2)
        if kind == "reg":
            sv = ps[:, sb * 512:(sb + 2) * 512]
            op("scalar", lambda e: e.activation(out=eb, in_=sv, func=AF.Exp, scale=0.125),
               reads=[("ps", sb), ("ps", sb + 1)], writes=[ek])
            op("vector", lambda e: e.tensor_tensor(
                out=pbuf.rearrange("p (r b q) -> p r b q", r=2, b=4), in0=eb.rearrange("p (r b q) -> p r b q", r=2, b=4),
                in1=mb[:, 2 * ri:2 * ri + 2, :].unsqueeze(2).to_broadcast([128, 2, 4, 128]), op=ALU.mult),
               reads=[ek, ("mk", h % 2)], writes=[pk])
        else:
            op("scalar", lambda e: e.activation(out=eb[:, 0:256:128], in_=bank(sb)[:, 0:256:128], func=AF.Exp, scale=0.125),
               reads=[("ps", sb)], writes=[ek])
            op("scalar", lambda e: e.activation(out=eb[0:1, 512:768:128], in_=bank(sb + 1)[0:1, 0:256:128], func=AF.Exp, scale=0.125),
               reads=[("ps", sb + 1)], writes=[ek])
            op("vector", lambda e: e.tensor_tensor(
                out=pbuf[:, 0:256:128], in0=eb[:, 0:256:128], in1=mb[:, 2 * ri, 0:1].to_broadcast([128, 2]), op=ALU.mult),
               reads=[ek, ("mk", h % 2)], writes=[pk])
            op("vector", lambda e: e.tensor_tensor(
                out=pbuf[0:1, 512:768:128], in0=eb[0:1, 512:768:128], in1=mb[0:1, 2 * ri + 1, 0:1].to_broadcast([1, 2]), op=ALU.mult),
               reads=[ek, ("mk", h % 2)], writes=[pk])
        for bi_, (x0, Nq, t0, t1) in enumerate(blks):
            for role, tix in ((0, t0), (1, t1)):
                nk = VLIST[tix][2]
                op("tensor", lambda e, role=role, tix=tix, nk=nk, bi_=bi_, Nq=Nq: e.matmul(
                    bank(ob)[0:65, bi_ * 128:bi_ * 128 + Nq], lhsT=vb[0:nk, tix, :], rhs=pbuf[0:nk, role * 512 + bi_ * 128:role * 512 + bi_ * 128 + Nq],
                    start=(role == 0), stop=(role == 1)),
                   reads=[pk, ("Vp", h % 2)], writes=[("ps", ob)])
        if kind == "reg":
            src = bank(ob)[0:65, :].rearrange("p (c q) -> p c q", c=4)
            if r == 1:
                dst = Oacc[0:65, 1 + 512 * g:1 + 512 * (g + 1)].rearrange("p (c q) -> p c q", c=4)
            elif r == 4:
                dst = Oacc[0:65, 1 + 512 * g:1 + 512 * (g + 1)].rearrange("p (q c) -> p c q", c=4)
            else:
                dst = Oacc[0:65, 1:1 + OWN].rearrange("p (q c) -> p c q", c=16)[:, 4 * g:4 * g + 4, :]
        else:
            src = bank(ob)[0:65, 0:256:128]
            dst = Oacc[0:65, 0:NX:NX - 1]
        if r == 1:
            op("vector", lambda e: e.tensor_copy(out=dst, in_=src), reads=[("ps", ob)], writes=["Oacc"])
        else:
            op("vector", lambda e: e.tensor_tensor(out=dst, in0=src, in1=dst, op=ALU.add), reads=[("ps", ob), "Oacc"], writes=["Oacc"])

    d_prep(0)
    for h in range(8):
        g0 = gctr[0]
        d_scores(h, g0, GL[0])
        if h + 1 < 8:
            d_prep(h + 1)
        for i, grp in enumerate(GL):
            if i + 1 < len(GL):
                d_scores(h, g0 + i + 1, GL[i + 1])
            d_rest(h, g0 + i, grp)
        gctr[0] += len(GL)
        oacc_to_tm(h, ya_tm, "Oacc", Oacc)
    tm_to_ynT(ya_tm, ynTa, "ynTa", Pb)
    dump("ynTa", ynTa, [128, 4, NX], BF16, ["ynTa"])
    dump("ya_tm", ya_tm, [128, 17, 512], F32, ["tm"])
    S.barrier()

    R1.off = R1.lo
    ckvT = R1.alloc([128, 2, S_LEN], BF16)
    KT = R1.alloc([96, S_LEN], BF16)
    R1s_lo = R1.off
    RY.off = RY.lo + 16448
    Wukv = RY.alloc([128, 2, 1024], BF16)
    A = Region(RY.off, SB_BYTES)
    Wkvl = A.alloc([128, 8, 288], BF16)
    stage2 = A.alloc([128, 8, 512], F32)
    xbuf = [A.alloc([128, 2, D], F32) for _ in range(2)]
    xnb = [A.alloc([128, 2, D], BF16) for _ in range(2)]
    hTk = [A.alloc([128, 8, 128], BF16) for _ in range(2)]
    ckvn = [A.alloc([128, 256], BF16) for _ in range(2)]
    kr_tm = A.alloc([128, 64, 32], F32)
    rkt = A.alloc([128, 2, 64, 16], F32)
    kr_pad = A.alloc([128, 64, 96], BF16)
    rt = [A.alloc([128, 64, 16], F32) for _ in range(2)]
    load_weight(Wkvl, w_in[:, 1920:2208], 8, 288, 0, stage2, "Wkvl")
    load_weight(Wukv[:, :, 0:512], w_ukv[:, 0:512], 2, 512, 11, stage2, "Wukv")
    load_weight(Wukv[:, :, 512:1024], w_ukv[:, 512:1024], 2, 512, 11, stage2, "Wukv")
    op("sync", lambda e: e.dma_start(out=rkt, in_=rk), writes=["rkt"], dma=True)
    op("gpsimd", lambda e: e.memset(kr_pad, 0.0), writes=["kr_pad"])
    def k_stageA1(g):
        bi = g % 2
        norm_tiles_to_T(xf[g * 256:(g + 1) * 256, :], 2, xbuf[bi], xnb[bi], ("k", bi))

    def k_stageA2(g):
        bi = g % 2
        for t in range(2):
            tt = g * 2 + t
            for k in range(8):
                op("tensor", lambda e, k=k: e.transpose(out=bankb(tt % 2)[:, k * 128:(k + 1) * 128], in_=xnb[bi][:, t, k * 128:(k + 1) * 128], identity=idb),
                   reads=[("xn", ("k", bi)), "idb"], writes=[("ps", tt % 2)])
        for t in range(2):
            tt = g * 2 + t
            hb = hTk[tt % 2]
            op("vector" if t == 0 else "scalar",
               (lambda e: e.tensor_copy(out=hb, in_=bankb(tt % 2).rearrange("p (k t) -> p k t", k=8))) if t == 0 else
               (lambda e: e.copy(out=hb, in_=bankb(tt % 2).rearrange("p (k t) -> p k t", k=8))),
               reads=[("ps", tt % 2)], writes=[("hTk", tt % 2)])
        for t in range(2):
            tt = g * 2 + t
            hb = hTk[tt % 2]
            pb = 2 + tt % 4
            for k in range(8):
                op("tensor", lambda e, k=k: e.matmul(bank(pb, 288), lhsT=hb[:, k, :], rhs=Wkvl[:, k, :], start=(k == 0), stop=(k == 7)),
                   reads=[("hTk", tt % 2), "Wkvl"], writes=[("ps", pb)])

    def k_stageB(g):
        for t in range(2):
            tt = g * 2 + t
            pb = 2 + tt % 4
            sc = 96 + (tt % 8)
            sk = ("stt4", tt % 8)
            op("scalar", lambda e: e.activation(out=junk[:, 0:256], in_=bank(pb, 256), func=AF.Square, accum_out=stt[:, sc:sc + 1]),
               reads=[("ps", pb)], writes=["junk", sk])
            op("scalar", lambda e: e.activation(out=stt[:, sc:sc + 1], in_=stt[:, sc:sc + 1], func=AF.Sqrt, bias=EPS, scale=1.0 / 256),
               reads=[sk], writes=[sk])
            op("vector", lambda e: e.reciprocal(out=stt[:, sc + 8:sc + 9], in_=stt[:, sc:sc + 1]), reads=[sk], writes=[sk])
            cb = ckvn[tt % 2]
            op("vector", lambda e: e.tensor_scalar(out=cb, in0=bank(pb, 256), scalar1=stt[:, sc + 8:sc + 9], scalar2=None, op0=ALU.mult),
               reads=[("ps", pb), sk], writes=[("ckvn", tt % 2)])
            op("vector", lambda e: e.tensor_copy(out=kr_tm[:, tt, :], in_=bank(pb, 288)[:, 256:288]), reads=[("ps", pb)], writes=["kr_tm"])
            pb2 = 6 + tt % 2
            for j in range(2):
                op("tensor", lambda e, j=j: e.transpose(out=bankb(pb2)[:, j * 128:(j + 1) * 128], in_=cb[:, j * 128:(j + 1) * 128], identity=idb),
                   reads=[("ckvn", tt % 2), "idb"], writes=[("ps", pb2)])
            op("scalar", lambda e: e.copy(out=ckvT[:, :, tt * 128:(tt + 1) * 128], in_=bankb(pb2)[:, 0:256].rearrange("p (j t) -> p j t", j=2)),
               reads=[("ps", pb2)], writes=["ckvT"])

    NGK = S_LEN // 256
    for g in range(NGK + 2):
        if g < NGK:
            k_stageA1(g)
        if 0 <= g - 1 < NGK:
            k_stageA2(g - 1)
        if 0 <= g - 2 < NGK:
            k_stageB(g - 2)
    x1, x2 = kr_tm[:, :, 0:16], kr_tm[:, :, 16:32]
    cosk, sink = rkt[:, 0], rkt[:, 1]
    op("vector", lambda e: e.tensor_tensor(out=rt[0], in0=x1, in1=cosk, op=ALU.mult), reads=["kr_tm", "rkt"], writes=["rt0"])
    op("vector", lambda e: e.tensor_tensor(out=rt[1], in0=x2, in1=sink, op=ALU.mult), reads=["kr_tm", "rkt"], writes=["rt1"])
    op("vector", lambda e: e.tensor_tensor(out=kr_pad[:, :, 64:80], in0=rt[0], in1=rt[1], op=ALU.subtract), reads=["rt0", "rt1", "kr_pad"], writes=["kr_pad"])
    op("vector", lambda e: e.tensor_tensor(out=rt[0], in0=x2, in1=cosk, op=ALU.mult), reads=["kr_tm", "rkt", "kr_pad"], writes=["rt0"])
    op("vector", lambda e: e.tensor_tensor(out=rt[1], in0=x1, in1=sink, op=ALU.mult), reads=["kr_tm", "rkt", "kr_pad"], writes=["rt1"])
    op("vector", lambda e: e.tensor_tensor(out=kr_pad[:, :, 80:96], in0=rt[0], in1=rt[1], op=ALU.add), reads=["rt0", "rt1", "kr_pad"], writes=["kr_pad"])
    for g8 in range(8):
        pb = 6 + g8 % 2
        for i in range(8):
            tt = g8 * 8 + i
            op("tensor", lambda e, i=i, tt=tt, pb=pb: e.transpose(out=bankb(pb)[0:96, i * 128:(i + 1) * 128], in_=kr_pad[:, tt, :], identity=idb),
               reads=["kr_pad", "idb"], writes=[("ps", pb)])
        op("vector", lambda e, pb=pb, g8=g8: e.tensor_copy(out=KT[64:96, g8 * 1024:(g8 + 1) * 1024], in_=bankb(pb)[64:96, :]),
           reads=[("ps", pb)], writes=["KTr"])
    dump("ckvT", ckvT, [128, 2, S_LEN], BF16, ["ckvT"])
    dump("KTr", KT[64:96, :], [32, S_LEN], BF16, ["KTr"])
    S.barrier()

    ynTb = RY.alloc([128, 4, NX], BF16)
    RT.off = RT.lo
    QBT = RT.alloc([96, 8, NX], BF16)
    R1s = Region(R1s_lo, R1.hi)
    Wuq = R1s.alloc([128, 3, 768], BF16)
    Wrot = R1s.alloc([128, 3, 8, 96], BF16)
    stage3 = R1s.alloc([128, 3, 768], F32)
    rqt = RT.alloc([96, 2, NX], F32)
    tq = [RT.alloc([96, 410], F32) for _ in range(2)]
    load_weight(Wuq, w_uq, 3, 768, 8, stage3, "Wuq")
    op("gpsimd", lambda e: e.memset(Wrot, 0.0), writes=["Wrot"])
    Wuq4 = Wuq.rearrange("p k (h c) -> p k h c", h=8)
    for k in range(3):
        op("gpsimd", lambda e, k=k: e.tensor_scalar(out=Wrot[:, k, :, 64:80], in0=Wuq4[:, k, :, 80:96], scalar1=-1.0, scalar2=None, op0=ALU.mult),
           reads=["Wuq", "Wrot"], writes=["Wrot"])
        op("gpsimd", lambda e, k=k: e.tensor_copy(out=Wrot[:, k, :, 80:96], in_=Wuq4[:, k, :, 64:80]), reads=["Wuq", "Wrot"], writes=["Wrot"])
    op("sync", lambda e: e.dma_start(out=rqt[64:96], in_=rq), writes=["rqt"], dma=True)
    qi = 0
    for h in range(8):
        for (c0, cn) in XBLK:
            pa, pr_ = 2 * (qi % 2), 1 + 2 * (qi % 2)
            tb_ = tq[qi % 2]
            tk = ("tq", qi % 2)
            qi += 1
            for k in range(3):
                op("tensor", lambda e, k=k, h=h, c0=c0, cn=cn, pa=pa: e.matmul(bank(pa, cn, 96), lhsT=Wuq[:, k, h * 96:(h + 1) * 96], rhs=cqT[:, k, c0:c0 + cn], start=(k == 0), stop=(k == 2)),
                   reads=["Wuq", "cqT"], writes=[("ps", pa)])
            for k in range(3):
                op("tensor", lambda e, k=k, h=h, c0=c0, cn=cn, pr_=pr_: e.matmul(bank(pr_, cn, 96), lhsT=Wrot[:, k, h, :], rhs=cqT[:, k, c0:c0 + cn], start=(k == 0), stop=(k == 2)),
                   reads=["Wrot", "cqT"], writes=[("ps", pr_)])
            op("scalar", lambda e, h=h, c0=c0, cn=cn, pa=pa: e.copy(out=QBT[0:64, h, c0:c0 + cn], in_=bank(pa, cn, 64)), reads=[("ps", pa)], writes=["QBT"])
            op("vector", lambda e, c0=c0, cn=cn, pa=pa, tb_=tb_: e.tensor_tensor(out=tb_[64:96, 0:cn], in0=bank(pa, cn, 32, 64), in1=rqt[64:96, 0, c0:c0 + cn], op=ALU.mult),
               reads=[("ps", pa), "rqt"], writes=[tk])
            op("vector", lambda e, h=h, c0=c0, cn=cn, pr_=pr_: e.tensor_tensor(out=QBT[64:96, h, c0:c0 + cn], in0=bank(pr_, cn, 32, 64), in1=rqt[64:96, 1, c0:c0 + cn], op=ALU.mult),
               reads=[("ps", pr_), "rqt"], writes=["QBT"])
            op("vector", lambda e, h=h, c0=c0, cn=cn, tb_=tb_: e.tensor_tensor(out=QBT[64:96, h, c0:c0 + cn], in0=QBT[64:96, h, c0:c0 + cn], in1=tb_[64:96, 0:cn], op=ALU.add),
               reads=[tk, "QBT"], writes=["QBT"])
    dump("QBT", QBT, [96, 8, NX], BF16, ["QBT"])
    S.barrier()
    RT.off = RT.lo + 32832
    yb_tm = RT.alloc([128, 17, 512], F32)
    ynb_tmp = [RT.alloc([128, 512], BF16) for _ in range(2)]
    R1s = Region(R1s_lo, R1.hi)
    Vb = [R1s.alloc([128, 64, 65], BF16) for _ in range(2)]
    PT = [R1s.alloc([128, 2, 512], BF16) for _ in range(3)]
    OaccB = R1s.alloc([128, NX], F32)
    for b_ in range(2):
        op("gpsimd", lambda e, b_=b_: e.memset(Vb[b_][:, :, 64], 1.0), writes=[("Vb", b_)])
    scale_b = 96.0 ** -0.5
    MB = [(1 + 512 * i, 512) for i in range(4)]
    QG = [(0, 2), (2, 2)]
    si = 0

    def m_prep_v(h):
        vb = Vb[h % 2]
        for k8 in range(8):
            pb = 6 + k8 % 2
            for i in range(8):
                kt = k8 * 8 + i
                for j in range(2):
                    op("tensor", lambda e, j=j, kt=kt, i=i: e.matmul(bank(pb)[:, i * 64:(i + 1) * 64], lhsT=ckvT[:, j, kt * 128:(kt + 1) * 128], rhs=Wukv[:, j, h * 128 + 64:h * 128 + 128], start=(j == 0), stop=(j == 1)),
                       reads=["Wukv", "ckvT"], writes=[("ps", pb)])
            op("vector", lambda e: e.tensor_copy(out=vb[:, k8 * 8:(k8 + 1) * 8, 0:64], in_=bank(pb).rearrange("p (t c) -> p t c", t=8)),
               reads=[("ps", pb)], writes=[("Vb", h % 2)])

    def m_prep_k(h):
        for nb_ in range(16):
            pb = 6 + nb_ % 2
            for j in range(2):
                op("tensor", lambda e, j=j: e.matmul(bank(pb, 512, 64), lhsT=Wukv[:, j, h * 128:h * 128 + 64], rhs=ckvT[:, j, nb_ * 512:(nb_ + 1) * 512], start=(j == 0), stop=(j == 1)),
                   reads=["Wukv", "ckvT"], writes=[("ps", pb)])
            op("vector", lambda e: e.tensor_copy(out=KT[0:64, nb_ * 512:(nb_ + 1) * 512], in_=bank(pb, 512, 64)),
               reads=[("ps", pb)], writes=["KTn"])

    m_prep_v(0)
    for h in range(8):
        vb = Vb[h % 2]
        m_prep_k(h)
        for gq, (b0, nbk) in enumerate(QG):
            ob = 4

            def emit_S(kt, si_):
                sbk = 2 * (si_ % 2)
                for bb in range(nbk):
                    c0, cn = MB[b0 + bb]
                    op("tensor", lambda e, bb=bb, c0=c0, cn=cn: e.matmul(bank(sbk + bb, cn), lhsT=KT[0:96, kt * 128:(kt + 1) * 128], rhs=QBT[0:96, h, c0:c0 + cn], start=True, stop=True),
                       reads=["KTn", "KTr", "QBT"], writes=[("ps", sbk + bb)])
            emit_S(0, si)
            for kt in range(64):
                sbk = 2 * (si % 2)
                pt = PT[si % 3]
                ptk = ("PT", si % 3)
                if kt + 1 < 64:
                    emit_S(kt + 1, si + 1)
                if gq == 0 and kt == 8 and h + 1 < 8:
                    m_prep_v(h + 1)
                sv = ps[:, sbk * 512:(sbk + nbk) * 512].rearrange("p (a b) -> p a b", a=nbk)
                op("scalar", lambda e: e.activation(out=pt[:, 0:nbk, :], in_=sv, func=AF.Exp, scale=scale_b),
                   reads=[("ps", sbk + bb) for bb in range(nbk)], writes=[ptk])
                for bb in range(nbk):
                    op("tensor", lambda e, bb=bb: e.matmul(bank(ob + bb, 512, 65), lhsT=vb[:, kt, :], rhs=pt[:, bb, :], start=(kt == 0), stop=(kt == 63)),
                       reads=[ptk, ("Vb", h % 2)], writes=[("ps", ob + bb)])
                si += 1
            ov = ps[0:65, ob * 512:(ob + nbk) * 512]
            c0 = MB[b0][0]
            op("vector", lambda e: e.tensor_copy(out=OaccB[0:65, c0:c0 + nbk * 512], in_=ov),
               reads=[("ps", ob + bb) for bb in range(nbk)], writes=["OaccB"])
        sbk = 2 * (si % 2)
        pt = PT[si % 3]
        ptk = ("PT", si % 3)
        qh = QBT[0:96, h, 0:NX:NX - 1]
        for kt in range(64):
            op("tensor", lambda e, kt=kt: e.matmul(bank(sbk)[:, 2 * kt:2 * kt + 2], lhsT=KT[0:96, kt * 128:(kt + 1) * 128], rhs=qh, start=True, stop=True),
               reads=["KTn", "KTr", "QBT"], writes=[("ps", sbk)])
        op("scalar", lambda e: e.activation(out=pt[:, 0, 0:128], in_=bank(sbk, 128), func=AF.Exp, scale=scale_b), reads=[("ps", sbk)], writes=[ptk])
        for kt in range(64):
            op("tensor", lambda e, kt=kt: e.matmul(bank(4, 2, 65), lhsT=vb[:, kt, :], rhs=pt[:, 0, 2 * kt:2 * kt + 2], start=(kt == 0), stop=(kt == 63)),
               reads=[ptk, ("Vb", h % 2)], writes=[("ps", 4)])
        si += 1
        op("vector", lambda e: e.tensor_copy(out=OaccB[0:65, 0:NX:NX - 1], in_=bank(4, 2, 65)), reads=[("ps", 4)], writes=["OaccB"])
        oacc_to_tm(h, yb_tm, "OaccB", OaccB)
    tm_to_ynT(yb_tm, ynTb, "ynTb", ynb_tmp)
    dump("ynTb", ynTb, [128, 4, NX], BF16, ["ynTb"])
    dump("yb_tm", yb_tm, [128, 17, 512], F32, ["tm"])
    S.barrier()

    RT.off = RT.lo
    hTf = RT.alloc([128, 8, NX], BF16)
    A = Region(R1.lo, RC.hi)
    Wo = A.alloc([128, 8, D], BF16)
    stage4 = A.alloc([128, 8, 256], F32)
    gpo = A.alloc([128, 2, D], F32)
    xt_ = [A.alloc([128, D], F32) for _ in range(2)]
    xm_ = [A.alloc([128, D], F32) for _ in range(2)]
    hn_ = [A.alloc([128, D], BF16) for _ in range(2)]
    for c in range(4):
        load_weight(Wo[:, :, c * 256:(c + 1) * 256], w_o[:, c * 256:(c + 1) * 256], 8, 256, 13, stage4, "Wo")
    op("sync", lambda e: e.dma_start(out=gpo, in_=gpost), writes=["gpo"], dma=True)
    def f0_p1(t):
        M = 128 if t < 16 else 2
        bi = t % 2
        xt, xm, hn = xt_[bi], xm_[bi], hn_[bi]
        xk, mkk, hk = ("xt", bi), ("xm", bi), ("hn", bi)
        if t < 16:
            op("sync", lambda e, t=t, xt=xt: e.dma_start(out=xt, in_=xw[OWN0 + 128 * t:OWN0 + 128 * (t + 1), :]), writes=[xk], dma=True)
            cols = lambda k, t=t: (ynTa if k < 4 else ynTb)[:, k % 4, 1 + 128 * t:1 + 128 * (t + 1)]
        else:
            op("sync", lambda e, xt=xt: e.dma_start(out=xt[0:1, :], in_=xw[OWN0 - 1:OWN0, :]), writes=[xk], dma=True)
            op("sync", lambda e, xt=xt: e.dma_start(out=xt[1:2, :], in_=xw[OWN0 + OWN:OWN0 + OWN + 1, :]), writes=[xk], dma=True)
            cols = lambda k: (ynTa if k < 4 else ynTb)[:, k % 4, 0:NX:NX - 1]
        pb = 2 * (t % 2)
        for n2 in range(2):
            for k in range(8):
                la = cols(k)
                op("tensor", lambda e, k=k, n2=n2, M=M, pb=pb, la=la: e.matmul(bank(pb + n2, 512, M), lhsT=la, rhs=Wo[:, k, n2 * 512:(n2 + 1) * 512], start=(k == 0), stop=(k == 7)),
                   reads=["ynTa", "ynTb", "Wo"], writes=[("ps", pb + n2)])

    def f0_p2(t):
        M = 128 if t < 16 else 2
        bi = t % 2
        xt, xm, hn = xt_[bi], xm_[bi], hn_[bi]
        xk, mkk, hk = ("xt", bi), ("xm", bi), ("hn", bi)
        pb = 2 * (t % 2)
        yv = ps[0:M, pb * 512:(pb + 2) * 512]
        sc = 128 + 2 * (t % 4)
        sk = ("stt5", t % 4)
        op("scalar", lambda e, yv=yv, M=M, sc=sc: e.activation(out=junk[0:M, :], in_=yv, func=AF.Square, accum_out=stt[0:M, sc:sc + 1]),
           reads=[("ps", pb), ("ps", pb + 1)], writes=["junk", sk])
        op("scalar", lambda e, M=M, sc=sc: e.activation(out=stt[0:M, sc:sc + 1], in_=stt[0:M, sc:sc + 1], func=AF.Sqrt, bias=EPS, scale=1.0 / D), reads=[sk], writes=[sk])
        op("vector", lambda e, M=M, sc=sc: e.reciprocal(out=stt[0:M, sc + 1:sc + 2], in_=stt[0:M, sc:sc + 1]), reads=[sk], writes=[sk])
        op("vector", lambda e, yv=yv, M=M, sc=sc, xm=xm: e.scalar_tensor_tensor(out=xm[0:M, :], in0=yv, scalar=stt[0:M, sc + 1:sc + 2], in1=gpo[0:M, 0, :], op0=ALU.mult, op1=ALU.mult),
           reads=[("ps", pb), ("ps", pb + 1), sk, "gpo"], writes=[mkk])
        op("vector", lambda e, M=M, xm=xm, xt=xt: e.tensor_tensor(out=xm[0:M, :], in0=xm[0:M, :], in1=xt[0:M, :], op=ALU.add), reads=[mkk, xk], writes=[mkk])
        if t < 16:
            op("sync", lambda e, t=t, xm=xm: e.dma_start(out=xmid[128 * t:128 * (t + 1), :], in_=xm), reads=[mkk], writes=["xmid"], dma=True)
        sc2 = 136 + 2 * (t % 4)
        sk2 = ("stt6", t % 4)
        op("scalar", lambda e, M=M, sc2=sc2, xm=xm: e.activation(out=junk[0:M, :], in_=xm[0:M, :], func=AF.Square, accum_out=stt[0:M, sc2:sc2 + 1]),
           reads=[mkk], writes=["junk", sk2])
        op("scalar", lambda e, M=M, sc2=sc2: e.activation(out=stt[0:M, sc2:sc2 + 1], in_=stt[0:M, sc2:sc2 + 1], func=AF.Sqrt, bias=EPS, scale=1.0 / D), reads=[sk2], writes=[sk2])
        op("vector", lambda e, M=M, sc2=sc2: e.reciprocal(out=stt[0:M, sc2 + 1:sc2 + 2], in_=stt[0:M, sc2:sc2 + 1]), reads=[sk2], writes=[sk2])
        op("vector", lambda e, M=M, sc2=sc2, xm=xm, hn=hn: e.tensor_scalar(out=hn[0:M, :], in0=xm[0:M, :], scalar1=stt[0:M, sc2 + 1:sc2 + 2], scalar2=None, op0=ALU.mult),
           reads=[mkk, sk2], writes=[hk])
        pb2 = 4 + t % 2
        for k in range(8):
            op("tensor", lambda e, k=k, M=M, hn=hn, pb2=pb2: e.transpose(out=bankb(pb2)[:, k * 128:k * 128 + M], in_=hn[0:M, k * 128:(k + 1) * 128], identity=idb[0:M, 0:M]),
               reads=[hk, "idb"], writes=[("ps", pb2)])
        srcv = bankb(pb2).rearrange("p (k t) -> p k t", k=8)[:, :, 0:M]
        dstv = hTf[:, :, 1 + 128 * t:1 + 128 * (t + 1)] if t < 16 else hTf[:, :, 0:NX:NX - 1]
        op("vector", lambda e, srcv=srcv, dstv=dstv: e.tensor_copy(out=dstv, in_=srcv), reads=[("ps", pb2)], writes=["hTf"])

    f0_p1(0)
    for t in range(17):
        if t + 1 < 17:
            f0_p1(t + 1)
        f0_p2(t)
    dump("hTf", hTf, [128, 8, NX], BF16, ["hTf"])
    S.barrier()

    aT = Region(R1.lo, RC.hi).alloc([128, 22, OWN], BF16)
    A = Region(RY.lo, RY.hi)
    stg = [A.alloc([128, 8, 256], F32) for _ in range(2)]
    Wub = [A.alloc([128, 8, 256], BF16) for _ in range(2)]
    cwt = A.alloc([128, 44, 4], F32)
    ufl = A.alloc([128, 2], F32)
    A = Region(RT.lo + 32832, SB_BYTES)
    cgb = [A.alloc([128, OWN], F32) for _ in range(2)]
    cvb = [A.alloc([128, OWN], F32) for _ in range(2)]
    op("sync", lambda e: e.dma_start(out=cwt, in_=cwb), writes=["cwt"], dma=True)
    op("sync", lambda e: e.dma_start(out=ufl, in_=uflag), writes=["ufl"], dma=True)
    OB = [(410 * i, min(410, OWN - 410 * i)) for i in range(5)]
    pbi = 0

    def f1_weights(j):
        bi = j % 2
        sg, wb = stg[bi], Wub[bi]
        sgk, wbk = ("stg", bi), ("Wub", bi)
        op("gpsimd", lambda e: e.dma_start(out=sg[:, :, 0:128], in_=w_up[:, j * 128:(j + 1) * 128].rearrange("(k p) n -> p k n", p=128)), writes=[sgk], dma=True)
        op("gpsimd", lambda e: e.dma_start(out=sg[:, :, 128:256], in_=w_up[:, DFF + j * 128:DFF + (j + 1) * 128].rearrange("(k p) n -> p k n", p=128)), writes=[sgk], dma=True)
        for k in range(8):
            if k % 2 == 0:
                op("vector", lambda e, k=k: e.tensor_scalar(out=wb[:, k, :], in0=sg[:, k, :], scalar1=gpt[:, 21 + k:22 + k], scalar2=None, op0=ALU.mult),
                   reads=[sgk, "gpt"], writes=[("grp", wbk, k)])
            else:
                op("scalar", lambda e, k=k: e.activation(out=wb[:, k, :], in_=sg[:, k, :], func=AF.Identity, scale=gpt[:, 21 + k:22 + k]),
                   reads=[sgk, "gpt"], writes=[("grp", wbk, k)])

    f1_weights(0)
    for j in range(22):
        bi = j % 2
        wb = Wub[bi]
        wbk = ("Wub", bi)
        if j + 1 < 22:
            f1_weights(j + 1)
        for half in range(2):
            cb = (cgb if half == 0 else cvb)[bi]
            ff = j + 22 * half
            for bx, (o0, n) in enumerate(OB):
                ck = ("cg" if half == 0 else "cv", bi, bx)
                pb = pbi % 8
                pbi += 1
                for k in range(8):
                    op("tensor", lambda e, k=k: e.matmul(bank(pb, n + 2), lhsT=wb[:, k, half * 128:(half + 1) * 128], rhs=hTf[:, k, o0:o0 + n + 2], start=(k == 0), stop=(k == 7)),
                       reads=[wbk, "hTf"], writes=[("ps", pb)])
                if bx == 0:
                    op("vector", lambda e: e.tensor_tensor(out=bank(pb, 1), in0=bank(pb, 1), in1=ufl[:, 0:1], op=ALU.mult), reads=[("ps", pb), "ufl"], writes=[("ps", pb)])
                if bx == 4:
                    op("vector", lambda e: e.tensor_tensor(out=bank(pb, n + 2)[:, n + 1:n + 2], in0=bank(pb, n + 2)[:, n + 1:n + 2], in1=ufl[:, 1:2], op=ALU.mult), reads=[("ps", pb), "ufl"], writes=[("ps", pb)])
                op("scalar", lambda e: e.activation(out=cb[:, o0:o0 + n], in_=bank(pb, n + 2)[:, 1:n + 1], func=AF.Identity, bias=cwt[:, ff, 3:4], scale=cwt[:, ff, 1:2]),
                   reads=[("ps", pb), "cwt"], writes=[ck])
                op("vector", lambda e: e.scalar_tensor_tensor(out=cb[:, o0:o0 + n], in0=bank(pb, n + 2)[:, 0:n], scalar=cwt[:, ff, 0:1], in1=cb[:, o0:o0 + n], op0=ALU.mult, op1=ALU.add),
                   reads=[("ps", pb), "cwt", ck], writes=[ck])
                op("vector", lambda e: e.scalar_tensor_tensor(out=cb[:, o0:o0 + n], in0=bank(pb, n + 2)[:, 2:n + 2], scalar=cwt[:, ff, 2:3], in1=cb[:, o0:o0 + n], op0=ALU.mult, op1=ALU.add),
                   reads=[("ps", pb), "cwt", ck], writes=[ck])
        cg, cv = cgb[bi], cvb[bi]
        cgk = [("cg", bi, bx) for bx in range(5)]
        cvk = [("cv", bi, bx) for bx in range(5)]
        op("scalar", lambda e: e.activation(out=cg, in_=cg, func=AF.Gelu_apprx_tanh), reads=cgk, writes=cgk)
        op("vector", lambda e: e.tensor_tensor(out=aT[:, j, :], in0=cg, in1=cv, op=ALU.mult), reads=cgk + cvk, writes=[("aT", j)])
    dump("aT", aT, [128, 22, OWN], BF16, [("aT", j) for j in range(22)])
    S.barrier()

    A = Region(RY.lo, SB_BYTES)
    Wdn = A.alloc([128, 22, D], BF16)
    stage5 = A.alloc([128, 2, D], F32)
    gpo2 = A.alloc([128, D], F32)
    xm2 = [A.alloc([128, D], F32) for _ in range(2)]
    ot = [A.alloc([128, D], F32) for _ in range(2)]
    for c in range(11):
        load_weight(Wdn[:, 2 * c:2 * c + 2, :], w_down[256 * c:256 * (c + 1), :], 2, D, None, stage5, "Wdn")
    op("sync", lambda e: e.dma_start(out=gpo2, in_=gpost[:, 1, :]), writes=["gpo2"], dma=True)
    for t in range(16):
        bi = t % 2
        xk, ok = ("xm2", bi), ("ot", bi)
        op("sync", lambda e, t=t, bi=bi: e.dma_start(out=xm2[bi], in_=xmid[128 * t:128 * (t + 1), :]), reads=["xmid"], writes=[xk], dma=True)
        pb = 2 * (t % 2)
        for n2 in range(2):
            for j in range(22):
                op("tensor", lambda e, j=j, n2=n2, t=t, pb=pb: e.matmul(bank(pb + n2), lhsT=aT[:, j, 128 * t:128 * (t + 1)], rhs=Wdn[:, j, n2 * 512:(n2 + 1) * 512], start=(j == 0), stop=(j == 21)),
                   reads=[("aT", j), "Wdn"], writes=[("ps", pb + n2)])
        yv = ps[:, pb * 512:(pb + 2) * 512]
        sc = 144 + 2 * (t % 4)
        sk = ("stt7", t % 4)
        op("scalar", lambda e, yv=yv, sc=sc: e.activation(out=junk, in_=yv, func=AF.Square, accum_out=stt[:, sc:sc + 1]),
           reads=[("ps", pb), ("ps", pb + 1)], writes=["junk", sk])
        op("scalar", lambda e, sc=sc: e.activation(out=stt[:, sc:sc + 1], in_=stt[:, sc:sc + 1], func=AF.Sqrt, bias=EPS, scale=1.0 / D), reads=[sk], writes=[sk])
        op("vector", lambda e, sc=sc: e.reciprocal(out=stt[:, sc + 1:sc + 2], in_=stt[:, sc:sc + 1]), reads=[sk], writes=[sk])
        op("vector", lambda e, yv=yv, sc=sc, bi=bi: e.scalar_tensor_tensor(out=ot[bi], in0=yv, scalar=stt[:, sc + 1:sc + 2], in1=gpo2, op0=ALU.mult, op1=ALU.mult),
           reads=[("ps", pb), ("ps", pb + 1), sk, "gpo2"], writes=[ok])
        op("vector", lambda e, bi=bi: e.tensor_tensor(out=ot[bi], in0=ot[bi], in1=xm2[bi], op=ALU.add), reads=[ok, xk], writes=[ok])
        op("sync", lambda e, t=t, bi=bi: e.dma_start(out=yout[128 * t:128 * (t + 1), :], in_=ot[bi]), reads=[ok], dma=True)
    S.emit()
    return nc


_CACHE = {}


def _consts():
    if "c" in _CACHE:
        return _CACHE["c"]
    slopes = np.exp2(-8.0 * np.arange(1, 9, dtype=np.float32) / 8).astype(np.float32)
    k = np.arange(128)[:, None]
    q = np.arange(128)[None, :]
    masks = np.zeros((8, 128, 6, 128), np.float32)
    for h in range(8):
        for ri, r in enumerate((1, 4, 16)):
            d0 = k - 64 - q
            d1 = k + 64 - q
            masks[h, :, 2 * ri, :] = np.where(k >= q, np.exp(-slopes[h] * (np.abs(d0) * r).astype(np.float32)), 0.0)
            masks[h, :, 2 * ri + 1, :] = np.where(k <= q, np.exp(-slopes[h] * (np.abs(d1) * r).astype(np.float32)), 0.0)
    inv_freq = np.exp(-np.log(10000.0) * np.arange(0, 32, 2, dtype=np.float32) / 32).astype(np.float32)
    pos = np.arange(S_LEN, dtype=np.float32)
    ang = pos[:, None] * inv_freq[None, :]
    cosk = np.cos(ang).astype(np.float32).reshape(64, 128, 16).transpose(1, 0, 2)
    sink = np.sin(ang).astype(np.float32).reshape(64, 128, 16).transpose(1, 0, 2)
    rk = np.ascontiguousarray(np.stack([cosk, sink], axis=1))
    c = dict(masks=masks, inv_freq=inv_freq, rk=rk, ident=np.eye(128, dtype=np.float32))
    _CACHE["c"] = c
    return c


def _core_inputs(c, x, shared):
    cst = _consts()
    b, qc = c // 4, c % 4
    T0 = qc * OWN
    pos_w = T0 - OWN0 + np.arange(NW)
    valid = (pos_w >= 0) & (pos_w < S_LEN)
    xw = np.zeros((NW, D), np.float32)
    xw[valid] = x[b, pos_w[valid]]
    kvf = np.zeros((128, NVT), np.float32)
    for i, (r, es, nk) in enumerate(VLIST):
        kvf[:nk, i] = valid[es + r * np.arange(nk)].astype(np.float32)
    posq = (T0 - 1 + np.arange(NX)).astype(np.float32)
    ang = posq[None, :] * cst["inv_freq"][:, None]
    cq, sq = np.cos(ang).astype(np.float32), np.sin(ang).astype(np.float32)
    rq = np.ascontiguousarray(np.stack([np.concatenate([cq, cq], 0), np.concatenate([sq, sq], 0)], axis=1))
    uflag = np.zeros((128, 2), np.float32)
    uflag[:, 0] = 1.0 if T0 > 0 else 0.0
    uflag[:, 1] = 1.0 if T0 + OWN < S_LEN else 0.0
    d = dict(shared)
    d.update(xw=xw, xf=np.ascontiguousarray(x[b]), kvf=kvf, rq=rq, uflag=uflag, masks=cst["masks"], rk=cst["rk"], ident=cst["ident"])
    return d


def kernel(x, norm_mix_pre, w_in, q_lat_norm, w_uq, kv_lat_norm, w_ukv, out_norm_a, out_norm_b, w_o,
           norm_mix_post, norm_ffn_pre, w_up, conv_w, conv_b, w_down, norm_ffn_post):
    f = lambda a: np.ascontiguousarray(np.asarray(a, dtype=np.float32))
    x = f(x)
    gp = np.zeros((128, 32), np.float32)
    gp[:, 0:8] = f(norm_mix_pre)[0].reshape(8, 128).T
    gp[:, 8:11] = f(q_lat_norm)[0].reshape(3, 128).T
    gp[:, 11:13] = f(kv_lat_norm)[0].reshape(2, 128).T
    gp[:, 13:21] = np.concatenate([f(out_norm_a)[0], f(out_norm_b)[0]]).reshape(8, 128).T
    gp[:, 21:29] = f(norm_ffn_pre)[0].reshape(8, 128).T
    gpost = np.ascontiguousarray(np.broadcast_to(np.stack([f(norm_mix_post)[0], f(norm_ffn_post)[0]])[None], (128, 2, D)))
    cwb = np.zeros((128, 44, 4), np.float32)
    cwb[:, :, 0:3] = f(conv_w)[0].T.reshape(44, 128, 3).transpose(1, 0, 2)
    cwb[:, :, 3] = f(conv_b)[0].reshape(44, 128).T
    shared = dict(w_in=f(w_in)[0], w_uq=f(w_uq)[0], w_ukv=f(w_ukv)[0], w_o=f(w_o)[0], w_up=f(w_up)[0], w_down=f(w_down)[0],
                  gp=gp, gpost=gpost, cwb=cwb)
    if "nc" not in _CACHE:
        _CACHE["nc"] = build()
    nc = _CACHE["nc"]
    in_maps = [_core_inputs(c, x, shared) for c in range(8)]
    res = run_bass_kernel_spmd(nc, in_maps, core_ids=list(range(8)))
    out = np.zeros((2, S_LEN, D), np.float32)
    for c in range(8):
        b, qc = c // 4, c % 4
        out[b, qc * OWN:(qc + 1) * OWN] = res.results[c]["y"]
    return out
```

```python
import contextlib
import types
import numpy as np
import ml_dtypes
import concourse.bass as bass
import concourse.mybir as mybir
from concourse.bass_utils import run_bass_kernel_spmd

F32 = mybir.dt.float32
BF16 = mybir.dt.bfloat16
U8 = mybir.dt.uint8
AF = mybir.ActivationFunctionType
ALU = mybir.AluOpType

S_LEN = 8192
D = 1024
OWN = 2048
OWN0 = 1152
NW = 4352
NX = 2050
DFF = 2816
EPS = 1e-6
ENGS = ("sync", "scalar", "gpsimd", "vector", "tensor")
XBLK = [(i * 410, 410) for i in range(5)]


class Sched:
    def __init__(self, nc, ndma_sems=8):
        self.nc = nc
        self.ops = []
        self.ndma = ndma_sems

    @staticmethod
    def _freeze(fn):
        if fn.__closure__ is None:
            return fn
        cells = []
        for c in fn.__closure__:
            try:
                cells.append(types.CellType(c.cell_contents))
            except ValueError:
                cells.append(c)
        return types.FunctionType(fn.__code__, fn.__globals__, fn.__name__, fn.__defaults__, tuple(cells))

    def op(self, eng, fn, reads=(), writes=(), dma=False):
        fn = self._freeze(fn)
        self.ops.append(dict(eng=eng, fn=fn, reads=tuple(reads), writes=tuple(writes), dma=dma, bar=False))

    def barrier(self):
        self.ops.append(dict(eng=None, fn=None, reads=(), writes=(), dma=False, bar=True))

    def emit(self, final_wait_eng="sync"):
        nc = self.nc
        ops = self.ops
        n = len(ops)
        groups = {}
        for o in ops:
            for k in o["reads"] + o["writes"]:
                if isinstance(k, tuple) and len(k) == 3 and k[0] == "grp":
                    groups.setdefault(k[1], set()).add(k)

        def expand(keys):
            out = []
            for k in keys:
                out.append(k)
                if k in groups:
                    out.extend(groups[k])
            return out

        last_writer = {}
        readers = {}
        deps = [dict() for _ in range(n)]
        since_bar = []
        pending_bar = {}
        for i, o in enumerate(ops):
            if o["bar"]:
                lastc = {}
                dl = set()
                for j in since_bar:
                    if ops[j]["dma"]:
                        dl.add(j)
                    else:
                        lastc[ops[j]["eng"]] = j
                dl.update(lastc.values())
                for e in ENGS:
                    pending_bar[e] = set(dl) | pending_bar.get(e, set())
                since_bar = []
                continue
            d = deps[i]
            if o["eng"] in pending_bar:
                for j in pending_bar.pop(o["eng"]):
                    d[j] = True
            rk, wk = expand(o["reads"]), expand(o["writes"])
            for r in rk:
                if r in last_writer:
                    d[last_writer[r]] = True
            for w in wk:
                if w in last_writer:
                    d.setdefault(last_writer[w], False)
                for j in readers.get(w, ()):
                    d.setdefault(j, False)
            d.pop(i, None)
            for w in wk:
                last_writer[w] = i
                readers[w] = []
            for r in rk:
                if r not in wk:
                    readers.setdefault(r, []).append(i)
            since_bar.append(i)
        needed = set()
        red = [None] * n
        for i, o in enumerate(ops):
            if o["bar"]:
                continue
            per_eng = {}
            dl = []
            for j, is_raw in deps[i].items():
                pj = ops[j]
                if pj["dma"]:
                    dl.append(j)
                    continue
                if pj["eng"] == o["eng"] and not o["dma"] and (o["eng"] == "tensor" or not is_raw):
                    continue
                e = pj["eng"]
                if e not in per_eng or per_eng[e] < j:
                    per_eng[e] = j
            dl.extend(per_eng.values())
            red[i] = dl
            needed.update(dl)
        cnt = {e: 0 for e in ENGS}
        dcnt = {}
        sig = [None] * n
        dma_idx = {e: 0 for e in ENGS}
        for i, o in enumerate(ops):
            if o["bar"]:
                continue
            if o["dma"]:
                k = dma_idx[o["eng"]] % self.ndma
                dma_idx[o["eng"]] += 1
                key = ("dma", o["eng"], k)
                prev = dcnt.get(key, 0)
                dcnt[key] = prev + 16
                sig[i] = (key, prev + 16)
                o["dma_prev"] = (key, prev) if prev > 0 else None
            elif i in needed:
                cnt[o["eng"]] += 1
                sig[i] = (("eng", o["eng"]), cnt[o["eng"]])
        semkeys = sorted({s[0] for s in sig if s is not None}, key=str)
        stack = contextlib.ExitStack()
        sems = {}
        for sk in semkeys:
            sems[sk] = stack.enter_context(nc.semaphore("s_" + "_".join(str(x) for x in sk)))
        by_eng = {e: [i for i, o in enumerate(ops) if o["eng"] == e] for e in ENGS}
        dma_final = list(dcnt.items())

        def run(engname, eng):
            waited = {}
            for i in by_eng[engname]:
                o = ops[i]
                wl = [sig[j] for j in red[i]]
                if o["dma"] and o.get("dma_prev"):
                    wl.append(o["dma_prev"])
                for (sk, v) in wl:
                    if waited.get(sk, 0) >= v:
                        continue
                    eng.wait_ge(sems[sk], v)
                    waited[sk] = v
                ins = o["fn"](eng)
                if sig[i] is not None:
                    ins.then_inc(sems[sig[i][0]], 16 if o["dma"] else 1)
            if engname == final_wait_eng:
                for sk, v in dma_final:
                    if waited.get(sk, 0) < v:
                        eng.wait_ge(sems[sk], v)
                for e in ENGS:
                    if cnt[e] > 0 and waited.get(("eng", e), 0) < cnt[e]:
                        eng.wait_ge(sems[("eng", e)], cnt[e])

        with stack:
            with nc.Block() as block:
                @block.sync
                def _(e):
                    run("sync", e)

                @block.scalar
                def _(e):
                    run("scalar", e)

                @block.gpsimd
                def _(e):
                    run("gpsimd", e)

                @block.vector
                def _(e):
                    run("vector", e)

                @block.tensor
                def _(e):
                    run("tensor", e)


def dil_tables():
    vt = {}

    def vtile(r, e0, nk):
        key = (r, e0, nk)
        if key not in vt:
            vt[key] = len(vt)
        return vt[key]

    groups = {1: [], 4: [], 16: []}
    for r in (1, 4, 16):
        def blk(x0, N):
            eq0 = OWN0 - 1 + x0
            t0 = vtile(r, eq0 - 64 * r, 128)
            t1 = vtile(r, eq0 + 64 * r, 128 if N > 1 else 1)
            return (x0, N, t0, t1)
        if r == 1:
            for g in range(4):
                groups[r].append(("reg", g, [blk(1 + 128 * (4 * g + b), 128) for b in range(4)]))
        elif r == 4:
            for g in range(4):
                groups[r].append(("reg", g, [blk(1 + c + 512 * g, 128) for c in range(4)]))
        else:
            for g in range(4):
                groups[r].append(("reg", g, [blk(1 + 4 * g + c, 128) for c in range(4)]))
        groups[r].append(("halo", 0, [blk(0, 1), blk(NX - 1, 1)]))
    vlist = [None] * len(vt)
    for k, i in vt.items():
        vlist[i] = k
    return vlist, groups


VLIST, DGROUPS = dil_tables()
NVT = len(VLIST)


def build(dbg=False):
    nc = bass.Bass("TRN2", target_bir_lowering=False)

    def din(name, shape, dt=F32):
        return nc.dram_tensor(name, list(shape), dt, kind="ExternalInput").ap()

    xw = din("xw", [NW, D])
    xf = din("xf", [S_LEN, D])
    w_in = din("w_in", [D, 2208])
    w_uq = din("w_uq", [384, 768])
    w_ukv = din("w_ukv", [256, 1024])
    w_o = din("w_o", [D, D])
    w_up = din("w_up", [D, 2 * DFF])
    w_down = din("w_down", [DFF, D])
    gp = din("gp", [128, 32])
    gpost = din("gpost", [128, 2, D])
    cwb = din("cwb", [128, 44, 4])
    masks = din("masks", [8, 128, 6, 128])
    kvf = din("kvf", [128, NVT])
    rq = din("rq", [32, 2, NX])
    rk = din("rk", [128, 2, 64, 16])
    ident = din("ident", [128, 128])
    uflag = din("uflag", [128, 2])
    yout = nc.dram_tensor("y", [OWN, D], F32, kind="ExternalOutput").ap()
    xmid = nc.dram_tensor("xmid", [OWN, D], F32, kind="Internal").ap()
    dbg_out = {}

    SB_BYTES = 212000
    big = nc.alloc_sbuf_tensor("big", [128, SB_BYTES], U8).ap()
    ps = nc.alloc_psum_tensor("ps", [128, 4096], F32).ap()
    psb = ps.bitcast(BF16)

    class Region:
        def __init__(self, lo, hi):
            assert hi <= SB_BYTES and lo <= hi, (lo, hi)
            self.lo, self.hi, self.off = lo, hi, lo

        def alloc(self, shape, dt, p0=0):
            esz = 4 if dt == F32 else 2
            nb = int(np.prod(shape[1:])) * esz
            nb_al = (nb + 63) // 64 * 64
            assert self.off + nb_al <= self.hi, ("SBUF region overflow", self.lo, self.hi, self.off, nb_al)
            v = big[p0:p0 + shape[0], self.off:self.off + nb].bitcast(dt)
            self.off += nb_al
            if len(shape) == 3:
                v = v.rearrange("p (a b) -> p a b", a=shape[1])
            elif len(shape) == 4:
                v = v.rearrange("p (a b c) -> p a b c", a=shape[1], b=shape[2])
            return v

    RP = Region(0, 4096)
    R1 = Region(RP.hi, RP.hi + 86080)
    RC = Region(R1.hi, R1.hi + 12352)
    RY = Region(RC.hi, RC.hi + 16448 + 4096 + 16448)
    RT = Region(RY.hi, SB_BYTES)
    A = RP
    S = Sched(nc)
    op = S.op

    def dump(name, ap, shape, dt, keys):
        if not dbg:
            return
        import os
        sel = os.environ.get("DBGSEL", "")
        if sel and name not in sel.split(","):
            return
        d = nc.dram_tensor("dbg_" + name, list(shape), dt, kind="ExternalOutput").ap()
        op("sync", lambda e: e.dma_start(out=d, in_=ap), reads=keys, dma=True)

    def bank(b, n=512, p=128, p0=0):
        return ps[p0:p0 + p, b * 512:b * 512 + n]

    def bankb(b, n=1024, p=128, p0=0):
        return psb[p0:p0 + p, b * 1024:b * 1024 + n]

    idf = A.alloc([128, 128], F32)
    idb = A.alloc([128, 128], BF16)
    gpt = A.alloc([128, 32], F32)
    stt = A.alloc([128, 256], F32)
    junk = A.alloc([128, 1024], BF16)
    rec = A.alloc([128, 8], F32)
    op("sync", lambda e: e.dma_start(out=idf, in_=ident), writes=["idf"], dma=True)
    op("sync", lambda e: e.dma_start(out=gpt, in_=gp), writes=["gpt"], dma=True)
    op("vector", lambda e: e.tensor_copy(out=idb, in_=idf), reads=["idf"], writes=["idb"])

    def rstd_from_ss(ss_ap, out_ap, n, key):
        tmp = ss_ap
        op("scalar", lambda e: e.activation(out=tmp, in_=ss_ap, func=AF.Sqrt, bias=EPS, scale=1.0 / n),
           reads=[key], writes=[key])
        op("vector", lambda e: e.reciprocal(out=out_ap, in_=tmp), reads=[key], writes=[key])

    def load_weight(dst, src_ap, kch, ncols, gcol, stage, wkey, negate_cols=None):
        op("gpsimd", lambda e: e.dma_start(out=stage[:, 0:kch, 0:ncols], in_=src_ap.rearrange("(k p) n -> p k n", p=128)),
           writes=[("stage", id(stage))], dma=True)
        for k in range(kch):
            eng = "vector" if k % 2 == 0 else "scalar"
            if gcol is None:
                if eng == "vector":
                    op(eng, lambda e, k=k: e.tensor_copy(out=dst[:, k, :], in_=stage[:, k, 0:ncols]),
                       reads=[("stage", id(stage))], writes=[("grp", wkey, (id(dst), k))])
                else:
                    op(eng, lambda e, k=k: e.copy(out=dst[:, k, :], in_=stage[:, k, 0:ncols]),
                       reads=[("stage", id(stage))], writes=[("grp", wkey, (id(dst), k))])
            else:
                if eng == "vector":
                    op(eng, lambda e, k=k: e.tensor_scalar(out=dst[:, k, :], in0=stage[:, k, 0:ncols],
                                                          scalar1=gpt[:, gcol + k:gcol + k + 1], scalar2=None, op0=ALU.mult),
                       reads=[("stage", id(stage)), "gpt"], writes=[("grp", wkey, (id(dst), k))])
                else:
                    op(eng, lambda e, k=k: e.activation(out=dst[:, k, :], in_=stage[:, k, 0:ncols], func=AF.Identity,
                                                        scale=gpt[:, gcol + k:gcol + k + 1]),
                       reads=[("stage", id(stage)), "gpt"], writes=[("grp", wkey, (id(dst), k))])

    def norm_tiles_to_T(src_dram_rows, ntile, xbuf, xnbuf, key, slot=0):
        c0 = 2 * slot
        sk = ("sttn", slot)
        op("sync", lambda e: e.dma_start(out=xbuf[:, 0:ntile, :], in_=src_dram_rows.rearrange("(t p) f -> p t f", p=128)),
           writes=[("x", key)], dma=True)
        for t in range(ntile):
            op("scalar", lambda e, t=t: e.activation(out=junk, in_=xbuf[:, t, :], func=AF.Square, accum_out=stt[:, c0 + t:c0 + t + 1]),
               reads=[("x", key)], writes=["junk", sk])
        rstd_from_ss(stt[:, c0:c0 + ntile], stt[:, 8 + c0:8 + c0 + ntile], D, sk)
        for t in range(ntile):
            if t % 2 == 0:
                op("vector", lambda e, t=t: e.tensor_scalar(out=xnbuf[:, t, :], in0=xbuf[:, t, :], scalar1=stt[:, 8 + c0 + t:9 + c0 + t], scalar2=None, op0=ALU.mult),
                   reads=[("x", key), sk], writes=[("grp", ("xn", key), t)])
            else:
                op("scalar", lambda e, t=t: e.activation(out=xnbuf[:, t, :], in_=xbuf[:, t, :], func=AF.Identity, scale=stt[:, 8 + c0 + t:9 + c0 + t]),
                   reads=[("x", key), sk], writes=[("grp", ("xn", key), t)])

    def transpose_tile(xn_tile, hT_dst, pbank, rkeys, wkey):
        for k in range(8):
            op("tensor", lambda e, k=k: e.transpose(out=bankb(pbank)[:, k * 128:(k + 1) * 128], in_=xn_tile[:, k * 128:(k + 1) * 128], identity=idb),
               reads=list(rkeys) + ["idb"], writes=[("ps", pbank)])
        op("vector", lambda e: e.tensor_copy(out=hT_dst, in_=bankb(pbank).rearrange("p (k t) -> p k t", k=8)),
           reads=[("ps", pbank)], writes=[wkey])

    KAT = R1.alloc([128, 4, NW], BF16)
    VAT = R1.alloc([128, 4, NW], BF16)
    QAT = R1.alloc([128, 4, NX], BF16)
    cqT = RC.alloc([128, 3, NX], BF16)
    A = Region(RC.hi, SB_BYTES)
    WA = A.alloc([128, 8, 1920], BF16)
    stage = A.alloc([128, 8, 480], F32)
    xbuf = [A.alloc([128, 2, D], F32) for _ in range(2)]
    xnb = [A.alloc([128, 2, D], BF16) for _ in range(2)]
    hTw = [A.alloc([128, 8, 256], BF16) for _ in range(2)]
    cqn = A.alloc([128, 384], BF16)
    for c in range(4):
        load_weight(WA[:, :, c * 480:(c + 1) * 480], w_in[:, c * 480:(c + 1) * 480], 8, 480, 0, stage, "WA")

    def cq_tile(cols_ap, M, xcol_ap_fn, hkey, pb):
        for k in range(8):
            la = cols_ap(k)
            op("tensor", lambda e, k=k, la=la: e.matmul(bank(pb, 384, M), lhsT=la, rhs=WA[:, k, 1536:1920], start=(k == 0), stop=(k == 7)),
               reads=[hkey, "WA"], writes=[("ps", pb)])
        op("scalar", lambda e: e.activation(out=junk[0:M, 0:384], in_=bank(pb, 384, M), func=AF.Square, accum_out=stt[0:M, 16:17]),
           reads=[("ps", pb)], writes=["junk", "stt2"])
        op("scalar", lambda e: e.activation(out=stt[0:M, 16:17], in_=stt[0:M, 16:17], func=AF.Sqrt, bias=EPS, scale=1.0 / 384),
           reads=["stt2"], writes=["stt2"])
        op("vector", lambda e: e.reciprocal(out=stt[0:M, 17:18], in_=stt[0:M, 16:17]), reads=["stt2"], writes=["stt2"])
        op("vector", lambda e: e.tensor_scalar(out=cqn[0:M, :], in0=bank(pb, 384, M), scalar1=stt[0:M, 17:18], scalar2=None, op0=ALU.mult),
           reads=[("ps", pb), "stt2"], writes=["cqn"])
        for j in range(3):
            op("tensor", lambda e, j=j: e.transpose(out=bankb(pb)[:, j * 128:j * 128 + M], in_=cqn[0:M, j * 128:(j + 1) * 128], identity=idb[0:M, 0:M]),
               reads=["cqn", "idb"], writes=[("ps", pb)])
        xc = xcol_ap_fn()
        op("vector", lambda e: e.tensor_copy(out=xc, in_=bankb(pb)[:, 0:384].rearrange("p (j t) -> p j t", j=3)[:, :, 0:M]),
           reads=[("ps", pb)], writes=["cqT"])

    NGW = NW // 256

    def w_stageA1(g):
        bi = g % 2
        e0 = g * 256
        norm_tiles_to_T(xw[e0:e0 + 256, :], 2, xbuf[bi], xnb[bi], ("w", bi), slot=bi)

    def w_stageA2(g):
        bi = g % 2
        for t in range(2):
            for k in range(8):
                op("tensor", lambda e, k=k: e.transpose(out=bankb(t)[:, k * 128:(k + 1) * 128], in_=xnb[bi][:, t, k * 128:(k + 1) * 128], identity=idb),
                   reads=[("xn", ("w", bi)), "idb"], writes=[("ps", t)])
        for t in range(2):
            op("vector" if t == 0 else "scalar",
               (lambda e: e.tensor_copy(out=hTw[bi][:, :, t * 128:(t + 1) * 128], in_=bankb(t).rearrange("p (k t) -> p k t", k=8))) if t == 0 else
               (lambda e: e.copy(out=hTw[bi][:, :, t * 128:(t + 1) * 128], in_=bankb(t).rearrange("p (k t) -> p k t", k=8))),
               reads=[("ps", t)], writes=[("grp", ("hTw", bi), t)])

    def w_stageB(g):
        bi = g % 2
        e0 = g * 256
        xlo, xhi = max(e0, OWN0 - 1), min(e0 + 256, OWN0 - 1 + NX)
        jobs = [("K", 512 + 128 * c, c) for c in range(4)] + [("V", 1024 + 128 * c, c) for c in range(4)]
        if xhi > xlo:
            jobs += [("Q", 128 * c, c) for c in range(4)]
        for ji, (kind, col0, c) in enumerate(jobs):
            pb = 2 + (ji % 4)
            for k in range(8):
                op("tensor", lambda e, k=k: e.matmul(bank(pb, 256), lhsT=WA[:, k, col0:col0 + 128], rhs=hTw[bi][:, k, :], start=(k == 0), stop=(k == 7)),
                   reads=[("hTw", bi), "WA"], writes=[("ps", pb)])
            if kind == "K":
                op("vector", lambda e: e.tensor_copy(out=KAT[:, c, e0:e0 + 256], in_=bank(pb, 256)), reads=[("ps", pb)], writes=["KAT"])
            elif kind == "V":
                op("scalar", lambda e: e.copy(out=VAT[:, c, e0:e0 + 256], in_=bank(pb, 256)), reads=[("ps", pb)], writes=["VAT"])
            else:
                op("vector", lambda e: e.tensor_copy(out=QAT[:, c, xlo - (OWN0 - 1):xhi - (OWN0 - 1)], in_=bank(pb, 256)[:, xlo - e0:xhi - e0]),
                   reads=[("ps", pb)], writes=["QAT"])
        for t in range(2):
            et = e0 + t * 128
            if OWN0 <= et < OWN0 + OWN:
                x0 = et - (OWN0 - 1)
                cq_tile(lambda k, t=t, bi=bi: hTw[bi][:, k, t * 128:(t + 1) * 128], 128, lambda x0=x0: cqT[:, :, x0:x0 + 128], ("hTw", bi), 6 + t)
            if et == OWN0 - 128:
                cq_tile(lambda k, t=t, bi=bi: hTw[bi][:, k, t * 128 + 127:t * 128 + 128], 1, lambda: cqT[:, :, 0:1], ("hTw", bi), 6 + t)
            if et == OWN0 + OWN:
                cq_tile(lambda k, t=t, bi=bi: hTw[bi][:, k, t * 128:t * 128 + 1], 1, lambda: cqT[:, :, NX - 1:NX], ("hTw", bi), 6 + t)

    for g in range(NGW + 2):
        if 0 <= g - 2 < NGW:
            w_stageB(g - 2)
        if 0 <= g - 1 < NGW:
            w_stageA2(g - 1)
        if g < NGW:
            w_stageA1(g)
    dump("KAT", KAT, [128, 4, NW], BF16, ["KAT"])
    dump("VAT", VAT, [128, 4, NW], BF16, ["VAT"])
    dump("QAT", QAT, [128, 4, NX], BF16, ["QAT"])
    dump("cqT", cqT, [128, 3, NX], BF16, ["cqT"])
    S.barrier()

    ynTa = RY.alloc([128, 4, NX], BF16)
    A = Region(RY.lo + 16448, SB_BYTES)
    mk = [A.alloc([128, 6, 128], F32) for _ in range(2)]
    kvft = A.alloc([128, NVT], F32)
    Vp = [A.alloc([128, NVT, 65], BF16) for _ in range(2)]
    Oacc = A.alloc([128, NX], F32)
    ya_tm = A.alloc([128, 17, 512], F32)
    Eb = [A.alloc([128, 1024], F32) for _ in range(2)]
    Pb = [A.alloc([128, 1024], BF16) for _ in range(2)]
    op("sync", lambda e: e.dma_start(out=kvft, in_=kvf), writes=["kvft"], dma=True)

    def oacc_to_tm(h, dst_tm, okey, Oacc):
        tiles = [(1 + 128 * m, 128, 1) for m in range(16)] + [(0, 2, NX - 1)]
        for g0 in range(0, 17, 4):
            tl = tiles[g0:g0 + 4]
            pb = 6 + (g0 // 4) % 2
            for i, (x0, M, step) in enumerate(tl):
                src = Oacc[0:65, x0:x0 + 128] if step == 1 else Oacc[0:65, 0:NX:NX - 1]
                op("tensor", lambda e, i=i, src=src, M=M, pb=pb: e.transpose(out=bank(pb)[0:M, i * 65:(i + 1) * 65], in_=src, identity=idf[0:65, 0:65]),
                   reads=[okey, "idf"], writes=[("ps", pb)])
            nt = len(tl)
            M = tl[0][1]
            pv = bank(pb)[0:M, 0:nt * 65].rearrange("p (t c) -> p t c", t=nt)
            op("vector", lambda e, pv=pv, nt=nt, M=M: e.reciprocal(out=rec[0:M, 0:nt], in_=pv[:, :, 64]), reads=[("ps", pb)], writes=["rec"])
            op("vector", lambda e, pv=pv, nt=nt, M=M, g0=g0: e.tensor_tensor(out=dst_tm[0:M, g0:g0 + nt, h * 64:(h + 1) * 64], in0=pv[:, :, 0:64],
                                                                         in1=rec[0:M, 0:nt].unsqueeze(2).to_broadcast([M, nt, 64]), op=ALU.mult),
               reads=[("ps", pb), "rec"], writes=["tm"])

    def tm_to_ynT(src_tm, dstT, nkey, Pb):
        for t in range(17):
            M = 128 if t < 16 else 2
            op("scalar", lambda e, t=t, M=M: e.activation(out=junk[0:M, 0:512], in_=src_tm[0:M, t, :], func=AF.Square, accum_out=stt[0:M, 32 + t:33 + t]),
               reads=["tm"], writes=["junk", "stt3"])
        rstd_from_ss(stt[:, 32:49], stt[:, 64:81], 512, "stt3")
        for t in range(17):
            M = 128 if t < 16 else 2
            pb = t % 2
            ynb = Pb[t % 2]
            op("vector", lambda e, t=t, M=M, ynb=ynb: e.tensor_scalar(out=ynb[0:M, 0:512], in0=src_tm[0:M, t, :], scalar1=stt[0:M, 64 + t:65 + t], scalar2=None, op0=ALU.mult),
               reads=["tm", "stt3"], writes=[("Pb", t % 2)])
            for j in range(4):
                op("tensor", lambda e, j=j, M=M, ynb=ynb, pb=pb: e.transpose(out=bankb(pb)[:, j * 128:j * 128 + M], in_=ynb[0:M, j * 128:(j + 1) * 128], identity=idb[0:M, 0:M]),
                   reads=[("Pb", t % 2), "idb"], writes=[("ps", pb)])
            srcv = bankb(pb)[:, 0:512].rearrange("p (j t) -> p j t", j=4)[:, :, 0:M]
            if t < 16:
                dstv = dstT[:, :, 1 + 128 * t:1 + 128 * (t + 1)]
            else:
                dstv = dstT[:, :, 0:NX:NX - 1]
            op("vector", lambda e, srcv=srcv, dstv=dstv: e.tensor_copy(out=dstv, in_=srcv), reads=[("ps", pb)], writes=[nkey])

    def d_prep(h):
        pr, hs = h // 2, (h % 2) * 64
        vb = Vp[h % 2]
        mb = mk[h % 2]
        op("sync", lambda e: e.dma_start(out=mb, in_=masks[h]), writes=[("mk", h % 2)], dma=True)
        for v0 in range(0, NVT, 8):
            vts = VLIST[v0:v0 + 8]
            pb = 4 + (v0 // 8) % 2
            for i, (r, es, nk) in enumerate(vts):
                op("tensor", lambda e, i=i, r=r, es=es, nk=nk: e.transpose(out=bankb(pb)[0:nk, i * 64:(i + 1) * 64],
                                                                        in_=VAT[hs:hs + 64, pr, es:es + r * (nk - 1) + 1:r], identity=idb[hs:hs + 64, hs:hs + 64]),
                   reads=["VAT", "idb"], writes=[("ps", pb)])
            i = 0
            while i < len(vts):
                nk = vts[i][2]
                j = i
                while j + 1 < len(vts) and vts[j + 1][2] == nk:
                    j += 1
                cnt_ = j - i + 1
                srcv = bankb(pb)[0:nk, i * 64:(j + 1) * 64].rearrange("p (t c) -> p t c", t=cnt_)
                op("vector", lambda e, srcv=srcv, i=i, cnt_=cnt_, nk=nk: e.tensor_tensor(
                    out=vb[0:nk, v0 + i:v0 + i + cnt_, 0:64], in0=srcv,
                    in1=kvft[0:nk, v0 + i:v0 + i + cnt_].unsqueeze(2).to_broadcast([nk, cnt_, 64]), op=ALU.mult),
                   reads=[("ps", pb), "kvft"], writes=[("Vp", h % 2)])
                i = j + 1
        op("gpsimd", lambda e: e.tensor_copy(out=vb[:, :, 64], in_=kvft), reads=["kvft"], writes=[("Vp", h % 2)])

    GL = [(ri, r, kind, g, blks) for ri, r in enumerate((1, 4, 16)) for (kind, g, blks) in DGROUPS[r]]
    gctr = [0]

    def d_scores(h, gi, grp):
        pr, hs = h // 2, (h % 2) * 64
        ri, r, kind, g, blks = grp
        sb = 2 * (gi % 2)
        for bi_, (x0, Nq, t0, t1) in enumerate(blks):
            qv = QAT[hs:hs + 64, pr, x0:x0 + r * (Nq - 1) + 1:r]
            for role, tix in ((0, t0), (1, t1)):
                (rr, es, nk) = VLIST[tix]
                op("tensor", lambda e, role=role, es=es, nk=nk, qv=qv, bi_=bi_, Nq=Nq: e.matmul(
                    bank(sb + role)[0:nk, bi_ * 128:bi_ * 128 + Nq], lhsT=KAT[hs:hs + 64, pr, es:es + r * (nk - 1) + 1:r], rhs=qv, start=True, stop=True),
                   reads=["KAT", "QAT"], writes=[("ps", sb + role)])

    def d_rest(h, gi, grp):
        ri, r, kind, g, blks = grp
        vb = Vp[h % 2]
        mb = mk[h % 2]
        sb = 2 * (gi % 2)
        eb = Eb[gi % 2]
        pbuf = Pb[gi % 2]
        ob = 6 + gi % 2
        ek, pk = ("Eb", gi % 2), ("Pbuf", gi % 2)
        if kind == "reg":
            sv = ps[:, sb * 512:(sb + 2) * 512]
            op("scalar", lambda e: e.activation(out=eb, in_=sv, func=AF.Exp, scale=0.125),
               reads=[("ps", sb), ("ps", sb + 1)], writes=[ek])
            op("vector", lambda e: e.tensor_tensor(
                out=pbuf.rearrange("p (r b q) -> p r b q", r=2, b=4), in0=eb.rearrange("p (r b q) -> p r b q", r=2, b=4),
                in1=mb[:, 2 * ri:2 * ri + 2, :].unsqueeze(2).to_broadcast([128, 2, 4, 128]), op=ALU.mult),
               reads=[ek, ("mk", h % 2)], writes=[pk])
        else:
            op("scalar", lambda e: e.activation(out=eb[:, 0:256:128], in_=bank(sb)[:, 0:256:128], func=AF.Exp, scale=0.125),
               reads=[("ps", sb)], writes=[ek])
            op("scalar", lambda e: e.activation(out=eb[0:1, 512:768:128], in_=bank(sb + 1)[0:1, 0:256:128], func=AF.Exp, scale=0.125),
               reads=[("ps", sb + 1)], writes=[ek])
            op("vector", lambda e: e.tensor_tensor(
                out=pbuf[:, 0:256:128], in0=eb[:, 0:256:128], in1=mb[:, 2 * ri, 0:1].to_broadcast([128, 2]), op=ALU.mult),
               reads=[ek, ("mk", h % 2)], writes=[pk])
            op("vector", lambda e: e.tensor_tensor(
                out=pbuf[0:1, 512:768:128], in0=eb[0:1, 512:768:128], in1=mb[0:1, 2 * ri + 1, 0:1].to_broadcast([1, 2]), op=ALU.mult),
               reads=[ek, ("mk", h % 2)], writes=[pk])
        for bi_, (x0, Nq, t0, t1) in enumerate(blks):
            for role, tix in ((0, t0), (1, t1)):
                nk = VLIST[tix][2]
                op("tensor", lambda e, role=role, tix=tix, nk=nk, bi_=bi_, Nq=Nq: e.matmul(
                    bank(ob)[0:65, bi_ * 128:bi_ * 128 + Nq], lhsT=vb[0:nk, tix, :], rhs=pbuf[0:nk, role * 512 + bi_ * 128:role * 512 + bi_ * 128 + Nq],
                    start=(role == 0), stop=(role == 1)),
                   reads=[pk, ("Vp", h % 2)], writes=[("ps", ob)])
        if kind == "reg":
            src = bank(ob)[0:65, :].rearrange("p (c q) -> p c q", c=4)
            if r == 1:
                dst = Oacc[0:65, 1 + 512 * g:1 + 512 * (g + 1)].rearrange("p (c q) -> p c q", c=4)
            elif r == 4:
                dst = Oacc[0:65, 1 + 512 * g:1 + 512 * (g + 1)].rearrange("p (q c) -> p c q", c=4)
            else:
                dst = Oacc[0:65, 1:1 + OWN].rearrange("p (q c) -> p c q", c=16)[:, 4 * g:4 * g + 4, :]
        else:
            src = bank(ob)[0:65, 0:256:128]
            dst = Oacc[0:65, 0:NX:NX - 1]
        if r == 1:
            op("vector", lambda e: e.tensor_copy(out=dst, in_=src), reads=[("ps", ob)], writes=["Oacc"])
        else:
            op("vector", lambda e: e.tensor_tensor(out=dst, in0=src, in1=dst, op=ALU.add), reads=[("ps", ob), "Oacc"], writes=["Oacc"])

    d_prep(0)
    for h in range(8):
        g0 = gctr[0]
        d_scores(h, g0, GL[0])
        if h + 1 < 8:
            d_prep(h + 1)
        for i, grp in enumerate(GL):
            if i + 1 < len(GL):
                d_scores(h, g0 + i + 1, GL[i + 1])
            d_rest(h, g0 + i, grp)
        gctr[0] += len(GL)
        oacc_to_tm(h, ya_tm, "Oacc", Oacc)
    tm_to_ynT(ya_tm, ynTa, "ynTa", Pb)
    dump("ynTa", ynTa, [128, 4, NX], BF16, ["ynTa"])
    dump("ya_tm", ya_tm, [128, 17, 512], F32, ["tm"])
    S.barrier()

    R1.off = R1.lo
    ckvT = R1.alloc([128, 2, S_LEN], BF16)
    KT = R1.alloc([96, S_LEN], BF16)
    R1s_lo = R1.off
    RY.off = RY.lo + 16448
    Wukv = RY.alloc([128, 2, 1024], BF16)
    A = Region(RY.off, SB_BYTES)
    Wkvl = A.alloc([128, 8, 288], BF16)
    stage2 = A.alloc([128, 8, 512], F32)
    xbuf = [A.alloc([128, 2, D], F32) for _ in range(2)]
    xnb = [A.alloc([128, 2, D], BF16) for _ in range(2)]
    hTk = [A.alloc([128, 8, 128], BF16) for _ in range(2)]
    ckvn = [A.alloc([128, 256], BF16) for _ in range(2)]
    kr_tm = A.alloc([128, 64, 32], F32)
    rkt = A.alloc([128, 2, 64, 16], F32)
    kr_pad = A.alloc([128, 64, 96], BF16)
    rt = [A.alloc([128, 64, 16], F32) for _ in range(2)]
    load_weight(Wkvl, w_in[:, 1920:2208], 8, 288, 0, stage2, "Wkvl")
    load_weight(Wukv[:, :, 0:512], w_ukv[:, 0:512], 2, 512, 11, stage2, "Wukv")
    load_weight(Wukv[:, :, 512:1024], w_ukv[:, 512:1024], 2, 512, 11, stage2, "Wukv")
    op("sync", lambda e: e.dma_start(out=rkt, in_=rk), writes=["rkt"], dma=True)
    op("gpsimd", lambda e: e.memset(kr_pad, 0.0), writes=["kr_pad"])
    def k_stageA1(g):
        bi = g % 2
        norm_tiles_to_T(xf[g * 256:(g + 1) * 256, :], 2, xbuf[bi], xnb[bi], ("k", bi), slot=bi)

    def k_stageA2(g):
        bi = g % 2
        for t in range(2):
            tt = g * 2 + t
            for k in range(8):
                op("tensor", lambda e, k=k: e.transpose(out=bankb(tt % 2)[:, k * 128:(k + 1) * 128], in_=xnb[bi][:, t, k * 128:(k + 1) * 128], identity=idb),
                   reads=[("xn", ("k", bi)), "idb"], writes=[("ps", tt % 2)])
        for t in range(2):
            tt = g * 2 + t
            hb = hTk[tt % 2]
            op("vector" if t == 0 else "scalar",
               (lambda e: e.tensor_copy(out=hb, in_=bankb(tt % 2).rearrange("p (k t) -> p k t", k=8))) if t == 0 else
               (lambda e: e.copy(out=hb, in_=bankb(tt % 2).rearrange("p (k t) -> p k t", k=8))),
               reads=[("ps", tt % 2)], writes=[("hTk", tt % 2)])
        for t in range(2):
            tt = g * 2 + t
            hb = hTk[tt % 2]
            pb = 2 + tt % 4
            for k in range(8):
                op("tensor", lambda e, k=k: e.matmul(bank(pb, 288), lhsT=hb[:, k, :], rhs=Wkvl[:, k, :], start=(k == 0), stop=(k == 7)),
                   reads=[("hTk", tt % 2), "Wkvl"], writes=[("ps", pb)])

    def k_stageB(g):
        for t in range(2):
            tt = g * 2 + t
            pb = 2 + tt % 4
            sc = 96 + (tt % 8)
            sk = ("stt4", tt % 8)
            op("scalar", lambda e: e.activation(out=junk[:, 0:256], in_=bank(pb, 256), func=AF.Square, accum_out=stt[:, sc:sc + 1]),
               reads=[("ps", pb)], writes=["junk", sk])
            op("scalar", lambda e: e.activation(out=stt[:, sc:sc + 1], in_=stt[:, sc:sc + 1], func=AF.Sqrt, bias=EPS, scale=1.0 / 256),
               reads=[sk], writes=[sk])
            op("vector", lambda e: e.reciprocal(out=stt[:, sc + 8:sc + 9], in_=stt[:, sc:sc + 1]), reads=[sk], writes=[sk])
            cb = ckvn[tt % 2]
            op("vector", lambda e: e.tensor_scalar(out=cb, in0=bank(pb, 256), scalar1=stt[:, sc + 8:sc + 9], scalar2=None, op0=ALU.mult),
               reads=[("ps", pb), sk], writes=[("ckvn", tt % 2)])
            op("vector", lambda e: e.tensor_copy(out=kr_tm[:, tt, :], in_=bank(pb, 288)[:, 256:288]), reads=[("ps", pb)], writes=["kr_tm"])
            pb2 = 6 + tt % 2
            for j in range(2):
                op("tensor", lambda e, j=j: e.transpose(out=bankb(pb2)[:, j * 128:(j + 1) * 128], in_=cb[:, j * 128:(j + 1) * 128], identity=idb),
                   reads=[("ckvn", tt % 2), "idb"], writes=[("ps", pb2)])
            op("scalar", lambda e: e.copy(out=ckvT[:, :, tt * 128:(tt + 1) * 128], in_=bankb(pb2)[:, 0:256].rearrange("p (j t) -> p j t", j=2)),
               reads=[("ps", pb2)], writes=["ckvT"])

    NGK = S_LEN // 256
    for g in range(NGK + 2):
        if 0 <= g - 2 < NGK:
            k_stageB(g - 2)
        if 0 <= g - 1 < NGK:
            k_stageA2(g - 1)
        if g < NGK:
            k_stageA1(g)
    x1, x2 = kr_tm[:, :, 0:16], kr_tm[:, :, 16:32]
    cosk, sink = rkt[:, 0], rkt[:, 1]
    op("vector", lambda e: e.tensor_tensor(out=rt[0], in0=x1, in1=cosk, op=ALU.mult), reads=["kr_tm", "rkt"], writes=["rt0"])
    op("vector", lambda e: e.tensor_tensor(out=rt[1], in0=x2, in1=sink, op=ALU.mult), reads=["kr_tm", "rkt"], writes=["rt1"])
    op("vector", lambda e: e.tensor_tensor(out=kr_pad[:, :, 64:80], in0=rt[0], in1=rt[1], op=ALU.subtract), reads=["rt0", "rt1", "kr_pad"], writes=["kr_pad"])
    op("vector", lambda e: e.tensor_tensor(out=rt[0], in0=x2, in1=cosk, op=ALU.mult), reads=["kr_tm", "rkt", "kr_pad"], writes=["rt0"])
    op("vector", lambda e: e.tensor_tensor(out=rt[1], in0=x1, in1=sink, op=ALU.mult), reads=["kr_tm", "rkt", "kr_pad"], writes=["rt1"])
    op("vector", lambda e: e.tensor_tensor(out=kr_pad[:, :, 80:96], in0=rt[0], in1=rt[1], op=ALU.add), reads=["rt0", "rt1", "kr_pad"], writes=["kr_pad"])
    for g8 in range(8):
        pb = 6 + g8 % 2
        for i in range(8):
            tt = g8 * 8 + i
            op("tensor", lambda e, i=i, tt=tt, pb=pb: e.transpose(out=bankb(pb)[0:96, i * 128:(i + 1) * 128], in_=kr_pad[:, tt, :], identity=idb),
               reads=["kr_pad", "idb"], writes=[("ps", pb)])
        op("vector", lambda e, pb=pb, g8=g8: e.tensor_copy(out=KT[64:96, g8 * 1024:(g8 + 1) * 1024], in_=bankb(pb)[64:96, :]),
           reads=[("ps", pb)], writes=["KTr"])
    dump("ckvT", ckvT, [128, 2, S_LEN], BF16, ["ckvT"])
    dump("KTr", KT[64:96, :], [32, S_LEN], BF16, ["KTr"])
    S.barrier()

    ynTb = RY.alloc([128, 4, NX], BF16)
    RT.off = RT.lo
    QBT = RT.alloc([96, 8, NX], BF16)
    R1s = Region(R1s_lo, R1.hi)
    Wuq = R1s.alloc([128, 3, 768], BF16)
    Wrot = R1s.alloc([128, 3, 8, 96], BF16)
    stage3 = R1s.alloc([128, 3, 768], F32)
    rqt = RT.alloc([96, 2, NX], F32)
    tq = [RT.alloc([96, 410], F32) for _ in range(2)]
    load_weight(Wuq, w_uq, 3, 768, 8, stage3, "Wuq")
    op("gpsimd", lambda e: e.memset(Wrot, 0.0), writes=["Wrot"])
    Wuq4 = Wuq.rearrange("p k (h c) -> p k h c", h=8)
    for k in range(3):
        op("gpsimd", lambda e, k=k: e.tensor_scalar(out=Wrot[:, k, :, 64:80], in0=Wuq4[:, k, :, 80:96], scalar1=-1.0, scalar2=None, op0=ALU.mult),
           reads=["Wuq", "Wrot"], writes=["Wrot"])
        op("gpsimd", lambda e, k=k: e.tensor_copy(out=Wrot[:, k, :, 80:96], in_=Wuq4[:, k, :, 64:80]), reads=["Wuq", "Wrot"], writes=["Wrot"])
    op("sync", lambda e: e.dma_start(out=rqt[64:96], in_=rq), writes=["rqt"], dma=True)
    qi = 0
    for h in range(8):
        for (c0, cn) in XBLK:
            pa, pr_ = 2 * (qi % 2), 1 + 2 * (qi % 2)
            tb_ = tq[qi % 2]
            tk = ("tq", qi % 2)
            qi += 1
            for k in range(3):
                op("tensor", lambda e, k=k, h=h, c0=c0, cn=cn, pa=pa: e.matmul(bank(pa, cn, 96), lhsT=Wuq[:, k, h * 96:(h + 1) * 96], rhs=cqT[:, k, c0:c0 + cn], start=(k == 0), stop=(k == 2)),
                   reads=["Wuq", "cqT"], writes=[("ps", pa)])
            for k in range(3):
                op("tensor", lambda e, k=k, h=h, c0=c0, cn=cn, pr_=pr_: e.matmul(bank(pr_, cn, 96), lhsT=Wrot[:, k, h, :], rhs=cqT[:, k, c0:c0 + cn], start=(k == 0), stop=(k == 2)),
                   reads=["Wrot", "cqT"], writes=[("ps", pr_)])
            op("scalar", lambda e, h=h, c0=c0, cn=cn, pa=pa: e.copy(out=QBT[0:64, h, c0:c0 + cn], in_=bank(pa, cn, 64)), reads=[("ps", pa)], writes=["QBT"])
            op("vector", lambda e, c0=c0, cn=cn, pa=pa, tb_=tb_: e.tensor_tensor(out=tb_[64:96, 0:cn], in0=bank(pa, cn, 32, 64), in1=rqt[64:96, 0, c0:c0 + cn], op=ALU.mult),
               reads=[("ps", pa), "rqt"], writes=[tk])
            op("vector", lambda e, h=h, c0=c0, cn=cn, pr_=pr_: e.tensor_tensor(out=QBT[64:96, h, c0:c0 + cn], in0=bank(pr_, cn, 32, 64), in1=rqt[64:96, 1, c0:c0 + cn], op=ALU.mult),
               reads=[("ps", pr_), "rqt"], writes=["QBT"])
            op("vector", lambda e, h=h, c0=c0, cn=cn, tb_=tb_: e.tensor_tensor(out=QBT[64:96, h, c0:c0 + cn], in0=QBT[64:96, h, c0:c0 + cn], in1=tb_[64:96, 0:cn], op=ALU.add),
               reads=[tk, "QBT"], writes=["QBT"])
    dump("QBT", QBT, [96, 8, NX], BF16, ["QBT"])
    S.barrier()
    RT.off = RT.lo + 32832
    yb_tm = RT.alloc([128, 17, 512], F32)
    ynb_tmp = [RT.alloc([128, 512], BF16) for _ in range(2)]
    R1s = Region(R1s_lo, R1.hi)
    Vb = [R1s.alloc([128, 64, 65], BF16) for _ in range(2)]
    PT = [R1s.alloc([128, 2, 512], BF16) for _ in range(3)]
    OaccB = R1s.alloc([128, NX], F32)
    for b_ in range(2):
        op("gpsimd", lambda e, b_=b_: e.memset(Vb[b_][:, :, 64], 1.0), writes=[("Vb", b_)])
    scale_b = 96.0 ** -0.5
    MB = [(1 + 512 * i, 512) for i in range(4)]
    QG = [(0, 2), (2, 2)]
    si = 0

    def m_prep_v(h):
        vb = Vb[h % 2]
        for k8 in range(8):
            pb = 6 + k8 % 2
            for i in range(8):
                kt = k8 * 8 + i
                for j in range(2):
                    op("tensor", lambda e, j=j, kt=kt, i=i: e.matmul(bank(pb)[:, i * 64:(i + 1) * 64], lhsT=ckvT[:, j, kt * 128:(kt + 1) * 128], rhs=Wukv[:, j, h * 128 + 64:h * 128 + 128], start=(j == 0), stop=(j == 1)),
                       reads=["Wukv", "ckvT"], writes=[("ps", pb)])
            op("vector", lambda e: e.tensor_copy(out=vb[:, k8 * 8:(k8 + 1) * 8, 0:64], in_=bank(pb).rearrange("p (t c) -> p t c", t=8)),
               reads=[("ps", pb)], writes=[("Vb", h % 2)])

    def m_prep_k(h):
        for nb_ in range(16):
            pb = 6 + nb_ % 2
            for j in range(2):
                op("tensor", lambda e, j=j: e.matmul(bank(pb, 512, 64), lhsT=Wukv[:, j, h * 128:h * 128 + 64], rhs=ckvT[:, j, nb_ * 512:(nb_ + 1) * 512], start=(j == 0), stop=(j == 1)),
                   reads=["Wukv", "ckvT"], writes=[("ps", pb)])
            op("vector", lambda e: e.tensor_copy(out=KT[0:64, nb_ * 512:(nb_ + 1) * 512], in_=bank(pb, 512, 64)),
               reads=[("ps", pb)], writes=["KTn"])

    m_prep_v(0)
    for h in range(8):
        vb = Vb[h % 2]
        m_prep_k(h)
        for gq, (b0, nbk) in enumerate(QG):
            ob = 4

            def emit_S(kt, si_):
                sbk = 2 * (si_ % 2)
                for bb in range(nbk):
                    c0, cn = MB[b0 + bb]
                    op("tensor", lambda e, bb=bb, c0=c0, cn=cn: e.matmul(bank(sbk + bb, cn), lhsT=KT[0:96, kt * 128:(kt + 1) * 128], rhs=QBT[0:96, h, c0:c0 + cn], start=True, stop=True),
                       reads=["KTn", "KTr", "QBT"], writes=[("ps", sbk + bb)])
            emit_S(0, si)
            for kt in range(64):
                sbk = 2 * (si % 2)
                pt = PT[si % 3]
                ptk = ("PT", si % 3)
                if kt + 1 < 64:
                    emit_S(kt + 1, si + 1)
                if gq == 0 and kt == 8 and h + 1 < 8:
                    m_prep_v(h + 1)
                sv = ps[:, sbk * 512:(sbk + nbk) * 512].rearrange("p (a b) -> p a b", a=nbk)
                op("scalar", lambda e: e.activation(out=pt[:, 0:nbk, :], in_=sv, func=AF.Exp, scale=scale_b),
                   reads=[("ps", sbk + bb) for bb in range(nbk)], writes=[ptk])
                for bb in range(nbk):
                    op("tensor", lambda e, bb=bb: e.matmul(bank(ob + bb, 512, 65), lhsT=vb[:, kt, :], rhs=pt[:, bb, :], start=(kt == 0), stop=(kt == 63)),
                       reads=[ptk, ("Vb", h % 2)], writes=[("ps", ob + bb)])
                si += 1
            ov = ps[0:65, ob * 512:(ob + nbk) * 512]
            c0 = MB[b0][0]
            op("vector", lambda e: e.tensor_copy(out=OaccB[0:65, c0:c0 + nbk * 512], in_=ov),
               reads=[("ps", ob + bb) for bb in range(nbk)], writes=["OaccB"])
        sbk = 2 * (si % 2)
        pt = PT[si % 3]
        ptk = ("PT", si % 3)
        qh = QBT[0:96, h, 0:NX:NX - 1]
        for kt in range(64):
            op("tensor", lambda e, kt=kt: e.matmul(bank(sbk)[:, 2 * kt:2 * kt + 2], lhsT=KT[0:96, kt * 128:(kt + 1) * 128], rhs=qh, start=True, stop=True),
               reads=["KTn", "KTr", "QBT"], writes=[("ps", sbk)])
        op("scalar", lambda e: e.activation(out=pt[:, 0, 0:128], in_=bank(sbk, 128), func=AF.Exp, scale=scale_b), reads=[("ps", sbk)], writes=[ptk])
        for kt in range(64):
            op("tensor", lambda e, kt=kt: e.matmul(bank(4, 2, 65), lhsT=vb[:, kt, :], rhs=pt[:, 0, 2 * kt:2 * kt + 2], start=(kt == 0), stop=(kt == 63)),
               reads=[ptk, ("Vb", h % 2)], writes=[("ps", 4)])
        si += 1
        op("vector", lambda e: e.tensor_copy(out=OaccB[0:65, 0:NX:NX - 1], in_=bank(4, 2, 65)), reads=[("ps", 4)], writes=["OaccB"])
        oacc_to_tm(h, yb_tm, "OaccB", OaccB)
    tm_to_ynT(yb_tm, ynTb, "ynTb", ynb_tmp)
    dump("ynTb", ynTb, [128, 4, NX], BF16, ["ynTb"])
    dump("yb_tm", yb_tm, [128, 17, 512], F32, ["tm"])
    S.barrier()

    RT.off = RT.lo
    hTf = RT.alloc([128, 8, NX], BF16)
    A = Region(R1.lo, RC.hi)
    Wo = A.alloc([128, 8, D], BF16)
    stage4 = A.alloc([128, 8, 256], F32)
    gpo = A.alloc([128, 2, D], F32)
    xt_ = [A.alloc([128, D], F32) for _ in range(2)]
    xm_ = [A.alloc([128, D], F32) for _ in range(2)]
    hn_ = [A.alloc([128, D], BF16) for _ in range(2)]
    for c in range(4):
        load_weight(Wo[:, :, c * 256:(c + 1) * 256], w_o[:, c * 256:(c + 1) * 256], 8, 256, 13, stage4, "Wo")
    op("sync", lambda e: e.dma_start(out=gpo, in_=gpost), writes=["gpo"], dma=True)
    def f0_p1(t):
        M = 128 if t < 16 else 2
        bi = t % 2
        xt, xm, hn = xt_[bi], xm_[bi], hn_[bi]
        xk, mkk, hk = ("xt", bi), ("xm", bi), ("hn", bi)
        if t < 16:
            op("sync", lambda e, t=t, xt=xt: e.dma_start(out=xt, in_=xw[OWN0 + 128 * t:OWN0 + 128 * (t + 1), :]), writes=[xk], dma=True)
            cols = lambda k, t=t: (ynTa if k < 4 else ynTb)[:, k % 4, 1 + 128 * t:1 + 128 * (t + 1)]
        else:
            op("sync", lambda e, xt=xt: e.dma_start(out=xt[0:1, :], in_=xw[OWN0 - 1:OWN0, :]), writes=[xk], dma=True)
            op("sync", lambda e, xt=xt: e.dma_start(out=xt[1:2, :], in_=xw[OWN0 + OWN:OWN0 + OWN + 1, :]), writes=[xk], dma=True)
            cols = lambda k: (ynTa if k < 4 else ynTb)[:, k % 4, 0:NX:NX - 1]
        pb = 2 * (t % 2)
        for n2 in range(2):
            for k in range(8):
                la = cols(k)
                op("tensor", lambda e, k=k, n2=n2, M=M, pb=pb, la=la: e.matmul(bank(pb + n2, 512, M), lhsT=la, rhs=Wo[:, k, n2 * 512:(n2 + 1) * 512], start=(k == 0), stop=(k == 7)),
                   reads=["ynTa", "ynTb", "Wo"], writes=[("ps", pb + n2)])

    def f0_p2(t):
        M = 128 if t < 16 else 2
        bi = t % 2
        xt, xm, hn = xt_[bi], xm_[bi], hn_[bi]
        xk, mkk, hk = ("xt", bi), ("xm", bi), ("hn", bi)
        pb = 2 * (t % 2)
        yv = ps[0:M, pb * 512:(pb + 2) * 512]
        sc = 128 + 2 * (t % 4)
        sk = ("stt5", t % 4)
        op("scalar", lambda e, yv=yv, M=M, sc=sc: e.activation(out=junk[0:M, :], in_=yv, func=AF.Square, accum_out=stt[0:M, sc:sc + 1]),
           reads=[("ps", pb), ("ps", pb + 1)], writes=["junk", sk])
        op("scalar", lambda e, M=M, sc=sc: e.activation(out=stt[0:M, sc:sc + 1], in_=stt[0:M, sc:sc + 1], func=AF.Sqrt, bias=EPS, scale=1.0 / D), reads=[sk], writes=[sk])
        op("vector", lambda e, M=M, sc=sc: e.reciprocal(out=stt[0:M, sc + 1:sc + 2], in_=stt[0:M, sc:sc + 1]), reads=[sk], writes=[sk])
        op("vector", lambda e, yv=yv, M=M, sc=sc, xm=xm: e.scalar_tensor_tensor(out=xm[0:M, :], in0=yv, scalar=stt[0:M, sc + 1:sc + 2], in1=gpo[0:M, 0, :], op0=ALU.mult, op1=ALU.mult),
           reads=[("ps", pb), ("ps", pb + 1), sk, "gpo"], writes=[mkk])
        op("vector", lambda e, M=M, xm=xm, xt=xt: e.tensor_tensor(out=xm[0:M, :], in0=xm[0:M, :], in1=xt[0:M, :], op=ALU.add), reads=[mkk, xk], writes=[mkk])
        if t < 16:
            op("sync", lambda e, t=t, xm=xm: e.dma_start(out=xmid[128 * t:128 * (t + 1), :], in_=xm), reads=[mkk], writes=["xmid"], dma=True)
        sc2 = 136 + 2 * (t % 4)
        sk2 = ("stt6", t % 4)
        op("scalar", lambda e, M=M, sc2=sc2, xm=xm: e.activation(out=junk[0:M, :], in_=xm[0:M, :], func=AF.Square, accum_out=stt[0:M, sc2:sc2 + 1]),
           reads=[mkk], writes=["junk", sk2])
        op("scalar", lambda e, M=M, sc2=sc2: e.activation(out=stt[0:M, sc2:sc2 + 1], in_=stt[0:M, sc2:sc2 + 1], func=AF.Sqrt, bias=EPS, scale=1.0 / D), reads=[sk2], writes=[sk2])
        op("vector", lambda e, M=M, sc2=sc2: e.reciprocal(out=stt[0:M, sc2 + 1:sc2 + 2], in_=stt[0:M, sc2:sc2 + 1]), reads=[sk2], writes=[sk2])
        op("vector", lambda e, M=M, sc2=sc2, xm=xm, hn=hn: e.tensor_scalar(out=hn[0:M, :], in0=xm[0:M, :], scalar1=stt[0:M, sc2 + 1:sc2 + 2], scalar2=None, op0=ALU.mult),
           reads=[mkk, sk2], writes=[hk])
        pb2 = 4 + t % 2
        for k in range(8):
            op("tensor", lambda e, k=k, M=M, hn=hn, pb2=pb2: e.transpose(out=bankb(pb2)[:, k * 128:k * 128 + M], in_=hn[0:M, k * 128:(k + 1) * 128], identity=idb[0:M, 0:M]),
               reads=[hk, "idb"], writes=[("ps", pb2)])
        srcv = bankb(pb2).rearrange("p (k t) -> p k t", k=8)[:, :, 0:M]
        dstv = hTf[:, :, 1 + 128 * t:1 + 128 * (t + 1)] if t < 16 else hTf[:, :, 0:NX:NX - 1]
        op("vector", lambda e, srcv=srcv, dstv=dstv: e.tensor_copy(out=dstv, in_=srcv), reads=[("ps", pb2)], writes=["hTf"])

    f0_p1(0)
    for t in range(17):
        if t + 1 < 17:
            f0_p1(t + 1)
        f0_p2(t)
    dump("hTf", hTf, [128, 8, NX], BF16, ["hTf"])
    S.barrier()

    aT = Region(R1.lo, RC.hi).alloc([128, 22, OWN], BF16)
    A = Region(RY.lo, RY.hi)
    stg = [A.alloc([128, 8, 256], F32) for _ in range(2)]
    Wub = [A.alloc([128, 8, 256], BF16) for _ in range(2)]
    cwt = A.alloc([128, 44, 4], F32)
    ufl = A.alloc([128, 2], F32)
    A = Region(RT.lo + 32832, SB_BYTES)
    cgb = [A.alloc([128, OWN], F32) for _ in range(2)]
    cvb = [A.alloc([128, OWN], F32) for _ in range(2)]
    op("sync", lambda e: e.dma_start(out=cwt, in_=cwb), writes=["cwt"], dma=True)
    op("sync", lambda e: e.dma_start(out=ufl, in_=uflag), writes=["ufl"], dma=True)
    OB = [(410 * i, min(410, OWN - 410 * i)) for i in range(5)]
    pbi = 0

    def f1_weights(j):
        bi = j % 2
        sg, wb = stg[bi], Wub[bi]
        sgk, wbk = ("stg", bi), ("Wub", bi)
        op("gpsimd", lambda e: e.dma_start(out=sg[:, :, 0:128], in_=w_up[:, j * 128:(j + 1) * 128].rearrange("(k p) n -> p k n", p=128)), writes=[sgk], dma=True)
        op("gpsimd", lambda e: e.dma_start(out=sg[:, :, 128:256], in_=w_up[:, DFF + j * 128:DFF + (j + 1) * 128].rearrange("(k p) n -> p k n", p=128)), writes=[sgk], dma=True)
        for k in range(8):
            if k % 2 == 0:
                op("vector", lambda e, k=k: e.tensor_scalar(out=wb[:, k, :], in0=sg[:, k, :], scalar1=gpt[:, 21 + k:22 + k], scalar2=None, op0=ALU.mult),
                   reads=[sgk, "gpt"], writes=[("grp", wbk, k)])
            else:
                op("scalar", lambda e, k=k: e.activation(out=wb[:, k, :], in_=sg[:, k, :], func=AF.Identity, scale=gpt[:, 21 + k:22 + k]),
                   reads=[sgk, "gpt"], writes=[("grp", wbk, k)])

    f1_weights(0)
    for j in range(22):
        bi = j % 2
        wb = Wub[bi]
        wbk = ("Wub", bi)
        if j + 1 < 22:
            f1_weights(j + 1)
        for half in range(2):
            cb = (cgb if half == 0 else cvb)[bi]
            ff = j + 22 * half
            for bx, (o0, n) in enumerate(OB):
                ck = ("cg" if half == 0 else "cv", bi, bx)
                pb = pbi % 8
                pbi += 1
                for k in range(8):
                    op("tensor", lambda e, k=k: e.matmul(bank(pb, n + 2), lhsT=wb[:, k, half * 128:(half + 1) * 128], rhs=hTf[:, k, o0:o0 + n + 2], start=(k == 0), stop=(k == 7)),
                       reads=[wbk, "hTf"], writes=[("ps", pb)])
                if bx == 0:
                    op("vector", lambda e: e.tensor_tensor(out=bank(pb, 1), in0=bank(pb, 1), in1=ufl[:, 0:1], op=ALU.mult), reads=[("ps", pb), "ufl"], writes=[("ps", pb)])
                if bx == 4:
                    op("vector", lambda e: e.tensor_tensor(out=bank(pb, n + 2)[:, n + 1:n + 2], in0=bank(pb, n + 2)[:, n + 1:n + 2], in1=ufl[:, 1:2], op=ALU.mult), reads=[("ps", pb), "ufl"], writes=[("ps", pb)])
                op("scalar", lambda e: e.activation(out=cb[:, o0:o0 + n], in_=bank(pb, n + 2)[:, 1:n + 1], func=AF.Identity, bias=cwt[:, ff, 3:4], scale=cwt[:, ff, 1:2]),
                   reads=[("ps", pb), "cwt"], writes=[ck])
                op("vector", lambda e: e.scalar_tensor_tensor(out=cb[:, o0:o0 + n], in0=bank(pb, n + 2)[:, 0:n], scalar=cwt[:, ff, 0:1], in1=cb[:, o0:o0 + n], op0=ALU.mult, op1=ALU.add),
                   reads=[("ps", pb), "cwt", ck], writes=[ck])
                op("vector", lambda e: e.scalar_tensor_tensor(out=cb[:, o0:o0 + n], in0=bank(pb, n + 2)[:, 2:n + 2], scalar=cwt[:, ff, 2:3], in1=cb[:, o0:o0 + n], op0=ALU.mult, op1=ALU.add),
                   reads=[("ps", pb), "cwt", ck], writes=[ck])
        cg, cv = cgb[bi], cvb[bi]
        cgk = [("cg", bi, bx) for bx in range(5)]
        cvk = [("cv", bi, bx) for bx in range(5)]
        op("scalar", lambda e: e.activation(out=cg, in_=cg, func=AF.Gelu_apprx_tanh), reads=cgk, writes=cgk)
        op("vector", lambda e: e.tensor_tensor(out=aT[:, j, :], in0=cg, in1=cv, op=ALU.mult), reads=cgk + cvk, writes=[("aT", j)])
    dump("aT", aT, [128, 22, OWN], BF16, [("aT", j) for j in range(22)])
    S.barrier()

    A = Region(RY.lo, SB_BYTES)
    Wdn = A.alloc([128, 22, D], BF16)
    stage5 = A.alloc([128, 2, D], F32)
    gpo2 = A.alloc([128, D], F32)
    xm2 = [A.alloc([128, D], F32) for _ in range(2)]
    ot = [A.alloc([128, D], F32) for _ in range(2)]
    for c in range(11):
        load_weight(Wdn[:, 2 * c:2 * c + 2, :], w_down[256 * c:256 * (c + 1), :], 2, D, None, stage5, "Wdn")
    op("sync", lambda e: e.dma_start(out=gpo2, in_=gpost[:, 1, :]), writes=["gpo2"], dma=True)
    for t in range(16):
        bi = t % 2
        xk, ok = ("xm2", bi), ("ot", bi)
        op("sync", lambda e, t=t, bi=bi: e.dma_start(out=xm2[bi], in_=xmid[128 * t:128 * (t + 1), :]), reads=["xmid"], writes=[xk], dma=True)
        pb = 2 * (t % 2)
        for n2 in range(2):
            for j in range(22):
                op("tensor", lambda e, j=j, n2=n2, t=t, pb=pb: e.matmul(bank(pb + n2), lhsT=aT[:, j, 128 * t:128 * (t + 1)], rhs=Wdn[:, j, n2 * 512:(n2 + 1) * 512], start=(j == 0), stop=(j == 21)),
                   reads=[("aT", j), "Wdn"], writes=[("ps", pb + n2)])
        yv = ps[:, pb * 512:(pb + 2) * 512]
        sc = 144 + 2 * (t % 4)
        sk = ("stt7", t % 4)
        op("scalar", lambda e, yv=yv, sc=sc: e.activation(out=junk, in_=yv, func=AF.Square, accum_out=stt[:, sc:sc + 1]),
           reads=[("ps", pb), ("ps", pb + 1)], writes=["junk", sk])
        op("scalar", lambda e, sc=sc: e.activation(out=stt[:, sc:sc + 1], in_=stt[:, sc:sc + 1], func=AF.Sqrt, bias=EPS, scale=1.0 / D), reads=[sk], writes=[sk])
        op("vector", lambda e, sc=sc: e.reciprocal(out=stt[:, sc + 1:sc + 2], in_=stt[:, sc:sc + 1]), reads=[sk], writes=[sk])
        op("vector", lambda e, yv=yv, sc=sc, bi=bi: e.scalar_tensor_tensor(out=ot[bi], in0=yv, scalar=stt[:, sc + 1:sc + 2], in1=gpo2, op0=ALU.mult, op1=ALU.mult),
           reads=[("ps", pb), ("ps", pb + 1), sk, "gpo2"], writes=[ok])
        op("vector", lambda e, bi=bi: e.tensor_tensor(out=ot[bi], in0=ot[bi], in1=xm2[bi], op=ALU.add), reads=[ok, xk], writes=[ok])
        op("sync", lambda e, t=t, bi=bi: e.dma_start(out=yout[128 * t:128 * (t + 1), :], in_=ot[bi]), reads=[ok], dma=True)
    S.emit()
    return nc


_CACHE = {}


def _consts():
    if "c" in _CACHE:
        return _CACHE["c"]
    slopes = np.exp2(-8.0 * np.arange(1, 9, dtype=np.float32) / 8).astype(np.float32)
    k = np.arange(128)[:, None]
    q = np.arange(128)[None, :]
    masks = np.zeros((8, 128, 6, 128), np.float32)
    for h in range(8):
        for ri, r in enumerate((1, 4, 16)):
            d0 = k - 64 - q
            d1 = k + 64 - q
            masks[h, :, 2 * ri, :] = np.where(k >= q, np.exp(-slopes[h] * (np.abs(d0) * r).astype(np.float32)), 0.0)
            masks[h, :, 2 * ri + 1, :] = np.where(k <= q, np.exp(-slopes[h] * (np.abs(d1) * r).astype(np.float32)), 0.0)
    inv_freq = np.exp(-np.log(10000.0) * np.arange(0, 32, 2, dtype=np.float32) / 32).astype(np.float32)
    pos = np.arange(S_LEN, dtype=np.float32)
    ang = pos[:, None] * inv_freq[None, :]
    cosk = np.cos(ang).astype(np.float32).reshape(64, 128, 16).transpose(1, 0, 2)
    sink = np.sin(ang).astype(np.float32).reshape(64, 128, 16).transpose(1, 0, 2)
    rk = np.ascontiguousarray(np.stack([cosk, sink], axis=1))
    c = dict(masks=masks, inv_freq=inv_freq, rk=rk, ident=np.eye(128, dtype=np.float32))
    _CACHE["c"] = c
    return c


def _core_inputs(c, x, shared):
    cst = _consts()
    b, qc = c // 4, c % 4
    T0 = qc * OWN
    pos_w = T0 - OWN0 + np.arange(NW)
    valid = (pos_w >= 0) & (pos_w < S_LEN)
    xw = np.zeros((NW, D), np.float32)
    xw[valid] = x[b, pos_w[valid]]
    kvf = np.zeros((128, NVT), np.float32)
    for i, (r, es, nk) in enumerate(VLIST):
        kvf[:nk, i] = valid[es + r * np.arange(nk)].astype(np.float32)
    posq = (T0 - 1 + np.arange(NX)).astype(np.float32)
    ang = posq[None, :] * cst["inv_freq"][:, None]
    cq, sq = np.cos(ang).astype(np.float32), np.sin(ang).astype(np.float32)
    rq = np.ascontiguousarray(np.stack([np.concatenate([cq, cq], 0), np.concatenate([sq, sq], 0)], axis=1))
    uflag = np.zeros((128, 2), np.float32)
    uflag[:, 0] = 1.0 if T0 > 0 else 0.0
    uflag[:, 1] = 1.0 if T0 + OWN < S_LEN else 0.0
    d = dict(shared)
    d.update(xw=xw, xf=np.ascontiguousarray(x[b]), kvf=kvf, rq=rq, uflag=uflag, masks=cst["masks"], rk=cst["rk"], ident=cst["ident"])
    return d


def kernel(x, norm_mix_pre, w_in, q_lat_norm, w_uq, kv_lat_norm, w_ukv, out_norm_a, out_norm_b, w_o,
           norm_mix_post, norm_ffn_pre, w_up, conv_w, conv_b, w_down, norm_ffn_post):
    f = lambda a: np.ascontiguousarray(np.asarray(a, dtype=np.float32))
    x = f(x)
    gp = np.zeros((128, 32), np.float32)
    gp[:, 0:8] = f(norm_mix_pre)[0].reshape(8, 128).T
    gp[:, 8:11] = f(q_lat_norm)[0].reshape(3, 128).T
    gp[:, 11:13] = f(kv_lat_norm)[0].reshape(2, 128).T
    gp[:, 13:21] = np.concatenate([f(out_norm_a)[0], f(out_norm_b)[0]]).reshape(8, 128).T
    gp[:, 21:29] = f(norm_ffn_pre)[0].reshape(8, 128).T
    gpost = np.ascontiguousarray(np.broadcast_to(np.stack([f(norm_mix_post)[0], f(norm_ffn_post)[0]])[None], (128, 2, D)))
    cwb = np.zeros((128, 44, 4), np.float32)
    cwb[:, :, 0:3] = f(conv_w)[0].T.reshape(44, 128, 3).transpose(1, 0, 2)
    cwb[:, :, 3] = f(conv_b)[0].reshape(44, 128).T
    shared = dict(w_in=f(w_in)[0], w_uq=f(w_uq)[0], w_ukv=f(w_ukv)[0], w_o=f(w_o)[0], w_up=f(w_up)[0], w_down=f(w_down)[0],
                  gp=gp, gpost=gpost, cwb=cwb)
    if "nc" not in _CACHE:
        _CACHE["nc"] = build()
    nc = _CACHE["nc"]
    in_maps = [_core_inputs(c, x, shared) for c in range(8)]
    res = run_bass_kernel_spmd(nc, in_maps, core_ids=list(range(8)))
    out = np.zeros((2, S_LEN, D), np.float32)
    for c in range(8):
        b, qc = c // 4, c % 4
        out[b, qc * OWN:(qc + 1) * OWN] = res.results[c]["y"]
    return out
```

```python
import contextlib
import types
import numpy as np
import ml_dtypes
import concourse.bass as bass
import concourse.mybir as mybir
from concourse.bass_utils import run_bass_kernel_spmd

F32 = mybir.dt.float32
BF16 = mybir.dt.bfloat16
U8 = mybir.dt.uint8
AF = mybir.ActivationFunctionType
ALU = mybir.AluOpType

S_LEN = 8192
D = 1024
OWN = 2048
OWN0 = 1152
NW = 4352
NX = 2050
DFF = 2816
EPS = 1e-6
ENGS = ("sync", "scalar", "gpsimd", "vector", "tensor")
XBLK = [(i * 410, 410) for i in range(5)]


class Sched:
    def __init__(self, nc, ndma_sems=8):
        self.nc = nc
        self.ops = []
        self.ndma = ndma_sems

    @staticmethod
    def _freeze(fn):
        if fn.__closure__ is None:
            return fn
        cells = []
        for c in fn.__closure__:
            try:
                cells.append(types.CellType(c.cell_contents))
            except ValueError:
                cells.append(c)
        return types.FunctionType(fn.__code__, fn.__globals__, fn.__name__, fn.__defaults__, tuple(cells))

    def op(self, eng, fn, reads=(), writes=(), dma=False):
        fn = self._freeze(fn)
        self.ops.append(dict(eng=eng, fn=fn, reads=tuple(reads), writes=tuple(writes), dma=dma, bar=False))

    def barrier(self):
        self.ops.append(dict(eng=None, fn=None, reads=(), writes=(), dma=False, bar=True))

    def emit(self, final_wait_eng="sync"):
        nc = self.nc
        ops = self.ops
        n = len(ops)
        groups = {}
        for o in ops:
            for k in o["reads"] + o["writes"]:
                if isinstance(k, tuple) and len(k) == 3 and k[0] == "grp":
                    groups.setdefault(k[1], set()).add(k)

        def expand(keys):
            out = []
            for k in keys:
                out.append(k)
                if k in groups:
                    out.extend(groups[k])
            return out

        last_writer = {}
        readers = {}
        deps = [dict() for _ in range(n)]
        since_bar = []
        pending_bar = {}
        for i, o in enumerate(ops):
            if o["bar"]:
                lastc = {}
                dl = set()
                for j in since_bar:
                    if ops[j]["dma"]:
                        dl.add(j)
                    else:
                        lastc[ops[j]["eng"]] = j
                dl.update(lastc.values())
                for e in ENGS:
                    pending_bar[e] = set(dl) | pending_bar.get(e, set())
                since_bar = []
                continue
            d = deps[i]
            if o["eng"] in pending_bar:
                for j in pending_bar.pop(o["eng"]):
                    d[j] = True
            rk, wk = expand(o["reads"]), expand(o["writes"])
            for r in rk:
                if r in last_writer:
                    d[last_writer[r]] = True
            for w in wk:
                if w in last_writer:
                    d.setdefault(last_writer[w], False)
                for j in readers.get(w, ()):
                    d.setdefault(j, False)
            d.pop(i, None)
            for w in wk:
                last_writer[w] = i
                readers[w] = []
            for r in rk:
                if r not in wk:
                    readers.setdefault(r, []).append(i)
            since_bar.append(i)
        needed = set()
        red = [None] * n
        for i, o in enumerate(ops):
            if o["bar"]:
                continue
            per_eng = {}
            dl = []
            for j, is_raw in deps[i].items():
                pj = ops[j]
                if pj["dma"]:
                    dl.append(j)
                    continue
                if pj["eng"] == o["eng"] and not o["dma"] and (o["eng"] == "tensor" or not is_raw):
                    continue
                e = pj["eng"]
                if e not in per_eng or per_eng[e] < j:
                    per_eng[e] = j
            dl.extend(per_eng.values())
            red[i] = dl
            needed.update(dl)
        cnt = {e: 0 for e in ENGS}
        dcnt = {}
        sig = [None] * n
        dma_idx = {e: 0 for e in ENGS}
        for i, o in enumerate(ops):
            if o["bar"]:
                continue
            if o["dma"]:
                k = dma_idx[o["eng"]] % self.ndma
                dma_idx[o["eng"]] += 1
                key = ("dma", o["eng"], k)
                prev = dcnt.get(key, 0)
                dcnt[key] = prev + 16
                sig[i] = (key, prev + 16)
                o["dma_prev"] = (key, prev) if prev > 0 else None
            elif i in needed:
                cnt[o["eng"]] += 1
                sig[i] = (("eng", o["eng"]), cnt[o["eng"]])
        semkeys = sorted({s[0] for s in sig if s is not None}, key=str)
        stack = contextlib.ExitStack()
        sems = {}
        for sk in semkeys:
            sems[sk] = stack.enter_context(nc.semaphore("s_" + "_".join(str(x) for x in sk)))
        by_eng = {e: [i for i, o in enumerate(ops) if o["eng"] == e] for e in ENGS}
        dma_final = list(dcnt.items())

        def run(engname, eng):
            waited = {}
            for i in by_eng[engname]:
                o = ops[i]
                wl = [sig[j] for j in red[i]]
                if o["dma"] and o.get("dma_prev"):
                    wl.append(o["dma_prev"])
                for (sk, v) in wl:
                    if waited.get(sk, 0) >= v:
                        continue
                    eng.wait_ge(sems[sk], v)
                    waited[sk] = v
                ins = o["fn"](eng)
                if sig[i] is not None:
                    ins.then_inc(sems[sig[i][0]], 16 if o["dma"] else 1)
            if engname == final_wait_eng:
                for sk, v in dma_final:
                    if waited.get(sk, 0) < v:
                        eng.wait_ge(sems[sk], v)
                for e in ENGS:
                    if cnt[e] > 0 and waited.get(("eng", e), 0) < cnt[e]:
                        eng.wait_ge(sems[("eng", e)], cnt[e])

        with stack:
            with nc.Block() as block:
                @block.sync
                def _(e):
                    run("sync", e)

                @block.scalar
                def _(e):
                    run("scalar", e)

                @block.gpsimd
                def _(e):
                    run("gpsimd", e)

                @block.vector
                def _(e):
                    run("vector", e)

                @block.tensor
                def _(e):
                    run("tensor", e)


def dil_tables():
    vt = {}

    def vtile(r, e0, nk):
        key = (r, e0, nk)
        if key not in vt:
            vt[key] = len(vt)
        return vt[key]

    groups = {1: [], 4: [], 16: []}
    for r in (1, 4, 16):
        def blk(x0, N):
            eq0 = OWN0 - 1 + x0
            t0 = vtile(r, eq0 - 64 * r, 128)
            t1 = vtile(r, eq0 + 64 * r, 128 if N > 1 else 1)
            return (x0, N, t0, t1)
        if r == 1:
            for g in range(4):
                groups[r].append(("reg", g, [blk(1 + 128 * (4 * g + b), 128) for b in range(4)]))
        elif r == 4:
            for g in range(4):
                groups[r].append(("reg", g, [blk(1 + c + 512 * g, 128) for c in range(4)]))
        else:
            for g in range(4):
                groups[r].append(("reg", g, [blk(1 + 4 * g + c, 128) for c in range(4)]))
        groups[r].append(("halo", 0, [blk(0, 1), blk(NX - 1, 1)]))
    vlist = [None] * len(vt)
    for k, i in vt.items():
        vlist[i] = k
    return vlist, groups


VLIST, DGROUPS = dil_tables()
NVT = len(VLIST)


def build(dbg=False):
    nc = bass.Bass("TRN2", target_bir_lowering=False)

    def din(name, shape, dt=F32):
        return nc.dram_tensor(name, list(shape), dt, kind="ExternalInput").ap()

    xw = din("xw", [NW, D])
    xf = din("xf", [S_LEN, D])
    w_in = din("w_in", [D, 2208])
    w_uq = din("w_uq", [384, 768])
    w_ukv = din("w_ukv", [256, 1024])
    w_o = din("w_o", [D, D])
    w_up = din("w_up", [D, 2 * DFF])
    w_down = din("w_down", [DFF, D])
    gp = din("gp", [128, 32])
    gpost = din("gpost", [128, 2, D])
    cwb = din("cwb", [128, 44, 4])
    masks = din("masks", [8, 128, 6, 128])
    kvf = din("kvf", [128, NVT])
    rq = din("rq", [32, 2, NX])
    rk = din("rk", [128, 2, 64, 16])
    ident = din("ident", [128, 128])
    uflag = din("uflag", [128, 2])
    yout = nc.dram_tensor("y", [OWN, D], F32, kind="ExternalOutput").ap()
    xmid = nc.dram_tensor("xmid", [OWN, D], F32, kind="Internal").ap()
    dbg_out = {}

    SB_BYTES = 212000
    big = nc.alloc_sbuf_tensor("big", [128, SB_BYTES], U8).ap()
    ps = nc.alloc_psum_tensor("ps", [128, 4096], F32).ap()
    psb = ps.bitcast(BF16)

    class Region:
        def __init__(self, lo, hi):
            assert hi <= SB_BYTES and lo <= hi, (lo, hi)
            self.lo, self.hi, self.off = lo, hi, lo

        def alloc(self, shape, dt, p0=0):
            esz = 4 if dt == F32 else 2
            nb = int(np.prod(shape[1:])) * esz
            nb_al = (nb + 63) // 64 * 64
            assert self.off + nb_al <= self.hi, ("SBUF region overflow", self.lo, self.hi, self.off, nb_al)
            v = big[p0:p0 + shape[0], self.off:self.off + nb].bitcast(dt)
            self.off += nb_al
            if len(shape) == 3:
                v = v.rearrange("p (a b) -> p a b", a=shape[1])
            elif len(shape) == 4:
                v = v.rearrange("p (a b c) -> p a b c", a=shape[1], b=shape[2])
            return v

    RP = Region(0, 4096)
    R1 = Region(RP.hi, RP.hi + 86080)
    RC = Region(R1.hi, R1.hi + 12352)
    RY = Region(RC.hi, RC.hi + 16448 + 4096 + 16448)
    RT = Region(RY.hi, SB_BYTES)
    A = RP
    S = Sched(nc)
    op = S.op

    def dump(name, ap, shape, dt, keys):
        if not dbg:
            return
        import os
        sel = os.environ.get("DBGSEL", "")
        if sel and name not in sel.split(","):
            return
        d = nc.dram_tensor("dbg_" + name, list(shape), dt, kind="ExternalOutput").ap()
        op("sync", lambda e: e.dma_start(out=d, in_=ap), reads=keys, dma=True)

    def bank(b, n=512, p=128, p0=0):
        return ps[p0:p0 + p, b * 512:b * 512 + n]

    def bankb(b, n=1024, p=128, p0=0):
        return psb[p0:p0 + p, b * 1024:b * 1024 + n]

    idf = A.alloc([128, 128], F32)
    idb = A.alloc([128, 128], BF16)
    gpt = A.alloc([128, 32], F32)
    stt = A.alloc([128, 256], F32)
    junk = A.alloc([128, 1024], BF16)
    rec = A.alloc([128, 8], F32)
    op("sync", lambda e: e.dma_start(out=idf, in_=ident), writes=["idf"], dma=True)
    op("sync", lambda e: e.dma_start(out=gpt, in_=gp), writes=["gpt"], dma=True)
    op("vector", lambda e: e.tensor_copy(out=idb, in_=idf), reads=["idf"], writes=["idb"])

    def rstd_from_ss(ss_ap, out_ap, n, key):
        tmp = ss_ap
        op("scalar", lambda e: e.activation(out=tmp, in_=ss_ap, func=AF.Sqrt, bias=EPS, scale=1.0 / n),
           reads=[key], writes=[key])
        op("vector", lambda e: e.reciprocal(out=out_ap, in_=tmp), reads=[key], writes=[key])

    def load_weight(dst, src_ap, kch, ncols, gcol, stage, wkey, negate_cols=None):
        op("gpsimd", lambda e: e.dma_start(out=stage[:, 0:kch, 0:ncols], in_=src_ap.rearrange("(k p) n -> p k n", p=128)),
           writes=[("stage", id(stage))], dma=True)
        for k in range(kch):
            eng = "vector" if k % 2 == 0 else "scalar"
            if gcol is None:
                if eng == "vector":
                    op(eng, lambda e, k=k: e.tensor_copy(out=dst[:, k, :], in_=stage[:, k, 0:ncols]),
                       reads=[("stage", id(stage))], writes=[("grp", wkey, (id(dst), k))])
                else:
                    op(eng, lambda e, k=k: e.copy(out=dst[:, k, :], in_=stage[:, k, 0:ncols]),
                       reads=[("stage", id(stage))], writes=[("grp", wkey, (id(dst), k))])
            else:
                if eng == "vector":
                    op(eng, lambda e, k=k: e.tensor_scalar(out=dst[:, k, :], in0=stage[:, k, 0:ncols],
                                                          scalar1=gpt[:, gcol + k:gcol + k + 1], scalar2=None, op0=ALU.mult),
                       reads=[("stage", id(stage)), "gpt"], writes=[("grp", wkey, (id(dst), k))])
                else:
                    op(eng, lambda e, k=k: e.activation(out=dst[:, k, :], in_=stage[:, k, 0:ncols], func=AF.Identity,
                                                        scale=gpt[:, gcol + k:gcol + k + 1]),
                       reads=[("stage", id(stage)), "gpt"], writes=[("grp", wkey, (id(dst), k))])

    def norm_tiles_to_T(src_dram_rows, ntile, xbuf, xnbuf, key, slot=0):
        c0 = 2 * slot
        sk = ("sttn", slot)
        op("sync", lambda e: e.dma_start(out=xbuf[:, 0:ntile, :], in_=src_dram_rows.rearrange("(t p) f -> p t f", p=128)),
           writes=[("x", key)], dma=True)
        for t in range(ntile):
            op("scalar", lambda e, t=t: e.activation(out=junk, in_=xbuf[:, t, :], func=AF.Square, accum_out=stt[:, c0 + t:c0 + t + 1]),
               reads=[("x", key)], writes=["junk", sk])
        rstd_from_ss(stt[:, c0:c0 + ntile], stt[:, 8 + c0:8 + c0 + ntile], D, sk)
        for t in range(ntile):
            if t % 2 == 0:
                op("vector", lambda e, t=t: e.tensor_scalar(out=xnbuf[:, t, :], in0=xbuf[:, t, :], scalar1=stt[:, 8 + c0 + t:9 + c0 + t], scalar2=None, op0=ALU.mult),
                   reads=[("x", key), sk], writes=[("grp", ("xn", key), t)])
            else:
                op("scalar", lambda e, t=t: e.activation(out=xnbuf[:, t, :], in_=xbuf[:, t, :], func=AF.Identity, scale=stt[:, 8 + c0 + t:9 + c0 + t]),
                   reads=[("x", key), sk], writes=[("grp", ("xn", key), t)])

    def transpose_tile(xn_tile, hT_dst, pbank, rkeys, wkey):
        for k in range(8):
            op("tensor", lambda e, k=k: e.transpose(out=bankb(pbank)[:, k * 128:(k + 1) * 128], in_=xn_tile[:, k * 128:(k + 1) * 128], identity=idb),
               reads=list(rkeys) + ["idb"], writes=[("ps", pbank)])
        op("vector", lambda e: e.tensor_copy(out=hT_dst, in_=bankb(pbank).rearrange("p (k t) -> p k t", k=8)),
           reads=[("ps", pbank)], writes=[wkey])

    KAT = R1.alloc([128, 4, NW], BF16)
    VAT = R1.alloc([128, 4, NW], BF16)
    QAT = R1.alloc([128, 4, NX], BF16)
    cqT = RC.alloc([128, 3, NX], BF16)
    A = Region(RC.hi, SB_BYTES)
    WA = A.alloc([128, 8, 1920], BF16)
    stage = A.alloc([128, 8, 480], F32)
    xbuf = [A.alloc([128, 2, D], F32) for _ in range(2)]
    xnb = [A.alloc([128, 2, D], BF16) for _ in range(2)]
    hTw = [A.alloc([128, 8, 256], BF16) for _ in range(2)]
    cqn = A.alloc([128, 384], BF16)
    for c in range(4):
        load_weight(WA[:, :, c * 480:(c + 1) * 480], w_in[:, c * 480:(c + 1) * 480], 8, 480, 0, stage, "WA")

    def cq_tile(cols_ap, M, xcol_ap_fn, hkey, pb):
        for k in range(8):
            la = cols_ap(k)
            op("tensor", lambda e, k=k, la=la: e.matmul(bank(pb, 384, M), lhsT=la, rhs=WA[:, k, 1536:1920], start=(k == 0), stop=(k == 7)),
               reads=[hkey, "WA"], writes=[("ps", pb)])
        op("scalar", lambda e: e.activation(out=junk[0:M, 0:384], in_=bank(pb, 384, M), func=AF.Square, accum_out=stt[0:M, 16:17]),
           reads=[("ps", pb)], writes=["junk", "stt2"])
        op("scalar", lambda e: e.activation(out=stt[0:M, 16:17], in_=stt[0:M, 16:17], func=AF.Sqrt, bias=EPS, scale=1.0 / 384),
           reads=["stt2"], writes=["stt2"])
        op("vector", lambda e: e.reciprocal(out=stt[0:M, 17:18], in_=stt[0:M, 16:17]), reads=["stt2"], writes=["stt2"])
        op("vector", lambda e: e.tensor_scalar(out=cqn[0:M, :], in0=bank(pb, 384, M), scalar1=stt[0:M, 17:18], scalar2=None, op0=ALU.mult),
           reads=[("ps", pb), "stt2"], writes=["cqn"])
        for j in range(3):
            op("tensor", lambda e, j=j: e.transpose(out=bankb(pb)[:, j * 128:j * 128 + M], in_=cqn[0:M, j * 128:(j + 1) * 128], identity=idb[0:M, 0:M]),
               reads=["cqn", "idb"], writes=[("ps", pb)])
        xc = xcol_ap_fn()
        op("vector", lambda e: e.tensor_copy(out=xc, in_=bankb(pb)[:, 0:384].rearrange("p (j t) -> p j t", j=3)[:, :, 0:M]),
           reads=[("ps", pb)], writes=["cqT"])

    NGW = NW // 256

    xbuf.append(A.alloc([128, 2, D], F32))

    def ws0(g):
        xb = xbuf[g % 3]
        e0 = g * 256
        op("sync", lambda e: e.dma_start(out=xb, in_=xw[e0:e0 + 256, :].rearrange("(t p) f -> p t f", p=128)), writes=[("wx", g % 3)], dma=True)

    def ws1(g):
        xb = xbuf[g % 3]
        c = 176 + 2 * (g % 4)
        for t in range(2):
            op("scalar", lambda e, t=t: e.activation(out=junk, in_=xb[:, t, :], func=AF.Square, accum_out=stt[:, c + t:c + t + 1]),
               reads=[("wx", g % 3)], writes=["junk", ("wss", g % 4)])
        op("scalar", lambda e: e.activation(out=stt[:, c:c + 2], in_=stt[:, c:c + 2], func=AF.Sqrt, bias=EPS, scale=1.0 / D), reads=[("wss", g % 4)], writes=[("wss", g % 4)])

    def ws2(g):
        xb = xbuf[g % 3]
        xn = xnb[g % 2]
        c = 176 + 2 * (g % 4)
        op("vector", lambda e: e.reciprocal(out=stt[:, c + 8:c + 10], in_=stt[:, c:c + 2]), reads=[("wss", g % 4)], writes=[("wrs", g % 4)])
        for t in range(2):
            op("vector", lambda e, t=t: e.tensor_scalar(out=xn[:, t, :], in0=xb[:, t, :], scalar1=stt[:, c + 8 + t:c + 9 + t], scalar2=None, op0=ALU.mult),
               reads=[("wx", g % 3), ("wrs", g % 4)], writes=[("wxn", g % 2)])

    def ws3(g):
        xn = xnb[g % 2]
        for t in range(2):
            pb = 2 * (g % 2) + t
            for k in range(8):
                op("tensor", lambda e, k=k, t=t: e.transpose(out=bankb(pb)[:, k * 128:(k + 1) * 128], in_=xn[:, t, k * 128:(k + 1) * 128], identity=idb),
                   reads=[("wxn", g % 2), "idb"], writes=[("ps", pb)])

    def ws4(g):
        bi = g % 2
        for t in range(2):
            pb = 2 * (g % 2) + t
            if t == 0:
                op("vector", lambda e, t=t: e.tensor_copy(out=hTw[bi][:, :, t * 128:(t + 1) * 128], in_=bankb(pb).rearrange("p (k t) -> p k t", k=8)),
                   reads=[("ps", pb)], writes=[("grp", ("hTw", bi), t)])
            else:
                op("scalar", lambda e, t=t: e.copy(out=hTw[bi][:, :, t * 128:(t + 1) * 128], in_=bankb(pb).rearrange("p (k t) -> p k t", k=8)),
                   reads=[("ps", pb)], writes=[("grp", ("hTw", bi), t)])

    def w_stageB(g):
        bi = g % 2
        e0 = g * 256
        xlo, xhi = max(e0, OWN0 - 1), min(e0 + 256, OWN0 - 1 + NX)
        jobs = [("K", 512 + 128 * c, c) for c in range(4)] + [("V", 1024 + 128 * c, c) for c in range(4)]
        if xhi > xlo:
            jobs += [("Q", 128 * c, c) for c in range(4)]
        for ji, (kind, col0, c) in enumerate(jobs):
            pb = 4 + ji % 2
            pk = ("ps", pb)
            pv = bank(pb, 256)
            for k in range(8):
                op("tensor", lambda e, k=k: e.matmul(pv, lhsT=WA[:, k, col0:col0 + 128], rhs=hTw[bi][:, k, :], start=(k == 0), stop=(k == 7)),
                   reads=[("hTw", bi), "WA"], writes=[pk])
            if kind == "K":
                op("vector", lambda e: e.tensor_copy(out=KAT[:, c, e0:e0 + 256], in_=pv), reads=[pk], writes=["KAT"])
            elif kind == "V":
                op("scalar", lambda e: e.copy(out=VAT[:, c, e0:e0 + 256], in_=pv), reads=[pk], writes=["VAT"])
            else:
                op("vector", lambda e: e.tensor_copy(out=QAT[:, c, xlo - (OWN0 - 1):xhi - (OWN0 - 1)], in_=pv[:, xlo - e0:xhi - e0]),
                   reads=[pk], writes=["QAT"])
        for t in range(2):
            et = e0 + t * 128
            if OWN0 <= et < OWN0 + OWN:
                x0 = et - (OWN0 - 1)
                cq_tile(lambda k, t=t, bi=bi: hTw[bi][:, k, t * 128:(t + 1) * 128], 128, lambda x0=x0: cqT[:, :, x0:x0 + 128], ("hTw", bi), 6 + t)
            if et == OWN0 - 128:
                cq_tile(lambda k, t=t, bi=bi: hTw[bi][:, k, t * 128 + 127:t * 128 + 128], 1, lambda: cqT[:, :, 0:1], ("hTw", bi), 6 + t)
            if et == OWN0 + OWN:
                cq_tile(lambda k, t=t, bi=bi: hTw[bi][:, k, t * 128:t * 128 + 1], 1, lambda: cqT[:, :, NX - 1:NX], ("hTw", bi), 6 + t)

    wst = [ws0, ws1, ws2, ws3, ws4, w_stageB]
    for it in range(NGW + len(wst) - 1):
        for k in range(len(wst) - 1, -1, -1):
            g = it - k
            if 0 <= g < NGW:
                wst[k](g)
    dump("KAT", KAT, [128, 4, NW], BF16, ["KAT"])
    dump("VAT", VAT, [128, 4, NW], BF16, ["VAT"])
    dump("QAT", QAT, [128, 4, NX], BF16, ["QAT"])
    dump("cqT", cqT, [128, 3, NX], BF16, ["cqT"])
    S.barrier()

    ynTa = RY.alloc([128, 4, NX], BF16)
    A = Region(RY.lo + 16448, SB_BYTES)
    mk = [A.alloc([128, 6, 128], F32) for _ in range(2)]
    kvft = A.alloc([128, NVT], F32)
    Vp = [A.alloc([128, NVT, 65], BF16) for _ in range(2)]
    Oacc = A.alloc([128, NX], F32)
    ya_tm = A.alloc([128, 17, 512], F32)
    Eb = [A.alloc([128, 1024], F32) for _ in range(2)]
    Pb = [A.alloc([128, 1024], BF16) for _ in range(2)]
    op("sync", lambda e: e.dma_start(out=kvft, in_=kvf), writes=["kvft"], dma=True)

    def oacc_to_tm(h, dst_tm, okey, Oacc):
        tiles = [(1 + 128 * m, 128, 1) for m in range(16)] + [(0, 2, NX - 1)]
        for g0 in range(0, 17, 4):
            tl = tiles[g0:g0 + 4]
            pb = 6 + (g0 // 4) % 2
            for i, (x0, M, step) in enumerate(tl):
                src = Oacc[0:65, x0:x0 + 128] if step == 1 else Oacc[0:65, 0:NX:NX - 1]
                op("tensor", lambda e, i=i, src=src, M=M, pb=pb: e.transpose(out=bank(pb)[0:M, i * 65:(i + 1) * 65], in_=src, identity=idf[0:65, 0:65]),
                   reads=[okey, "idf"], writes=[("ps", pb)])
            nt = len(tl)
            M = tl[0][1]
            pv = bank(pb)[0:M, 0:nt * 65].rearrange("p (t c) -> p t c", t=nt)
            op("vector", lambda e, pv=pv, nt=nt, M=M: e.reciprocal(out=rec[0:M, 0:nt], in_=pv[:, :, 64]), reads=[("ps", pb)], writes=["rec"])
            op("vector", lambda e, pv=pv, nt=nt, M=M, g0=g0: e.tensor_tensor(out=dst_tm[0:M, g0:g0 + nt, h * 64:(h + 1) * 64], in0=pv[:, :, 0:64],
                                                                         in1=rec[0:M, 0:nt].unsqueeze(2).to_broadcast([M, nt, 64]), op=ALU.mult),
               reads=[("ps", pb), "rec"], writes=["tm"])

    def tm_to_ynT(src_tm, dstT, nkey, Pb):
        for t in range(17):
            M = 128 if t < 16 else 2
            op("scalar", lambda e, t=t, M=M: e.activation(out=junk[0:M, 0:512], in_=src_tm[0:M, t, :], func=AF.Square, accum_out=stt[0:M, 32 + t:33 + t]),
               reads=["tm"], writes=["junk", "stt3"])
        rstd_from_ss(stt[:, 32:49], stt[:, 64:81], 512, "stt3")
        for t in range(17):
            M = 128 if t < 16 else 2
            pb = t % 2
            ynb = Pb[t % 2]
            op("vector", lambda e, t=t, M=M, ynb=ynb: e.tensor_scalar(out=ynb[0:M, 0:512], in0=src_tm[0:M, t, :], scalar1=stt[0:M, 64 + t:65 + t], scalar2=None, op0=ALU.mult),
               reads=["tm", "stt3"], writes=[("Pb", t % 2)])
            for j in range(4):
                op("tensor", lambda e, j=j, M=M, ynb=ynb, pb=pb: e.transpose(out=bankb(pb)[:, j * 128:j * 128 + M], in_=ynb[0:M, j * 128:(j + 1) * 128], identity=idb[0:M, 0:M]),
                   reads=[("Pb", t % 2), "idb"], writes=[("ps", pb)])
            srcv = bankb(pb)[:, 0:512].rearrange("p (j t) -> p j t", j=4)[:, :, 0:M]
            if t < 16:
                dstv = dstT[:, :, 1 + 128 * t:1 + 128 * (t + 1)]
            else:
                dstv = dstT[:, :, 0:NX:NX - 1]
            op("vector", lambda e, srcv=srcv, dstv=dstv: e.tensor_copy(out=dstv, in_=srcv), reads=[("ps", pb)], writes=[nkey])

    def d_prep(h):
        pr, hs = h // 2, (h % 2) * 64
        vb = Vp[h % 2]
        mb = mk[h % 2]
        op("sync", lambda e: e.dma_start(out=mb, in_=masks[h]), writes=[("mk", h % 2)], dma=True)
        for v0 in range(0, NVT, 8):
            vts = VLIST[v0:v0 + 8]
            pb = 4 + (v0 // 8) % 2
            for i, (r, es, nk) in enumerate(vts):
                op("tensor", lambda e, i=i, r=r, es=es, nk=nk: e.transpose(out=bankb(pb)[0:nk, i * 64:(i + 1) * 64],
                                                                        in_=VAT[hs:hs + 64, pr, es:es + r * (nk - 1) + 1:r], identity=idb[hs:hs + 64, hs:hs + 64]),
                   reads=["VAT", "idb"], writes=[("ps", pb)])
            i = 0
            while i < len(vts):
                nk = vts[i][2]
                j = i
                while j + 1 < len(vts) and vts[j + 1][2] == nk:
                    j += 1
                cnt_ = j - i + 1
                srcv = bankb(pb)[0:nk, i * 64:(j + 1) * 64].rearrange("p (t c) -> p t c", t=cnt_)
                op("vector", lambda e, srcv=srcv, i=i, cnt_=cnt_, nk=nk: e.tensor_tensor(
                    out=vb[0:nk, v0 + i:v0 + i + cnt_, 0:64], in0=srcv,
                    in1=kvft[0:nk, v0 + i:v0 + i + cnt_].unsqueeze(2).to_broadcast([nk, cnt_, 64]), op=ALU.mult),
                   reads=[("ps", pb), "kvft"], writes=[("Vp", h % 2)])
                i = j + 1
        op("gpsimd", lambda e: e.tensor_copy(out=vb[:, :, 64], in_=kvft), reads=["kvft"], writes=[("Vp", h % 2)])

    GL = [(ri, r, kind, g, blks) for ri, r in enumerate((1, 4, 16)) for (kind, g, blks) in DGROUPS[r]]
    gctr = [0]

    def d_scores(h, gi, grp):
        pr, hs = h // 2, (h % 2) * 64
        ri, r, kind, g, blks = grp
        sb = 2 * (gi % 2)
        for bi_, (x0, Nq, t0, t1) in enumerate(blks):
            qv = QAT[hs:hs + 64, pr, x0:x0 + r * (Nq - 1) + 1:r]
            for role, tix in ((0, t0), (1, t1)):
                (rr, es, nk) = VLIST[tix]
                op("tensor", lambda e, role=role, es=es, nk=nk, qv=qv, bi_=bi_, Nq=Nq: e.matmul(
                    bank(sb + role)[0:nk, bi_ * 128:bi_ * 128 + Nq], lhsT=KAT[hs:hs + 64, pr, es:es + r * (nk - 1) + 1:r], rhs=qv, start=True, stop=True),
                   reads=["KAT", "QAT"], writes=[("ps", sb + role)])

    def d_rest(h, gi, grp):
        ri, r, kind, g, blks = grp
        vb = Vp[h % 2]
        mb = mk[h % 2]
        sb = 2 * (gi % 2)
        eb = Eb[gi % 2]
        pbuf = Pb[gi % 2]
        ob = 6 + gi % 2
        ek, pk = ("Eb", gi % 2), ("Pbuf", gi % 2)
        if kind == "reg":
            sv = ps[:, sb * 512:(sb + 2) * 512]
            op("scalar", lambda e: e.activation(out=eb, in_=sv, func=AF.Exp, scale=0.125),
               reads=[("ps", sb), ("ps", sb + 1)], writes=[ek])
            op("vector", lambda e: e.tensor_tensor(
                out=pbuf.rearrange("p (r b q) -> p r b q", r=2, b=4), in0=eb.rearrange("p (r b q) -> p r b q", r=2, b=4),
                in1=mb[:, 2 * ri:2 * ri + 2, :].unsqueeze(2).to_broadcast([128, 2, 4, 128]), op=ALU.mult),
               reads=[ek, ("mk", h % 2)], writes=[pk])
        else:
            op("scalar", lambda e: e.activation(out=eb[:, 0:256:128], in_=bank(sb)[:, 0:256:128], func=AF.Exp, scale=0.125),
               reads=[("ps", sb)], writes=[ek])
            op("scalar", lambda e: e.activation(out=eb[0:1, 512:768:128], in_=bank(sb + 1)[0:1, 0:256:128], func=AF.Exp, scale=0.125),
               reads=[("ps", sb + 1)], writes=[ek])
            op("vector", lambda e: e.tensor_tensor(
                out=pbuf[:, 0:256:128], in0=eb[:, 0:256:128], in1=mb[:, 2 * ri, 0:1].to_broadcast([128, 2]), op=ALU.mult),
               reads=[ek, ("mk", h % 2)], writes=[pk])
            op("vector", lambda e: e.tensor_tensor(
                out=pbuf[0:1, 512:768:128], in0=eb[0:1, 512:768:128], in1=mb[0:1, 2 * ri + 1, 0:1].to_broadcast([1, 2]), op=ALU.mult),
               reads=[ek, ("mk", h % 2)], writes=[pk])
        for bi_, (x0, Nq, t0, t1) in enumerate(blks):
            for role, tix in ((0, t0), (1, t1)):
                nk = VLIST[tix][2]
                op("tensor", lambda e, role=role, tix=tix, nk=nk, bi_=bi_, Nq=Nq: e.matmul(
                    bank(ob)[0:65, bi_ * 128:bi_ * 128 + Nq], lhsT=vb[0:nk, tix, :], rhs=pbuf[0:nk, role * 512 + bi_ * 128:role * 512 + bi_ * 128 + Nq],
                    start=(role == 0), stop=(role == 1)),
                   reads=[pk, ("Vp", h % 2)], writes=[("ps", ob)])
        if kind == "reg":
            src = bank(ob)[0:65, :].rearrange("p (c q) -> p c q", c=4)
            if r == 1:
                dst = Oacc[0:65, 1 + 512 * g:1 + 512 * (g + 1)].rearrange("p (c q) -> p c q", c=4)
            elif r == 4:
                dst = Oacc[0:65, 1 + 512 * g:1 + 512 * (g + 1)].rearrange("p (q c) -> p c q", c=4)
            else:
                dst = Oacc[0:65, 1:1 + OWN].rearrange("p (q c) -> p c q", c=16)[:, 4 * g:4 * g + 4, :]
        else:
            src = bank(ob)[0:65, 0:256:128]
            dst = Oacc[0:65, 0:NX:NX - 1]
        if r == 1:
            op("vector", lambda e: e.tensor_copy(out=dst, in_=src), reads=[("ps", ob)], writes=["Oacc"])
        else:
            op("vector", lambda e: e.tensor_tensor(out=dst, in0=src, in1=dst, op=ALU.add), reads=[("ps", ob), "Oacc"], writes=["Oacc"])

    d_prep(0)
    for h in range(8):
        g0 = gctr[0]
        d_scores(h, g0, GL[0])
        if h + 1 < 8:
            d_prep(h + 1)
        for i, grp in enumerate(GL):
            if i + 1 < len(GL):
                d_scores(h, g0 + i + 1, GL[i + 1])
            d_rest(h, g0 + i, grp)
        gctr[0] += len(GL)
        oacc_to_tm(h, ya_tm, "Oacc", Oacc)
    tm_to_ynT(ya_tm, ynTa, "ynTa", Pb)
    dump("ynTa", ynTa, [128, 4, NX], BF16, ["ynTa"])
    dump("ya_tm", ya_tm, [128, 17, 512], F32, ["tm"])
    S.barrier()

    R1.off = R1.lo
    ckvT = R1.alloc([128, 2, S_LEN], BF16)
    KT = R1.alloc([96, S_LEN], BF16)
    R1s_lo = R1.off
    RY.off = RY.lo + 16448
    Wukv = RY.alloc([128, 2, 1024], BF16)
    A = Region(RY.off, SB_BYTES)
    Wkvl = A.alloc([128, 8, 288], BF16)
    stage2 = A.alloc([128, 8, 512], F32)
    xbuf = [A.alloc([128, 2, D], F32) for _ in range(2)]
    xnb = [A.alloc([128, 2, D], BF16) for _ in range(2)]
    hTk = [A.alloc([128, 8, 128], BF16) for _ in range(2)]
    ckvn = [A.alloc([128, 256], BF16) for _ in range(2)]
    kr_tm = A.alloc([128, 64, 32], F32)
    rkt = A.alloc([128, 2, 64, 16], F32)
    kr_pad = A.alloc([128, 64, 96], BF16)
    rt = [A.alloc([128, 64, 16], F32) for _ in range(2)]
    load_weight(Wkvl, w_in[:, 1920:2208], 8, 288, 0, stage2, "Wkvl")
    load_weight(Wukv[:, :, 0:512], w_ukv[:, 0:512], 2, 512, 11, stage2, "Wukv")
    load_weight(Wukv[:, :, 512:1024], w_ukv[:, 512:1024], 2, 512, 11, stage2, "Wukv")
    op("sync", lambda e: e.dma_start(out=rkt, in_=rk), writes=["rkt"], dma=True)
    op("gpsimd", lambda e: e.memset(kr_pad, 0.0), writes=["kr_pad"])
    NTK = S_LEN // 128
    xk_ = [xbuf[0][:, 0, :], xbuf[0][:, 1, :], xbuf[1][:, 0, :], xbuf[1][:, 1, :]]
    xnk_ = [xnb[0][:, 0, :], xnb[0][:, 1, :], xnb[1][:, 0, :]]

    def ks0(tt):
        xb = xk_[tt % 4]
        op("sync", lambda e: e.dma_start(out=xb, in_=xf[tt * 128:(tt + 1) * 128, :]), writes=[("kx", tt % 4)], dma=True)

    def ks1(tt):
        xb = xk_[tt % 4]
        c = 160 + tt % 4
        op("scalar", lambda e: e.activation(out=junk, in_=xb, func=AF.Square, accum_out=stt[:, c:c + 1]), reads=[("kx", tt % 4)], writes=["junk", ("kss", tt % 4)])
        op("scalar", lambda e: e.activation(out=stt[:, c:c + 1], in_=stt[:, c:c + 1], func=AF.Sqrt, bias=EPS, scale=1.0 / D), reads=[("kss", tt % 4)], writes=[("kss", tt % 4)])

    def ks2(tt):
        xb = xk_[tt % 4]
        xn = xnk_[tt % 3]
        c = 160 + tt % 4
        op("vector", lambda e: e.reciprocal(out=stt[:, c + 4:c + 5], in_=stt[:, c:c + 1]), reads=[("kss", tt % 4)], writes=[("krs", tt % 4)])
        op("vector", lambda e: e.tensor_scalar(out=xn, in0=xb, scalar1=stt[:, c + 4:c + 5], scalar2=None, op0=ALU.mult),
           reads=[("kx", tt % 4), ("krs", tt % 4)], writes=[("kxn", tt % 3)])

    def ks3(tt):
        xn = xnk_[tt % 3]
        pb = tt % 2
        for k in range(8):
            op("tensor", lambda e, k=k: e.transpose(out=bankb(pb)[:, k * 128:(k + 1) * 128], in_=xn[:, k * 128:(k + 1) * 128], identity=idb),
               reads=[("kxn", tt % 3), "idb"], writes=[("ps", pb)])

    def ks4(tt):
        pb = tt % 2
        hb = hTk[tt % 2]
        if tt % 2 == 0:
            op("vector", lambda e: e.tensor_copy(out=hb, in_=bankb(pb).rearrange("p (k t) -> p k t", k=8)), reads=[("ps", pb)], writes=[("hTk", tt % 2)])
        else:
            op("scalar", lambda e: e.copy(out=hb, in_=bankb(pb).rearrange("p (k t) -> p k t", k=8)), reads=[("ps", pb)], writes=[("hTk", tt % 2)])

    def ks5(tt):
        hb = hTk[tt % 2]
        pb = 2 + tt % 3
        for k in range(8):
            op("tensor", lambda e, k=k: e.matmul(bank(pb, 288), lhsT=hb[:, k, :], rhs=Wkvl[:, k, :], start=(k == 0), stop=(k == 7)),
               reads=[("hTk", tt % 2), "Wkvl"], writes=[("ps", pb)])

    def ks6(tt):
        pb = 2 + tt % 3
        c = 168 + tt % 4
        op("scalar", lambda e: e.activation(out=junk[:, 0:256], in_=bank(pb, 256), func=AF.Square, accum_out=stt[:, c:c + 1]), reads=[("ps", pb)], writes=["junk", ("kss2", tt % 4)])
        op("scalar", lambda e: e.activation(out=stt[:, c:c + 1], in_=stt[:, c:c + 1], func=AF.Sqrt, bias=EPS, scale=1.0 / 256), reads=[("kss2", tt % 4)], writes=[("kss2", tt % 4)])

    def ks7(tt):
        pb = 2 + tt % 3
        c = 168 + tt % 4
        cb = ckvn[tt % 2]
        op("vector", lambda e: e.reciprocal(out=stt[:, c + 4:c + 5], in_=stt[:, c:c + 1]), reads=[("kss2", tt % 4)], writes=[("krs2", tt % 4)])
        op("vector", lambda e: e.tensor_scalar(out=cb, in0=bank(pb, 256), scalar1=stt[:, c + 4:c + 5], scalar2=None, op0=ALU.mult),
           reads=[("ps", pb), ("krs2", tt % 4)], writes=[("ckvn", tt % 2)])
        op("vector", lambda e: e.tensor_copy(out=kr_tm[:, tt, :], in_=bank(pb, 288)[:, 256:288]), reads=[("ps", pb)], writes=["kr_tm"])

    def ks8(tt):
        cb = ckvn[tt % 2]
        pb2 = 5 + tt % 2
        for j in range(2):
            op("tensor", lambda e, j=j: e.transpose(out=bankb(pb2)[:, j * 128:(j + 1) * 128], in_=cb[:, j * 128:(j + 1) * 128], identity=idb),
               reads=[("ckvn", tt % 2), "idb"], writes=[("ps", pb2)])

    def ks9(tt):
        pb2 = 5 + tt % 2
        op("scalar", lambda e: e.copy(out=ckvT[:, :, tt * 128:(tt + 1) * 128], in_=bankb(pb2)[:, 0:256].rearrange("p (j t) -> p j t", j=2)),
           reads=[("ps", pb2)], writes=["ckvT"])

    kst = [ks0, ks1, ks2, ks3, ks4, ks5, ks6, ks7, ks8, ks9]
    for it in range(NTK + len(kst) - 1):
        for k in range(len(kst) - 1, -1, -1):
            tt = it - k
            if 0 <= tt < NTK:
                kst[k](tt)
    x1, x2 = kr_tm[:, :, 0:16], kr_tm[:, :, 16:32]
    cosk, sink = rkt[:, 0], rkt[:, 1]
    op("vector", lambda e: e.tensor_tensor(out=rt[0], in0=x1, in1=cosk, op=ALU.mult), reads=["kr_tm", "rkt"], writes=["rt0"])
    op("vector", lambda e: e.tensor_tensor(out=rt[1], in0=x2, in1=sink, op=ALU.mult), reads=["kr_tm", "rkt"], writes=["rt1"])
    op("vector", lambda e: e.tensor_tensor(out=kr_pad[:, :, 64:80], in0=rt[0], in1=rt[1], op=ALU.subtract), reads=["rt0", "rt1", "kr_pad"], writes=["kr_pad"])
    op("vector", lambda e: e.tensor_tensor(out=rt[0], in0=x2, in1=cosk, op=ALU.mult), reads=["kr_tm", "rkt", "kr_pad"], writes=["rt0"])
    op("vector", lambda e: e.tensor_tensor(out=rt[1], in0=x1, in1=sink, op=ALU.mult), reads=["kr_tm", "rkt", "kr_pad"], writes=["rt1"])
    op("vector", lambda e: e.tensor_tensor(out=kr_pad[:, :, 80:96], in0=rt[0], in1=rt[1], op=ALU.add), reads=["rt0", "rt1", "kr_pad"], writes=["kr_pad"])
    for g8 in range(8):
        pb = 6 + g8 % 2
        for i in range(8):
            tt = g8 * 8 + i
            op("tensor", lambda e, i=i, tt=tt, pb=pb: e.transpose(out=bankb(pb)[0:96, i * 128:(i + 1) * 128], in_=kr_pad[:, tt, :], identity=idb),
               reads=["kr_pad", "idb"], writes=[("ps", pb)])
        op("vector", lambda e, pb=pb, g8=g8: e.tensor_copy(out=KT[64:96, g8 * 1024:(g8 + 1) * 1024], in_=bankb(pb)[64:96, :]),
           reads=[("ps", pb)], writes=["KTr"])
    dump("ckvT", ckvT, [128, 2, S_LEN], BF16, ["ckvT"])
    dump("KTr", KT[64:96, :], [32, S_LEN], BF16, ["KTr"])
    S.barrier()

    ynTb = RY.alloc([128, 4, NX], BF16)
    RT.off = RT.lo
    QBT = RT.alloc([96, 8, NX], BF16)
    R1s = Region(R1s_lo, R1.hi)
    Wuq = R1s.alloc([128, 3, 768], BF16)
    Wrot = R1s.alloc([128, 3, 8, 96], BF16)
    stage3 = R1s.alloc([128, 3, 768], F32)
    rqt = RT.alloc([96, 2, NX], F32)
    tq = [RT.alloc([96, 410], F32) for _ in range(2)]
    load_weight(Wuq, w_uq, 3, 768, 8, stage3, "Wuq")
    op("gpsimd", lambda e: e.memset(Wrot, 0.0), writes=["Wrot"])
    Wuq4 = Wuq.rearrange("p k (h c) -> p k h c", h=8)
    for k in range(3):
        op("gpsimd", lambda e, k=k: e.tensor_scalar(out=Wrot[:, k, :, 64:80], in0=Wuq4[:, k, :, 80:96], scalar1=-1.0, scalar2=None, op0=ALU.mult),
           reads=["Wuq", "Wrot"], writes=["Wrot"])
        op("gpsimd", lambda e, k=k: e.tensor_copy(out=Wrot[:, k, :, 80:96], in_=Wuq4[:, k, :, 64:80]), reads=["Wuq", "Wrot"], writes=["Wrot"])
    op("sync", lambda e: e.dma_start(out=rqt[64:96], in_=rq), writes=["rqt"], dma=True)
    qi = 0
    for h in range(8):
        for (c0, cn) in XBLK:
            pa, pr_ = 2 * (qi % 2), 1 + 2 * (qi % 2)
            tb_ = tq[qi % 2]
            tk = ("tq", qi % 2)
            qi += 1
            for k in range(3):
                op("tensor", lambda e, k=k, h=h, c0=c0, cn=cn, pa=pa: e.matmul(bank(pa, cn, 96), lhsT=Wuq[:, k, h * 96:(h + 1) * 96], rhs=cqT[:, k, c0:c0 + cn], start=(k == 0), stop=(k == 2)),
                   reads=["Wuq", "cqT"], writes=[("ps", pa)])
            for k in range(3):
                op("tensor", lambda e, k=k, h=h, c0=c0, cn=cn, pr_=pr_: e.matmul(bank(pr_, cn, 96), lhsT=Wrot[:, k, h, :], rhs=cqT[:, k, c0:c0 + cn], start=(k == 0), stop=(k == 2)),
                   reads=["Wrot", "cqT"], writes=[("ps", pr_)])
            op("scalar", lambda e, h=h, c0=c0, cn=cn, pa=pa: e.copy(out=QBT[0:64, h, c0:c0 + cn], in_=bank(pa, cn, 64)), reads=[("ps", pa)], writes=["QBT"])
            op("vector", lambda e, c0=c0, cn=cn, pa=pa, tb_=tb_: e.tensor_tensor(out=tb_[64:96, 0:cn], in0=bank(pa, cn, 32, 64), in1=rqt[64:96, 0, c0:c0 + cn], op=ALU.mult),
               reads=[("ps", pa), "rqt"], writes=[tk])
            op("vector", lambda e, h=h, c0=c0, cn=cn, pr_=pr_: e.tensor_tensor(out=QBT[64:96, h, c0:c0 + cn], in0=bank(pr_, cn, 32, 64), in1=rqt[64:96, 1, c0:c0 + cn], op=ALU.mult),
               reads=[("ps", pr_), "rqt"], writes=["QBT"])
            op("vector", lambda e, h=h, c0=c0, cn=cn, tb_=tb_: e.tensor_tensor(out=QBT[64:96, h, c0:c0 + cn], in0=QBT[64:96, h, c0:c0 + cn], in1=tb_[64:96, 0:cn], op=ALU.add),
               reads=[tk, "QBT"], writes=["QBT"])
    dump("QBT", QBT, [96, 8, NX], BF16, ["QBT"])
    S.barrier()
    RT.off = RT.lo + 32832
    yb_tm = RT.alloc([128, 17, 512], F32)
    ynb_tmp = [RT.alloc([128, 512], BF16) for _ in range(2)]
    R1s = Region(R1s_lo, R1.hi)
    Vb = [R1s.alloc([128, 64, 65], BF16) for _ in range(2)]
    PT = [R1s.alloc([128, 2, 512], BF16) for _ in range(3)]
    OaccB = R1s.alloc([128, NX], F32)
    for b_ in range(2):
        op("gpsimd", lambda e, b_=b_: e.memset(Vb[b_][:, :, 64], 1.0), writes=[("Vb", b_)])
    scale_b = 96.0 ** -0.5
    MB = [(1 + 512 * i, 512) for i in range(4)]
    QG = [(0, 2), (2, 2)]
    si = 0

    def m_prep_v(h):
        vb = Vb[h % 2]
        for k8 in range(8):
            pb = 6 + k8 % 2
            for i in range(8):
                kt = k8 * 8 + i
                for j in range(2):
                    op("tensor", lambda e, j=j, kt=kt, i=i: e.matmul(bank(pb)[:, i * 64:(i + 1) * 64], lhsT=ckvT[:, j, kt * 128:(kt + 1) * 128], rhs=Wukv[:, j, h * 128 + 64:h * 128 + 128], start=(j == 0), stop=(j == 1)),
                       reads=["Wukv", "ckvT"], writes=[("ps", pb)])
            op("vector", lambda e: e.tensor_copy(out=vb[:, k8 * 8:(k8 + 1) * 8, 0:64], in_=bank(pb).rearrange("p (t c) -> p t c", t=8)),
               reads=[("ps", pb)], writes=[("Vb", h % 2)])

    def m_prep_k(h):
        for nb_ in range(16):
            pb = 6 + nb_ % 2
            for j in range(2):
                op("tensor", lambda e, j=j: e.matmul(bank(pb, 512, 64), lhsT=Wukv[:, j, h * 128:h * 128 + 64], rhs=ckvT[:, j, nb_ * 512:(nb_ + 1) * 512], start=(j == 0), stop=(j == 1)),
                   reads=["Wukv", "ckvT"], writes=[("ps", pb)])
            op("vector", lambda e: e.tensor_copy(out=KT[0:64, nb_ * 512:(nb_ + 1) * 512], in_=bank(pb, 512, 64)),
               reads=[("ps", pb)], writes=["KTn"])

    m_prep_v(0)
    for h in range(8):
        vb = Vb[h % 2]
        m_prep_k(h)
        for gq, (b0, nbk) in enumerate(QG):
            ob = 4

            def emit_S(kt, si_):
                sbk = 2 * (si_ % 2)
                for bb in range(nbk):
                    c0, cn = MB[b0 + bb]
                    op("tensor", lambda e, bb=bb, c0=c0, cn=cn: e.matmul(bank(sbk + bb, cn), lhsT=KT[0:96, kt * 128:(kt + 1) * 128], rhs=QBT[0:96, h, c0:c0 + cn], start=True, stop=True),
                       reads=["KTn", "KTr", "QBT"], writes=[("ps", sbk + bb)])
            emit_S(0, si)
            for kt in range(64):
                sbk = 2 * (si % 2)
                pt = PT[si % 3]
                ptk = ("PT", si % 3)
                if kt + 1 < 64:
                    emit_S(kt + 1, si + 1)
                if gq == 0 and kt == 8 and h + 1 < 8:
                    m_prep_v(h + 1)
                sv = ps[:, sbk * 512:(sbk + nbk) * 512].rearrange("p (a b) -> p a b", a=nbk)
                op("scalar", lambda e: e.activation(out=pt[:, 0:nbk, :], in_=sv, func=AF.Exp, scale=scale_b),
                   reads=[("ps", sbk + bb) for bb in range(nbk)], writes=[ptk])
                for bb in range(nbk):
                    op("tensor", lambda e, bb=bb: e.matmul(bank(ob + bb, 512, 65), lhsT=vb[:, kt, :], rhs=pt[:, bb, :], start=(kt == 0), stop=(kt == 63)),
                       reads=[ptk, ("Vb", h % 2)], writes=[("ps", ob + bb)])
                si += 1
            ov = ps[0:65, ob * 512:(ob + nbk) * 512]
            c0 = MB[b0][0]
            op("vector", lambda e: e.tensor_copy(out=OaccB[0:65, c0:c0 + nbk * 512], in_=ov),
               reads=[("ps", ob + bb) for bb in range(nbk)], writes=["OaccB"])
        sbk = 2 * (si % 2)
        pt = PT[si % 3]
        ptk = ("PT", si % 3)
        qh = QBT[0:96, h, 0:NX:NX - 1]
        for kt in range(64):
            op("tensor", lambda e, kt=kt: e.matmul(bank(sbk)[:, 2 * kt:2 * kt + 2], lhsT=KT[0:96, kt * 128:(kt + 1) * 128], rhs=qh, start=True, stop=True),
               reads=["KTn", "KTr", "QBT"], writes=[("ps", sbk)])
        op("scalar", lambda e: e.activation(out=pt[:, 0, 0:128], in_=bank(sbk, 128), func=AF.Exp, scale=scale_b), reads=[("ps", sbk)], writes=[ptk])
        for kt in range(64):
            op("tensor", lambda e, kt=kt: e.matmul(bank(4, 2, 65), lhsT=vb[:, kt, :], rhs=pt[:, 0, 2 * kt:2 * kt + 2], start=(kt == 0), stop=(kt == 63)),
               reads=[ptk, ("Vb", h % 2)], writes=[("ps", 4)])
        si += 1
        op("vector", lambda e: e.tensor_copy(out=OaccB[0:65, 0:NX:NX - 1], in_=bank(4, 2, 65)), reads=[("ps", 4)], writes=["OaccB"])
        oacc_to_tm(h, yb_tm, "OaccB", OaccB)
    tm_to_ynT(yb_tm, ynTb, "ynTb", ynb_tmp)
    dump("ynTb", ynTb, [128, 4, NX], BF16, ["ynTb"])
    dump("yb_tm", yb_tm, [128, 17, 512], F32, ["tm"])
    S.barrier()

    RT.off = RT.lo
    hTf = RT.alloc([128, 8, NX], BF16)
    A = Region(R1.lo, RC.hi)
    Wo = A.alloc([128, 8, D], BF16)
    stage4 = A.alloc([128, 8, 256], F32)
    gpo = A.alloc([128, 2, D], F32)
    xt_ = [A.alloc([128, D], F32) for _ in range(2)]
    xm_ = [A.alloc([128, D], F32) for _ in range(2)]
    hn_ = [A.alloc([128, D], BF16) for _ in range(2)]
    for c in range(4):
        load_weight(Wo[:, :, c * 256:(c + 1) * 256], w_o[:, c * 256:(c + 1) * 256], 8, 256, 13, stage4, "Wo")
    op("sync", lambda e: e.dma_start(out=gpo, in_=gpost), writes=["gpo"], dma=True)
    def f0_p1(t):
        M = 128 if t < 16 else 2
        bi = t % 2
        xt, xm, hn = xt_[bi], xm_[bi], hn_[bi]
        xk, mkk, hk = ("xt", bi), ("xm", bi), ("hn", bi)
        if t < 16:
            op("sync", lambda e, t=t, xt=xt: e.dma_start(out=xt, in_=xw[OWN0 + 128 * t:OWN0 + 128 * (t + 1), :]), writes=[xk], dma=True)
            cols = lambda k, t=t: (ynTa if k < 4 else ynTb)[:, k % 4, 1 + 128 * t:1 + 128 * (t + 1)]
        else:
            op("sync", lambda e, xt=xt: e.dma_start(out=xt[0:1, :], in_=xw[OWN0 - 1:OWN0, :]), writes=[xk], dma=True)
            op("sync", lambda e, xt=xt: e.dma_start(out=xt[1:2, :], in_=xw[OWN0 + OWN:OWN0 + OWN + 1, :]), writes=[xk], dma=True)
            cols = lambda k: (ynTa if k < 4 else ynTb)[:, k % 4, 0:NX:NX - 1]
        pb = 2 * (t % 2)
        for n2 in range(2):
            for k in range(8):
                la = cols(k)
                op("tensor", lambda e, k=k, n2=n2, M=M, pb=pb, la=la: e.matmul(bank(pb + n2, 512, M), lhsT=la, rhs=Wo[:, k, n2 * 512:(n2 + 1) * 512], start=(k == 0), stop=(k == 7)),
                   reads=["ynTa", "ynTb", "Wo"], writes=[("ps", pb + n2)])

    def f0_p2(t):
        M = 128 if t < 16 else 2
        bi = t % 2
        xt, xm, hn = xt_[bi], xm_[bi], hn_[bi]
        xk, mkk, hk = ("xt", bi), ("xm", bi), ("hn", bi)
        pb = 2 * (t % 2)
        yv = ps[0:M, pb * 512:(pb + 2) * 512]
        sc = 128 + 2 * (t % 4)
        sk = ("stt5", t % 4)
        op("scalar", lambda e, yv=yv, M=M, sc=sc: e.activation(out=junk[0:M, :], in_=yv, func=AF.Square, accum_out=stt[0:M, sc:sc + 1]),
           reads=[("ps", pb), ("ps", pb + 1)], writes=["junk", sk])
        op("scalar", lambda e, M=M, sc=sc: e.activation(out=stt[0:M, sc:sc + 1], in_=stt[0:M, sc:sc + 1], func=AF.Sqrt, bias=EPS, scale=1.0 / D), reads=[sk], writes=[sk])
        op("vector", lambda e, M=M, sc=sc: e.reciprocal(out=stt[0:M, sc + 1:sc + 2], in_=stt[0:M, sc:sc + 1]), reads=[sk], writes=[sk])
        op("vector", lambda e, yv=yv, M=M, sc=sc, xm=xm: e.scalar_tensor_tensor(out=xm[0:M, :], in0=yv, scalar=stt[0:M, sc + 1:sc + 2], in1=gpo[0:M, 0, :], op0=ALU.mult, op1=ALU.mult),
           reads=[("ps", pb), ("ps", pb + 1), sk, "gpo"], writes=[mkk])
        op("vector", lambda e, M=M, xm=xm, xt=xt: e.tensor_tensor(out=xm[0:M, :], in0=xm[0:M, :], in1=xt[0:M, :], op=ALU.add), reads=[mkk, xk], writes=[mkk])
        if t < 16:
            op("sync", lambda e, t=t, xm=xm: e.dma_start(out=xmid[128 * t:128 * (t + 1), :], in_=xm), reads=[mkk], writes=["xmid"], dma=True)
        sc2 = 136 + 2 * (t % 4)
        sk2 = ("stt6", t % 4)
        op("scalar", lambda e, M=M, sc2=sc2, xm=xm: e.activation(out=junk[0:M, :], in_=xm[0:M, :], func=AF.Square, accum_out=stt[0:M, sc2:sc2 + 1]),
           reads=[mkk], writes=["junk", sk2])
        op("scalar", lambda e, M=M, sc2=sc2: e.activation(out=stt[0:M, sc2:sc2 + 1], in_=stt[0:M, sc2:sc2 + 1], func=AF.Sqrt, bias=EPS, scale=1.0 / D), reads=[sk2], writes=[sk2])
        op("vector", lambda e, M=M, sc2=sc2: e.reciprocal(out=stt[0:M, sc2 + 1:sc2 + 2], in_=stt[0:M, sc2:sc2 + 1]), reads=[sk2], writes=[sk2])
        op("vector", lambda e, M=M, sc2=sc2, xm=xm, hn=hn: e.tensor_scalar(out=hn[0:M, :], in0=xm[0:M, :], scalar1=stt[0:M, sc2 + 1:sc2 + 2], scalar2=None, op0=ALU.mult),
           reads=[mkk, sk2], writes=[hk])
        pb2 = 4 + t % 2
        for k in range(8):
            op("tensor", lambda e, k=k, M=M, hn=hn, pb2=pb2: e.transpose(out=bankb(pb2)[:, k * 128:k * 128 + M], in_=hn[0:M, k * 128:(k + 1) * 128], identity=idb[0:M, 0:M]),
               reads=[hk, "idb"], writes=[("ps", pb2)])
        srcv = bankb(pb2).rearrange("p (k t) -> p k t", k=8)[:, :, 0:M]
        dstv = hTf[:, :, 1 + 128 * t:1 + 128 * (t + 1)] if t < 16 else hTf[:, :, 0:NX:NX - 1]
        op("vector", lambda e, srcv=srcv, dstv=dstv: e.tensor_copy(out=dstv, in_=srcv), reads=[("ps", pb2)], writes=["hTf"])

    f0_p1(0)
    for t in range(17):
        if t + 1 < 17:
            f0_p1(t + 1)
        f0_p2(t)
    dump("hTf", hTf, [128, 8, NX], BF16, ["hTf"])
    S.barrier()

    aT = Region(R1.lo, RC.hi).alloc([128, 22, OWN], BF16)
    A = Region(RY.lo, RY.hi)
    stg = [A.alloc([128, 8, 256], F32) for _ in range(2)]
    Wub = [A.alloc([128, 8, 256], BF16) for _ in range(2)]
    cwt = A.alloc([128, 44, 4], F32)
    ufl = A.alloc([128, 2], F32)
    A = Region(RT.lo + 32832, SB_BYTES)
    cgb = [A.alloc([128, OWN], F32) for _ in range(2)]
    cvb = [A.alloc([128, OWN], F32) for _ in range(2)]
    op("sync", lambda e: e.dma_start(out=cwt, in_=cwb), writes=["cwt"], dma=True)
    op("sync", lambda e: e.dma_start(out=ufl, in_=uflag), writes=["ufl"], dma=True)
    OB = [(410 * i, min(410, OWN - 410 * i)) for i in range(5)]
    pbi = 0

    def f1_weights(j):
        bi = j % 2
        sg, wb = stg[bi], Wub[bi]
        sgk, wbk = ("stg", bi), ("Wub", bi)
        op("gpsimd", lambda e: e.dma_start(out=sg[:, :, 0:128], in_=w_up[:, j * 128:(j + 1) * 128].rearrange("(k p) n -> p k n", p=128)), writes=[sgk], dma=True)
        op("gpsimd", lambda e: e.dma_start(out=sg[:, :, 128:256], in_=w_up[:, DFF + j * 128:DFF + (j + 1) * 128].rearrange("(k p) n -> p k n", p=128)), writes=[sgk], dma=True)
        for k in range(8):
            if k % 2 == 0:
                op("vector", lambda e, k=k: e.tensor_scalar(out=wb[:, k, :], in0=sg[:, k, :], scalar1=gpt[:, 21 + k:22 + k], scalar2=None, op0=ALU.mult),
                   reads=[sgk, "gpt"], writes=[("grp", wbk, k)])
            else:
                op("scalar", lambda e, k=k: e.activation(out=wb[:, k, :], in_=sg[:, k, :], func=AF.Identity, scale=gpt[:, 21 + k:22 + k]),
                   reads=[sgk, "gpt"], writes=[("grp", wbk, k)])

    f1_weights(0)
    for j in range(22):
        bi = j % 2
        wb = Wub[bi]
        wbk = ("Wub", bi)
        if j + 1 < 22:
            f1_weights(j + 1)
        for half in range(2):
            cb = (cgb if half == 0 else cvb)[bi]
            ff = j + 22 * half
            for bx, (o0, n) in enumerate(OB):
                ck = ("cg" if half == 0 else "cv", bi, bx)
                pb = pbi % 8
                pbi += 1
                for k in range(8):
                    op("tensor", lambda e, k=k: e.matmul(bank(pb, n + 2), lhsT=wb[:, k, half * 128:(half + 1) * 128], rhs=hTf[:, k, o0:o0 + n + 2], start=(k == 0), stop=(k == 7)),
                       reads=[wbk, "hTf"], writes=[("ps", pb)])
                if bx == 0:
                    op("vector", lambda e: e.tensor_tensor(out=bank(pb, 1), in0=bank(pb, 1), in1=ufl[:, 0:1], op=ALU.mult), reads=[("ps", pb), "ufl"], writes=[("ps", pb)])
                if bx == 4:
                    op("vector", lambda e: e.tensor_tensor(out=bank(pb, n + 2)[:, n + 1:n + 2], in0=bank(pb, n + 2)[:, n + 1:n + 2], in1=ufl[:, 1:2], op=ALU.mult), reads=[("ps", pb), "ufl"], writes=[("ps", pb)])
                op("scalar", lambda e: e.activation(out=cb[:, o0:o0 + n], in_=bank(pb, n + 2)[:, 1:n + 1], func=AF.Identity, bias=cwt[:, ff, 3:4], scale=cwt[:, ff, 1:2]),
                   reads=[("ps", pb), "cwt"], writes=[ck])
                op("vector", lambda e: e.scalar_tensor_tensor(out=cb[:, o0:o0 + n], in0=bank(pb, n + 2)[:, 0:n], scalar=cwt[:, ff, 0:1], in1=cb[:, o0:o0 + n], op0=ALU.mult, op1=ALU.add),
                   reads=[("ps", pb), "cwt", ck], writes=[ck])
                op("vector", lambda e: e.scalar_tensor_tensor(out=cb[:, o0:o0 + n], in0=bank(pb, n + 2)[:, 2:n + 2], scalar=cwt[:, ff, 2:3], in1=cb[:, o0:o0 + n], op0=ALU.mult, op1=ALU.add),
                   reads=[("ps", pb), "cwt", ck], writes=[ck])
        cg, cv = cgb[bi], cvb[bi]
        cgk = [("cg", bi, bx) for bx in range(5)]
        cvk = [("cv", bi, bx) for bx in range(5)]
        op("scalar", lambda e: e.activation(out=cg, in_=cg, func=AF.Gelu_apprx_tanh), reads=cgk, writes=cgk)
        op("vector", lambda e: e.tensor_tensor(out=aT[:, j, :], in0=cg, in1=cv, op=ALU.mult), reads=cgk + cvk, writes=[("aT", j)])
    dump("aT", aT, [128, 22, OWN], BF16, [("aT", j) for j in range(22)])
    S.barrier()

    A = Region(RY.lo, SB_BYTES)
    Wdn = A.alloc([128, 22, D], BF16)
    stage5 = A.alloc([128, 2, D], F32)
    gpo2 = A.alloc([128, D], F32)
    xm2 = [A.alloc([128, D], F32) for _ in range(2)]
    ot = [A.alloc([128, D], F32) for _ in range(2)]
    for c in range(11):
        load_weight(Wdn[:, 2 * c:2 * c + 2, :], w_down[256 * c:256 * (c + 1), :], 2, D, None, stage5, "Wdn")
    op("sync", lambda e: e.dma_start(out=gpo2, in_=gpost[:, 1, :]), writes=["gpo2"], dma=True)
    for t in range(16):
        bi = t % 2
        xk, ok = ("xm2", bi), ("ot", bi)
        op("sync", lambda e, t=t, bi=bi: e.dma_start(out=xm2[bi], in_=xmid[128 * t:128 * (t + 1), :]), reads=["xmid"], writes=[xk], dma=True)
        pb = 2 * (t % 2)
        for n2 in range(2):
            for j in range(22):
                op("tensor", lambda e, j=j, n2=n2, t=t, pb=pb: e.matmul(bank(pb + n2), lhsT=aT[:, j, 128 * t:128 * (t + 1)], rhs=Wdn[:, j, n2 * 512:(n2 + 1) * 512], start=(j == 0), stop=(j == 21)),
                   reads=[("aT", j), "Wdn"], writes=[("ps", pb + n2)])
        yv = ps[:, pb * 512:(pb + 2) * 512]
        sc = 144 + 2 * (t % 4)
        sk = ("stt7", t % 4)
        op("scalar", lambda e, yv=yv, sc=sc: e.activation(out=junk, in_=yv, func=AF.Square, accum_out=stt[:, sc:sc + 1]),
           reads=[("ps", pb), ("ps", pb + 1)], writes=["junk", sk])
        op("scalar", lambda e, sc=sc: e.activation(out=stt[:, sc:sc + 1], in_=stt[:, sc:sc + 1], func=AF.Sqrt, bias=EPS, scale=1.0 / D), reads=[sk], writes=[sk])
        op("vector", lambda e, sc=sc: e.reciprocal(out=stt[:, sc + 1:sc + 2], in_=stt[:, sc:sc + 1]), reads=[sk], writes=[sk])
        op("vector", lambda e, yv=yv, sc=sc, bi=bi: e.scalar_tensor_tensor(out=ot[bi], in0=yv, scalar=stt[:, sc + 1:sc + 2], in1=gpo2, op0=ALU.mult, op1=ALU.mult),
           reads=[("ps", pb), ("ps", pb + 1), sk, "gpo2"], writes=[ok])
        op("vector", lambda e, bi=bi: e.tensor_tensor(out=ot[bi], in0=ot[bi], in1=xm2[bi], op=ALU.add), reads=[ok, xk], writes=[ok])
        op("sync", lambda e, t=t, bi=bi: e.dma_start(out=yout[128 * t:128 * (t + 1), :], in_=ot[bi]), reads=[ok], dma=True)
    S.emit()
    return nc


_CACHE = {}


def _consts():
    if "c" in _CACHE:
        return _CACHE["c"]
    slopes = np.exp2(-8.0 * np.arange(1, 9, dtype=np.float32) / 8).astype(np.float32)
    k = np.arange(128)[:, None]
    q = np.arange(128)[None, :]
    masks = np.zeros((8, 128, 6, 128), np.float32)
    for h in range(8):
        for ri, r in enumerate((1, 4, 16)):
            d0 = k - 64 - q
            d1 = k + 64 - q
            masks[h, :, 2 * ri, :] = np.where(k >= q, np.exp(-slopes[h] * (np.abs(d0) * r).astype(np.float32)), 0.0)
            masks[h, :, 2 * ri + 1, :] = np.where(k <= q, np.exp(-slopes[h] * (np.abs(d1) * r).astype(np.float32)), 0.0)
    inv_freq = np.exp(-np.log(10000.0) * np.arange(0, 32, 2, dtype=np.float32) / 32).astype(np.float32)
    pos = np.arange(S_LEN, dtype=np.float32)
    ang = pos[:, None] * inv_freq[None, :]
    cosk = np.cos(ang).astype(np.float32).reshape(64, 128, 16).transpose(1, 0, 2)
    sink = np.sin(ang).astype(np.float32).reshape(64, 128, 16).transpose(1, 0, 2)
    rk = np.ascontiguousarray(np.stack([cosk, sink], axis=1))
    c = dict(masks=masks, inv_freq=inv_freq, rk=rk, ident=np.eye(128, dtype=np.float32))
    _CACHE["c"] = c
    return c


def _core_inputs(c, x, shared):
    cst = _consts()
    b, qc = c // 4, c % 4
    T0 = qc * OWN
    pos_w = T0 - OWN0 + np.arange(NW)
    valid = (pos_w >= 0) & (pos_w < S_LEN)
    xw = np.zeros((NW, D), np.float32)
    xw[valid] = x[b, pos_w[valid]]
    kvf = np.zeros((128, NVT), np.float32)
    for i, (r, es, nk) in enumerate(VLIST):
        kvf[:nk, i] = valid[es + r * np.arange(nk)].astype(np.float32)
    posq = (T0 - 1 + np.arange(NX)).astype(np.float32)
    ang = posq[None, :] * cst["inv_freq"][:, None]
    cq, sq = np.cos(ang).astype(np.float32), np.sin(ang).astype(np.float32)
    rq = np.ascontiguousarray(np.stack([np.concatenate([cq, cq], 0), np.concatenate([sq, sq], 0)], axis=1))
    uflag = np.zeros((128, 2), np.float32)
    uflag[:, 0] = 1.0 if T0 > 0 else 0.0
    uflag[:, 1] = 1.0 if T0 + OWN < S_LEN else 0.0
    d = dict(shared)
    d.update(xw=xw, xf=np.ascontiguousarray(x[b]), kvf=kvf, rq=rq, uflag=uflag, masks=cst["masks"], rk=cst["rk"], ident=cst["ident"])
    return d


def kernel(x, norm_mix_pre, w_in, q_lat_norm, w_uq, kv_lat_norm, w_ukv, out_norm_a, out_norm_b, w_o,
           norm_mix_post, norm_ffn_pre, w_up, conv_w, conv_b, w_down, norm_ffn_post):
    f = lambda a: np.ascontiguousarray(np.asarray(a, dtype=np.float32))
    x = f(x)
    gp = np.zeros((128, 32), np.float32)
    gp[:, 0:8] = f(norm_mix_pre)[0].reshape(8, 128).T
    gp[:, 8:11] = f(q_lat_norm)[0].reshape(3, 128).T
    gp[:, 11:13] = f(kv_lat_norm)[0].reshape(2, 128).T
    gp[:, 13:21] = np.concatenate([f(out_norm_a)[0], f(out_norm_b)[0]]).reshape(8, 128).T
    gp[:, 21:29] = f(norm_ffn_pre)[0].reshape(8, 128).T
    gpost = np.ascontiguousarray(np.broadcast_to(np.stack([f(norm_mix_post)[0], f(norm_ffn_post)[0]])[None], (128, 2, D)))
    cwb = np.zeros((128, 44, 4), np.float32)
    cwb[:, :, 0:3] = f(conv_w)[0].T.reshape(44, 128, 3).transpose(1, 0, 2)
    cwb[:, :, 3] = f(conv_b)[0].reshape(44, 128).T
    shared = dict(w_in=f(w_in)[0], w_uq=f(w_uq)[0], w_ukv=f(w_ukv)[0], w_o=f(w_o)[0], w_up=f(w_up)[0], w_down=f(w_down)[0],
                  gp=gp, gpost=gpost, cwb=cwb)
    if "nc" not in _CACHE:
        _CACHE["nc"] = build()
    nc = _CACHE["nc"]
    in_maps = [_core_inputs(c, x, shared) for c in range(8)]
    res = run_bass_kernel_spmd(nc, in_maps, core_ids=list(range(8)))
    out = np.zeros((2, S_LEN, D), np.float32)
    for c in range(8):
        b, qc = c // 4, c % 4
        out[b, qc * OWN:(qc + 1) * OWN] = res.results[c]["y"]
    return out
```

```python
import contextlib
import types
import numpy as np
import ml_dtypes
import concourse.bass as bass
import concourse.mybir as mybir
from concourse.bass_utils import run_bass_kernel_spmd

F32 = mybir.dt.float32
BF16 = mybir.dt.bfloat16
U8 = mybir.dt.uint8
AF = mybir.ActivationFunctionType
ALU = mybir.AluOpType

S_LEN = 8192
D = 1024
OWN = 2048
OWN0 = 1152
NW = 4352
NX = 2050
DFF = 2816
EPS = 1e-6
ENGS = ("sync", "scalar", "gpsimd", "vector", "tensor")
XBLK = [(i * 410, 410) for i in range(5)]


class Sched:
    def __init__(self, nc, ndma_sems=8):
        self.nc = nc
        self.ops = []
        self.ndma = ndma_sems

    @staticmethod
    def _freeze(fn):
        if fn.__closure__ is None:
            return fn
        cells = []
        for c in fn.__closure__:
            try:
                cells.append(types.CellType(c.cell_contents))
            except ValueError:
                cells.append(c)
        return types.FunctionType(fn.__code__, fn.__globals__, fn.__name__, fn.__defaults__, tuple(cells))

    def op(self, eng, fn, reads=(), writes=(), dma=False):
        fn = self._freeze(fn)
        self.ops.append(dict(eng=eng, fn=fn, reads=tuple(reads), writes=tuple(writes), dma=dma, bar=False))

    def barrier(self):
        self.ops.append(dict(eng=None, fn=None, reads=(), writes=(), dma=False, bar=True))

    def emit(self, final_wait_eng="sync"):
        nc = self.nc
        ops = self.ops
        n = len(ops)
        groups = {}
        for o in ops:
            for k in o["reads"] + o["writes"]:
                if isinstance(k, tuple) and len(k) == 3 and k[0] == "grp":
                    groups.setdefault(k[1], set()).add(k)

        def expand(keys):
            out = []
            for k in keys:
                out.append(k)
                if k in groups:
                    out.extend(groups[k])
            return out

        last_writer = {}
        readers = {}
        deps = [dict() for _ in range(n)]
        since_bar = []
        pending_bar = {}
        for i, o in enumerate(ops):
            if o["bar"]:
                lastc = {}
                dl = set()
                for j in since_bar:
                    if ops[j]["dma"]:
                        dl.add(j)
                    else:
                        lastc[ops[j]["eng"]] = j
                dl.update(lastc.values())
                for e in ENGS:
                    pending_bar[e] = set(dl) | pending_bar.get(e, set())
                since_bar = []
                continue
            d = deps[i]
            if o["eng"] in pending_bar:
                for j in pending_bar.pop(o["eng"]):
                    d[j] = True
            rk, wk = expand(o["reads"]), expand(o["writes"])
            for r in rk:
                if r in last_writer:
                    d[last_writer[r]] = True
            for w in wk:
                if w in last_writer:
                    d.setdefault(last_writer[w], False)
                for j in readers.get(w, ()):
                    d.setdefault(j, False)
            d.pop(i, None)
            for w in wk:
                last_writer[w] = i
                readers[w] = []
            for r in rk:
                if r not in wk:
                    readers.setdefault(r, []).append(i)
            since_bar.append(i)
        needed = set()
        red = [None] * n
        for i, o in enumerate(ops):
            if o["bar"]:
                continue
            per_eng = {}
            dl = []
            for j, is_raw in deps[i].items():
                pj = ops[j]
                if pj["dma"]:
                    dl.append(j)
                    continue
                if pj["eng"] == o["eng"] and not o["dma"] and (o["eng"] == "tensor" or not is_raw):
                    continue
                e = pj["eng"]
                if e not in per_eng or per_eng[e] < j:
                    per_eng[e] = j
            dl.extend(per_eng.values())
            red[i] = dl
            needed.update(dl)
        cnt = {e: 0 for e in ENGS}
        dcnt = {}
        sig = [None] * n
        dma_idx = {e: 0 for e in ENGS}
        for i, o in enumerate(ops):
            if o["bar"]:
                continue
            if o["dma"]:
                k = dma_idx[o["eng"]] % self.ndma
                dma_idx[o["eng"]] += 1
                key = ("dma", o["eng"], k)
                prev = dcnt.get(key, 0)
                dcnt[key] = prev + 16
                sig[i] = (key, prev + 16)
                o["dma_prev"] = (key, prev) if prev > 0 else None
            elif i in needed:
                cnt[o["eng"]] += 1
                sig[i] = (("eng", o["eng"]), cnt[o["eng"]])
        semkeys = sorted({s[0] for s in sig if s is not None}, key=str)
        stack = contextlib.ExitStack()
        sems = {}
        for sk in semkeys:
            sems[sk] = stack.enter_context(nc.semaphore("s_" + "_".join(str(x) for x in sk)))
        by_eng = {e: [i for i, o in enumerate(ops) if o["eng"] == e] for e in ENGS}
        dma_final = list(dcnt.items())

        def run(engname, eng):
            waited = {}
            for i in by_eng[engname]:
                o = ops[i]
                wl = [sig[j] for j in red[i]]
                if o["dma"] and o.get("dma_prev"):
                    wl.append(o["dma_prev"])
                for (sk, v) in wl:
                    if waited.get(sk, 0) >= v:
                        continue
                    eng.wait_ge(sems[sk], v)
                    waited[sk] = v
                ins = o["fn"](eng)
                if sig[i] is not None:
                    ins.then_inc(sems[sig[i][0]], 16 if o["dma"] else 1)
            if engname == final_wait_eng:
                for sk, v in dma_final:
                    if waited.get(sk, 0) < v:
                        eng.wait_ge(sems[sk], v)
                for e in ENGS:
                    if cnt[e] > 0 and waited.get(("eng", e), 0) < cnt[e]:
                        eng.wait_ge(sems[("eng", e)], cnt[e])

        with stack:
            with nc.Block() as block:
                @block.sync
                def _(e):
                    run("sync", e)

                @block.scalar
                def _(e):
                    run("scalar", e)

                @block.gpsimd
                def _(e):
                    run("gpsimd", e)

                @block.vector
                def _(e):
                    run("vector", e)

                @block.tensor
                def _(e):
                    run("tensor", e)


def dil_tables():
    vt = {}

    def vtile(r, e0, nk):
        key = (r, e0, nk)
        if key not in vt:
            vt[key] = len(vt)
        return vt[key]

    groups = {1: [], 4: [], 16: []}
    for r in (1, 4, 16):
        def blk(x0, N):
            eq0 = OWN0 - 1 + x0
            t0 = vtile(r, eq0 - 64 * r, 128)
            t1 = vtile(r, eq0 + 64 * r, 128 if N > 1 else 1)
            return (x0, N, t0, t1)
        if r == 1:
            for g in range(4):
                groups[r].append(("reg", g, [blk(1 + 128 * (4 * g + b), 128) for b in range(4)]))
        elif r == 4:
            for g in range(4):
                groups[r].append(("reg", g, [blk(1 + c + 512 * g, 128) for c in range(4)]))
        else:
            for g in range(4):
                groups[r].append(("reg", g, [blk(1 + 4 * g + c, 128) for c in range(4)]))
        groups[r].append(("halo", 0, [blk(0, 1), blk(NX - 1, 1)]))
    vlist = [None] * len(vt)
    for k, i in vt.items():
        vlist[i] = k
    return vlist, groups


VLIST, DGROUPS = dil_tables()
NVT = len(VLIST)


def build(dbg=False):
    nc = bass.Bass("TRN2", target_bir_lowering=False)

    def din(name, shape, dt=F32):
        return nc.dram_tensor(name, list(shape), dt, kind="ExternalInput").ap()

    xw = din("xw", [NW, D])
    xf = din("xf", [S_LEN, D])
    w_in = din("w_in", [D, 2208])
    w_uq = din("w_uq", [384, 768])
    w_ukv = din("w_ukv", [256, 1024])
    w_o = din("w_o", [D, D])
    w_up = din("w_up", [D, 2 * DFF])
    w_down = din("w_down", [DFF, D])
    gp = din("gp", [128, 32])
    gpost = din("gpost", [128, 2, D])
    cwb = din("cwb", [128, 44, 4])
    masks = din("masks", [8, 128, 6, 128])
    kvf = din("kvf", [128, NVT])
    rq = din("rq", [32, 2, NX])
    rk = din("rk", [128, 2, 64, 16])
    ident = din("ident", [128, 128])
    uflag = din("uflag", [128, 2])
    yout = nc.dram_tensor("y", [OWN, D], F32, kind="ExternalOutput").ap()
    xmid = nc.dram_tensor("xmid", [OWN, D], F32, kind="Internal").ap()
    dbg_out = {}

    SB_BYTES = 212000
    big = nc.alloc_sbuf_tensor("big", [128, SB_BYTES], U8).ap()
    ps = nc.alloc_psum_tensor("ps", [128, 4096], F32).ap()
    psb = ps.bitcast(BF16)

    class Region:
        def __init__(self, lo, hi):
            assert hi <= SB_BYTES and lo <= hi, (lo, hi)
            self.lo, self.hi, self.off = lo, hi, lo

        def alloc(self, shape, dt, p0=0):
            esz = 4 if dt == F32 else 2
            nb = int(np.prod(shape[1:])) * esz
            nb_al = (nb + 63) // 64 * 64
            assert self.off + nb_al <= self.hi, ("SBUF region overflow", self.lo, self.hi, self.off, nb_al)
            v = big[p0:p0 + shape[0], self.off:self.off + nb].bitcast(dt)
            self.off += nb_al
            if len(shape) == 3:
                v = v.rearrange("p (a b) -> p a b", a=shape[1])
            elif len(shape) == 4:
                v = v.rearrange("p (a b c) -> p a b c", a=shape[1], b=shape[2])
            return v

    RP = Region(0, 4096)
    R1 = Region(RP.hi, RP.hi + 86080)
    RC = Region(R1.hi, R1.hi + 12352)
    RY = Region(RC.hi, RC.hi + 16448 + 4096 + 16448)
    RT = Region(RY.hi, SB_BYTES)
    A = RP
    S = Sched(nc)
    op = S.op

    def dump(name, ap, shape, dt, keys):
        if not dbg:
            return
        import os
        sel = os.environ.get("DBGSEL", "")
        if sel and name not in sel.split(","):
            return
        d = nc.dram_tensor("dbg_" + name, list(shape), dt, kind="ExternalOutput").ap()
        op("sync", lambda e: e.dma_start(out=d, in_=ap), reads=keys, dma=True)

    def bank(b, n=512, p=128, p0=0):
        return ps[p0:p0 + p, b * 512:b * 512 + n]

    def bankb(b, n=1024, p=128, p0=0):
        return psb[p0:p0 + p, b * 1024:b * 1024 + n]

    idf = A.alloc([128, 128], F32)
    idb = A.alloc([128, 128], BF16)
    gpt = A.alloc([128, 32], F32)
    stt = A.alloc([128, 256], F32)
    junk = A.alloc([128, 1024], BF16)
    rec = A.alloc([128, 8], F32)
    op("sync", lambda e: e.dma_start(out=idf, in_=ident), writes=["idf"], dma=True)
    op("sync", lambda e: e.dma_start(out=gpt, in_=gp), writes=["gpt"], dma=True)
    op("vector", lambda e: e.tensor_copy(out=idb, in_=idf), reads=["idf"], writes=["idb"])

    def rstd_from_ss(ss_ap, out_ap, n, key):
        tmp = ss_ap
        op("scalar", lambda e: e.activation(out=tmp, in_=ss_ap, func=AF.Sqrt, bias=EPS, scale=1.0 / n),
           reads=[key], writes=[key])
        op("vector", lambda e: e.reciprocal(out=out_ap, in_=tmp), reads=[key], writes=[key])

    def load_weight(dst, src_ap, kch, ncols, gcol, stage, wkey, negate_cols=None):
        op("gpsimd", lambda e: e.dma_start(out=stage[:, 0:kch, 0:ncols], in_=src_ap.rearrange("(k p) n -> p k n", p=128)),
           writes=[("stage", id(stage))], dma=True)
        for k in range(kch):
            eng = "vector" if k % 2 == 0 else "scalar"
            if gcol is None:
                if eng == "vector":
                    op(eng, lambda e, k=k: e.tensor_copy(out=dst[:, k, :], in_=stage[:, k, 0:ncols]),
                       reads=[("stage", id(stage))], writes=[("grp", wkey, (id(dst), k))])
                else:
                    op(eng, lambda e, k=k: e.copy(out=dst[:, k, :], in_=stage[:, k, 0:ncols]),
                       reads=[("stage", id(stage))], writes=[("grp", wkey, (id(dst), k))])
            else:
                if eng == "vector":
                    op(eng, lambda e, k=k: e.tensor_scalar(out=dst[:, k, :], in0=stage[:, k, 0:ncols],
                                                          scalar1=gpt[:, gcol + k:gcol + k + 1], scalar2=None, op0=ALU.mult),
                       reads=[("stage", id(stage)), "gpt"], writes=[("grp", wkey, (id(dst), k))])
                else:
                    op(eng, lambda e, k=k: e.activation(out=dst[:, k, :], in_=stage[:, k, 0:ncols], func=AF.Identity,
                                                        scale=gpt[:, gcol + k:gcol + k + 1]),
                       reads=[("stage", id(stage)), "gpt"], writes=[("grp", wkey, (id(dst), k))])

    def norm_tiles_to_T(src_dram_rows, ntile, xbuf, xnbuf, key, slot=0):
        c0 = 2 * slot
        sk = ("sttn", slot)
        op("sync", lambda e: e.dma_start(out=xbuf[:, 0:ntile, :], in_=src_dram_rows.rearrange("(t p) f -> p t f", p=128)),
           writes=[("x", key)], dma=True)
        for t in range(ntile):
            op("scalar", lambda e, t=t: e.activation(out=junk, in_=xbuf[:, t, :], func=AF.Square, accum_out=stt[:, c0 + t:c0 + t + 1]),
               reads=[("x", key)], writes=["junk", sk])
        rstd_from_ss(stt[:, c0:c0 + ntile], stt[:, 8 + c0:8 + c0 + ntile], D, sk)
        for t in range(ntile):
            if t % 2 == 0:
                op("vector", lambda e, t=t: e.tensor_scalar(out=xnbuf[:, t, :], in0=xbuf[:, t, :], scalar1=stt[:, 8 + c0 + t:9 + c0 + t], scalar2=None, op0=ALU.mult),
                   reads=[("x", key), sk], writes=[("grp", ("xn", key), t)])
            else:
                op("scalar", lambda e, t=t: e.activation(out=xnbuf[:, t, :], in_=xbuf[:, t, :], func=AF.Identity, scale=stt[:, 8 + c0 + t:9 + c0 + t]),
                   reads=[("x", key), sk], writes=[("grp", ("xn", key), t)])

    def transpose_tile(xn_tile, hT_dst, pbank, rkeys, wkey):
        for k in range(8):
            op("tensor", lambda e, k=k: e.transpose(out=bankb(pbank)[:, k * 128:(k + 1) * 128], in_=xn_tile[:, k * 128:(k + 1) * 128], identity=idb),
               reads=list(rkeys) + ["idb"], writes=[("ps", pbank)])
        op("vector", lambda e: e.tensor_copy(out=hT_dst, in_=bankb(pbank).rearrange("p (k t) -> p k t", k=8)),
           reads=[("ps", pbank)], writes=[wkey])

    KAT = R1.alloc([128, 4, NW], BF16)
    VAT = R1.alloc([128, 4, NW], BF16)
    QAT = R1.alloc([128, 4, NX], BF16)
    cqT = RC.alloc([128, 3, NX], BF16)
    A = Region(RC.hi, SB_BYTES)
    WA = A.alloc([128, 8, 1920], BF16)
    stage = A.alloc([128, 8, 480], F32)
    xbuf = [A.alloc([128, 2, D], F32) for _ in range(2)]
    xnb = [A.alloc([128, 2, D], BF16) for _ in range(2)]
    hTw = [A.alloc([128, 8, 256], BF16) for _ in range(2)]
    cqn = A.alloc([128, 384], BF16)
    for c in range(4):
        load_weight(WA[:, :, c * 480:(c + 1) * 480], w_in[:, c * 480:(c + 1) * 480], 8, 480, 0, stage, "WA")

    def cq_tile(cols_ap, M, xcol_ap_fn, hkey, pb):
        for k in range(8):
            la = cols_ap(k)
            op("tensor", lambda e, k=k, la=la: e.matmul(bank(pb, 384, M), lhsT=la, rhs=WA[:, k, 1536:1920], start=(k == 0), stop=(k == 7)),
               reads=[hkey, "WA"], writes=[("ps", pb)])
        op("scalar", lambda e: e.activation(out=junk[0:M, 0:384], in_=bank(pb, 384, M), func=AF.Square, accum_out=stt[0:M, 16:17]),
           reads=[("ps", pb)], writes=["junk", "stt2"])
        op("scalar", lambda e: e.activation(out=stt[0:M, 16:17], in_=stt[0:M, 16:17], func=AF.Sqrt, bias=EPS, scale=1.0 / 384),
           reads=["stt2"], writes=["stt2"])
        op("vector", lambda e: e.reciprocal(out=stt[0:M, 17:18], in_=stt[0:M, 16:17]), reads=["stt2"], writes=["stt2"])
        op("vector", lambda e: e.tensor_scalar(out=cqn[0:M, :], in0=bank(pb, 384, M), scalar1=stt[0:M, 17:18], scalar2=None, op0=ALU.mult),
           reads=[("ps", pb), "stt2"], writes=["cqn"])
        for j in range(3):
            op("tensor", lambda e, j=j: e.transpose(out=bankb(pb)[:, j * 128:j * 128 + M], in_=cqn[0:M, j * 128:(j + 1) * 128], identity=idb[0:M, 0:M]),
               reads=["cqn", "idb"], writes=[("ps", pb)])
        xc = xcol_ap_fn()
        op("vector", lambda e: e.tensor_copy(out=xc, in_=bankb(pb)[:, 0:384].rearrange("p (j t) -> p j t", j=3)[:, :, 0:M]),
           reads=[("ps", pb)], writes=["cqT"])

    NGW = NW // 256

    xbuf.append(A.alloc([128, 2, D], F32))

    def ws0(g):
        xb = xbuf[g % 3]
        e0 = g * 256
        op("sync", lambda e: e.dma_start(out=xb, in_=xw[e0:e0 + 256, :].rearrange("(t p) f -> p t f", p=128)), writes=[("wx", g % 3)], dma=True)

    def ws1(g):
        xb = xbuf[g % 3]
        c = 176 + 2 * (g % 4)
        for t in range(2):
            op("scalar", lambda e, t=t: e.activation(out=junk, in_=xb[:, t, :], func=AF.Square, accum_out=stt[:, c + t:c + t + 1]),
               reads=[("wx", g % 3)], writes=["junk", ("wss", g % 4)])
        op("scalar", lambda e: e.activation(out=stt[:, c:c + 2], in_=stt[:, c:c + 2], func=AF.Sqrt, bias=EPS, scale=1.0 / D), reads=[("wss", g % 4)], writes=[("wss", g % 4)])

    def ws2(g):
        xb = xbuf[g % 3]
        xn = xnb[g % 2]
        c = 176 + 2 * (g % 4)
        op("vector", lambda e: e.reciprocal(out=stt[:, c + 8:c + 10], in_=stt[:, c:c + 2]), reads=[("wss", g % 4)], writes=[("wrs", g % 4)])
        for t in range(2):
            op("vector", lambda e, t=t: e.tensor_scalar(out=xn[:, t, :], in0=xb[:, t, :], scalar1=stt[:, c + 8 + t:c + 9 + t], scalar2=None, op0=ALU.mult),
               reads=[("wx", g % 3), ("wrs", g % 4)], writes=[("wxn", g % 2)])

    def ws3(g):
        xn = xnb[g % 2]
        for t in range(2):
            pb = 2 * (g % 2) + t
            for k in range(8):
                op("tensor", lambda e, k=k, t=t: e.transpose(out=bankb(pb)[:, k * 128:(k + 1) * 128], in_=xn[:, t, k * 128:(k + 1) * 128], identity=idb),
                   reads=[("wxn", g % 2), "idb"], writes=[("ps", pb)])

    def ws4(g):
        bi = g % 2
        for t in range(2):
            pb = 2 * (g % 2) + t
            if t == 0:
                op("vector", lambda e, t=t: e.tensor_copy(out=hTw[bi][:, :, t * 128:(t + 1) * 128], in_=bankb(pb).rearrange("p (k t) -> p k t", k=8)),
                   reads=[("ps", pb)], writes=[("grp", ("hTw", bi), t)])
            else:
                op("scalar", lambda e, t=t: e.copy(out=hTw[bi][:, :, t * 128:(t + 1) * 128], in_=bankb(pb).rearrange("p (k t) -> p k t", k=8)),
                   reads=[("ps", pb)], writes=[("grp", ("hTw", bi), t)])

    def w_stageB(g):
        bi = g % 2
        e0 = g * 256
        xlo, xhi = max(e0, OWN0 - 1), min(e0 + 256, OWN0 - 1 + NX)
        jobs = [("K", 512 + 128 * c, c) for c in range(4)] + [("V", 1024 + 128 * c, c) for c in range(4)]
        if xhi > xlo:
            jobs += [("Q", 128 * c, c) for c in range(4)]
        for ji, (kind, col0, c) in enumerate(jobs):
            pb = 4 + ji % 2
            pk = ("ps", pb)
            pv = bank(pb, 256)
            for k in range(8):
                op("tensor", lambda e, k=k: e.matmul(pv, lhsT=WA[:, k, col0:col0 + 128], rhs=hTw[bi][:, k, :], start=(k == 0), stop=(k == 7)),
                   reads=[("hTw", bi), "WA"], writes=[pk])
            if kind == "K":
                op("vector", lambda e: e.tensor_copy(out=KAT[:, c, e0:e0 + 256], in_=pv), reads=[pk], writes=["KAT"])
            elif kind == "V":
                op("scalar", lambda e: e.copy(out=VAT[:, c, e0:e0 + 256], in_=pv), reads=[pk], writes=["VAT"])
            else:
                op("vector", lambda e: e.tensor_copy(out=QAT[:, c, xlo - (OWN0 - 1):xhi - (OWN0 - 1)], in_=pv[:, xlo - e0:xhi - e0]),
                   reads=[pk], writes=["QAT"])
        for t in range(2):
            et = e0 + t * 128
            if OWN0 <= et < OWN0 + OWN:
                x0 = et - (OWN0 - 1)
                cq_tile(lambda k, t=t, bi=bi: hTw[bi][:, k, t * 128:(t + 1) * 128], 128, lambda x0=x0: cqT[:, :, x0:x0 + 128], ("hTw", bi), 6 + t)
            if et == OWN0 - 128:
                cq_tile(lambda k, t=t, bi=bi: hTw[bi][:, k, t * 128 + 127:t * 128 + 128], 1, lambda: cqT[:, :, 0:1], ("hTw", bi), 6 + t)
            if et == OWN0 + OWN:
                cq_tile(lambda k, t=t, bi=bi: hTw[bi][:, k, t * 128:t * 128 + 1], 1, lambda: cqT[:, :, NX - 1:NX], ("hTw", bi), 6 + t)

    wst = [ws0, ws1, ws2, ws3, ws4, w_stageB]
    for it in range(NGW + len(wst) - 1):
        for k in range(len(wst) - 1, -1, -1):
            g = it - k
            if 0 <= g < NGW:
                wst[k](g)
    dump("KAT", KAT, [128, 4, NW], BF16, ["KAT"])
    dump("VAT", VAT, [128, 4, NW], BF16, ["VAT"])
    dump("QAT", QAT, [128, 4, NX], BF16, ["QAT"])
    dump("cqT", cqT, [128, 3, NX], BF16, ["cqT"])
    S.barrier()

    ynTa = RY.alloc([128, 4, NX], BF16)
    A = Region(RY.lo + 16448, SB_BYTES)
    mk = [A.alloc([128, 6, 128], F32) for _ in range(2)]
    kvft = A.alloc([128, NVT], F32)
    Vp = [A.alloc([128, NVT, 65], BF16) for _ in range(2)]
    Oacc = A.alloc([128, NX], F32)
    ya_tm = A.alloc([128, 17, 512], F32)
    Eb = [A.alloc([128, 1024], F32) for _ in range(2)]
    Pb = [A.alloc([128, 1024], BF16) for _ in range(2)]
    op("sync", lambda e: e.dma_start(out=kvft, in_=kvf), writes=["kvft"], dma=True)

    def oacc_to_tm(h, dst_tm, okey, Oacc):
        tiles = [(1 + 128 * m, 128, 1) for m in range(16)] + [(0, 2, NX - 1)]
        for g0 in range(0, 17, 4):
            tl = tiles[g0:g0 + 4]
            pb = 6 + (g0 // 4) % 2
            for i, (x0, M, step) in enumerate(tl):
                src = Oacc[0:65, x0:x0 + 128] if step == 1 else Oacc[0:65, 0:NX:NX - 1]
                op("tensor", lambda e, i=i, src=src, M=M, pb=pb: e.transpose(out=bank(pb)[0:M, i * 65:(i + 1) * 65], in_=src, identity=idf[0:65, 0:65]),
                   reads=[okey, "idf"], writes=[("ps", pb)])
            nt = len(tl)
            M = tl[0][1]
            pv = bank(pb)[0:M, 0:nt * 65].rearrange("p (t c) -> p t c", t=nt)
            op("vector", lambda e, pv=pv, nt=nt, M=M: e.reciprocal(out=rec[0:M, 0:nt], in_=pv[:, :, 64]), reads=[("ps", pb)], writes=["rec"])
            op("vector", lambda e, pv=pv, nt=nt, M=M, g0=g0: e.tensor_tensor(out=dst_tm[0:M, g0:g0 + nt, h * 64:(h + 1) * 64], in0=pv[:, :, 0:64],
                                                                         in1=rec[0:M, 0:nt].unsqueeze(2).to_broadcast([M, nt, 64]), op=ALU.mult),
               reads=[("ps", pb), "rec"], writes=["tm"])

    def tm_to_ynT(src_tm, dstT, nkey, Pb):
        for t in range(17):
            M = 128 if t < 16 else 2
            op("scalar", lambda e, t=t, M=M: e.activation(out=junk[0:M, 0:512], in_=src_tm[0:M, t, :], func=AF.Square, accum_out=stt[0:M, 32 + t:33 + t]),
               reads=["tm"], writes=["junk", "stt3"])
        rstd_from_ss(stt[:, 32:49], stt[:, 64:81], 512, "stt3")
        for t in range(17):
            M = 128 if t < 16 else 2
            pb = t % 2
            ynb = Pb[t % 2]
            op("vector", lambda e, t=t, M=M, ynb=ynb: e.tensor_scalar(out=ynb[0:M, 0:512], in0=src_tm[0:M, t, :], scalar1=stt[0:M, 64 + t:65 + t], scalar2=None, op0=ALU.mult),
               reads=["tm", "stt3"], writes=[("Pb", t % 2)])
            for j in range(4):
                op("tensor", lambda e, j=j, M=M, ynb=ynb, pb=pb: e.transpose(out=bankb(pb)[:, j * 128:j * 128 + M], in_=ynb[0:M, j * 128:(j + 1) * 128], identity=idb[0:M, 0:M]),
                   reads=[("Pb", t % 2), "idb"], writes=[("ps", pb)])
            srcv = bankb(pb)[:, 0:512].rearrange("p (j t) -> p j t", j=4)[:, :, 0:M]
            if t < 16:
                dstv = dstT[:, :, 1 + 128 * t:1 + 128 * (t + 1)]
            else:
                dstv = dstT[:, :, 0:NX:NX - 1]
            op("vector", lambda e, srcv=srcv, dstv=dstv: e.tensor_copy(out=dstv, in_=srcv), reads=[("ps", pb)], writes=[nkey])

    def d_prep(h):
        pr, hs = h // 2, (h % 2) * 64
        vb = Vp[h % 2]
        mb = mk[h % 2]
        op("sync", lambda e: e.dma_start(out=mb, in_=masks[h]), writes=[("mk", h % 2)], dma=True)
        for v0 in range(0, NVT, 8):
            vts = VLIST[v0:v0 + 8]
            pb = 4 + (v0 // 8) % 2
            for i, (r, es, nk) in enumerate(vts):
                op("tensor", lambda e, i=i, r=r, es=es, nk=nk: e.transpose(out=bankb(pb)[0:nk, i * 64:(i + 1) * 64],
                                                                        in_=VAT[hs:hs + 64, pr, es:es + r * (nk - 1) + 1:r], identity=idb[hs:hs + 64, hs:hs + 64]),
                   reads=["VAT", "idb"], writes=[("ps", pb)])
            i = 0
            while i < len(vts):
                nk = vts[i][2]
                j = i
                while j + 1 < len(vts) and vts[j + 1][2] == nk:
                    j += 1
                cnt_ = j - i + 1
                srcv = bankb(pb)[0:nk, i * 64:(j + 1) * 64].rearrange("p (t c) -> p t c", t=cnt_)
                op("vector", lambda e, srcv=srcv, i=i, cnt_=cnt_, nk=nk: e.tensor_tensor(
                    out=vb[0:nk, v0 + i:v0 + i + cnt_, 0:64], in0=srcv,
                    in1=kvft[0:nk, v0 + i:v0 + i + cnt_].unsqueeze(2).to_broadcast([nk, cnt_, 64]), op=ALU.mult),
                   reads=[("ps", pb), "kvft"], writes=[("Vp", h % 2)])
                i = j + 1
        op("gpsimd", lambda e: e.tensor_copy(out=vb[:, :, 64], in_=kvft), reads=["kvft"], writes=[("Vp", h % 2)])

    GL = [(ri, r, kind, g, blks) for ri, r in enumerate((1, 4, 16)) for (kind, g, blks) in DGROUPS[r]]
    gctr = [0]

    def d_scores(h, gi, grp):
        pr, hs = h // 2, (h % 2) * 64
        ri, r, kind, g, blks = grp
        sb = 2 * (gi % 2)
        for bi_, (x0, Nq, t0, t1) in enumerate(blks):
            qv = QAT[hs:hs + 64, pr, x0:x0 + r * (Nq - 1) + 1:r]
            for role, tix in ((0, t0), (1, t1)):
                (rr, es, nk) = VLIST[tix]
                op("tensor", lambda e, role=role, es=es, nk=nk, qv=qv, bi_=bi_, Nq=Nq: e.matmul(
                    bank(sb + role)[0:nk, bi_ * 128:bi_ * 128 + Nq], lhsT=KAT[hs:hs + 64, pr, es:es + r * (nk - 1) + 1:r], rhs=qv, start=True, stop=True),
                   reads=["KAT", "QAT"], writes=[("ps", sb + role)])

    def d_rest(h, gi, grp):
        ri, r, kind, g, blks = grp
        vb = Vp[h % 2]
        mb = mk[h % 2]
        sb = 2 * (gi % 2)
        eb = Eb[gi % 2]
        pbuf = Pb[gi % 2]
        ob = 6 + gi % 2
        ek, pk = ("Eb", gi % 2), ("Pbuf", gi % 2)
        if kind == "reg":
            sv = ps[:, sb * 512:(sb + 2) * 512]
            op("scalar", lambda e: e.activation(out=eb, in_=sv, func=AF.Exp, scale=0.125),
               reads=[("ps", sb), ("ps", sb + 1)], writes=[ek])
            op("vector", lambda e: e.tensor_tensor(
                out=pbuf.rearrange("p (r b q) -> p r b q", r=2, b=4), in0=eb.rearrange("p (r b q) -> p r b q", r=2, b=4),
                in1=mb[:, 2 * ri:2 * ri + 2, :].unsqueeze(2).to_broadcast([128, 2, 4, 128]), op=ALU.mult),
               reads=[ek, ("mk", h % 2)], writes=[pk])
        else:
            op("scalar", lambda e: e.activation(out=eb[:, 0:256:128], in_=bank(sb)[:, 0:256:128], func=AF.Exp, scale=0.125),
               reads=[("ps", sb)], writes=[ek])
            op("scalar", lambda e: e.activation(out=eb[0:1, 512:768:128], in_=bank(sb + 1)[0:1, 0:256:128], func=AF.Exp, scale=0.125),
               reads=[("ps", sb + 1)], writes=[ek])
            op("vector", lambda e: e.tensor_tensor(
                out=pbuf[:, 0:256:128], in0=eb[:, 0:256:128], in1=mb[:, 2 * ri, 0:1].to_broadcast([128, 2]), op=ALU.mult),
               reads=[ek, ("mk", h % 2)], writes=[pk])
            op("vector", lambda e: e.tensor_tensor(
                out=pbuf[0:1, 512:768:128], in0=eb[0:1, 512:768:128], in1=mb[0:1, 2 * ri + 1, 0:1].to_broadcast([1, 2]), op=ALU.mult),
               reads=[ek, ("mk", h % 2)], writes=[pk])
        for bi_, (x0, Nq, t0, t1) in enumerate(blks):
            for role, tix in ((0, t0), (1, t1)):
                nk = VLIST[tix][2]
                op("tensor", lambda e, role=role, tix=tix, nk=nk, bi_=bi_, Nq=Nq: e.matmul(
                    bank(ob)[0:65, bi_ * 128:bi_ * 128 + Nq], lhsT=vb[0:nk, tix, :], rhs=pbuf[0:nk, role * 512 + bi_ * 128:role * 512 + bi_ * 128 + Nq],
                    start=(role == 0), stop=(role == 1)),
                   reads=[pk, ("Vp", h % 2)], writes=[("ps", ob)])
        if kind == "reg":
            src = bank(ob)[0:65, :].rearrange("p (c q) -> p c q", c=4)
            if r == 1:
                dst = Oacc[0:65, 1 + 512 * g:1 + 512 * (g + 1)].rearrange("p (c q) -> p c q", c=4)
            elif r == 4:
                dst = Oacc[0:65, 1 + 512 * g:1 + 512 * (g + 1)].rearrange("p (q c) -> p c q", c=4)
            else:
                dst = Oacc[0:65, 1:1 + OWN].rearrange("p (q c) -> p c q", c=16)[:, 4 * g:4 * g + 4, :]
        else:
            src = bank(ob)[0:65, 0:256:128]
            dst = Oacc[0:65, 0:NX:NX - 1]
        if r == 1:
            op("vector", lambda e: e.tensor_copy(out=dst, in_=src), reads=[("ps", ob)], writes=["Oacc"])
        else:
            op("vector", lambda e: e.tensor_tensor(out=dst, in0=src, in1=dst, op=ALU.add), reads=[("ps", ob), "Oacc"], writes=["Oacc"])

    d_prep(0)
    for h in range(8):
        g0 = gctr[0]
        d_scores(h, g0, GL[0])
        if h + 1 < 8:
            d_prep(h + 1)
        for i, grp in enumerate(GL):
            if i + 1 < len(GL):
                d_scores(h, g0 + i + 1, GL[i + 1])
            d_rest(h, g0 + i, grp)
        gctr[0] += len(GL)
        oacc_to_tm(h, ya_tm, "Oacc", Oacc)
    tm_to_ynT(ya_tm, ynTa, "ynTa", Pb)
    dump("ynTa", ynTa, [128, 4, NX], BF16, ["ynTa"])
    dump("ya_tm", ya_tm, [128, 17, 512], F32, ["tm"])
    S.barrier()

    R1.off = R1.lo
    ckvT = R1.alloc([128, 2, S_LEN], BF16)
    KT = R1.alloc([96, S_LEN], BF16)
    R1s_lo = R1.off
    RY.off = RY.lo + 16448
    Wukv = RY.alloc([128, 2, 1024], BF16)
    A = Region(RY.off, SB_BYTES)
    Wkvl = A.alloc([128, 8, 288], BF16)
    stage2 = A.alloc([128, 8, 512], F32)
    xbuf = [A.alloc([128, 2, D], F32) for _ in range(2)]
    xnb = [A.alloc([128, 2, D], BF16) for _ in range(2)]
    hTk = [A.alloc([128, 8, 128], BF16) for _ in range(2)]
    ckvn = [A.alloc([128, 256], BF16) for _ in range(2)]
    kr_tm = A.alloc([128, 64, 32], F32)
    rkt = A.alloc([128, 2, 64, 16], F32)
    kr_pad = A.alloc([128, 64, 96], BF16)
    rt = [A.alloc([128, 64, 16], F32) for _ in range(2)]
    load_weight(Wkvl, w_in[:, 1920:2208], 8, 288, 0, stage2, "Wkvl")
    load_weight(Wukv[:, :, 0:512], w_ukv[:, 0:512], 2, 512, 11, stage2, "Wukv")
    load_weight(Wukv[:, :, 512:1024], w_ukv[:, 512:1024], 2, 512, 11, stage2, "Wukv")
    op("sync", lambda e: e.dma_start(out=rkt, in_=rk), writes=["rkt"], dma=True)
    op("gpsimd", lambda e: e.memset(kr_pad, 0.0), writes=["kr_pad"])
    NTK = S_LEN // 128
    xk_ = [xbuf[0][:, 0, :], xbuf[0][:, 1, :], xbuf[1][:, 0, :], xbuf[1][:, 1, :]]
    xnk_ = [xnb[0][:, 0, :], xnb[0][:, 1, :], xnb[1][:, 0, :]]

    def ks0(tt):
        xb = xk_[tt % 4]
        op("sync", lambda e: e.dma_start(out=xb, in_=xf[tt * 128:(tt + 1) * 128, :]), writes=[("kx", tt % 4)], dma=True)

    def ks1(tt):
        xb = xk_[tt % 4]
        c = 160 + tt % 4
        op("scalar", lambda e: e.activation(out=junk, in_=xb, func=AF.Square, accum_out=stt[:, c:c + 1]), reads=[("kx", tt % 4)], writes=["junk", ("kss", tt % 4)])
        op("scalar", lambda e: e.activation(out=stt[:, c:c + 1], in_=stt[:, c:c + 1], func=AF.Sqrt, bias=EPS, scale=1.0 / D), reads=[("kss", tt % 4)], writes=[("kss", tt % 4)])

    def ks2(tt):
        xb = xk_[tt % 4]
        xn = xnk_[tt % 3]
        c = 160 + tt % 4
        op("vector", lambda e: e.reciprocal(out=stt[:, c + 4:c + 5], in_=stt[:, c:c + 1]), reads=[("kss", tt % 4)], writes=[("krs", tt % 4)])
        op("vector", lambda e: e.tensor_scalar(out=xn, in0=xb, scalar1=stt[:, c + 4:c + 5], scalar2=None, op0=ALU.mult),
           reads=[("kx", tt % 4), ("krs", tt % 4)], writes=[("kxn", tt % 3)])

    def ks3(tt):
        xn = xnk_[tt % 3]
        pb = tt % 2
        for k in range(8):
            op("tensor", lambda e, k=k: e.transpose(out=bankb(pb)[:, k * 128:(k + 1) * 128], in_=xn[:, k * 128:(k + 1) * 128], identity=idb),
               reads=[("kxn", tt % 3), "idb"], writes=[("ps", pb)])

    def ks4(tt):
        pb = tt % 2
        hb = hTk[tt % 2]
        if tt % 2 == 0:
            op("vector", lambda e: e.tensor_copy(out=hb, in_=bankb(pb).rearrange("p (k t) -> p k t", k=8)), reads=[("ps", pb)], writes=[("hTk", tt % 2)])
        else:
            op("scalar", lambda e: e.copy(out=hb, in_=bankb(pb).rearrange("p (k t) -> p k t", k=8)), reads=[("ps", pb)], writes=[("hTk", tt % 2)])

    def ks5(tt):
        hb = hTk[tt % 2]
        pb = 2 + tt % 3
        for k in range(8):
            op("tensor", lambda e, k=k: e.matmul(bank(pb, 288), lhsT=hb[:, k, :], rhs=Wkvl[:, k, :], start=(k == 0), stop=(k == 7)),
               reads=[("hTk", tt % 2), "Wkvl"], writes=[("ps", pb)])

    def ks6(tt):
        pb = 2 + tt % 3
        c = 168 + tt % 4
        op("scalar", lambda e: e.activation(out=junk[:, 0:256], in_=bank(pb, 256), func=AF.Square, accum_out=stt[:, c:c + 1]), reads=[("ps", pb)], writes=["junk", ("kss2", tt % 4)])
        op("scalar", lambda e: e.activation(out=stt[:, c:c + 1], in_=stt[:, c:c + 1], func=AF.Sqrt, bias=EPS, scale=1.0 / 256), reads=[("kss2", tt % 4)], writes=[("kss2", tt % 4)])

    def ks7(tt):
        pb = 2 + tt % 3
        c = 168 + tt % 4
        cb = ckvn[tt % 2]
        op("vector", lambda e: e.reciprocal(out=stt[:, c + 4:c + 5], in_=stt[:, c:c + 1]), reads=[("kss2", tt % 4)], writes=[("krs2", tt % 4)])
        op("vector", lambda e: e.tensor_scalar(out=cb, in0=bank(pb, 256), scalar1=stt[:, c + 4:c + 5], scalar2=None, op0=ALU.mult),
           reads=[("ps", pb), ("krs2", tt % 4)], writes=[("ckvn", tt % 2)])
        op("vector", lambda e: e.tensor_copy(out=kr_tm[:, tt, :], in_=bank(pb, 288)[:, 256:288]), reads=[("ps", pb)], writes=["kr_tm"])

    def ks8(tt):
        cb = ckvn[tt % 2]
        pb2 = 5 + tt % 2
        for j in range(2):
            op("tensor", lambda e, j=j: e.transpose(out=bankb(pb2)[:, j * 128:(j + 1) * 128], in_=cb[:, j * 128:(j + 1) * 128], identity=idb),
               reads=[("ckvn", tt % 2), "idb"], writes=[("ps", pb2)])

    def ks9(tt):
        pb2 = 5 + tt % 2
        op("scalar", lambda e: e.copy(out=ckvT[:, :, tt * 128:(tt + 1) * 128], in_=bankb(pb2)[:, 0:256].rearrange("p (j t) -> p j t", j=2)),
           reads=[("ps", pb2)], writes=["ckvT"])

    kst = [ks0, ks1, ks2, ks3, ks4, ks5, ks6, ks7, ks8, ks9]
    for it in range(NTK + len(kst) - 1):
        for k in range(len(kst) - 1, -1, -1):
            tt = it - k
            if 0 <= tt < NTK:
                kst[k](tt)
    x1, x2 = kr_tm[:, :, 0:16], kr_tm[:, :, 16:32]
    cosk, sink = rkt[:, 0], rkt[:, 1]
    op("vector", lambda e: e.tensor_tensor(out=rt[0], in0=x1, in1=cosk, op=ALU.mult), reads=["kr_tm", "rkt"], writes=["rt0"])
    op("vector", lambda e: e.tensor_tensor(out=rt[1], in0=x2, in1=sink, op=ALU.mult), reads=["kr_tm", "rkt"], writes=["rt1"])
    op("vector", lambda e: e.tensor_tensor(out=kr_pad[:, :, 64:80], in0=rt[0], in1=rt[1], op=ALU.subtract), reads=["rt0", "rt1", "kr_pad"], writes=["kr_pad"])
    op("vector", lambda e: e.tensor_tensor(out=rt[0], in0=x2, in1=cosk, op=ALU.mult), reads=["kr_tm", "rkt", "kr_pad"], writes=["rt0"])
    op("vector", lambda e: e.tensor_tensor(out=rt[1], in0=x1, in1=sink, op=ALU.mult), reads=["kr_tm", "rkt", "kr_pad"], writes=["rt1"])
    op("vector", lambda e: e.tensor_tensor(out=kr_pad[:, :, 80:96], in0=rt[0], in1=rt[1], op=ALU.add), reads=["rt0", "rt1", "kr_pad"], writes=["kr_pad"])
    for g8 in range(8):
        pb = 6 + g8 % 2
        for i in range(8):
            tt = g8 * 8 + i
            op("tensor", lambda e, i=i, tt=tt, pb=pb: e.transpose(out=bankb(pb)[0:96, i * 128:(i + 1) * 128], in_=kr_pad[:, tt, :], identity=idb),
               reads=["kr_pad", "idb"], writes=[("ps", pb)])
        op("vector", lambda e, pb=pb, g8=g8: e.tensor_copy(out=KT[64:96, g8 * 1024:(g8 + 1) * 1024], in_=bankb(pb)[64:96, :]),
           reads=[("ps", pb)], writes=["KTr"])
    dump("ckvT", ckvT, [128, 2, S_LEN], BF16, ["ckvT"])
    dump("KTr", KT[64:96, :], [32, S_LEN], BF16, ["KTr"])
    S.barrier()

    ynTb = RY.alloc([128, 4, NX], BF16)
    RT.off = RT.lo
    QBT = RT.alloc([96, 8, NX], BF16)
    R1s = Region(R1s_lo, R1.hi)
    Wuq = R1s.alloc([128, 3, 768], BF16)
    Wrot = R1s.alloc([128, 3, 8, 96], BF16)
    stage3 = R1s.alloc([128, 3, 768], F32)
    rqt = RT.alloc([96, 2, NX], F32)
    tq = [RT.alloc([96, 410], F32) for _ in range(2)]
    load_weight(Wuq, w_uq, 3, 768, 8, stage3, "Wuq")
    op("gpsimd", lambda e: e.memset(Wrot, 0.0), writes=["Wrot"])
    Wuq4 = Wuq.rearrange("p k (h c) -> p k h c", h=8)
    for k in range(3):
        op("gpsimd", lambda e, k=k: e.tensor_scalar(out=Wrot[:, k, :, 64:80], in0=Wuq4[:, k, :, 80:96], scalar1=-1.0, scalar2=None, op0=ALU.mult),
           reads=["Wuq", "Wrot"], writes=["Wrot"])
        op("gpsimd", lambda e, k=k: e.tensor_copy(out=Wrot[:, k, :, 80:96], in_=Wuq4[:, k, :, 64:80]), reads=["Wuq", "Wrot"], writes=["Wrot"])
    op("sync", lambda e: e.dma_start(out=rqt[64:96], in_=rq), writes=["rqt"], dma=True)
    qi = 0
    for h in range(8):
        for (c0, cn) in XBLK:
            pa, pr_ = 2 * (qi % 2), 1 + 2 * (qi % 2)
            tb_ = tq[qi % 2]
            tk = ("tq", qi % 2)
            qi += 1
            for k in range(3):
                op("tensor", lambda e, k=k, h=h, c0=c0, cn=cn, pa=pa: e.matmul(bank(pa, cn, 96), lhsT=Wuq[:, k, h * 96:(h + 1) * 96], rhs=cqT[:, k, c0:c0 + cn], start=(k == 0), stop=(k == 2)),
                   reads=["Wuq", "cqT"], writes=[("ps", pa)])
            for k in range(3):
                op("tensor", lambda e, k=k, h=h, c0=c0, cn=cn, pr_=pr_: e.matmul(bank(pr_, cn, 96), lhsT=Wrot[:, k, h, :], rhs=cqT[:, k, c0:c0 + cn], start=(k == 0), stop=(k == 2)),
                   reads=["Wrot", "cqT"], writes=[("ps", pr_)])
            op("scalar", lambda e, h=h, c0=c0, cn=cn, pa=pa: e.copy(out=QBT[0:64, h, c0:c0 + cn], in_=bank(pa, cn, 64)), reads=[("ps", pa)], writes=["QBT"])
            op("vector", lambda e, c0=c0, cn=cn, pa=pa, tb_=tb_: e.tensor_tensor(out=tb_[64:96, 0:cn], in0=bank(pa, cn, 32, 64), in1=rqt[64:96, 0, c0:c0 + cn], op=ALU.mult),
               reads=[("ps", pa), "rqt"], writes=[tk])
            op("vector", lambda e, h=h, c0=c0, cn=cn, pr_=pr_: e.tensor_tensor(out=QBT[64:96, h, c0:c0 + cn], in0=bank(pr_, cn, 32, 64), in1=rqt[64:96, 1, c0:c0 + cn], op=ALU.mult),
               reads=[("ps", pr_), "rqt"], writes=["QBT"])
            op("vector", lambda e, h=h, c0=c0, cn=cn, tb_=tb_: e.tensor_tensor(out=QBT[64:96, h, c0:c0 + cn], in0=QBT[64:96, h, c0:c0 + cn], in1=tb_[64:96, 0:cn], op=ALU.add),
               reads=[tk, "QBT"], writes=["QBT"])
    dump("QBT", QBT, [96, 8, NX], BF16, ["QBT"])
    S.barrier()
    RT.off = RT.lo + 32832
    yb_tm = RT.alloc([128, 17, 512], F32)
    ynb_tmp = [RT.alloc([128, 512], BF16) for _ in range(2)]
    R1s = Region(R1s_lo, R1.hi)
    Vb = [R1s.alloc([128, 64, 65], BF16) for _ in range(2)]
    PT = [R1s.alloc([128, 2, 512], BF16) for _ in range(3)]
    OaccB = R1s.alloc([128, NX], F32)
    for b_ in range(2):
        op("gpsimd", lambda e, b_=b_: e.memset(Vb[b_][:, :, 64], 1.0), writes=[("Vb", b_)])
    scale_b = 96.0 ** -0.5
    MB = [(1 + 512 * i, 512) for i in range(4)]
    QG = [(0, 2), (2, 2)]
    sctr = [0]
    pctr = [0]

    def slot():
        s = 2 * (sctr[0] % 3)
        sctr[0] += 1
        return s

    def m_prep_v(h):
        vb = Vb[h % 2]
        for k8 in range(8):
            if k8 % 2 == 0:
                sb_ = slot()
            pb = sb_ + k8 % 2
            for i in range(8):
                kt = k8 * 8 + i
                for j in range(2):
                    op("tensor", lambda e, j=j, kt=kt, i=i: e.matmul(bank(pb)[:, i * 64:(i + 1) * 64], lhsT=ckvT[:, j, kt * 128:(kt + 1) * 128], rhs=Wukv[:, j, h * 128 + 64:h * 128 + 128], start=(j == 0), stop=(j == 1)),
                       reads=["Wukv", "ckvT"], writes=[("ps", pb)])
            op("vector", lambda e: e.tensor_copy(out=vb[:, k8 * 8:(k8 + 1) * 8, 0:64], in_=bank(pb).rearrange("p (t c) -> p t c", t=8)),
               reads=[("ps", pb)], writes=[("Vb", h % 2)])

    def m_prep_k(h):
        for nb_ in range(16):
            if nb_ % 2 == 0:
                sb_ = slot()
            pb = sb_ + nb_ % 2
            for j in range(2):
                op("tensor", lambda e, j=j: e.matmul(bank(pb, 512, 64), lhsT=Wukv[:, j, h * 128:h * 128 + 64], rhs=ckvT[:, j, nb_ * 512:(nb_ + 1) * 512], start=(j == 0), stop=(j == 1)),
                   reads=["Wukv", "ckvT"], writes=[("ps", pb)])
            if nb_ % 2 == 0:
                op("vector", lambda e: e.tensor_copy(out=KT[0:64, nb_ * 512:(nb_ + 1) * 512], in_=bank(pb, 512, 64)),
                   reads=[("ps", pb)], writes=[("grp", "KTn", nb_)])
            else:
                op("scalar", lambda e: e.copy(out=KT[0:64, nb_ * 512:(nb_ + 1) * 512], in_=bank(pb, 512, 64)),
                   reads=[("ps", pb)], writes=[("grp", "KTn", nb_)])

    m_prep_v(0)
    for h in range(8):
        vb = Vb[h % 2]
        m_prep_k(h)
        for gq, (b0, nbk) in enumerate(QG):
            ob = 6
            sl = {}

            def emit_S(kt):
                sbk = slot()
                sl[kt] = sbk
                for bb in range(nbk):
                    c0, cn = MB[b0 + bb]
                    op("tensor", lambda e, bb=bb, c0=c0, cn=cn: e.matmul(bank(sbk + bb, cn), lhsT=KT[0:96, kt * 128:(kt + 1) * 128], rhs=QBT[0:96, h, c0:c0 + cn], start=True, stop=True),
                       reads=[("grp", "KTn", kt // 4), "KTr", "QBT"], writes=[("ps", sbk + bb)])
            emit_S(0)
            emit_S(1)
            for kt in range(64):
                if kt + 2 < 64:
                    emit_S(kt + 2)
                sbk = sl[kt]
                pt = PT[pctr[0] % 3]
                ptk = ("PT", pctr[0] % 3)
                pctr[0] += 1
                sv = ps[:, sbk * 512:(sbk + nbk) * 512].rearrange("p (a b) -> p a b", a=nbk)
                op("scalar", lambda e: e.activation(out=pt[:, 0:nbk, :], in_=sv, func=AF.Exp, scale=scale_b),
                   reads=[("ps", sbk + bb) for bb in range(nbk)], writes=[ptk])
                for bb in range(nbk):
                    op("tensor", lambda e, bb=bb: e.matmul(bank(ob + bb, 512, 65), lhsT=vb[:, kt, :], rhs=pt[:, bb, :], start=(kt == 0), stop=(kt == 63)),
                       reads=[ptk, ("Vb", h % 2)], writes=[("ps", ob + bb)])
            if gq == 0 and h + 1 < 8:
                m_prep_v(h + 1)
            ov = ps[0:65, ob * 512:(ob + nbk) * 512]
            c0 = MB[b0][0]
            op("vector", lambda e: e.tensor_copy(out=OaccB[0:65, c0:c0 + nbk * 512], in_=ov),
               reads=[("ps", ob + bb) for bb in range(nbk)], writes=["OaccB"])
        sbk = slot()
        pt = PT[pctr[0] % 3]
        ptk = ("PT", pctr[0] % 3)
        pctr[0] += 1
        qh = QBT[0:96, h, 0:NX:NX - 1]
        for kt in range(64):
            op("tensor", lambda e, kt=kt: e.matmul(bank(sbk)[:, 2 * kt:2 * kt + 2], lhsT=KT[0:96, kt * 128:(kt + 1) * 128], rhs=qh, start=True, stop=True),
               reads=["KTn", "KTr", "QBT"], writes=[("ps", sbk)])
        op("scalar", lambda e: e.activation(out=pt[:, 0, 0:128], in_=bank(sbk, 128), func=AF.Exp, scale=scale_b), reads=[("ps", sbk)], writes=[ptk])
        for kt in range(64):
            op("tensor", lambda e, kt=kt: e.matmul(bank(6, 2, 65), lhsT=vb[:, kt, :], rhs=pt[:, 0, 2 * kt:2 * kt + 2], start=(kt == 0), stop=(kt == 63)),
               reads=[ptk, ("Vb", h % 2)], writes=[("ps", 6)])
        op("vector", lambda e: e.tensor_copy(out=OaccB[0:65, 0:NX:NX - 1], in_=bank(6, 2, 65)), reads=[("ps", 6)], writes=["OaccB"])
        oacc_to_tm(h, yb_tm, "OaccB", OaccB)
    tm_to_ynT(yb_tm, ynTb, "ynTb", ynb_tmp)
    dump("ynTb", ynTb, [128, 4, NX], BF16, ["ynTb"])
    dump("yb_tm", yb_tm, [128, 17, 512], F32, ["tm"])
    S.barrier()

    RT.off = RT.lo
    hTf = RT.alloc([128, 8, NX], BF16)
    A = Region(R1.lo, RC.hi)
    Wo = A.alloc([128, 8, D], BF16)
    stage4 = A.alloc([128, 8, 256], F32)
    gpo = A.alloc([128, 2, D], F32)
    xt_ = [A.alloc([128, D], F32) for _ in range(4)]
    xm_ = [A.alloc([128, D], F32) for _ in range(3)]
    hn_ = [A.alloc([128, D], BF16) for _ in range(2)]
    for c in range(4):
        load_weight(Wo[:, :, c * 256:(c + 1) * 256], w_o[:, c * 256:(c + 1) * 256], 8, 256, 13, stage4, "Wo")
    op("sync", lambda e: e.dma_start(out=gpo, in_=gpost), writes=["gpo"], dma=True)

    def f0M(t):
        return 128 if t < 16 else 2

    def f0s0(t):
        xt = xt_[t % 4]
        xk = ("xt", t % 4)
        if t < 16:
            op("sync", lambda e: e.dma_start(out=xt, in_=xw[OWN0 + 128 * t:OWN0 + 128 * (t + 1), :]), writes=[xk], dma=True)
        else:
            op("sync", lambda e: e.dma_start(out=xt[0:1, :], in_=xw[OWN0 - 1:OWN0, :]), writes=[xk], dma=True)
            op("sync", lambda e: e.dma_start(out=xt[1:2, :], in_=xw[OWN0 + OWN:OWN0 + OWN + 1, :]), writes=[xk], dma=True)

    def f0s1(t):
        M = f0M(t)
        pb = 2 * (t % 3)
        for n2 in range(2):
            for k in range(8):
                if t < 16:
                    la = (ynTa if k < 4 else ynTb)[:, k % 4, 1 + 128 * t:1 + 128 * (t + 1)]
                else:
                    la = (ynTa if k < 4 else ynTb)[:, k % 4, 0:NX:NX - 1]
                op("tensor", lambda e, k=k, n2=n2, la=la: e.matmul(bank(pb + n2, 512, M), lhsT=la, rhs=Wo[:, k, n2 * 512:(n2 + 1) * 512], start=(k == 0), stop=(k == 7)),
                   reads=["ynTa", "ynTb", "Wo"], writes=[("ps", pb + n2)])

    def f0s2(t):
        M = f0M(t)
        pb = 2 * (t % 3)
        yv = ps[0:M, pb * 512:(pb + 2) * 512]
        sc = 192 + 2 * (t % 4)
        sk = ("stt5", t % 4)
        op("scalar", lambda e: e.activation(out=junk[0:M, :], in_=yv, func=AF.Square, accum_out=stt[0:M, sc:sc + 1]),
           reads=[("ps", pb), ("ps", pb + 1)], writes=["junk", sk])
        op("scalar", lambda e: e.activation(out=stt[0:M, sc:sc + 1], in_=stt[0:M, sc:sc + 1], func=AF.Sqrt, bias=EPS, scale=1.0 / D), reads=[sk], writes=[sk])

    def f0s3(t):
        M = f0M(t)
        pb = 2 * (t % 3)
        yv = ps[0:M, pb * 512:(pb + 2) * 512]
        sc = 192 + 2 * (t % 4)
        sk = ("stt5", t % 4)
        xm, xt = xm_[t % 3], xt_[t % 4]
        mkk, xk = ("xm", t % 3), ("xt", t % 4)
        op("vector", lambda e: e.reciprocal(out=stt[0:M, sc + 1:sc + 2], in_=stt[0:M, sc:sc + 1]), reads=[sk], writes=[("stt5r", t % 4)])
        op("vector", lambda e: e.scalar_tensor_tensor(out=xm[0:M, :], in0=yv, scalar=stt[0:M, sc + 1:sc + 2], in1=gpo[0:M, 0, :], op0=ALU.mult, op1=ALU.mult),
           reads=[("ps", pb), ("ps", pb + 1), ("stt5r", t % 4), "gpo"], writes=[mkk])
        op("vector", lambda e: e.tensor_tensor(out=xm[0:M, :], in0=xm[0:M, :], in1=xt[0:M, :], op=ALU.add), reads=[mkk, xk], writes=[mkk])

    def f0s4(t):
        M = f0M(t)
        xm = xm_[t % 3]
        mkk = ("xm", t % 3)
        if t < 16:
            op("sync", lambda e: e.dma_start(out=xmid[128 * t:128 * (t + 1), :], in_=xm), reads=[mkk], writes=["xmid"], dma=True)
        sc2 = 200 + 2 * (t % 4)
        sk2 = ("stt6", t % 4)
        op("scalar", lambda e: e.activation(out=junk[0:M, :], in_=xm[0:M, :], func=AF.Square, accum_out=stt[0:M, sc2:sc2 + 1]),
           reads=[mkk], writes=["junk", sk2])
        op("scalar", lambda e: e.activation(out=stt[0:M, sc2:sc2 + 1], in_=stt[0:M, sc2:sc2 + 1], func=AF.Sqrt, bias=EPS, scale=1.0 / D), reads=[sk2], writes=[sk2])

    def f0s5(t):
        M = f0M(t)
        xm, hn = xm_[t % 3], hn_[t % 2]
        sc2 = 200 + 2 * (t % 4)
        op("vector", lambda e: e.reciprocal(out=stt[0:M, sc2 + 1:sc2 + 2], in_=stt[0:M, sc2:sc2 + 1]), reads=[("stt6", t % 4)], writes=[("stt6r", t % 4)])
        op("vector", lambda e: e.tensor_scalar(out=hn[0:M, :], in0=xm[0:M, :], scalar1=stt[0:M, sc2 + 1:sc2 + 2], scalar2=None, op0=ALU.mult),
           reads=[("xm", t % 3), ("stt6r", t % 4)], writes=[("hn", t % 2)])

    def f0s6(t):
        M = f0M(t)
        hn = hn_[t % 2]
        pb2 = 6 + t % 2
        for k in range(8):
            op("tensor", lambda e, k=k: e.transpose(out=bankb(pb2)[:, k * 128:k * 128 + M], in_=hn[0:M, k * 128:(k + 1) * 128], identity=idb[0:M, 0:M]),
               reads=[("hn", t % 2), "idb"], writes=[("ps", pb2)])

    def f0s7(t):
        M = f0M(t)
        pb2 = 6 + t % 2
        srcv = bankb(pb2).rearrange("p (k t) -> p k t", k=8)[:, :, 0:M]
        dstv = hTf[:, :, 1 + 128 * t:1 + 128 * (t + 1)] if t < 16 else hTf[:, :, 0:NX:NX - 1]
        if t % 2 == 0:
            op("vector", lambda e: e.tensor_copy(out=dstv, in_=srcv), reads=[("ps", pb2)], writes=[("grp", "hTf", t)])
        else:
            op("scalar", lambda e: e.copy(out=dstv, in_=srcv), reads=[("ps", pb2)], writes=[("grp", "hTf", t)])

    f0st = [f0s0, f0s1, f0s2, f0s3, f0s4, f0s5, f0s6, f0s7]
    for it in range(17 + len(f0st) - 1):
        for k in range(len(f0st) - 1, -1, -1):
            tt_ = it - k
            if 0 <= tt_ < 17:
                f0st[k](tt_)
    dump("hTf", hTf, [128, 8, NX], BF16, ["hTf"])
    S.barrier()

    aT = Region(R1.lo, RC.hi).alloc([128, 22, OWN], BF16)
    A = Region(RY.lo, RY.hi)
    stg = [A.alloc([128, 8, 256], F32) for _ in range(2)]
    Wub = [A.alloc([128, 8, 256], BF16) for _ in range(2)]
    cwt = A.alloc([128, 44, 4], F32)
    ufl = A.alloc([128, 2], F32)
    A = Region(RT.lo + 32832, SB_BYTES)
    cgb = [A.alloc([128, OWN], F32) for _ in range(2)]
    cvb = [A.alloc([128, OWN], F32) for _ in range(2)]
    op("sync", lambda e: e.dma_start(out=cwt, in_=cwb), writes=["cwt"], dma=True)
    op("sync", lambda e: e.dma_start(out=ufl, in_=uflag), writes=["ufl"], dma=True)
    OB = [(410 * i, min(410, OWN - 410 * i)) for i in range(5)]
    pbi = 0

    def f1_weights(j):
        bi = j % 2
        sg, wb = stg[bi], Wub[bi]
        sgk, wbk = ("stg", bi), ("Wub", bi)
        op("gpsimd", lambda e: e.dma_start(out=sg[:, :, 0:128], in_=w_up[:, j * 128:(j + 1) * 128].rearrange("(k p) n -> p k n", p=128)), writes=[sgk], dma=True)
        op("gpsimd", lambda e: e.dma_start(out=sg[:, :, 128:256], in_=w_up[:, DFF + j * 128:DFF + (j + 1) * 128].rearrange("(k p) n -> p k n", p=128)), writes=[sgk], dma=True)
        for k in range(8):
            if k % 2 == 0:
                op("vector", lambda e, k=k: e.tensor_scalar(out=wb[:, k, :], in0=sg[:, k, :], scalar1=gpt[:, 21 + k:22 + k], scalar2=None, op0=ALU.mult),
                   reads=[sgk, "gpt"], writes=[("grp", wbk, k)])
            else:
                op("scalar", lambda e, k=k: e.activation(out=wb[:, k, :], in_=sg[:, k, :], func=AF.Identity, scale=gpt[:, 21 + k:22 + k]),
                   reads=[sgk, "gpt"], writes=[("grp", wbk, k)])

    def f1_tail(j):
        bi = j % 2
        cg, cv = cgb[bi], cvb[bi]
        cgk = [("cg", bi, bx) for bx in range(5)]
        cvk = [("cv", bi, bx) for bx in range(5)]
        op("scalar", lambda e: e.activation(out=cg, in_=cg, func=AF.Gelu_apprx_tanh), reads=cgk, writes=cgk)
        op("vector", lambda e: e.tensor_tensor(out=aT[:, j, :], in0=cg, in1=cv, op=ALU.mult), reads=cgk + cvk, writes=[("aT", j)])

    f1_weights(0)
    for j in range(22):
        bi = j % 2
        wb = Wub[bi]
        wbk = ("Wub", bi)
        if j + 1 < 22:
            f1_weights(j + 1)
        for half in range(2):
            cb = (cgb if half == 0 else cvb)[bi]
            ff = j + 22 * half
            for bx, (o0, n) in enumerate(OB):
                ck = ("cg" if half == 0 else "cv", bi, bx)
                pb = pbi % 8
                pbi += 1
                for k in range(8):
                    op("tensor", lambda e, k=k: e.matmul(bank(pb, n + 2), lhsT=wb[:, k, half * 128:(half + 1) * 128], rhs=hTf[:, k, o0:o0 + n + 2], start=(k == 0), stop=(k == 7)),
                       reads=[wbk, "hTf"], writes=[("ps", pb)])
                if bx == 0:
                    op("vector", lambda e: e.tensor_tensor(out=bank(pb, 1), in0=bank(pb, 1), in1=ufl[:, 0:1], op=ALU.mult), reads=[("ps", pb), "ufl"], writes=[("ps", pb)])
                if bx == 4:
                    op("vector", lambda e: e.tensor_tensor(out=bank(pb, n + 2)[:, n + 1:n + 2], in0=bank(pb, n + 2)[:, n + 1:n + 2], in1=ufl[:, 1:2], op=ALU.mult), reads=[("ps", pb), "ufl"], writes=[("ps", pb)])
                op("scalar", lambda e: e.activation(out=cb[:, o0:o0 + n], in_=bank(pb, n + 2)[:, 1:n + 1], func=AF.Identity, bias=cwt[:, ff, 3:4], scale=cwt[:, ff, 1:2]),
                   reads=[("ps", pb), "cwt"], writes=[ck])
                op("vector", lambda e: e.scalar_tensor_tensor(out=cb[:, o0:o0 + n], in0=bank(pb, n + 2)[:, 0:n], scalar=cwt[:, ff, 0:1], in1=cb[:, o0:o0 + n], op0=ALU.mult, op1=ALU.add),
                   reads=[("ps", pb), "cwt", ck], writes=[ck])
                op("vector", lambda e: e.scalar_tensor_tensor(out=cb[:, o0:o0 + n], in0=bank(pb, n + 2)[:, 2:n + 2], scalar=cwt[:, ff, 2:3], in1=cb[:, o0:o0 + n], op0=ALU.mult, op1=ALU.add),
                   reads=[("ps", pb), "cwt", ck], writes=[ck])
                if half == 0 and bx == 2 and j > 0:
                    f1_tail(j - 1)
    f1_tail(21)
    dump("aT", aT, [128, 22, OWN], BF16, [("aT", j) for j in range(22)])
    S.barrier()

    A = Region(RY.lo, SB_BYTES)
    Wdn = A.alloc([128, 22, D], BF16)
    stage5 = A.alloc([128, 2, D], F32)
    gpo2 = A.alloc([128, D], F32)
    xm2 = [A.alloc([128, D], F32) for _ in range(2)]
    ot = [A.alloc([128, D], F32) for _ in range(2)]
    for c in range(11):
        load_weight(Wdn[:, 2 * c:2 * c + 2, :], w_down[256 * c:256 * (c + 1), :], 2, D, None, stage5, "Wdn")
    op("sync", lambda e: e.dma_start(out=gpo2, in_=gpost[:, 1, :]), writes=["gpo2"], dma=True)
    for t in range(16):
        bi = t % 2
        xk, ok = ("xm2", bi), ("ot", bi)
        op("sync", lambda e, t=t, bi=bi: e.dma_start(out=xm2[bi], in_=xmid[128 * t:128 * (t + 1), :]), reads=["xmid"], writes=[xk], dma=True)
        pb = 2 * (t % 2)
        for n2 in range(2):
            for j in range(22):
                op("tensor", lambda e, j=j, n2=n2, t=t, pb=pb: e.matmul(bank(pb + n2), lhsT=aT[:, j, 128 * t:128 * (t + 1)], rhs=Wdn[:, j, n2 * 512:(n2 + 1) * 512], start=(j == 0), stop=(j == 21)),
                   reads=[("aT", j), "Wdn"], writes=[("ps", pb + n2)])
        yv = ps[:, pb * 512:(pb + 2) * 512]
        sc = 144 + 2 * (t % 4)
        sk = ("stt7", t % 4)
        op("scalar", lambda e, yv=yv, sc=sc: e.activation(out=junk, in_=yv, func=AF.Square, accum_out=stt[:, sc:sc + 1]),
           reads=[("ps", pb), ("ps", pb + 1)], writes=["junk", sk])
        op("scalar", lambda e, sc=sc: e.activation(out=stt[:, sc:sc + 1], in_=stt[:, sc:sc + 1], func=AF.Sqrt, bias=EPS, scale=1.0 / D), reads=[sk], writes=[sk])
        op("vector", lambda e, sc=sc: e.reciprocal(out=stt[:, sc + 1:sc + 2], in_=stt[:, sc:sc + 1]), reads=[sk], writes=[sk])
        op("vector", lambda e, yv=yv, sc=sc, bi=bi: e.scalar_tensor_tensor(out=ot[bi], in0=yv, scalar=stt[:, sc + 1:sc + 2], in1=gpo2, op0=ALU.mult, op1=ALU.mult),
           reads=[("ps", pb), ("ps", pb + 1), sk, "gpo2"], writes=[ok])
        op("vector", lambda e, bi=bi: e.tensor_tensor(out=ot[bi], in0=ot[bi], in1=xm2[bi], op=ALU.add), reads=[ok, xk], writes=[ok])
        op("sync", lambda e, t=t, bi=bi: e.dma_start(out=yout[128 * t:128 * (t + 1), :], in_=ot[bi]), reads=[ok], dma=True)
    S.emit()
    return nc


_CACHE = {}


def _consts():
    if "c" in _CACHE:
        return _CACHE["c"]
    slopes = np.exp2(-8.0 * np.arange(1, 9, dtype=np.float32) / 8).astype(np.float32)
    k = np.arange(128)[:, None]
    q = np.arange(128)[None, :]
    masks = np.zeros((8, 128, 6, 128), np.float32)
    for h in range(8):
        for ri, r in enumerate((1, 4, 16)):
            d0 = k - 64 - q
            d1 = k + 64 - q
            masks[h, :, 2 * ri, :] = np.where(k >= q, np.exp(-slopes[h] * (np.abs(d0) * r).astype(np.float32)), 0.0)
            masks[h, :, 2 * ri + 1, :] = np.where(k <= q, np.exp(-slopes[h] * (np.abs(d1) * r).astype(np.float32)), 0.0)
    inv_freq = np.exp(-np.log(10000.0) * np.arange(0, 32, 2, dtype=np.float32) / 32).astype(np.float32)
    pos = np.arange(S_LEN, dtype=np.float32)
    ang = pos[:, None] * inv_freq[None, :]
    cosk = np.cos(ang).astype(np.float32).reshape(64, 128, 16).transpose(1, 0, 2)
    sink = np.sin(ang).astype(np.float32).reshape(64, 128, 16).transpose(1, 0, 2)
    rk = np.ascontiguousarray(np.stack([cosk, sink], axis=1))
    c = dict(masks=masks, inv_freq=inv_freq, rk=rk, ident=np.eye(128, dtype=np.float32))
    _CACHE["c"] = c
    return c


def _core_inputs(c, x, shared):
    cst = _consts()
    b, qc = c // 4, c % 4
    T0 = qc * OWN
    pos_w = T0 - OWN0 + np.arange(NW)
    valid = (pos_w >= 0) & (pos_w < S_LEN)
    xw = np.zeros((NW, D), np.float32)
    xw[valid] = x[b, pos_w[valid]]
    kvf = np.zeros((128, NVT), np.float32)
    for i, (r, es, nk) in enumerate(VLIST):
        kvf[:nk, i] = valid[es + r * np.arange(nk)].astype(np.float32)
    posq = (T0 - 1 + np.arange(NX)).astype(np.float32)
    ang = posq[None, :] * cst["inv_freq"][:, None]
    cq, sq = np.cos(ang).astype(np.float32), np.sin(ang).astype(np.float32)
    rq = np.ascontiguousarray(np.stack([np.concatenate([cq, cq], 0), np.concatenate([sq, sq], 0)], axis=1))
    uflag = np.zeros((128, 2), np.float32)
    uflag[:, 0] = 1.0 if T0 > 0 else 0.0
    uflag[:, 1] = 1.0 if T0 + OWN < S_LEN else 0.0
    d = dict(shared)
    d.update(xw=xw, xf=np.ascontiguousarray(x[b]), kvf=kvf, rq=rq, uflag=uflag, masks=cst["masks"], rk=cst["rk"], ident=cst["ident"])
    return d


def kernel(x, norm_mix_pre, w_in, q_lat_norm, w_uq, kv_lat_norm, w_ukv, out_norm_a, out_norm_b, w_o,
           norm_mix_post, norm_ffn_pre, w_up, conv_w, conv_b, w_down, norm_ffn_post):
    f = lambda a: np.ascontiguousarray(np.asarray(a, dtype=np.float32))
    x = f(x)
    gp = np.zeros((128, 32), np.float32)
    gp[:, 0:8] = f(norm_mix_pre)[0].reshape(8, 128).T
    gp[:, 8:11] = f(q_lat_norm)[0].reshape(3, 128).T
    gp[:, 11:13] = f(kv_lat_norm)[0].reshape(2, 128).T
    gp[:, 13:21] = np.concatenate([f(out_norm_a)[0], f(out_norm_b)[0]]).reshape(8, 128).T
    gp[:, 21:29] = f(norm_ffn_pre)[0].reshape(8, 128).T
    gpost = np.ascontiguousarray(np.broadcast_to(np.stack([f(norm_mix_post)[0], f(norm_ffn_post)[0]])[None], (128, 2, D)))
    cwb = np.zeros((128, 44, 4), np.float32)
    cwb[:, :, 0:3] = f(conv_w)[0].T.reshape(44, 128, 3).transpose(1, 0, 2)
    cwb[:, :, 3] = f(conv_b)[0].reshape(44, 128).T
    shared = dict(w_in=f(w_in)[0], w_uq=f(w_uq)[0], w_ukv=f(w_ukv)[0], w_o=f(w_o)[0], w_up=f(w_up)[0], w_down=f(w_down)[0],
                  gp=gp, gpost=gpost, cwb=cwb)
    if "nc" not in _CACHE:
        _CACHE["nc"] = build()
    nc = _CACHE["nc"]
    in_maps = [_core_inputs(c, x, shared) for c in range(8)]
    res = run_bass_kernel_spmd(nc, in_maps, core_ids=list(range(8)))
    out = np.zeros((2, S_LEN, D), np.float32)
    for c in range(8):
        b, qc = c // 4, c % 4
        out[b, qc * OWN:(qc + 1) * OWN] = res.results[c]["y"]
    return out
```

```python
import contextlib
import types
import numpy as np
import ml_dtypes
import concourse.bass as bass
import concourse.mybir as mybir
from concourse.bass_utils import run_bass_kernel_spmd

F32 = mybir.dt.float32
BF16 = mybir.dt.bfloat16
U8 = mybir.dt.uint8
AF = mybir.ActivationFunctionType
ALU = mybir.AluOpType

S_LEN = 8192
D = 1024
OWN = 2048
OWN0 = 1152
NW = 4352
NX = 2050
DFF = 2816
EPS = 1e-6
ENGS = ("sync", "scalar", "gpsimd", "vector", "tensor")
XBLK = [(i * 410, 410) for i in range(5)]


class Sched:
    def __init__(self, nc, ndma_sems=8):
        self.nc = nc
        self.ops = []
        self.ndma = ndma_sems

    @staticmethod
    def _freeze(fn):
        if fn.__closure__ is None:
            return fn
        cells = []
        for c in fn.__closure__:
            try:
                cells.append(types.CellType(c.cell_contents))
            except ValueError:
                cells.append(c)
        return types.FunctionType(fn.__code__, fn.__globals__, fn.__name__, fn.__defaults__, tuple(cells))

    def op(self, eng, fn, reads=(), writes=(), dma=False):
        fn = self._freeze(fn)
        self.ops.append(dict(eng=eng, fn=fn, reads=tuple(reads), writes=tuple(writes), dma=dma, bar=False))

    def barrier(self):
        self.ops.append(dict(eng=None, fn=None, reads=(), writes=(), dma=False, bar=True))

    def emit(self, final_wait_eng="sync"):
        nc = self.nc
        ops = self.ops
        n = len(ops)
        groups = {}
        for o in ops:
            for k in o["reads"] + o["writes"]:
                if isinstance(k, tuple) and len(k) == 3 and k[0] == "grp":
                    groups.setdefault(k[1], set()).add(k)

        def expand(keys):
            out = []
            for k in keys:
                out.append(k)
                if k in groups:
                    out.extend(groups[k])
            return out

        last_writer = {}
        readers = {}
        deps = [dict() for _ in range(n)]
        since_bar = []
        pending_bar = {}
        for i, o in enumerate(ops):
            if o["bar"]:
                lastc = {}
                dl = set()
                for j in since_bar:
                    if ops[j]["dma"]:
                        dl.add(j)
                    else:
                        lastc[ops[j]["eng"]] = j
                dl.update(lastc.values())
                for e in ENGS:
                    pending_bar[e] = set(dl) | pending_bar.get(e, set())
                since_bar = []
                continue
            d = deps[i]
            if o["eng"] in pending_bar:
                for j in pending_bar.pop(o["eng"]):
                    d[j] = True
            rk, wk = expand(o["reads"]), expand(o["writes"])
            for r in rk:
                if r in last_writer:
                    d[last_writer[r]] = True
            for w in wk:
                if w in last_writer:
                    d.setdefault(last_writer[w], False)
                for j in readers.get(w, ()):
                    d.setdefault(j, False)
            d.pop(i, None)
            for w in wk:
                last_writer[w] = i
                readers[w] = []
            for r in rk:
                if r not in wk:
                    readers.setdefault(r, []).append(i)
            since_bar.append(i)
        needed = set()
        red = [None] * n
        for i, o in enumerate(ops):
            if o["bar"]:
                continue
            per_eng = {}
            dl = []
            for j, is_raw in deps[i].items():
                pj = ops[j]
                if pj["dma"]:
                    dl.append(j)
                    continue
                if pj["eng"] == o["eng"] and not o["dma"] and (o["eng"] == "tensor" or not is_raw):
                    continue
                e = pj["eng"]
                if e not in per_eng or per_eng[e] < j:
                    per_eng[e] = j
            dl.extend(per_eng.values())
            red[i] = dl
            needed.update(dl)
        cnt = {e: 0 for e in ENGS}
        dcnt = {}
        sig = [None] * n
        dma_idx = {e: 0 for e in ENGS}
        for i, o in enumerate(ops):
            if o["bar"]:
                continue
            if o["dma"]:
                k = dma_idx[o["eng"]] % self.ndma
                dma_idx[o["eng"]] += 1
                key = ("dma", o["eng"], k)
                prev = dcnt.get(key, 0)
                dcnt[key] = prev + 16
                sig[i] = (key, prev + 16)
                o["dma_prev"] = (key, prev) if prev > 0 else None
            elif i in needed:
                cnt[o["eng"]] += 1
                sig[i] = (("eng", o["eng"]), cnt[o["eng"]])
        semkeys = sorted({s[0] for s in sig if s is not None}, key=str)
        stack = contextlib.ExitStack()
        sems = {}
        for sk in semkeys:
            sems[sk] = stack.enter_context(nc.semaphore("s_" + "_".join(str(x) for x in sk)))
        by_eng = {e: [i for i, o in enumerate(ops) if o["eng"] == e] for e in ENGS}
        dma_final = list(dcnt.items())

        def run(engname, eng):
            waited = {}
            for i in by_eng[engname]:
                o = ops[i]
                wl = [sig[j] for j in red[i]]
                if o["dma"] and o.get("dma_prev"):
                    wl.append(o["dma_prev"])
                for (sk, v) in wl:
                    if waited.get(sk, 0) >= v:
                        continue
                    eng.wait_ge(sems[sk], v)
                    waited[sk] = v
                ins = o["fn"](eng)
                if sig[i] is not None:
                    ins.then_inc(sems[sig[i][0]], 16 if o["dma"] else 1)
            if engname == final_wait_eng:
                for sk, v in dma_final:
                    if waited.get(sk, 0) < v:
                        eng.wait_ge(sems[sk], v)
                for e in ENGS:
                    if cnt[e] > 0 and waited.get(("eng", e), 0) < cnt[e]:
                        eng.wait_ge(sems[("eng", e)], cnt[e])

        with stack:
            with nc.Block() as block:
                @block.sync
                def _(e):
                    run("sync", e)

                @block.scalar
                def _(e):
                    run("scalar", e)

                @block.gpsimd
                def _(e):
                    run("gpsimd", e)

                @block.vector
                def _(e):
                    run("vector", e)

                @block.tensor
                def _(e):
                    run("tensor", e)


def dil_tables():
    vt = {}

    def vtile(r, e0, nk):
        key = (r, e0, nk)
        if key not in vt:
            vt[key] = len(vt)
        return vt[key]

    groups = {1: [], 4: [], 16: []}
    for r in (1, 4, 16):
        def blk(x0, N):
            eq0 = OWN0 - 1 + x0
            t0 = vtile(r, eq0 - 64 * r, 128)
            t1 = vtile(r, eq0 + 64 * r, 128 if N > 1 else 1)
            return (x0, N, t0, t1)
        if r == 1:
            for g in range(4):
                groups[r].append(("reg", g, [blk(1 + 128 * (4 * g + b), 128) for b in range(4)]))
        elif r == 4:
            for g in range(4):
                groups[r].append(("reg", g, [blk(1 + c + 512 * g, 128) for c in range(4)]))
        else:
            for g in range(4):
                groups[r].append(("reg", g, [blk(1 + 4 * g + c, 128) for c in range(4)]))
        groups[r].append(("halo", 0, [blk(0, 1), blk(NX - 1, 1)]))
    vlist = [None] * len(vt)
    for k, i in vt.items():
        vlist[i] = k
    return vlist, groups


VLIST, DGROUPS = dil_tables()
NVT = len(VLIST)


def build(dbg=False):
    nc = bass.Bass("TRN2", target_bir_lowering=False)

    def din(name, shape, dt=F32):
        return nc.dram_tensor(name, list(shape), dt, kind="ExternalInput").ap()

    xw = din("xw", [NW, D])
    xf = din("xf", [S_LEN, D])
    w_in = din("w_in", [D, 2208])
    w_uq = din("w_uq", [384, 768])
    w_ukv = din("w_ukv", [256, 1024])
    w_o = din("w_o", [D, D])
    w_up = din("w_up", [D, 2 * DFF])
    w_down = din("w_down", [DFF, D])
    gp = din("gp", [128, 32])
    gpost = din("gpost", [128, 2, D])
    cwb = din("cwb", [128, 44, 4])
    masks = din("masks", [8, 128, 6, 128])
    kvf = din("kvf", [128, NVT])
    rq = din("rq", [32, 2, NX])
    rk = din("rk", [128, 2, 64, 16])
    ident = din("ident", [128, 128])
    uflag = din("uflag", [128, 2])
    yout = nc.dram_tensor("y", [OWN, D], F32, kind="ExternalOutput").ap()
    xmid = nc.dram_tensor("xmid", [OWN, D], F32, kind="Internal").ap()
    dbg_out = {}

    SB_BYTES = 212000
    big = nc.alloc_sbuf_tensor("big", [128, SB_BYTES], U8).ap()
    ps = nc.alloc_psum_tensor("ps", [128, 4096], F32).ap()
    psb = ps.bitcast(BF16)

    class Region:
        def __init__(self, lo, hi):
            assert hi <= SB_BYTES and lo <= hi, (lo, hi)
            self.lo, self.hi, self.off = lo, hi, lo

        def alloc(self, shape, dt, p0=0):
            esz = 4 if dt == F32 else 2
            nb = int(np.prod(shape[1:])) * esz
            nb_al = (nb + 63) // 64 * 64
            assert self.off + nb_al <= self.hi, ("SBUF region overflow", self.lo, self.hi, self.off, nb_al)
            v = big[p0:p0 + shape[0], self.off:self.off + nb].bitcast(dt)
            self.off += nb_al
            if len(shape) == 3:
                v = v.rearrange("p (a b) -> p a b", a=shape[1])
            elif len(shape) == 4:
                v = v.rearrange("p (a b c) -> p a b c", a=shape[1], b=shape[2])
            return v

    RP = Region(0, 4096)
    R1 = Region(RP.hi, RP.hi + 86080)
    RC = Region(R1.hi, R1.hi + 12352)
    RY = Region(RC.hi, RC.hi + 16448 + 4096 + 16448)
    RT = Region(RY.hi, SB_BYTES)
    A = RP
    S = Sched(nc)
    op = S.op

    def dump(name, ap, shape, dt, keys):
        if not dbg:
            return
        import os
        sel = os.environ.get("DBGSEL", "")
        if sel and name not in sel.split(","):
            return
        d = nc.dram_tensor("dbg_" + name, list(shape), dt, kind="ExternalOutput").ap()
        op("sync", lambda e: e.dma_start(out=d, in_=ap), reads=keys, dma=True)

    def bank(b, n=512, p=128, p0=0):
        return ps[p0:p0 + p, b * 512:b * 512 + n]

    def bankb(b, n=1024, p=128, p0=0):
        return psb[p0:p0 + p, b * 1024:b * 1024 + n]

    idf = A.alloc([128, 128], F32)
    idb = A.alloc([128, 128], BF16)
    gpt = A.alloc([128, 32], F32)
    stt = A.alloc([128, 256], F32)
    junk = A.alloc([128, 1024], BF16)
    rec = A.alloc([128, 8], F32)
    op("sync", lambda e: e.dma_start(out=idf, in_=ident), writes=["idf"], dma=True)
    op("sync", lambda e: e.dma_start(out=gpt, in_=gp), writes=["gpt"], dma=True)
    op("vector", lambda e: e.tensor_copy(out=idb, in_=idf), reads=["idf"], writes=["idb"])

    def rstd_from_ss(ss_ap, out_ap, n, key):
        tmp = ss_ap
        op("scalar", lambda e: e.activation(out=tmp, in_=ss_ap, func=AF.Sqrt, bias=EPS, scale=1.0 / n),
           reads=[key], writes=[key])
        op("vector", lambda e: e.reciprocal(out=out_ap, in_=tmp), reads=[key], writes=[key])

    def load_weight(dst, src_ap, kch, ncols, gcol, stage, wkey, negate_cols=None):
        op("gpsimd", lambda e: e.dma_start(out=stage[:, 0:kch, 0:ncols], in_=src_ap.rearrange("(k p) n -> p k n", p=128)),
           writes=[("stage", id(stage))], dma=True)
        for k in range(kch):
            eng = "vector" if k % 2 == 0 else "scalar"
            if gcol is None:
                if eng == "vector":
                    op(eng, lambda e, k=k: e.tensor_copy(out=dst[:, k, :], in_=stage[:, k, 0:ncols]),
                       reads=[("stage", id(stage))], writes=[("grp", wkey, (id(dst), k))])
                else:
                    op(eng, lambda e, k=k: e.copy(out=dst[:, k, :], in_=stage[:, k, 0:ncols]),
                       reads=[("stage", id(stage))], writes=[("grp", wkey, (id(dst), k))])
            else:
                if eng == "vector":
                    op(eng, lambda e, k=k: e.tensor_scalar(out=dst[:, k, :], in0=stage[:, k, 0:ncols],
                                                          scalar1=gpt[:, gcol + k:gcol + k + 1], scalar2=None, op0=ALU.mult),
                       reads=[("stage", id(stage)), "gpt"], writes=[("grp", wkey, (id(dst), k))])
                else:
                    op(eng, lambda e, k=k: e.activation(out=dst[:, k, :], in_=stage[:, k, 0:ncols], func=AF.Identity,
                                                        scale=gpt[:, gcol + k:gcol + k + 1]),
                       reads=[("stage", id(stage)), "gpt"], writes=[("grp", wkey, (id(dst), k))])

    def norm_tiles_to_T(src_dram_rows, ntile, xbuf, xnbuf, key, slot=0):
        c0 = 2 * slot
        sk = ("sttn", slot)
        op("sync", lambda e: e.dma_start(out=xbuf[:, 0:ntile, :], in_=src_dram_rows.rearrange("(t p) f -> p t f", p=128)),
           writes=[("x", key)], dma=True)
        for t in range(ntile):
            op("scalar", lambda e, t=t: e.activation(out=junk, in_=xbuf[:, t, :], func=AF.Square, accum_out=stt[:, c0 + t:c0 + t + 1]),
               reads=[("x", key)], writes=["junk", sk])
        rstd_from_ss(stt[:, c0:c0 + ntile], stt[:, 8 + c0:8 + c0 + ntile], D, sk)
        for t in range(ntile):
            if t % 2 == 0:
                op("vector", lambda e, t=t: e.tensor_scalar(out=xnbuf[:, t, :], in0=xbuf[:, t, :], scalar1=stt[:, 8 + c0 + t:9 + c0 + t], scalar2=None, op0=ALU.mult),
                   reads=[("x", key), sk], writes=[("grp", ("xn", key), t)])
            else:
                op("scalar", lambda e, t=t: e.activation(out=xnbuf[:, t, :], in_=xbuf[:, t, :], func=AF.Identity, scale=stt[:, 8 + c0 + t:9 + c0 + t]),
                   reads=[("x", key), sk], writes=[("grp", ("xn", key), t)])

    def transpose_tile(xn_tile, hT_dst, pbank, rkeys, wkey):
        for k in range(8):
            op("tensor", lambda e, k=k: e.transpose(out=bankb(pbank)[:, k * 128:(k + 1) * 128], in_=xn_tile[:, k * 128:(k + 1) * 128], identity=idb),
               reads=list(rkeys) + ["idb"], writes=[("ps", pbank)])
        op("vector", lambda e: e.tensor_copy(out=hT_dst, in_=bankb(pbank).rearrange("p (k t) -> p k t", k=8)),
           reads=[("ps", pbank)], writes=[wkey])

    KAT = R1.alloc([128, 4, NW], BF16)
    VAT = R1.alloc([128, 4, NW], BF16)
    QAT = R1.alloc([128, 4, NX], BF16)
    cqT = RC.alloc([128, 3, NX], BF16)
    A = Region(RC.hi, SB_BYTES)
    WA = A.alloc([128, 8, 1920], BF16)
    stage = A.alloc([128, 8, 480], F32)
    xbuf = [A.alloc([128, 2, D], F32) for _ in range(2)]
    xnb = [A.alloc([128, 2, D], BF16) for _ in range(2)]
    hTw = [A.alloc([128, 8, 256], BF16) for _ in range(2)]
    cqn = A.alloc([128, 384], BF16)
    for c in range(4):
        load_weight(WA[:, :, c * 480:(c + 1) * 480], w_in[:, c * 480:(c + 1) * 480], 8, 480, 0, stage, "WA")

    def cq_tile(cols_ap, M, xcol_ap_fn, hkey, pb):
        for k in range(8):
            la = cols_ap(k)
            op("tensor", lambda e, k=k, la=la: e.matmul(bank(pb, 384, M), lhsT=la, rhs=WA[:, k, 1536:1920], start=(k == 0), stop=(k == 7)),
               reads=[hkey, "WA"], writes=[("ps", pb)])
        op("scalar", lambda e: e.activation(out=junk[0:M, 0:384], in_=bank(pb, 384, M), func=AF.Square, accum_out=stt[0:M, 16:17]),
           reads=[("ps", pb)], writes=["junk", "stt2"])
        op("scalar", lambda e: e.activation(out=stt[0:M, 16:17], in_=stt[0:M, 16:17], func=AF.Sqrt, bias=EPS, scale=1.0 / 384),
           reads=["stt2"], writes=["stt2"])
        op("vector", lambda e: e.reciprocal(out=stt[0:M, 17:18], in_=stt[0:M, 16:17]), reads=["stt2"], writes=["stt2"])
        op("vector", lambda e: e.tensor_scalar(out=cqn[0:M, :], in0=bank(pb, 384, M), scalar1=stt[0:M, 17:18], scalar2=None, op0=ALU.mult),
           reads=[("ps", pb), "stt2"], writes=["cqn"])
        for j in range(3):
            op("tensor", lambda e, j=j: e.transpose(out=bankb(pb)[:, j * 128:j * 128 + M], in_=cqn[0:M, j * 128:(j + 1) * 128], identity=idb[0:M, 0:M]),
               reads=["cqn", "idb"], writes=[("ps", pb)])
        xc = xcol_ap_fn()
        op("vector", lambda e: e.tensor_copy(out=xc, in_=bankb(pb)[:, 0:384].rearrange("p (j t) -> p j t", j=3)[:, :, 0:M]),
           reads=[("ps", pb)], writes=["cqT"])

    NGW = NW // 256

    xbuf.append(A.alloc([128, 2, D], F32))

    def ws0(g):
        xb = xbuf[g % 3]
        e0 = g * 256
        op("sync", lambda e: e.dma_start(out=xb, in_=xw[e0:e0 + 256, :].rearrange("(t p) f -> p t f", p=128)), writes=[("wx", g % 3)], dma=True)

    def ws1(g):
        xb = xbuf[g % 3]
        c = 176 + 2 * (g % 4)
        for t in range(2):
            op("scalar", lambda e, t=t: e.activation(out=junk, in_=xb[:, t, :], func=AF.Square, accum_out=stt[:, c + t:c + t + 1]),
               reads=[("wx", g % 3)], writes=["junk", ("wss", g % 4)])
        op("scalar", lambda e: e.activation(out=stt[:, c:c + 2], in_=stt[:, c:c + 2], func=AF.Sqrt, bias=EPS, scale=1.0 / D), reads=[("wss", g % 4)], writes=[("wss", g % 4)])

    def ws2(g):
        xb = xbuf[g % 3]
        xn = xnb[g % 2]
        c = 176 + 2 * (g % 4)
        op("vector", lambda e: e.reciprocal(out=stt[:, c + 8:c + 10], in_=stt[:, c:c + 2]), reads=[("wss", g % 4)], writes=[("wrs", g % 4)])
        for t in range(2):
            op("vector", lambda e, t=t: e.tensor_scalar(out=xn[:, t, :], in0=xb[:, t, :], scalar1=stt[:, c + 8 + t:c + 9 + t], scalar2=None, op0=ALU.mult),
               reads=[("wx", g % 3), ("wrs", g % 4)], writes=[("wxn", g % 2)])

    def ws3(g):
        xn = xnb[g % 2]
        for t in range(2):
            pb = 2 * (g % 2) + t
            for k in range(8):
                op("tensor", lambda e, k=k, t=t: e.transpose(out=bankb(pb)[:, k * 128:(k + 1) * 128], in_=xn[:, t, k * 128:(k + 1) * 128], identity=idb),
                   reads=[("wxn", g % 2), "idb"], writes=[("ps", pb)])

    def ws4(g):
        bi = g % 2
        for t in range(2):
            pb = 2 * (g % 2) + t
            if t == 0:
                op("vector", lambda e, t=t: e.tensor_copy(out=hTw[bi][:, :, t * 128:(t + 1) * 128], in_=bankb(pb).rearrange("p (k t) -> p k t", k=8)),
                   reads=[("ps", pb)], writes=[("grp", ("hTw", bi), t)])
            else:
                op("scalar", lambda e, t=t: e.copy(out=hTw[bi][:, :, t * 128:(t + 1) * 128], in_=bankb(pb).rearrange("p (k t) -> p k t", k=8)),
                   reads=[("ps", pb)], writes=[("grp", ("hTw", bi), t)])

    def w_stageB(g):
        bi = g % 2
        e0 = g * 256
        xlo, xhi = max(e0, OWN0 - 1), min(e0 + 256, OWN0 - 1 + NX)
        jobs = [("K", 512 + 128 * c, c) for c in range(4)] + [("V", 1024 + 128 * c, c) for c in range(4)]
        if xhi > xlo:
            jobs += [("Q", 128 * c, c) for c in range(4)]
        for ji, (kind, col0, c) in enumerate(jobs):
            pb = 4 + ji % 2
            pk = ("ps", pb)
            pv = bank(pb, 256)
            for k in range(8):
                op("tensor", lambda e, k=k: e.matmul(pv, lhsT=WA[:, k, col0:col0 + 128], rhs=hTw[bi][:, k, :], start=(k == 0), stop=(k == 7)),
                   reads=[("hTw", bi), "WA"], writes=[pk])
            if kind == "K":
                op("vector", lambda e: e.tensor_copy(out=KAT[:, c, e0:e0 + 256], in_=pv), reads=[pk], writes=["KAT"])
            elif kind == "V":
                op("scalar", lambda e: e.copy(out=VAT[:, c, e0:e0 + 256], in_=pv), reads=[pk], writes=["VAT"])
            else:
                op("vector", lambda e: e.tensor_copy(out=QAT[:, c, xlo - (OWN0 - 1):xhi - (OWN0 - 1)], in_=pv[:, xlo - e0:xhi - e0]),
                   reads=[pk], writes=["QAT"])
        for t in range(2):
            et = e0 + t * 128
            if OWN0 <= et < OWN0 + OWN:
                x0 = et - (OWN0 - 1)
                cq_tile(lambda k, t=t, bi=bi: hTw[bi][:, k, t * 128:(t + 1) * 128], 128, lambda x0=x0: cqT[:, :, x0:x0 + 128], ("hTw", bi), 6 + t)
            if et == OWN0 - 128:
                cq_tile(lambda k, t=t, bi=bi: hTw[bi][:, k, t * 128 + 127:t * 128 + 128], 1, lambda: cqT[:, :, 0:1], ("hTw", bi), 6 + t)
            if et == OWN0 + OWN:
                cq_tile(lambda k, t=t, bi=bi: hTw[bi][:, k, t * 128:t * 128 + 1], 1, lambda: cqT[:, :, NX - 1:NX], ("hTw", bi), 6 + t)

    wst = [ws0, ws1, ws2, ws3, ws4, w_stageB]
    for it in range(NGW + len(wst) - 1):
        for k in range(len(wst) - 1, -1, -1):
            g = it - k
            if 0 <= g < NGW:
                wst[k](g)
    dump("KAT", KAT, [128, 4, NW], BF16, ["KAT"])
    dump("VAT", VAT, [128, 4, NW], BF16, ["VAT"])
    dump("QAT", QAT, [128, 4, NX], BF16, ["QAT"])
    dump("cqT", cqT, [128, 3, NX], BF16, ["cqT"])
    S.barrier()

    ynTa = RY.alloc([128, 4, NX], BF16)
    A = Region(RY.lo + 16448, SB_BYTES)
    mk = [A.alloc([128, 6, 128], F32) for _ in range(2)]
    kvft = A.alloc([128, NVT], F32)
    Vp = [A.alloc([128, NVT, 65], BF16) for _ in range(2)]
    Oacc = A.alloc([128, NX], F32)
    ya_tm = A.alloc([128, 17, 512], F32)
    Eb = [A.alloc([128, 1024], F32) for _ in range(2)]
    Pb = [A.alloc([128, 1024], BF16) for _ in range(2)]
    op("sync", lambda e: e.dma_start(out=kvft, in_=kvf), writes=["kvft"], dma=True)

    def oacc_to_tm(h, dst_tm, okey, Oacc):
        tiles = [(1 + 128 * m, 128, 1) for m in range(16)] + [(0, 2, NX - 1)]
        for g0 in range(0, 17, 4):
            tl = tiles[g0:g0 + 4]
            pb = 6 + (g0 // 4) % 2
            for i, (x0, M, step) in enumerate(tl):
                src = Oacc[0:65, x0:x0 + 128] if step == 1 else Oacc[0:65, 0:NX:NX - 1]
                op("tensor", lambda e, i=i, src=src, M=M, pb=pb: e.transpose(out=bank(pb)[0:M, i * 65:(i + 1) * 65], in_=src, identity=idf[0:65, 0:65]),
                   reads=[okey, "idf"], writes=[("ps", pb)])
            nt = len(tl)
            M = tl[0][1]
            pv = bank(pb)[0:M, 0:nt * 65].rearrange("p (t c) -> p t c", t=nt)
            op("vector", lambda e, pv=pv, nt=nt, M=M: e.reciprocal(out=rec[0:M, 0:nt], in_=pv[:, :, 64]), reads=[("ps", pb)], writes=["rec"])
            op("vector", lambda e, pv=pv, nt=nt, M=M, g0=g0: e.tensor_tensor(out=dst_tm[0:M, g0:g0 + nt, h * 64:(h + 1) * 64], in0=pv[:, :, 0:64],
                                                                         in1=rec[0:M, 0:nt].unsqueeze(2).to_broadcast([M, nt, 64]), op=ALU.mult),
               reads=[("ps", pb), "rec"], writes=["tm"])

    def tm_to_ynT(src_tm, dstT, nkey, Pb):
        for t in range(17):
            M = 128 if t < 16 else 2
            op("scalar", lambda e, t=t, M=M: e.activation(out=junk[0:M, 0:512], in_=src_tm[0:M, t, :], func=AF.Square, accum_out=stt[0:M, 32 + t:33 + t]),
               reads=["tm"], writes=["junk", "stt3"])
        rstd_from_ss(stt[:, 32:49], stt[:, 64:81], 512, "stt3")
        for t in range(17):
            M = 128 if t < 16 else 2
            pb = t % 2
            ynb = Pb[t % 2]
            op("vector", lambda e, t=t, M=M, ynb=ynb: e.tensor_scalar(out=ynb[0:M, 0:512], in0=src_tm[0:M, t, :], scalar1=stt[0:M, 64 + t:65 + t], scalar2=None, op0=ALU.mult),
               reads=["tm", "stt3"], writes=[("Pb", t % 2)])
            for j in range(4):
                op("tensor", lambda e, j=j, M=M, ynb=ynb, pb=pb: e.transpose(out=bankb(pb)[:, j * 128:j * 128 + M], in_=ynb[0:M, j * 128:(j + 1) * 128], identity=idb[0:M, 0:M]),
                   reads=[("Pb", t % 2), "idb"], writes=[("ps", pb)])
            srcv = bankb(pb)[:, 0:512].rearrange("p (j t) -> p j t", j=4)[:, :, 0:M]
            if t < 16:
                dstv = dstT[:, :, 1 + 128 * t:1 + 128 * (t + 1)]
            else:
                dstv = dstT[:, :, 0:NX:NX - 1]
            op("vector", lambda e, srcv=srcv, dstv=dstv: e.tensor_copy(out=dstv, in_=srcv), reads=[("ps", pb)], writes=[nkey])

    def d_prep(h):
        pr, hs = h // 2, (h % 2) * 64
        vb = Vp[h % 2]
        mb = mk[h % 2]
        op("sync", lambda e: e.dma_start(out=mb, in_=masks[h]), writes=[("mk", h % 2)], dma=True)
        for v0 in range(0, NVT, 8):
            vts = VLIST[v0:v0 + 8]
            pb = 4 + (v0 // 8) % 2
            for i, (r, es, nk) in enumerate(vts):
                op("tensor", lambda e, i=i, r=r, es=es, nk=nk: e.transpose(out=bankb(pb)[0:nk, i * 64:(i + 1) * 64],
                                                                        in_=VAT[hs:hs + 64, pr, es:es + r * (nk - 1) + 1:r], identity=idb[hs:hs + 64, hs:hs + 64]),
                   reads=["VAT", "idb"], writes=[("ps", pb)])
            i = 0
            while i < len(vts):
                nk = vts[i][2]
                j = i
                while j + 1 < len(vts) and vts[j + 1][2] == nk:
                    j += 1
                cnt_ = j - i + 1
                srcv = bankb(pb)[0:nk, i * 64:(j + 1) * 64].rearrange("p (t c) -> p t c", t=cnt_)
                op("vector", lambda e, srcv=srcv, i=i, cnt_=cnt_, nk=nk: e.tensor_tensor(
                    out=vb[0:nk, v0 + i:v0 + i + cnt_, 0:64], in0=srcv,
                    in1=kvft[0:nk, v0 + i:v0 + i + cnt_].unsqueeze(2).to_broadcast([nk, cnt_, 64]), op=ALU.mult),
                   reads=[("ps", pb), "kvft"], writes=[("Vp", h % 2)])
                i = j + 1
        op("gpsimd", lambda e: e.tensor_copy(out=vb[:, :, 64], in_=kvft), reads=["kvft"], writes=[("Vp", h % 2)])

    GL = [(ri, r, kind, g, blks) for ri, r in enumerate((1, 4, 16)) for (kind, g, blks) in DGROUPS[r]]
    gctr = [0]

    def d_scores(h, gi, grp):
        pr, hs = h // 2, (h % 2) * 64
        ri, r, kind, g, blks = grp
        sb = 2 * (gi % 2)
        for bi_, (x0, Nq, t0, t1) in enumerate(blks):
            qv = QAT[hs:hs + 64, pr, x0:x0 + r * (Nq - 1) + 1:r]
            for role, tix in ((0, t0), (1, t1)):
                (rr, es, nk) = VLIST[tix]
                op("tensor", lambda e, role=role, es=es, nk=nk, qv=qv, bi_=bi_, Nq=Nq: e.matmul(
                    bank(sb + role)[0:nk, bi_ * 128:bi_ * 128 + Nq], lhsT=KAT[hs:hs + 64, pr, es:es + r * (nk - 1) + 1:r], rhs=qv, start=True, stop=True),
                   reads=["KAT", "QAT"], writes=[("ps", sb + role)])

    def d_rest(h, gi, grp):
        ri, r, kind, g, blks = grp
        vb = Vp[h % 2]
        mb = mk[h % 2]
        sb = 2 * (gi % 2)
        eb = Eb[gi % 2]
        pbuf = Pb[gi % 2]
        ob = 6 + gi % 2
        ek, pk = ("Eb", gi % 2), ("Pbuf", gi % 2)
        if kind == "reg":
            sv = ps[:, sb * 512:(sb + 2) * 512]
            op("scalar", lambda e: e.activation(out=eb, in_=sv, func=AF.Exp, scale=0.125),
               reads=[("ps", sb), ("ps", sb + 1)], writes=[ek])
            op("vector", lambda e: e.tensor_tensor(
                out=pbuf.rearrange("p (r b q) -> p r b q", r=2, b=4), in0=eb.rearrange("p (r b q) -> p r b q", r=2, b=4),
                in1=mb[:, 2 * ri:2 * ri + 2, :].unsqueeze(2).to_broadcast([128, 2, 4, 128]), op=ALU.mult),
               reads=[ek, ("mk", h % 2)], writes=[pk])
        else:
            op("scalar", lambda e: e.activation(out=eb[:, 0:256:128], in_=bank(sb)[:, 0:256:128], func=AF.Exp, scale=0.125),
               reads=[("ps", sb)], writes=[ek])
            op("scalar", lambda e: e.activation(out=eb[0:1, 512:768:128], in_=bank(sb + 1)[0:1, 0:256:128], func=AF.Exp, scale=0.125),
               reads=[("ps", sb + 1)], writes=[ek])
            op("vector", lambda e: e.tensor_tensor(
                out=pbuf[:, 0:256:128], in0=eb[:, 0:256:128], in1=mb[:, 2 * ri, 0:1].to_broadcast([128, 2]), op=ALU.mult),
               reads=[ek, ("mk", h % 2)], writes=[pk])
            op("vector", lambda e: e.tensor_tensor(
                out=pbuf[0:1, 512:768:128], in0=eb[0:1, 512:768:128], in1=mb[0:1, 2 * ri + 1, 0:1].to_broadcast([1, 2]), op=ALU.mult),
               reads=[ek, ("mk", h % 2)], writes=[pk])
        for bi_, (x0, Nq, t0, t1) in enumerate(blks):
            for role, tix in ((0, t0), (1, t1)):
                nk = VLIST[tix][2]
                op("tensor", lambda e, role=role, tix=tix, nk=nk, bi_=bi_, Nq=Nq: e.matmul(
                    bank(ob)[0:65, bi_ * 128:bi_ * 128 + Nq], lhsT=vb[0:nk, tix, :], rhs=pbuf[0:nk, role * 512 + bi_ * 128:role * 512 + bi_ * 128 + Nq],
                    start=(role == 0), stop=(role == 1)),
                   reads=[pk, ("Vp", h % 2)], writes=[("ps", ob)])
        if kind == "reg":
            src = bank(ob)[0:65, :].rearrange("p (c q) -> p c q", c=4)
            if r == 1:
                dst = Oacc[0:65, 1 + 512 * g:1 + 512 * (g + 1)].rearrange("p (c q) -> p c q", c=4)
            elif r == 4:
                dst = Oacc[0:65, 1 + 512 * g:1 + 512 * (g + 1)].rearrange("p (q c) -> p c q", c=4)
            else:
                dst = Oacc[0:65, 1:1 + OWN].rearrange("p (q c) -> p c q", c=16)[:, 4 * g:4 * g + 4, :]
        else:
            src = bank(ob)[0:65, 0:256:128]
            dst = Oacc[0:65, 0:NX:NX - 1]
        if r == 1:
            op("vector", lambda e: e.tensor_copy(out=dst, in_=src), reads=[("ps", ob)], writes=["Oacc"])
        else:
            op("vector", lambda e: e.tensor_tensor(out=dst, in0=src, in1=dst, op=ALU.add), reads=[("ps", ob), "Oacc"], writes=["Oacc"])

    d_prep(0)
    for h in range(8):
        g0 = gctr[0]
        d_scores(h, g0, GL[0])
        if h + 1 < 8:
            d_prep(h + 1)
        for i, grp in enumerate(GL):
            if i + 1 < len(GL):
                d_scores(h, g0 + i + 1, GL[i + 1])
            d_rest(h, g0 + i, grp)
        gctr[0] += len(GL)
        oacc_to_tm(h, ya_tm, "Oacc", Oacc)
    tm_to_ynT(ya_tm, ynTa, "ynTa", Pb)
    dump("ynTa", ynTa, [128, 4, NX], BF16, ["ynTa"])
    dump("ya_tm", ya_tm, [128, 17, 512], F32, ["tm"])
    S.barrier()

    R1.off = R1.lo
    ckvT = R1.alloc([128, 2, S_LEN], BF16)
    KT = R1.alloc([96, S_LEN], BF16)
    R1s_lo = R1.off
    RY.off = RY.lo + 16448
    Wukv = RY.alloc([128, 2, 1024], BF16)
    A = Region(RY.off, SB_BYTES)
    Wkvl = A.alloc([128, 8, 288], BF16)
    stage2 = A.alloc([128, 8, 512], F32)
    xbuf = [A.alloc([128, 2, D], F32) for _ in range(2)]
    xnb = [A.alloc([128, 2, D], BF16) for _ in range(2)]
    hTk = [A.alloc([128, 8, 128], BF16) for _ in range(2)]
    ckvn = [A.alloc([128, 256], BF16) for _ in range(2)]
    kr_tm = A.alloc([128, 64, 32], F32)
    rkt = A.alloc([128, 2, 64, 16], F32)
    kr_pad = A.alloc([128, 64, 96], BF16)
    rt = [A.alloc([128, 64, 16], F32) for _ in range(2)]
    load_weight(Wkvl, w_in[:, 1920:2208], 8, 288, 0, stage2, "Wkvl")
    load_weight(Wukv[:, :, 0:512], w_ukv[:, 0:512], 2, 512, 11, stage2, "Wukv")
    load_weight(Wukv[:, :, 512:1024], w_ukv[:, 512:1024], 2, 512, 11, stage2, "Wukv")
    op("sync", lambda e: e.dma_start(out=rkt, in_=rk), writes=["rkt"], dma=True)
    op("gpsimd", lambda e: e.memset(kr_pad, 0.0), writes=["kr_pad"])
    NTK = S_LEN // 128
    xk_ = [xbuf[0][:, 0, :], xbuf[0][:, 1, :], xbuf[1][:, 0, :], xbuf[1][:, 1, :]]
    xnk_ = [xnb[0][:, 0, :], xnb[0][:, 1, :], xnb[1][:, 0, :]]

    def ks0(tt):
        xb = xk_[tt % 4]
        op("sync", lambda e: e.dma_start(out=xb, in_=xf[tt * 128:(tt + 1) * 128, :]), writes=[("kx", tt % 4)], dma=True)

    def ks1(tt):
        xb = xk_[tt % 4]
        c = 160 + tt % 4
        op("scalar", lambda e: e.activation(out=junk, in_=xb, func=AF.Square, accum_out=stt[:, c:c + 1]), reads=[("kx", tt % 4)], writes=["junk", ("kss", tt % 4)])
        op("scalar", lambda e: e.activation(out=stt[:, c:c + 1], in_=stt[:, c:c + 1], func=AF.Sqrt, bias=EPS, scale=1.0 / D), reads=[("kss", tt % 4)], writes=[("kss", tt % 4)])

    def ks2(tt):
        xb = xk_[tt % 4]
        xn = xnk_[tt % 3]
        c = 160 + tt % 4
        op("vector", lambda e: e.reciprocal(out=stt[:, c + 4:c + 5], in_=stt[:, c:c + 1]), reads=[("kss", tt % 4)], writes=[("krs", tt % 4)])
        op("vector", lambda e: e.tensor_scalar(out=xn, in0=xb, scalar1=stt[:, c + 4:c + 5], scalar2=None, op0=ALU.mult),
           reads=[("kx", tt % 4), ("krs", tt % 4)], writes=[("kxn", tt % 3)])

    def ks3(tt):
        xn = xnk_[tt % 3]
        pb = tt % 2
        for k in range(8):
            op("tensor", lambda e, k=k: e.transpose(out=bankb(pb)[:, k * 128:(k + 1) * 128], in_=xn[:, k * 128:(k + 1) * 128], identity=idb),
               reads=[("kxn", tt % 3), "idb"], writes=[("ps", pb)])

    def ks4(tt):
        pb = tt % 2
        hb = hTk[tt % 2]
        if tt % 2 == 0:
            op("vector", lambda e: e.tensor_copy(out=hb, in_=bankb(pb).rearrange("p (k t) -> p k t", k=8)), reads=[("ps", pb)], writes=[("hTk", tt % 2)])
        else:
            op("scalar", lambda e: e.copy(out=hb, in_=bankb(pb).rearrange("p (k t) -> p k t", k=8)), reads=[("ps", pb)], writes=[("hTk", tt % 2)])

    def ks5(tt):
        hb = hTk[tt % 2]
        pb = 2 + tt % 3
        for k in range(8):
            op("tensor", lambda e, k=k: e.matmul(bank(pb, 288), lhsT=hb[:, k, :], rhs=Wkvl[:, k, :], start=(k == 0), stop=(k == 7)),
               reads=[("hTk", tt % 2), "Wkvl"], writes=[("ps", pb)])

    def ks6(tt):
        pb = 2 + tt % 3
        c = 168 + tt % 4
        op("scalar", lambda e: e.activation(out=junk[:, 0:256], in_=bank(pb, 256), func=AF.Square, accum_out=stt[:, c:c + 1]), reads=[("ps", pb)], writes=["junk", ("kss2", tt % 4)])
        op("scalar", lambda e: e.activation(out=stt[:, c:c + 1], in_=stt[:, c:c + 1], func=AF.Sqrt, bias=EPS, scale=1.0 / 256), reads=[("kss2", tt % 4)], writes=[("kss2", tt % 4)])

    def ks7(tt):
        pb = 2 + tt % 3
        c = 168 + tt % 4
        cb = ckvn[tt % 2]
        op("vector", lambda e: e.reciprocal(out=stt[:, c + 4:c + 5], in_=stt[:, c:c + 1]), reads=[("kss2", tt % 4)], writes=[("krs2", tt % 4)])
        op("vector", lambda e: e.tensor_scalar(out=cb, in0=bank(pb, 256), scalar1=stt[:, c + 4:c + 5], scalar2=None, op0=ALU.mult),
           reads=[("ps", pb), ("krs2", tt % 4)], writes=[("ckvn", tt % 2)])
        op("vector", lambda e: e.tensor_copy(out=kr_tm[:, tt, :], in_=bank(pb, 288)[:, 256:288]), reads=[("ps", pb)], writes=["kr_tm"])

    def ks8(tt):
        cb = ckvn[tt % 2]
        pb2 = 5 + tt % 2
        for j in range(2):
            op("tensor", lambda e, j=j: e.transpose(out=bankb(pb2)[:, j * 128:(j + 1) * 128], in_=cb[:, j * 128:(j + 1) * 128], identity=idb),
               reads=[("ckvn", tt % 2), "idb"], writes=[("ps", pb2)])

    def ks9(tt):
        pb2 = 5 + tt % 2
        op("scalar", lambda e: e.copy(out=ckvT[:, :, tt * 128:(tt + 1) * 128], in_=bankb(pb2)[:, 0:256].rearrange("p (j t) -> p j t", j=2)),
           reads=[("ps", pb2)], writes=["ckvT"])

    kst = [ks0, ks1, ks2, ks3, ks4, ks5, ks6, ks7, ks8, ks9]
    for it in range(NTK + len(kst) - 1):
        for k in range(len(kst) - 1, -1, -1):
            tt = it - k
            if 0 <= tt < NTK:
                kst[k](tt)
    x1, x2 = kr_tm[:, :, 0:16], kr_tm[:, :, 16:32]
    cosk, sink = rkt[:, 0], rkt[:, 1]
    op("vector", lambda e: e.tensor_tensor(out=rt[0], in0=x1, in1=cosk, op=ALU.mult), reads=["kr_tm", "rkt"], writes=["rt0"])
    op("vector", lambda e: e.tensor_tensor(out=rt[1], in0=x2, in1=sink, op=ALU.mult), reads=["kr_tm", "rkt"], writes=["rt1"])
    op("vector", lambda e: e.tensor_tensor(out=kr_pad[:, :, 64:80], in0=rt[0], in1=rt[1], op=ALU.subtract), reads=["rt0", "rt1", "kr_pad"], writes=["kr_pad"])
    op("vector", lambda e: e.tensor_tensor(out=rt[0], in0=x2, in1=cosk, op=ALU.mult), reads=["kr_tm", "rkt", "kr_pad"], writes=["rt0"])
    op("vector", lambda e: e.tensor_tensor(out=rt[1], in0=x1, in1=sink, op=ALU.mult), reads=["kr_tm", "rkt", "kr_pad"], writes=["rt1"])
    op("vector", lambda e: e.tensor_tensor(out=kr_pad[:, :, 80:96], in0=rt[0], in1=rt[1], op=ALU.add), reads=["rt0", "rt1", "kr_pad"], writes=["kr_pad"])
    for g8 in range(8):
        pb = 6 + g8 % 2
        for i in range(8):
            tt = g8 * 8 + i
            op("tensor", lambda e, i=i, tt=tt, pb=pb: e.transpose(out=bankb(pb)[0:96, i * 128:(i + 1) * 128], in_=kr_pad[:, tt, :], identity=idb),
               reads=["kr_pad", "idb"], writes=[("ps", pb)])
        op("vector", lambda e, pb=pb, g8=g8: e.tensor_copy(out=KT[64:96, g8 * 1024:(g8 + 1) * 1024], in_=bankb(pb)[64:96, :]),
           reads=[("ps", pb)], writes=["KTr"])
    dump("ckvT", ckvT, [128, 2, S_LEN], BF16, ["ckvT"])
    dump("KTr", KT[64:96, :], [32, S_LEN], BF16, ["KTr"])
    S.barrier()

    ynTb = RY.alloc([128, 4, NX], BF16)
    RT.off = RT.lo
    QBT = RT.alloc([96, 8, NX], BF16)
    R1s = Region(R1s_lo, R1.hi)
    Wuq = R1s.alloc([128, 3, 768], BF16)
    Wrot = R1s.alloc([128, 3, 8, 96], BF16)
    stage3 = R1s.alloc([128, 3, 768], F32)
    rqt = RT.alloc([96, 2, NX], F32)
    tq = [RT.alloc([96, 410], F32) for _ in range(2)]
    load_weight(Wuq, w_uq, 3, 768, 8, stage3, "Wuq")
    op("gpsimd", lambda e: e.memset(Wrot, 0.0), writes=["Wrot"])
    Wuq4 = Wuq.rearrange("p k (h c) -> p k h c", h=8)
    for k in range(3):
        op("gpsimd", lambda e, k=k: e.tensor_scalar(out=Wrot[:, k, :, 64:80], in0=Wuq4[:, k, :, 80:96], scalar1=-1.0, scalar2=None, op0=ALU.mult),
           reads=["Wuq", "Wrot"], writes=["Wrot"])
        op("gpsimd", lambda e, k=k: e.tensor_copy(out=Wrot[:, k, :, 80:96], in_=Wuq4[:, k, :, 64:80]), reads=["Wuq", "Wrot"], writes=["Wrot"])
    op("sync", lambda e: e.dma_start(out=rqt[64:96], in_=rq), writes=["rqt"], dma=True)
    qi = 0
    for h in range(8):
        for (c0, cn) in XBLK:
            pa, pr_ = 2 * (qi % 2), 1 + 2 * (qi % 2)
            tb_ = tq[qi % 2]
            tk = ("tq", qi % 2)
            qi += 1
            for k in range(3):
                op("tensor", lambda e, k=k, h=h, c0=c0, cn=cn, pa=pa: e.matmul(bank(pa, cn, 96), lhsT=Wuq[:, k, h * 96:(h + 1) * 96], rhs=cqT[:, k, c0:c0 + cn], start=(k == 0), stop=(k == 2)),
                   reads=["Wuq", "cqT"], writes=[("ps", pa)])
            for k in range(3):
                op("tensor", lambda e, k=k, h=h, c0=c0, cn=cn, pr_=pr_: e.matmul(bank(pr_, cn, 96), lhsT=Wrot[:, k, h, :], rhs=cqT[:, k, c0:c0 + cn], start=(k == 0), stop=(k == 2)),
                   reads=["Wrot", "cqT"], writes=[("ps", pr_)])
            op("scalar", lambda e, h=h, c0=c0, cn=cn, pa=pa: e.copy(out=QBT[0:64, h, c0:c0 + cn], in_=bank(pa, cn, 64)), reads=[("ps", pa)], writes=["QBT"])
            op("vector", lambda e, c0=c0, cn=cn, pa=pa, tb_=tb_: e.tensor_tensor(out=tb_[64:96, 0:cn], in0=bank(pa, cn, 32, 64), in1=rqt[64:96, 0, c0:c0 + cn], op=ALU.mult),
               reads=[("ps", pa), "rqt"], writes=[tk])
            op("vector", lambda e, h=h, c0=c0, cn=cn, pr_=pr_: e.tensor_tensor(out=QBT[64:96, h, c0:c0 + cn], in0=bank(pr_, cn, 32, 64), in1=rqt[64:96, 1, c0:c0 + cn], op=ALU.mult),
               reads=[("ps", pr_), "rqt"], writes=["QBT"])
            op("vector", lambda e, h=h, c0=c0, cn=cn, tb_=tb_: e.tensor_tensor(out=QBT[64:96, h, c0:c0 + cn], in0=QBT[64:96, h, c0:c0 + cn], in1=tb_[64:96, 0:cn], op=ALU.add),
               reads=[tk, "QBT"], writes=["QBT"])
    dump("QBT", QBT, [96, 8, NX], BF16, ["QBT"])
    S.barrier()
    RT.off = RT.lo + 32832
    yb_tm = RT.alloc([128, 17, 512], F32)
    ynb_tmp = [RT.alloc([128, 512], BF16) for _ in range(2)]
    R1s = Region(R1s_lo, R1.hi)
    Vb = [R1s.alloc([128, 64, 65], BF16) for _ in range(2)]
    PT = [R1s.alloc([128, 2, 512], BF16) for _ in range(3)]
    OaccB = R1s.alloc([128, NX], F32)
    for b_ in range(2):
        op("gpsimd", lambda e, b_=b_: e.memset(Vb[b_][:, :, 64], 1.0), writes=[("Vb", b_)])
    scale_b = 96.0 ** -0.5
    MB = [(1 + 512 * i, 512) for i in range(4)]
    QG = [(0, 2), (2, 2)]
    sctr = [0]
    pctr = [0]

    def slot():
        s = 2 * (sctr[0] % 3)
        sctr[0] += 1
        return s

    def m_prep_v(h):
        vb = Vb[h % 2]
        for k8 in range(8):
            if k8 % 2 == 0:
                sb_ = slot()
            pb = sb_ + k8 % 2
            for i in range(8):
                kt = k8 * 8 + i
                for j in range(2):
                    op("tensor", lambda e, j=j, kt=kt, i=i: e.matmul(bank(pb)[:, i * 64:(i + 1) * 64], lhsT=ckvT[:, j, kt * 128:(kt + 1) * 128], rhs=Wukv[:, j, h * 128 + 64:h * 128 + 128], start=(j == 0), stop=(j == 1)),
                       reads=["Wukv", "ckvT"], writes=[("ps", pb)])
            op("vector", lambda e: e.tensor_copy(out=vb[:, k8 * 8:(k8 + 1) * 8, 0:64], in_=bank(pb).rearrange("p (t c) -> p t c", t=8)),
               reads=[("ps", pb)], writes=[("Vb", h % 2)])

    def m_prep_k(h):
        for nb_ in range(16):
            if nb_ % 2 == 0:
                sb_ = slot()
            pb = sb_ + nb_ % 2
            for j in range(2):
                op("tensor", lambda e, j=j: e.matmul(bank(pb, 512, 64), lhsT=Wukv[:, j, h * 128:h * 128 + 64], rhs=ckvT[:, j, nb_ * 512:(nb_ + 1) * 512], start=(j == 0), stop=(j == 1)),
                   reads=["Wukv", "ckvT"], writes=[("ps", pb)])
            if nb_ % 2 == 0:
                op("vector", lambda e: e.tensor_copy(out=KT[0:64, nb_ * 512:(nb_ + 1) * 512], in_=bank(pb, 512, 64)),
                   reads=[("ps", pb)], writes=[("grp", "KTn", nb_)])
            else:
                op("scalar", lambda e: e.copy(out=KT[0:64, nb_ * 512:(nb_ + 1) * 512], in_=bank(pb, 512, 64)),
                   reads=[("ps", pb)], writes=[("grp", "KTn", nb_)])

    m_prep_v(0)
    for h in range(8):
        vb = Vb[h % 2]
        m_prep_k(h)
        for gq, (b0, nbk) in enumerate(QG):
            ob = 6
            sl = {}

            def emit_S(kt):
                sbk = slot()
                sl[kt] = sbk
                for bb in range(nbk):
                    c0, cn = MB[b0 + bb]
                    op("tensor", lambda e, bb=bb, c0=c0, cn=cn: e.matmul(bank(sbk + bb, cn), lhsT=KT[0:96, kt * 128:(kt + 1) * 128], rhs=QBT[0:96, h, c0:c0 + cn], start=True, stop=True),
                       reads=[("grp", "KTn", kt // 4), "KTr", "QBT"], writes=[("ps", sbk + bb)])
            emit_S(0)
            emit_S(1)
            for kt in range(64):
                if kt + 2 < 64:
                    emit_S(kt + 2)
                sbk = sl[kt]
                pt = PT[pctr[0] % 3]
                ptk = ("PT", pctr[0] % 3)
                pctr[0] += 1
                sv = ps[:, sbk * 512:(sbk + nbk) * 512].rearrange("p (a b) -> p a b", a=nbk)
                op("scalar", lambda e: e.activation(out=pt[:, 0:nbk, :], in_=sv, func=AF.Exp, scale=scale_b),
                   reads=[("ps", sbk + bb) for bb in range(nbk)], writes=[ptk])
                for bb in range(nbk):
                    op("tensor", lambda e, bb=bb: e.matmul(bank(ob + bb, 512, 65), lhsT=vb[:, kt, :], rhs=pt[:, bb, :], start=(kt == 0), stop=(kt == 63)),
                       reads=[ptk, ("Vb", h % 2)], writes=[("ps", ob + bb)])
            if gq == 0 and h + 1 < 8:
                m_prep_v(h + 1)
            ov = ps[0:65, ob * 512:(ob + nbk) * 512]
            c0 = MB[b0][0]
            op("vector", lambda e: e.tensor_copy(out=OaccB[0:65, c0:c0 + nbk * 512], in_=ov),
               reads=[("ps", ob + bb) for bb in range(nbk)], writes=["OaccB"])
        sbk = slot()
        pt = PT[pctr[0] % 3]
        ptk = ("PT", pctr[0] % 3)
        pctr[0] += 1
        qh = QBT[0:96, h, 0:NX:NX - 1]
        for kt in range(64):
            op("tensor", lambda e, kt=kt: e.matmul(bank(sbk)[:, 2 * kt:2 * kt + 2], lhsT=KT[0:96, kt * 128:(kt + 1) * 128], rhs=qh, start=True, stop=True),
               reads=["KTn", "KTr", "QBT"], writes=[("ps", sbk)])
        op("scalar", lambda e: e.activation(out=pt[:, 0, 0:128], in_=bank(sbk, 128), func=AF.Exp, scale=scale_b), reads=[("ps", sbk)], writes=[ptk])
        for kt in range(64):
            op("tensor", lambda e, kt=kt: e.matmul(bank(6, 2, 65), lhsT=vb[:, kt, :], rhs=pt[:, 0, 2 * kt:2 * kt + 2], start=(kt == 0), stop=(kt == 63)),
               reads=[ptk, ("Vb", h % 2)], writes=[("ps", 6)])
        op("vector", lambda e: e.tensor_copy(out=OaccB[0:65, 0:NX:NX - 1], in_=bank(6, 2, 65)), reads=[("ps", 6)], writes=["OaccB"])
        oacc_to_tm(h, yb_tm, "OaccB", OaccB)
    tm_to_ynT(yb_tm, ynTb, "ynTb", ynb_tmp)
    dump("ynTb", ynTb, [128, 4, NX], BF16, ["ynTb"])
    dump("yb_tm", yb_tm, [128, 17, 512], F32, ["tm"])
    S.barrier()

    RT.off = RT.lo
    hTf = RT.alloc([128, 8, NX], BF16)
    A = Region(R1.lo, RC.hi)
    Wo = A.alloc([128, 8, D], BF16)
    stage4 = A.alloc([128, 8, 256], F32)
    gpo = A.alloc([128, 2, D], F32)
    xt_ = [A.alloc([128, D], F32) for _ in range(4)]
    xm_ = [A.alloc([128, D], F32) for _ in range(3)]
    hn_ = [A.alloc([128, D], BF16) for _ in range(2)]
    for c in range(4):
        load_weight(Wo[:, :, c * 256:(c + 1) * 256], w_o[:, c * 256:(c + 1) * 256], 8, 256, 13, stage4, "Wo")
    op("sync", lambda e: e.dma_start(out=gpo, in_=gpost), writes=["gpo"], dma=True)

    def f0M(t):
        return 128 if t < 16 else 2

    def f0s0(t):
        xt = xt_[t % 4]
        xk = ("xt", t % 4)
        if t < 16:
            op("sync", lambda e: e.dma_start(out=xt, in_=xw[OWN0 + 128 * t:OWN0 + 128 * (t + 1), :]), writes=[xk], dma=True)
        else:
            op("sync", lambda e: e.dma_start(out=xt[0:1, :], in_=xw[OWN0 - 1:OWN0, :]), writes=[xk], dma=True)
            op("sync", lambda e: e.dma_start(out=xt[1:2, :], in_=xw[OWN0 + OWN:OWN0 + OWN + 1, :]), writes=[xk], dma=True)

    def f0s1(t):
        M = f0M(t)
        pb = 2 * (t % 3)
        for n2 in range(2):
            for k in range(8):
                if t < 16:
                    la = (ynTa if k < 4 else ynTb)[:, k % 4, 1 + 128 * t:1 + 128 * (t + 1)]
                else:
                    la = (ynTa if k < 4 else ynTb)[:, k % 4, 0:NX:NX - 1]
                op("tensor", lambda e, k=k, n2=n2, la=la: e.matmul(bank(pb + n2, 512, M), lhsT=la, rhs=Wo[:, k, n2 * 512:(n2 + 1) * 512], start=(k == 0), stop=(k == 7)),
                   reads=["ynTa", "ynTb", "Wo"], writes=[("ps", pb + n2)])

    def f0s2(t):
        M = f0M(t)
        pb = 2 * (t % 3)
        yv = ps[0:M, pb * 512:(pb + 2) * 512]
        sc = 192 + 2 * (t % 4)
        sk = ("stt5", t % 4)
        op("scalar", lambda e: e.activation(out=junk[0:M, :], in_=yv, func=AF.Square, accum_out=stt[0:M, sc:sc + 1]),
           reads=[("ps", pb), ("ps", pb + 1)], writes=["junk", sk])
        op("scalar", lambda e: e.activation(out=stt[0:M, sc:sc + 1], in_=stt[0:M, sc:sc + 1], func=AF.Sqrt, bias=EPS, scale=1.0 / D), reads=[sk], writes=[sk])

    def f0s3(t):
        M = f0M(t)
        pb = 2 * (t % 3)
        yv = ps[0:M, pb * 512:(pb + 2) * 512]
        sc = 192 + 2 * (t % 4)
        sk = ("stt5", t % 4)
        xm, xt = xm_[t % 3], xt_[t % 4]
        mkk, xk = ("xm", t % 3), ("xt", t % 4)
        op("vector", lambda e: e.reciprocal(out=stt[0:M, sc + 1:sc + 2], in_=stt[0:M, sc:sc + 1]), reads=[sk], writes=[("stt5r", t % 4)])
        op("vector", lambda e: e.scalar_tensor_tensor(out=xm[0:M, :], in0=yv, scalar=stt[0:M, sc + 1:sc + 2], in1=gpo[0:M, 0, :], op0=ALU.mult, op1=ALU.mult),
           reads=[("ps", pb), ("ps", pb + 1), ("stt5r", t % 4), "gpo"], writes=[mkk])
        op("vector", lambda e: e.tensor_tensor(out=xm[0:M, :], in0=xm[0:M, :], in1=xt[0:M, :], op=ALU.add), reads=[mkk, xk], writes=[mkk])

    def f0s4(t):
        M = f0M(t)
        xm = xm_[t % 3]
        mkk = ("xm", t % 3)
        if t < 16:
            op("sync", lambda e: e.dma_start(out=xmid[128 * t:128 * (t + 1), :], in_=xm), reads=[mkk], writes=["xmid"], dma=True)
        sc2 = 200 + 2 * (t % 4)
        sk2 = ("stt6", t % 4)
        op("scalar", lambda e: e.activation(out=junk[0:M, :], in_=xm[0:M, :], func=AF.Square, accum_out=stt[0:M, sc2:sc2 + 1]),
           reads=[mkk], writes=["junk", sk2])
        op("scalar", lambda e: e.activation(out=stt[0:M, sc2:sc2 + 1], in_=stt[0:M, sc2:sc2 + 1], func=AF.Sqrt, bias=EPS, scale=1.0 / D), reads=[sk2], writes=[sk2])

    def f0s5(t):
        M = f0M(t)
        xm, hn = xm_[t % 3], hn_[t % 2]
        sc2 = 200 + 2 * (t % 4)
        op("vector", lambda e: e.reciprocal(out=stt[0:M, sc2 + 1:sc2 + 2], in_=stt[0:M, sc2:sc2 + 1]), reads=[("stt6", t % 4)], writes=[("stt6r", t % 4)])
        op("vector", lambda e: e.tensor_scalar(out=hn[0:M, :], in0=xm[0:M, :], scalar1=stt[0:M, sc2 + 1:sc2 + 2], scalar2=None, op0=ALU.mult),
           reads=[("xm", t % 3), ("stt6r", t % 4)], writes=[("hn", t % 2)])

    def f0s6(t):
        M = f0M(t)
        hn = hn_[t % 2]
        pb2 = 6 + t % 2
        for k in range(8):
            op("tensor", lambda e, k=k: e.transpose(out=bankb(pb2)[:, k * 128:k * 128 + M], in_=hn[0:M, k * 128:(k + 1) * 128], identity=idb[0:M, 0:M]),
               reads=[("hn", t % 2), "idb"], writes=[("ps", pb2)])

    def f0s7(t):
        M = f0M(t)
        pb2 = 6 + t % 2
        srcv = bankb(pb2).rearrange("p (k t) -> p k t", k=8)[:, :, 0:M]
        dstv = hTf[:, :, 1 + 128 * t:1 + 128 * (t + 1)] if t < 16 else hTf[:, :, 0:NX:NX - 1]
        if t % 2 == 0:
            op("vector", lambda e: e.tensor_copy(out=dstv, in_=srcv), reads=[("ps", pb2)], writes=[("grp", "hTf", t)])
        else:
            op("scalar", lambda e: e.copy(out=dstv, in_=srcv), reads=[("ps", pb2)], writes=[("grp", "hTf", t)])

    f0st = [f0s0, f0s1, f0s2, f0s3, f0s4, f0s5, f0s6, f0s7]
    for it in range(17 + len(f0st) - 1):
        for k in range(len(f0st) - 1, -1, -1):
            tt_ = it - k
            if 0 <= tt_ < 17:
                f0st[k](tt_)
    dump("hTf", hTf, [128, 8, NX], BF16, ["hTf"])
    S.barrier()

    aT = Region(R1.lo, RC.hi).alloc([128, 22, OWN], BF16)
    A = Region(RY.lo, RY.hi)
    stg = [A.alloc([128, 8, 256], F32) for _ in range(2)]
    Wub = [A.alloc([128, 8, 256], BF16) for _ in range(2)]
    cwt = A.alloc([128, 44, 4], F32)
    ufl = A.alloc([128, 2], F32)
    A = Region(RT.lo + 32832, SB_BYTES)
    cgb = [A.alloc([128, OWN], F32) for _ in range(2)]
    cvb = [A.alloc([128, OWN], F32) for _ in range(2)]
    op("sync", lambda e: e.dma_start(out=cwt, in_=cwb), writes=["cwt"], dma=True)
    op("sync", lambda e: e.dma_start(out=ufl, in_=uflag), writes=["ufl"], dma=True)
    op("vector", lambda e: e.tensor_tensor(out=hTf[:, :, 0:NX:NX - 1], in0=hTf[:, :, 0:NX:NX - 1], in1=ufl.unsqueeze(1).to_broadcast([128, 8, 2]), op=ALU.mult),
       reads=["hTf", "ufl"], writes=["hTf"])
    OB = [(410 * i, min(410, OWN - 410 * i)) for i in range(5)]
    pbi = 0

    def f1_weights(j):
        bi = j % 2
        sg, wb = stg[bi], Wub[bi]
        sgk, wbk = ("stg", bi), ("Wub", bi)
        op("gpsimd", lambda e: e.dma_start(out=sg[:, :, 0:128], in_=w_up[:, j * 128:(j + 1) * 128].rearrange("(k p) n -> p k n", p=128)), writes=[sgk], dma=True)
        op("gpsimd", lambda e: e.dma_start(out=sg[:, :, 128:256], in_=w_up[:, DFF + j * 128:DFF + (j + 1) * 128].rearrange("(k p) n -> p k n", p=128)), writes=[sgk], dma=True)
        for k in range(8):
            if k % 2 == 0:
                op("vector", lambda e, k=k: e.tensor_scalar(out=wb[:, k, :], in0=sg[:, k, :], scalar1=gpt[:, 21 + k:22 + k], scalar2=None, op0=ALU.mult),
                   reads=[sgk, "gpt"], writes=[("grp", wbk, k)])
            else:
                op("scalar", lambda e, k=k: e.activation(out=wb[:, k, :], in_=sg[:, k, :], func=AF.Identity, scale=gpt[:, 21 + k:22 + k]),
                   reads=[sgk, "gpt"], writes=[("grp", wbk, k)])

    def f1_tail(j):
        bi = j % 2
        cg, cv = cgb[bi], cvb[bi]
        cgk = [("cg", bi, bx) for bx in range(5)]
        cvk = [("cv", bi, bx) for bx in range(5)]
        op("scalar", lambda e: e.activation(out=cg, in_=cg, func=AF.Gelu_apprx_tanh), reads=cgk, writes=cgk)
        op("vector", lambda e: e.tensor_tensor(out=aT[:, j, :], in0=cg, in1=cv, op=ALU.mult), reads=cgk + cvk, writes=[("aT", j)])

    f1_weights(0)
    for j in range(22):
        bi = j % 2
        wb = Wub[bi]
        wbk = ("Wub", bi)
        if j + 1 < 22:
            f1_weights(j + 1)
        for half in range(2):
            cb = (cgb if half == 0 else cvb)[bi]
            ff = j + 22 * half
            for bx, (o0, n) in enumerate(OB):
                ck = ("cg" if half == 0 else "cv", bi, bx)
                pb = pbi % 8
                pbi += 1
                for k in range(8):
                    op("tensor", lambda e, k=k: e.matmul(bank(pb, n + 2), lhsT=wb[:, k, half * 128:(half + 1) * 128], rhs=hTf[:, k, o0:o0 + n + 2], start=(k == 0), stop=(k == 7)),
                       reads=[wbk, "hTf"], writes=[("ps", pb)])
                op("scalar", lambda e: e.activation(out=cb[:, o0:o0 + n], in_=bank(pb, n + 2)[:, 1:n + 1], func=AF.Identity, bias=cwt[:, ff, 3:4], scale=cwt[:, ff, 1:2]),
                   reads=[("ps", pb), "cwt"], writes=[ck])
                op("vector", lambda e: e.scalar_tensor_tensor(out=cb[:, o0:o0 + n], in0=bank(pb, n + 2)[:, 0:n], scalar=cwt[:, ff, 0:1], in1=cb[:, o0:o0 + n], op0=ALU.mult, op1=ALU.add),
                   reads=[("ps", pb), "cwt", ck], writes=[ck])
                op("vector", lambda e: e.scalar_tensor_tensor(out=cb[:, o0:o0 + n], in0=bank(pb, n + 2)[:, 2:n + 2], scalar=cwt[:, ff, 2:3], in1=cb[:, o0:o0 + n], op0=ALU.mult, op1=ALU.add),
                   reads=[("ps", pb), "cwt", ck], writes=[ck])
                if half == 0 and bx == 2 and j > 0:
                    f1_tail(j - 1)
    f1_tail(21)
    dump("aT", aT, [128, 22, OWN], BF16, [("aT", j) for j in range(22)])
    S.barrier()

    A = Region(RY.lo, SB_BYTES)
    Wdn = A.alloc([128, 22, D], BF16)
    stage5 = [A.alloc([128, 2, D], F32) for _ in range(2)]
    gpo2 = A.alloc([128, D], F32)
    xm2 = [A.alloc([128, D], F32) for _ in range(2)]
    ot = [A.alloc([128, D], F32) for _ in range(2)]
    for c in range(11):
        load_weight(Wdn[:, 2 * c:2 * c + 2, :], w_down[256 * c:256 * (c + 1), :], 2, D, None, stage5[c % 2], "Wdn")
    op("sync", lambda e: e.dma_start(out=gpo2, in_=gpost[:, 1, :]), writes=["gpo2"], dma=True)
    for t in range(16):
        bi = t % 2
        xk, ok = ("xm2", bi), ("ot", bi)
        op("sync", lambda e, t=t, bi=bi: e.dma_start(out=xm2[bi], in_=xmid[128 * t:128 * (t + 1), :]), reads=["xmid"], writes=[xk], dma=True)
        pb = 2 * (t % 2)
        for n2 in range(2):
            for j in range(22):
                op("tensor", lambda e, j=j, n2=n2, t=t, pb=pb: e.matmul(bank(pb + n2), lhsT=aT[:, j, 128 * t:128 * (t + 1)], rhs=Wdn[:, j, n2 * 512:(n2 + 1) * 512], start=(j == 0), stop=(j == 21)),
                   reads=[("aT", j), "Wdn"], writes=[("ps", pb + n2)])
        yv = ps[:, pb * 512:(pb + 2) * 512]
        sc = 144 + 2 * (t % 4)
        sk = ("stt7", t % 4)
        op("scalar", lambda e, yv=yv, sc=sc: e.activation(out=junk, in_=yv, func=AF.Square, accum_out=stt[:, sc:sc + 1]),
           reads=[("ps", pb), ("ps", pb + 1)], writes=["junk", sk])
        op("scalar", lambda e, sc=sc: e.activation(out=stt[:, sc:sc + 1], in_=stt[:, sc:sc + 1], func=AF.Sqrt, bias=EPS, scale=1.0 / D), reads=[sk], writes=[sk])
        op("vector", lambda e, sc=sc: e.reciprocal(out=stt[:, sc + 1:sc + 2], in_=stt[:, sc:sc + 1]), reads=[sk], writes=[sk])
        op("vector", lambda e, yv=yv, sc=sc, bi=bi: e.scalar_tensor_tensor(out=ot[bi], in0=yv, scalar=stt[:, sc + 1:sc + 2], in1=gpo2, op0=ALU.mult, op1=ALU.mult),
           reads=[("ps", pb), ("ps", pb + 1), sk, "gpo2"], writes=[ok])
        op("vector", lambda e, bi=bi: e.tensor_tensor(out=ot[bi], in0=ot[bi], in1=xm2[bi], op=ALU.add), reads=[ok, xk], writes=[ok])
        op("sync", lambda e, t=t, bi=bi: e.dma_start(out=yout[128 * t:128 * (t + 1), :], in_=ot[bi]), reads=[ok], dma=True)
    S.emit()
    return nc


_CACHE = {}


def _consts():
    if "c" in _CACHE:
        return _CACHE["c"]
    slopes = np.exp2(-8.0 * np.arange(1, 9, dtype=np.float32) / 8).astype(np.float32)
    k = np.arange(128)[:, None]
    q = np.arange(128)[None, :]
    masks = np.zeros((8, 128, 6, 128), np.float32)
    for h in range(8):
        for ri, r in enumerate((1, 4, 16)):
            d0 = k - 64 - q
            d1 = k + 64 - q
            masks[h, :, 2 * ri, :] = np.where(k >= q, np.exp(-slopes[h] * (np.abs(d0) * r).astype(np.float32)), 0.0)
            masks[h, :, 2 * ri + 1, :] = np.where(k <= q, np.exp(-slopes[h] * (np.abs(d1) * r).astype(np.float32)), 0.0)
    inv_freq = np.exp(-np.log(10000.0) * np.arange(0, 32, 2, dtype=np.float32) / 32).astype(np.float32)
    pos = np.arange(S_LEN, dtype=np.float32)
    ang = pos[:, None] * inv_freq[None, :]
    cosk = np.cos(ang).astype(np.float32).reshape(64, 128, 16).transpose(1, 0, 2)
    sink = np.sin(ang).astype(np.float32).reshape(64, 128, 16).transpose(1, 0, 2)
    rk = np.ascontiguousarray(np.stack([cosk, sink], axis=1))
    c = dict(masks=masks, inv_freq=inv_freq, rk=rk, ident=np.eye(128, dtype=np.float32))
    _CACHE["c"] = c
    return c


def _core_inputs(c, x, shared):
    cst = _consts()
    b, qc = c // 4, c % 4
    T0 = qc * OWN
    pos_w = T0 - OWN0 + np.arange(NW)
    valid = (pos_w >= 0) & (pos_w < S_LEN)
    xw = np.zeros((NW, D), np.float32)
    xw[valid] = x[b, pos_w[valid]]
    kvf = np.zeros((128, NVT), np.float32)
    for i, (r, es, nk) in enumerate(VLIST):
        kvf[:nk, i] = valid[es + r * np.arange(nk)].astype(np.float32)
    posq = (T0 - 1 + np.arange(NX)).astype(np.float32)
    ang = posq[None, :] * cst["inv_freq"][:, None]
    cq, sq = np.cos(ang).astype(np.float32), np.sin(ang).astype(np.float32)
    rq = np.ascontiguousarray(np.stack([np.concatenate([cq, cq], 0), np.concatenate([sq, sq], 0)], axis=1))
    uflag = np.zeros((128, 2), np.float32)
    uflag[:, 0] = 1.0 if T0 > 0 else 0.0
    uflag[:, 1] = 1.0 if T0 + OWN < S_LEN else 0.0
    d = dict(shared)
    d.update(xw=xw, xf=np.ascontiguousarray(x[b]), kvf=kvf, rq=rq, uflag=uflag, masks=cst["masks"], rk=cst["rk"], ident=cst["ident"])
    return d


def kernel(x, norm_mix_pre, w_in, q_lat_norm, w_uq, kv_lat_norm, w_ukv, out_norm_a, out_norm_b, w_o,
           norm_mix_post, norm_ffn_pre, w_up, conv_w, conv_b, w_down, norm_ffn_post):
    f = lambda a: np.ascontiguousarray(np.asarray(a, dtype=np.float32))
    x = f(x)
    gp = np.zeros((128, 32), np.float32)
    gp[:, 0:8] = f(norm_mix_pre)[0].reshape(8, 128).T
    gp[:, 8:11] = f(q_lat_norm)[0].reshape(3, 128).T
    gp[:, 11:13] = f(kv_lat_norm)[0].reshape(2, 128).T
    gp[:, 13:21] = np.concatenate([f(out_norm_a)[0], f(out_norm_b)[0]]).reshape(8, 128).T
    gp[:, 21:29] = f(norm_ffn_pre)[0].reshape(8, 128).T
    gpost = np.ascontiguousarray(np.broadcast_to(np.stack([f(norm_mix_post)[0], f(norm_ffn_post)[0]])[None], (128, 2, D)))
    cwb = np.zeros((128, 44, 4), np.float32)
    cwb[:, :, 0:3] = f(conv_w)[0].T.reshape(44, 128, 3).transpose(1, 0, 2)
    cwb[:, :, 3] = f(conv_b)[0].reshape(44, 128).T
    shared = dict(w_in=f(w_in)[0], w_uq=f(w_uq)[0], w_ukv=f(w_ukv)[0], w_o=f(w_o)[0], w_up=f(w_up)[0], w_down=f(w_down)[0],
                  gp=gp, gpost=gpost, cwb=cwb)
    if "nc" not in _CACHE:
        _CACHE["nc"] = build()
    nc = _CACHE["nc"]
    in_maps = [_core_inputs(c, x, shared) for c in range(8)]
    res = run_bass_kernel_spmd(nc, in_maps, core_ids=list(range(8)))
    out = np.zeros((2, S_LEN, D), np.float32)
    for c in range(8):
        b, qc = c // 4, c % 4
        out[b, qc * OWN:(qc + 1) * OWN] = res.results[c]["y"]
    return out
```

```python
import contextlib
import types
import numpy as np
import ml_dtypes
import concourse.bass as bass
import concourse.mybir as mybir
from concourse.bass_utils import run_bass_kernel_spmd

F32 = mybir.dt.float32
BF16 = mybir.dt.bfloat16
U8 = mybir.dt.uint8
AF = mybir.ActivationFunctionType
ALU = mybir.AluOpType

S_LEN = 8192
D = 1024
OWN = 2048
OWN0 = 1152
NW = 4352
NX = 2050
DFF = 2816
EPS = 1e-6
ENGS = ("sync", "scalar", "gpsimd", "vector", "tensor")
XBLK = [(i * 410, 410) for i in range(5)]


class Sched:
    def __init__(self, nc, ndma_sems=8):
        self.nc = nc
        self.ops = []
        self.ndma = ndma_sems

    @staticmethod
    def _freeze(fn):
        if fn.__closure__ is None:
            return fn
        cells = []
        for c in fn.__closure__:
            try:
                cells.append(types.CellType(c.cell_contents))
            except ValueError:
                cells.append(c)
        return types.FunctionType(fn.__code__, fn.__globals__, fn.__name__, fn.__defaults__, tuple(cells))

    def op(self, eng, fn, reads=(), writes=(), dma=False):
        fn = self._freeze(fn)
        self.ops.append(dict(eng=eng, fn=fn, reads=tuple(reads), writes=tuple(writes), dma=dma, bar=False))

    def barrier(self):
        self.ops.append(dict(eng=None, fn=None, reads=(), writes=(), dma=False, bar=True))

    def emit(self, final_wait_eng="sync"):
        nc = self.nc
        ops = self.ops
        n = len(ops)
        groups = {}
        for o in ops:
            for k in o["reads"] + o["writes"]:
                if isinstance(k, tuple) and len(k) == 3 and k[0] == "grp":
                    groups.setdefault(k[1], set()).add(k)

        def expand(keys):
            out = []
            for k in keys:
                out.append(k)
                if k in groups:
                    out.extend(groups[k])
            return out

        last_writer = {}
        readers = {}
        deps = [dict() for _ in range(n)]
        since_bar = []
        pending_bar = {}
        for i, o in enumerate(ops):
            if o["bar"]:
                lastc = {}
                dl = set()
                for j in since_bar:
                    if ops[j]["dma"]:
                        dl.add(j)
                    else:
                        lastc[ops[j]["eng"]] = j
                dl.update(lastc.values())
                for e in ENGS:
                    pending_bar[e] = set(dl) | pending_bar.get(e, set())
                since_bar = []
                continue
            d = deps[i]
            if o["eng"] in pending_bar:
                for j in pending_bar.pop(o["eng"]):
                    d[j] = True
            rk, wk = expand(o["reads"]), expand(o["writes"])
            for r in rk:
                if r in last_writer:
                    d[last_writer[r]] = True
            for w in wk:
                if w in last_writer:
                    d.setdefault(last_writer[w], False)
                for j in readers.get(w, ()):
                    d.setdefault(j, False)
            d.pop(i, None)
            for w in wk:
                last_writer[w] = i
                readers[w] = []
            for r in rk:
                if r not in wk:
                    readers.setdefault(r, []).append(i)
            since_bar.append(i)
        needed = set()
        red = [None] * n
        for i, o in enumerate(ops):
            if o["bar"]:
                continue
            per_eng = {}
            dl = []
            for j, is_raw in deps[i].items():
                pj = ops[j]
                if pj["dma"]:
                    dl.append(j)
                    continue
                if pj["eng"] == o["eng"] and not o["dma"] and (o["eng"] == "tensor" or not is_raw):
                    continue
                e = pj["eng"]
                if e not in per_eng or per_eng[e] < j:
                    per_eng[e] = j
            dl.extend(per_eng.values())
            red[i] = dl
            needed.update(dl)
        cnt = {e: 0 for e in ENGS}
        dcnt = {}
        sig = [None] * n
        dma_idx = {e: 0 for e in ENGS}
        for i, o in enumerate(ops):
            if o["bar"]:
                continue
            if o["dma"]:
                k = dma_idx[o["eng"]] % self.ndma
                dma_idx[o["eng"]] += 1
                key = ("dma", o["eng"], k)
                prev = dcnt.get(key, 0)
                dcnt[key] = prev + 16
                sig[i] = (key, prev + 16)
                o["dma_prev"] = (key, prev) if prev > 0 else None
            elif i in needed:
                cnt[o["eng"]] += 1
                sig[i] = (("eng", o["eng"]), cnt[o["eng"]])
        semkeys = sorted({s[0] for s in sig if s is not None}, key=str)
        stack = contextlib.ExitStack()
        sems = {}
        for sk in semkeys:
            sems[sk] = stack.enter_context(nc.semaphore("s_" + "_".join(str(x) for x in sk)))
        by_eng = {e: [i for i, o in enumerate(ops) if o["eng"] == e] for e in ENGS}
        dma_final = list(dcnt.items())

        def run(engname, eng):
            waited = {}
            for i in by_eng[engname]:
                o = ops[i]
                wl = [sig[j] for j in red[i]]
                if o["dma"] and o.get("dma_prev"):
                    wl.append(o["dma_prev"])
                for (sk, v) in wl:
                    if waited.get(sk, 0) >= v:
                        continue
                    eng.wait_ge(sems[sk], v)
                    waited[sk] = v
                ins = o["fn"](eng)
                if sig[i] is not None:
                    ins.then_inc(sems[sig[i][0]], 16 if o["dma"] else 1)
            if engname == final_wait_eng:
                for sk, v in dma_final:
                    if waited.get(sk, 0) < v:
                        eng.wait_ge(sems[sk], v)
                for e in ENGS:
                    if cnt[e] > 0 and waited.get(("eng", e), 0) < cnt[e]:
                        eng.wait_ge(sems[("eng", e)], cnt[e])

        with stack:
            with nc.Block() as block:
                @block.sync
                def _(e):
                    run("sync", e)

                @block.scalar
                def _(e):
                    run("scalar", e)

                @block.gpsimd
                def _(e):
                    run("gpsimd", e)

                @block.vector
                def _(e):
                    run("vector", e)

                @block.tensor
                def _(e):
                    run("tensor", e)


def dil_tables():
    vt = {}

    def vtile(r, e0, nk):
        key = (r, e0, nk)
        if key not in vt:
            vt[key] = len(vt)
        return vt[key]

    groups = {1: [], 4: [], 16: []}
    for r in (1, 4, 16):
        def blk(x0, N):
            eq0 = OWN0 - 1 + x0
            t0 = vtile(r, eq0 - 64 * r, 128)
            t1 = vtile(r, eq0 + 64 * r, 128 if N > 1 else 1)
            return (x0, N, t0, t1)
        if r == 1:
            for g in range(4):
                groups[r].append(("reg", g, [blk(1 + 128 * (4 * g + b), 128) for b in range(4)]))
        elif r == 4:
            for g in range(4):
                groups[r].append(("reg", g, [blk(1 + c + 512 * g, 128) for c in range(4)]))
        else:
            for g in range(4):
                groups[r].append(("reg", g, [blk(1 + 4 * g + c, 128) for c in range(4)]))
        groups[r].append(("halo", 0, [blk(0, 1), blk(NX - 1, 1)]))
    vlist = [None] * len(vt)
    for k, i in vt.items():
        vlist[i] = k
    return vlist, groups


VLIST, DGROUPS = dil_tables()
NVT = len(VLIST)


def build(dbg=False):
    nc = bass.Bass("TRN2", target_bir_lowering=False)

    def din(name, shape, dt=F32):
        return nc.dram_tensor(name, list(shape), dt, kind="ExternalInput").ap()

    xw = din("xw", [NW, D])
    xf = din("xf", [S_LEN, D])
    w_in = din("w_in", [D, 2208])
    w_uq = din("w_uq", [384, 768])
    w_ukv = din("w_ukv", [256, 1024])
    w_o = din("w_o", [D, D])
    w_up = din("w_up", [D, 2 * DFF])
    w_down = din("w_down", [DFF, D])
    gp = din("gp", [128, 32])
    gpost = din("gpost", [128, 2, D])
    cwb = din("cwb", [128, 44, 4])
    masks = din("masks", [8, 128, 6, 128])
    kvf = din("kvf", [128, NVT])
    rq = din("rq", [32, 2, NX])
    rk = din("rk", [128, 2, 64, 16])
    ident = din("ident", [128, 128])
    uflag = din("uflag", [128, 2])
    yout = nc.dram_tensor("y", [OWN, D], F32, kind="ExternalOutput").ap()
    xmid = nc.dram_tensor("xmid", [OWN, D], F32, kind="Internal").ap()
    dbg_out = {}

    SB_BYTES = 212000
    big = nc.alloc_sbuf_tensor("big", [128, SB_BYTES], U8).ap()
    ps = nc.alloc_psum_tensor("ps", [128, 4096], F32).ap()
    psb = ps.bitcast(BF16)

    class Region:
        def __init__(self, lo, hi):
            assert hi <= SB_BYTES and lo <= hi, (lo, hi)
            self.lo, self.hi, self.off = lo, hi, lo

        def alloc(self, shape, dt, p0=0):
            esz = 4 if dt == F32 else 2
            nb = int(np.prod(shape[1:])) * esz
            nb_al = (nb + 63) // 64 * 64
            assert self.off + nb_al <= self.hi, ("SBUF region overflow", self.lo, self.hi, self.off, nb_al)
            v = big[p0:p0 + shape[0], self.off:self.off + nb].bitcast(dt)
            self.off += nb_al
            if len(shape) == 3:
                v = v.rearrange("p (a b) -> p a b", a=shape[1])
            elif len(shape) == 4:
                v = v.rearrange("p (a b c) -> p a b c", a=shape[1], b=shape[2])
            return v

    RP = Region(0, 4096)
    R1 = Region(RP.hi, RP.hi + 86080)
    RC = Region(R1.hi, R1.hi + 12352)
    RY = Region(RC.hi, RC.hi + 16448 + 4096 + 16448)
    RT = Region(RY.hi, SB_BYTES)
    A = RP
    S = Sched(nc)
    op = S.op

    def dump(name, ap, shape, dt, keys):
        if not dbg:
            return
        import os
        sel = os.environ.get("DBGSEL", "")
        if sel and name not in sel.split(","):
            return
        d = nc.dram_tensor("dbg_" + name, list(shape), dt, kind="ExternalOutput").ap()
        op("sync", lambda e: e.dma_start(out=d, in_=ap), reads=keys, dma=True)

    def bank(b, n=512, p=128, p0=0):
        return ps[p0:p0 + p, b * 512:b * 512 + n]

    def bankb(b, n=1024, p=128, p0=0):
        return psb[p0:p0 + p, b * 1024:b * 1024 + n]

    idf = A.alloc([128, 128], F32)
    idb = A.alloc([128, 128], BF16)
    gpt = A.alloc([128, 32], F32)
    stt = A.alloc([128, 256], F32)
    junk = A.alloc([128, 1024], BF16)
    rec = A.alloc([128, 8], F32)
    op("sync", lambda e: e.dma_start(out=idf, in_=ident), writes=["idf"], dma=True)
    op("sync", lambda e: e.dma_start(out=gpt, in_=gp), writes=["gpt"], dma=True)
    op("vector", lambda e: e.tensor_copy(out=idb, in_=idf), reads=["idf"], writes=["idb"])

    def rstd_from_ss(ss_ap, out_ap, n, key):
        tmp = ss_ap
        op("scalar", lambda e: e.activation(out=tmp, in_=ss_ap, func=AF.Sqrt, bias=EPS, scale=1.0 / n),
           reads=[key], writes=[key])
        op("vector", lambda e: e.reciprocal(out=out_ap, in_=tmp), reads=[key], writes=[key])

    def load_weight(dst, src_ap, kch, ncols, gcol, stage, wkey, negate_cols=None):
        op("gpsimd", lambda e: e.dma_start(out=stage[:, 0:kch, 0:ncols], in_=src_ap.rearrange("(k p) n -> p k n", p=128)),
           writes=[("stage", id(stage))], dma=True)
        for k in range(kch):
            eng = "vector" if k % 2 == 0 else "scalar"
            if gcol is None:
                if eng == "vector":
                    op(eng, lambda e, k=k: e.tensor_copy(out=dst[:, k, :], in_=stage[:, k, 0:ncols]),
                       reads=[("stage", id(stage))], writes=[("grp", wkey, (id(dst), k))])
                else:
                    op(eng, lambda e, k=k: e.copy(out=dst[:, k, :], in_=stage[:, k, 0:ncols]),
                       reads=[("stage", id(stage))], writes=[("grp", wkey, (id(dst), k))])
            else:
                if eng == "vector":
                    op(eng, lambda e, k=k: e.tensor_scalar(out=dst[:, k, :], in0=stage[:, k, 0:ncols],
                                                          scalar1=gpt[:, gcol + k:gcol + k + 1], scalar2=None, op0=ALU.mult),
                       reads=[("stage", id(stage)), "gpt"], writes=[("grp", wkey, (id(dst), k))])
                else:
                    op(eng, lambda e, k=k: e.activation(out=dst[:, k, :], in_=stage[:, k, 0:ncols], func=AF.Identity,
                                                        scale=gpt[:, gcol + k:gcol + k + 1]),
                       reads=[("stage", id(stage)), "gpt"], writes=[("grp", wkey, (id(dst), k))])

    def norm_tiles_to_T(src_dram_rows, ntile, xbuf, xnbuf, key, slot=0):
        c0 = 2 * slot
        sk = ("sttn", slot)
        op("sync", lambda e: e.dma_start(out=xbuf[:, 0:ntile, :], in_=src_dram_rows.rearrange("(t p) f -> p t f", p=128)),
           writes=[("x", key)], dma=True)
        for t in range(ntile):
            op("scalar", lambda e, t=t: e.activation(out=junk, in_=xbuf[:, t, :], func=AF.Square, accum_out=stt[:, c0 + t:c0 + t + 1]),
               reads=[("x", key)], writes=["junk", sk])
        rstd_from_ss(stt[:, c0:c0 + ntile], stt[:, 8 + c0:8 + c0 + ntile], D, sk)
        for t in range(ntile):
            if t % 2 == 0:
                op("vector", lambda e, t=t: e.tensor_scalar(out=xnbuf[:, t, :], in0=xbuf[:, t, :], scalar1=stt[:, 8 + c0 + t:9 + c0 + t], scalar2=None, op0=ALU.mult),
                   reads=[("x", key), sk], writes=[("grp", ("xn", key), t)])
            else:
                op("scalar", lambda e, t=t: e.activation(out=xnbuf[:, t, :], in_=xbuf[:, t, :], func=AF.Identity, scale=stt[:, 8 + c0 + t:9 + c0 + t]),
                   reads=[("x", key), sk], writes=[("grp", ("xn", key), t)])

    def transpose_tile(xn_tile, hT_dst, pbank, rkeys, wkey):
        for k in range(8):
            op("tensor", lambda e, k=k: e.transpose(out=bankb(pbank)[:, k * 128:(k + 1) * 128], in_=xn_tile[:, k * 128:(k + 1) * 128], identity=idb),
               reads=list(rkeys) + ["idb"], writes=[("ps", pbank)])
        op("vector", lambda e: e.tensor_copy(out=hT_dst, in_=bankb(pbank).rearrange("p (k t) -> p k t", k=8)),
           reads=[("ps", pbank)], writes=[wkey])

    KAT = R1.alloc([128, 4, NW], BF16)
    VAT = R1.alloc([128, 4, NW], BF16)
    QAT = R1.alloc([128, 4, NX], BF16)
    cqT = RC.alloc([128, 3, NX], BF16)
    A = Region(RC.hi, SB_BYTES)
    WA = A.alloc([128, 8, 1920], BF16)
    stage = A.alloc([128, 8, 480], F32)
    xbuf = [A.alloc([128, 2, D], F32) for _ in range(2)]
    xnb = [A.alloc([128, 2, D], BF16) for _ in range(2)]
    hTw = [A.alloc([128, 8, 256], BF16) for _ in range(2)]
    cqn = A.alloc([128, 384], BF16)
    cqn2 = [cqn, A.alloc([128, 384], BF16)]
    for c in range(4):
        load_weight(WA[:, :, c * 480:(c + 1) * 480], w_in[:, c * 480:(c + 1) * 480], 8, 480, 0, stage, "WA")

    def cq_mm(la_list, M, pb, hkey):
        for k in range(8):
            la = la_list[k]
            op("tensor", lambda e, k=k, la=la: e.matmul(bank(pb, 384, M), lhsT=la, rhs=WA[:, k, 1536:1920], start=(k == 0), stop=(k == 7)),
               reads=[hkey, "WA"], writes=[("ps", pb)])

    def cq_stats(M, pb, slot):
        c = 224 + 2 * slot
        sk = ("cqs", slot)
        cb = cqn2[slot]
        op("scalar", lambda e: e.activation(out=junk[0:M, 0:384], in_=bank(pb, 384, M), func=AF.Square, accum_out=stt[0:M, c:c + 1]),
           reads=[("ps", pb)], writes=["junk", sk])
        op("scalar", lambda e: e.activation(out=stt[0:M, c:c + 1], in_=stt[0:M, c:c + 1], func=AF.Sqrt, bias=EPS, scale=1.0 / 384),
           reads=[sk], writes=[sk])
        op("vector", lambda e: e.reciprocal(out=stt[0:M, c + 1:c + 2], in_=stt[0:M, c:c + 1]), reads=[sk], writes=[sk])
        op("vector", lambda e: e.tensor_scalar(out=cb[0:M, :], in0=bank(pb, 384, M), scalar1=stt[0:M, c + 1:c + 2], scalar2=None, op0=ALU.mult),
           reads=[("ps", pb), sk], writes=[("cqn", slot)])

    def cq_fin(M, pb, slot, xc):
        cb = cqn2[slot]
        for j in range(3):
            op("tensor", lambda e, j=j: e.transpose(out=bankb(pb)[:, j * 128:j * 128 + M], in_=cb[0:M, j * 128:(j + 1) * 128], identity=idb[0:M, 0:M]),
               reads=[("cqn", slot), "idb"], writes=[("ps", pb)])
        op("vector", lambda e: e.tensor_copy(out=xc, in_=bankb(pb)[:, 0:384].rearrange("p (j t) -> p j t", j=3)[:, :, 0:M]),
           reads=[("ps", pb)], writes=["cqT"])

    NGW = NW // 256

    xbuf.append(A.alloc([128, 2, D], F32))

    def ws0(g):
        xb = xbuf[g % 3]
        e0 = g * 256
        op("sync", lambda e: e.dma_start(out=xb, in_=xw[e0:e0 + 256, :].rearrange("(t p) f -> p t f", p=128)), writes=[("wx", g % 3)], dma=True)

    def ws1(g):
        xb = xbuf[g % 3]
        c = 176 + 2 * (g % 4)
        for t in range(2):
            op("scalar", lambda e, t=t: e.activation(out=junk, in_=xb[:, t, :], func=AF.Square, accum_out=stt[:, c + t:c + t + 1]),
               reads=[("wx", g % 3)], writes=["junk", ("wss", g % 4)])
        op("scalar", lambda e: e.activation(out=stt[:, c:c + 2], in_=stt[:, c:c + 2], func=AF.Sqrt, bias=EPS, scale=1.0 / D), reads=[("wss", g % 4)], writes=[("wss", g % 4)])

    def ws2(g):
        xb = xbuf[g % 3]
        xn = xnb[g % 2]
        c = 176 + 2 * (g % 4)
        op("vector", lambda e: e.reciprocal(out=stt[:, c + 8:c + 10], in_=stt[:, c:c + 2]), reads=[("wss", g % 4)], writes=[("wrs", g % 4)])
        for t in range(2):
            op("vector", lambda e, t=t: e.tensor_scalar(out=xn[:, t, :], in0=xb[:, t, :], scalar1=stt[:, c + 8 + t:c + 9 + t], scalar2=None, op0=ALU.mult),
               reads=[("wx", g % 3), ("wrs", g % 4)], writes=[("wxn", g % 2)])

    def ws3(g):
        xn = xnb[g % 2]
        for t in range(2):
            pb = 2 * (g % 2) + t
            for k in range(8):
                op("tensor", lambda e, k=k, t=t: e.transpose(out=bankb(pb)[:, k * 128:(k + 1) * 128], in_=xn[:, t, k * 128:(k + 1) * 128], identity=idb),
                   reads=[("wxn", g % 2), "idb"], writes=[("ps", pb)])

    def ws4(g):
        bi = g % 2
        for t in range(2):
            pb = 2 * (g % 2) + t
            if t == 0:
                op("vector", lambda e, t=t: e.tensor_copy(out=hTw[bi][:, :, t * 128:(t + 1) * 128], in_=bankb(pb).rearrange("p (k t) -> p k t", k=8)),
                   reads=[("ps", pb)], writes=[("grp", ("hTw", bi), t)])
            else:
                op("scalar", lambda e, t=t: e.copy(out=hTw[bi][:, :, t * 128:(t + 1) * 128], in_=bankb(pb).rearrange("p (k t) -> p k t", k=8)),
                   reads=[("ps", pb)], writes=[("grp", ("hTw", bi), t)])

    def w_stageB(g):
        bi = g % 2
        e0 = g * 256
        xlo, xhi = max(e0, OWN0 - 1), min(e0 + 256, OWN0 - 1 + NX)
        cqjobs = []
        for t in range(2):
            et = e0 + t * 128
            if OWN0 <= et < OWN0 + OWN:
                x0 = et - (OWN0 - 1)
                cqjobs.append(([hTw[bi][:, k, t * 128:(t + 1) * 128] for k in range(8)], 128, 6 + t, t, cqT[:, :, x0:x0 + 128]))
            if et == OWN0 - 128:
                cqjobs.append(([hTw[bi][:, k, t * 128 + 127:t * 128 + 128] for k in range(8)], 1, 6 + t, t, cqT[:, :, 0:1]))
            if et == OWN0 + OWN:
                cqjobs.append(([hTw[bi][:, k, t * 128:t * 128 + 1] for k in range(8)], 1, 6 + t, t, cqT[:, :, NX - 1:NX]))
        for (la_list, M, pb, slot, xc) in cqjobs:
            cq_mm(la_list, M, pb, ("hTw", bi))
        for (la_list, M, pb, slot, xc) in cqjobs:
            cq_stats(M, pb, slot)
        cqjobs = [(M, pb, slot, xc) for (la_list, M, pb, slot, xc) in cqjobs]
        jobs = [("K", 512 + 128 * c, c) for c in range(4)] + [("V", 1024 + 128 * c, c) for c in range(4)]
        if xhi > xlo:
            jobs += [("Q", 128 * c, c) for c in range(4)]
        for ji, (kind, col0, c) in enumerate(jobs):
            pb = 4 + ji % 2
            pk = ("ps", pb)
            pv = bank(pb, 256)
            for k in range(8):
                op("tensor", lambda e, k=k: e.matmul(pv, lhsT=WA[:, k, col0:col0 + 128], rhs=hTw[bi][:, k, :], start=(k == 0), stop=(k == 7)),
                   reads=[("hTw", bi), "WA"], writes=[pk])
            if kind == "K":
                op("vector", lambda e: e.tensor_copy(out=KAT[:, c, e0:e0 + 256], in_=pv), reads=[pk], writes=["KAT"])
            elif kind == "V":
                op("scalar", lambda e: e.copy(out=VAT[:, c, e0:e0 + 256], in_=pv), reads=[pk], writes=["VAT"])
            else:
                op("vector", lambda e: e.tensor_copy(out=QAT[:, c, xlo - (OWN0 - 1):xhi - (OWN0 - 1)], in_=pv[:, xlo - e0:xhi - e0]),
                   reads=[pk], writes=["QAT"])
        for (M, pb, slot, xc) in cqjobs:
            cq_fin(M, pb, slot, xc)

    wst = [ws0, ws1, ws2, ws3, ws4, w_stageB]
    for it in range(NGW + len(wst) - 1):
        for k in range(len(wst) - 1, -1, -1):
            g = it - k
            if 0 <= g < NGW:
                wst[k](g)
    dump("KAT", KAT, [128, 4, NW], BF16, ["KAT"])
    dump("VAT", VAT, [128, 4, NW], BF16, ["VAT"])
    dump("QAT", QAT, [128, 4, NX], BF16, ["QAT"])
    dump("cqT", cqT, [128, 3, NX], BF16, ["cqT"])
    S.barrier()

    ynTa = RY.alloc([128, 4, NX], BF16)
    A = Region(RY.lo + 16448, SB_BYTES)
    mk = [A.alloc([128, 6, 128], F32) for _ in range(2)]
    kvft = A.alloc([128, NVT], F32)
    Vp = [A.alloc([128, NVT, 65], BF16) for _ in range(2)]
    Oacc = A.alloc([128, NX], F32)
    ya_tm = A.alloc([128, 17, 512], F32)
    Eb = [A.alloc([128, 1024], F32) for _ in range(2)]
    Pb = [A.alloc([128, 1024], BF16) for _ in range(2)]
    op("sync", lambda e: e.dma_start(out=kvft, in_=kvf), writes=["kvft"], dma=True)

    def oacc_to_tm(h, dst_tm, okey, Oacc):
        tiles = [(1 + 128 * m, 128, 1) for m in range(16)] + [(0, 2, NX - 1)]
        for g0 in range(0, 17, 4):
            tl = tiles[g0:g0 + 4]
            pb = 6 + (g0 // 4) % 2
            for i, (x0, M, step) in enumerate(tl):
                src = Oacc[0:65, x0:x0 + 128] if step == 1 else Oacc[0:65, 0:NX:NX - 1]
                op("tensor", lambda e, i=i, src=src, M=M, pb=pb: e.transpose(out=bank(pb)[0:M, i * 65:(i + 1) * 65], in_=src, identity=idf[0:65, 0:65]),
                   reads=[okey, "idf"], writes=[("ps", pb)])
            nt = len(tl)
            M = tl[0][1]
            pv = bank(pb)[0:M, 0:nt * 65].rearrange("p (t c) -> p t c", t=nt)
            op("vector", lambda e, pv=pv, nt=nt, M=M: e.reciprocal(out=rec[0:M, 0:nt], in_=pv[:, :, 64]), reads=[("ps", pb)], writes=["rec"])
            op("vector", lambda e, pv=pv, nt=nt, M=M, g0=g0: e.tensor_tensor(out=dst_tm[0:M, g0:g0 + nt, h * 64:(h + 1) * 64], in0=pv[:, :, 0:64],
                                                                         in1=rec[0:M, 0:nt].unsqueeze(2).to_broadcast([M, nt, 64]), op=ALU.mult),
               reads=[("ps", pb), "rec"], writes=["tm"])

    def tm_to_ynT(src_tm, dstT, nkey, Pb):
        for t in range(17):
            M = 128 if t < 16 else 2
            op("scalar", lambda e, t=t, M=M: e.activation(out=junk[0:M, 0:512], in_=src_tm[0:M, t, :], func=AF.Square, accum_out=stt[0:M, 32 + t:33 + t]),
               reads=["tm"], writes=["junk", "stt3"])
        rstd_from_ss(stt[:, 32:49], stt[:, 64:81], 512, "stt3")
        for t in range(17):
            M = 128 if t < 16 else 2
            pb = t % 2
            ynb = Pb[t % 2]
            op("vector", lambda e, t=t, M=M, ynb=ynb: e.tensor_scalar(out=ynb[0:M, 0:512], in0=src_tm[0:M, t, :], scalar1=stt[0:M, 64 + t:65 + t], scalar2=None, op0=ALU.mult),
               reads=["tm", "stt3"], writes=[("Pb", t % 2)])
            for j in range(4):
                op("tensor", lambda e, j=j, M=M, ynb=ynb, pb=pb: e.transpose(out=bankb(pb)[:, j * 128:j * 128 + M], in_=ynb[0:M, j * 128:(j + 1) * 128], identity=idb[0:M, 0:M]),
                   reads=[("Pb", t % 2), "idb"], writes=[("ps", pb)])
            srcv = bankb(pb)[:, 0:512].rearrange("p (j t) -> p j t", j=4)[:, :, 0:M]
            if t < 16:
                dstv = dstT[:, :, 1 + 128 * t:1 + 128 * (t + 1)]
            else:
                dstv = dstT[:, :, 0:NX:NX - 1]
            op("vector", lambda e, srcv=srcv, dstv=dstv: e.tensor_copy(out=dstv, in_=srcv), reads=[("ps", pb)], writes=[nkey])

    def d_prep(h):
        pr, hs = h // 2, (h % 2) * 64
        vb = Vp[h % 2]
        mb = mk[h % 2]
        op("sync", lambda e: e.dma_start(out=mb, in_=masks[h]), writes=[("mk", h % 2)], dma=True)
        for v0 in range(0, NVT, 8):
            vts = VLIST[v0:v0 + 8]
            pb = 4 + (v0 // 8) % 2
            for i, (r, es, nk) in enumerate(vts):
                op("tensor", lambda e, i=i, r=r, es=es, nk=nk: e.transpose(out=bankb(pb)[0:nk, i * 64:(i + 1) * 64],
                                                                        in_=VAT[hs:hs + 64, pr, es:es + r * (nk - 1) + 1:r], identity=idb[hs:hs + 64, hs:hs + 64]),
                   reads=["VAT", "idb"], writes=[("ps", pb)])
            i = 0
            while i < len(vts):
                nk = vts[i][2]
                j = i
                while j + 1 < len(vts) and vts[j + 1][2] == nk:
                    j += 1
                cnt_ = j - i + 1
                srcv = bankb(pb)[0:nk, i * 64:(j + 1) * 64].rearrange("p (t c) -> p t c", t=cnt_)
                op("vector", lambda e, srcv=srcv, i=i, cnt_=cnt_, nk=nk: e.tensor_tensor(
                    out=vb[0:nk, v0 + i:v0 + i + cnt_, 0:64], in0=srcv,
                    in1=kvft[0:nk, v0 + i:v0 + i + cnt_].unsqueeze(2).to_broadcast([nk, cnt_, 64]), op=ALU.mult),
                   reads=[("ps", pb), "kvft"], writes=[("Vp", h % 2)])
                i = j + 1
        op("gpsimd", lambda e: e.tensor_copy(out=vb[:, :, 64], in_=kvft), reads=["kvft"], writes=[("Vp", h % 2)])

    GL = [(ri, r, kind, g, blks) for ri, r in enumerate((1, 4, 16)) for (kind, g, blks) in DGROUPS[r]]
    gctr = [0]

    def d_scores(h, gi, grp):
        pr, hs = h // 2, (h % 2) * 64
        ri, r, kind, g, blks = grp
        sb = 2 * (gi % 2)
        for bi_, (x0, Nq, t0, t1) in enumerate(blks):
            qv = QAT[hs:hs + 64, pr, x0:x0 + r * (Nq - 1) + 1:r]
            for role, tix in ((0, t0), (1, t1)):
                (rr, es, nk) = VLIST[tix]
                op("tensor", lambda e, role=role, es=es, nk=nk, qv=qv, bi_=bi_, Nq=Nq: e.matmul(
                    bank(sb + role)[0:nk, bi_ * 128:bi_ * 128 + Nq], lhsT=KAT[hs:hs + 64, pr, es:es + r * (nk - 1) + 1:r], rhs=qv, start=True, stop=True),
                   reads=["KAT", "QAT"], writes=[("ps", sb + role)])

    def d_rest(h, gi, grp):
        ri, r, kind, g, blks = grp
        vb = Vp[h % 2]
        mb = mk[h % 2]
        sb = 2 * (gi % 2)
        eb = Eb[gi % 2]
        pbuf = Pb[gi % 2]
        ob = 6 + gi % 2
        ek, pk = ("Eb", gi % 2), ("Pbuf", gi % 2)
        if kind == "reg":
            sv = ps[:, sb * 512:(sb + 2) * 512]
            op("scalar", lambda e: e.activation(out=eb, in_=sv, func=AF.Exp, scale=0.125),
               reads=[("ps", sb), ("ps", sb + 1)], writes=[ek])
            op("vector", lambda e: e.tensor_tensor(
                out=pbuf.rearrange("p (r b q) -> p r b q", r=2, b=4), in0=eb.rearrange("p (r b q) -> p r b q", r=2, b=4),
                in1=mb[:, 2 * ri:2 * ri + 2, :].unsqueeze(2).to_broadcast([128, 2, 4, 128]), op=ALU.mult),
               reads=[ek, ("mk", h % 2)], writes=[pk])
        else:
            op("scalar", lambda e: e.activation(out=eb[:, 0:256:128], in_=bank(sb)[:, 0:256:128], func=AF.Exp, scale=0.125),
               reads=[("ps", sb)], writes=[ek])
            op("scalar", lambda e: e.activation(out=eb[0:1, 512:768:128], in_=bank(sb + 1)[0:1, 0:256:128], func=AF.Exp, scale=0.125),
               reads=[("ps", sb + 1)], writes=[ek])
            op("vector", lambda e: e.tensor_tensor(
                out=pbuf[:, 0:256:128], in0=eb[:, 0:256:128], in1=mb[:, 2 * ri, 0:1].to_broadcast([128, 2]), op=ALU.mult),
               reads=[ek, ("mk", h % 2)], writes=[pk])
            op("vector", lambda e: e.tensor_tensor(
                out=pbuf[0:1, 512:768:128], in0=eb[0:1, 512:768:128], in1=mb[0:1, 2 * ri + 1, 0:1].to_broadcast([1, 2]), op=ALU.mult),
               reads=[ek, ("mk", h % 2)], writes=[pk])
        for bi_, (x0, Nq, t0, t1) in enumerate(blks):
            for role, tix in ((0, t0), (1, t1)):
                nk = VLIST[tix][2]
                op("tensor", lambda e, role=role, tix=tix, nk=nk, bi_=bi_, Nq=Nq: e.matmul(
                    bank(ob)[0:65, bi_ * 128:bi_ * 128 + Nq], lhsT=vb[0:nk, tix, :], rhs=pbuf[0:nk, role * 512 + bi_ * 128:role * 512 + bi_ * 128 + Nq],
                    start=(role == 0), stop=(role == 1)),
                   reads=[pk, ("Vp", h % 2)], writes=[("ps", ob)])
        if kind == "reg":
            src = bank(ob)[0:65, :].rearrange("p (c q) -> p c q", c=4)
            if r == 1:
                dst = Oacc[0:65, 1 + 512 * g:1 + 512 * (g + 1)].rearrange("p (c q) -> p c q", c=4)
            elif r == 4:
                dst = Oacc[0:65, 1 + 512 * g:1 + 512 * (g + 1)].rearrange("p (q c) -> p c q", c=4)
            else:
                dst = Oacc[0:65, 1:1 + OWN].rearrange("p (q c) -> p c q", c=16)[:, 4 * g:4 * g + 4, :]
        else:
            src = bank(ob)[0:65, 0:256:128]
            dst = Oacc[0:65, 0:NX:NX - 1]
        if r == 1:
            op("vector", lambda e: e.tensor_copy(out=dst, in_=src), reads=[("ps", ob)], writes=["Oacc"])
        else:
            op("vector", lambda e: e.tensor_tensor(out=dst, in0=src, in1=dst, op=ALU.add), reads=[("ps", ob), "Oacc"], writes=["Oacc"])

    d_prep(0)
    for h in range(8):
        g0 = gctr[0]
        d_scores(h, g0, GL[0])
        if h + 1 < 8:
            d_prep(h + 1)
        for i, grp in enumerate(GL):
            if i + 1 < len(GL):
                d_scores(h, g0 + i + 1, GL[i + 1])
            d_rest(h, g0 + i, grp)
        gctr[0] += len(GL)
        oacc_to_tm(h, ya_tm, "Oacc", Oacc)
    tm_to_ynT(ya_tm, ynTa, "ynTa", Pb)
    dump("ynTa", ynTa, [128, 4, NX], BF16, ["ynTa"])
    dump("ya_tm", ya_tm, [128, 17, 512], F32, ["tm"])
    S.barrier()

    R1.off = R1.lo
    ckvT = R1.alloc([128, 2, S_LEN], BF16)
    KT = R1.alloc([96, S_LEN], BF16)
    R1s_lo = R1.off
    RY.off = RY.lo + 16448
    Wukv = RY.alloc([128, 2, 1024], BF16)
    A = Region(RY.off, SB_BYTES)
    Wkvl = A.alloc([128, 8, 288], BF16)
    stage2 = A.alloc([128, 8, 512], F32)
    xbuf = [A.alloc([128, 2, D], F32) for _ in range(2)]
    xnb = [A.alloc([128, 2, D], BF16) for _ in range(2)]
    hTk = [A.alloc([128, 8, 128], BF16) for _ in range(2)]
    ckvn = [A.alloc([128, 256], BF16) for _ in range(2)]
    kr_tm = A.alloc([128, 64, 32], F32)
    rkt = A.alloc([128, 2, 64, 16], F32)
    kr_pad = A.alloc([128, 64, 96], BF16)
    rt = [A.alloc([128, 64, 16], F32) for _ in range(2)]
    load_weight(Wkvl, w_in[:, 1920:2208], 8, 288, 0, stage2, "Wkvl")
    load_weight(Wukv[:, :, 0:512], w_ukv[:, 0:512], 2, 512, 11, stage2, "Wukv")
    load_weight(Wukv[:, :, 512:1024], w_ukv[:, 512:1024], 2, 512, 11, stage2, "Wukv")
    op("sync", lambda e: e.dma_start(out=rkt, in_=rk), writes=["rkt"], dma=True)
    op("gpsimd", lambda e: e.memset(kr_pad, 0.0), writes=["kr_pad"])
    NTK = S_LEN // 128
    xk_ = [xbuf[0][:, 0, :], xbuf[0][:, 1, :], xbuf[1][:, 0, :], xbuf[1][:, 1, :]]
    xnk_ = [xnb[0][:, 0, :], xnb[0][:, 1, :], xnb[1][:, 0, :]]

    def ks0(tt):
        xb = xk_[tt % 4]
        op("sync", lambda e: e.dma_start(out=xb, in_=xf[tt * 128:(tt + 1) * 128, :]), writes=[("kx", tt % 4)], dma=True)

    def ks1(tt):
        xb = xk_[tt % 4]
        c = 160 + tt % 4
        op("scalar", lambda e: e.activation(out=junk, in_=xb, func=AF.Square, accum_out=stt[:, c:c + 1]), reads=[("kx", tt % 4)], writes=["junk", ("kss", tt % 4)])
        op("scalar", lambda e: e.activation(out=stt[:, c:c + 1], in_=stt[:, c:c + 1], func=AF.Sqrt, bias=EPS, scale=1.0 / D), reads=[("kss", tt % 4)], writes=[("kss", tt % 4)])

    def ks2(tt):
        xb = xk_[tt % 4]
        xn = xnk_[tt % 3]
        c = 160 + tt % 4
        op("vector", lambda e: e.reciprocal(out=stt[:, c + 4:c + 5], in_=stt[:, c:c + 1]), reads=[("kss", tt % 4)], writes=[("krs", tt % 4)])
        op("vector", lambda e: e.tensor_scalar(out=xn, in0=xb, scalar1=stt[:, c + 4:c + 5], scalar2=None, op0=ALU.mult),
           reads=[("kx", tt % 4), ("krs", tt % 4)], writes=[("kxn", tt % 3)])

    def ks3(tt):
        xn = xnk_[tt % 3]
        pb = tt % 2
        for k in range(8):
            op("tensor", lambda e, k=k: e.transpose(out=bankb(pb)[:, k * 128:(k + 1) * 128], in_=xn[:, k * 128:(k + 1) * 128], identity=idb),
               reads=[("kxn", tt % 3), "idb"], writes=[("ps", pb)])

    def ks4(tt):
        pb = tt % 2
        hb = hTk[tt % 2]
        if tt % 2 == 0:
            op("vector", lambda e: e.tensor_copy(out=hb, in_=bankb(pb).rearrange("p (k t) -> p k t", k=8)), reads=[("ps", pb)], writes=[("hTk", tt % 2)])
        else:
            op("scalar", lambda e: e.copy(out=hb, in_=bankb(pb).rearrange("p (k t) -> p k t", k=8)), reads=[("ps", pb)], writes=[("hTk", tt % 2)])

    def ks5(tt):
        hb = hTk[tt % 2]
        pb = 2 + tt % 3
        for k in range(8):
            op("tensor", lambda e, k=k: e.matmul(bank(pb, 288), lhsT=hb[:, k, :], rhs=Wkvl[:, k, :], start=(k == 0), stop=(k == 7)),
               reads=[("hTk", tt % 2), "Wkvl"], writes=[("ps", pb)])

    def ks6(tt):
        pb = 2 + tt % 3
        c = 168 + tt % 4
        op("scalar", lambda e: e.activation(out=junk[:, 0:256], in_=bank(pb, 256), func=AF.Square, accum_out=stt[:, c:c + 1]), reads=[("ps", pb)], writes=["junk", ("kss2", tt % 4)])
        op("scalar", lambda e: e.activation(out=stt[:, c:c + 1], in_=stt[:, c:c + 1], func=AF.Sqrt, bias=EPS, scale=1.0 / 256), reads=[("kss2", tt % 4)], writes=[("kss2", tt % 4)])

    def ks7(tt):
        pb = 2 + tt % 3
        c = 168 + tt % 4
        cb = ckvn[tt % 2]
        op("vector", lambda e: e.reciprocal(out=stt[:, c + 4:c + 5], in_=stt[:, c:c + 1]), reads=[("kss2", tt % 4)], writes=[("krs2", tt % 4)])
        op("vector", lambda e: e.tensor_scalar(out=cb, in0=bank(pb, 256), scalar1=stt[:, c + 4:c + 5], scalar2=None, op0=ALU.mult),
           reads=[("ps", pb), ("krs2", tt % 4)], writes=[("ckvn", tt % 2)])
        op("vector", lambda e: e.tensor_copy(out=kr_tm[:, tt, :], in_=bank(pb, 288)[:, 256:288]), reads=[("ps", pb)], writes=["kr_tm"])

    def ks8(tt):
        cb = ckvn[tt % 2]
        pb2 = 5 + tt % 2
        for j in range(2):
            op("tensor", lambda e, j=j: e.transpose(out=bankb(pb2)[:, j * 128:(j + 1) * 128], in_=cb[:, j * 128:(j + 1) * 128], identity=idb),
               reads=[("ckvn", tt % 2), "idb"], writes=[("ps", pb2)])

    def ks9(tt):
        pb2 = 5 + tt % 2
        op("scalar", lambda e: e.copy(out=ckvT[:, :, tt * 128:(tt + 1) * 128], in_=bankb(pb2)[:, 0:256].rearrange("p (j t) -> p j t", j=2)),
           reads=[("ps", pb2)], writes=["ckvT"])

    kst = [ks0, ks1, ks2, ks3, ks4, ks5, ks6, ks7, ks8, ks9]
    for it in range(NTK + len(kst) - 1):
        for k in range(len(kst) - 1, -1, -1):
            tt = it - k
            if 0 <= tt < NTK:
                kst[k](tt)
    x1, x2 = kr_tm[:, :, 0:16], kr_tm[:, :, 16:32]
    cosk, sink = rkt[:, 0], rkt[:, 1]
    op("vector", lambda e: e.tensor_tensor(out=rt[0], in0=x1, in1=cosk, op=ALU.mult), reads=["kr_tm", "rkt"], writes=["rt0"])
    op("vector", lambda e: e.tensor_tensor(out=rt[1], in0=x2, in1=sink, op=ALU.mult), reads=["kr_tm", "rkt"], writes=["rt1"])
    op("vector", lambda e: e.tensor_tensor(out=kr_pad[:, :, 64:80], in0=rt[0], in1=rt[1], op=ALU.subtract), reads=["rt0", "rt1", "kr_pad"], writes=["kr_pad"])
    op("vector", lambda e: e.tensor_tensor(out=rt[0], in0=x2, in1=cosk, op=ALU.mult), reads=["kr_tm", "rkt", "kr_pad"], writes=["rt0"])
    op("vector", lambda e: e.tensor_tensor(out=rt[1], in0=x1, in1=sink, op=ALU.mult), reads=["kr_tm", "rkt", "kr_pad"], writes=["rt1"])
    op("vector", lambda e: e.tensor_tensor(out=kr_pad[:, :, 80:96], in0=rt[0], in1=rt[1], op=ALU.add), reads=["rt0", "rt1", "kr_pad"], writes=["kr_pad"])
    for g8 in range(8):
        pb = 6 + g8 % 2
        for i in range(8):
            tt = g8 * 8 + i
            op("tensor", lambda e, i=i, tt=tt, pb=pb: e.transpose(out=bankb(pb)[0:96, i * 128:(i + 1) * 128], in_=kr_pad[:, tt, :], identity=idb),
               reads=["kr_pad", "idb"], writes=[("ps", pb)])
        op("vector", lambda e, pb=pb, g8=g8: e.tensor_copy(out=KT[64:96, g8 * 1024:(g8 + 1) * 1024], in_=bankb(pb)[64:96, :]),
           reads=[("ps", pb)], writes=["KTr"])
    dump("ckvT", ckvT, [128, 2, S_LEN], BF16, ["ckvT"])
    dump("KTr", KT[64:96, :], [32, S_LEN], BF16, ["KTr"])
    S.barrier()

    ynTb = RY.alloc([128, 4, NX], BF16)
    RT.off = RT.lo
    QBT = RT.alloc([96, 8, NX], BF16)
    R1s = Region(R1s_lo, R1.hi)
    Wuq = R1s.alloc([128, 3, 768], BF16)
    Wrot = R1s.alloc([128, 3, 8, 96], BF16)
    stage3 = R1s.alloc([128, 3, 768], F32)
    rqt = RT.alloc([96, 2, NX], F32)
    tq = [RT.alloc([96, 410], F32) for _ in range(2)]
    load_weight(Wuq, w_uq, 3, 768, 8, stage3, "Wuq")
    op("gpsimd", lambda e: e.memset(Wrot, 0.0), writes=["Wrot"])
    Wuq4 = Wuq.rearrange("p k (h c) -> p k h c", h=8)
    for k in range(3):
        op("gpsimd", lambda e, k=k: e.tensor_scalar(out=Wrot[:, k, :, 64:80], in0=Wuq4[:, k, :, 80:96], scalar1=-1.0, scalar2=None, op0=ALU.mult),
           reads=["Wuq", "Wrot"], writes=["Wrot"])
        op("gpsimd", lambda e, k=k: e.tensor_copy(out=Wrot[:, k, :, 80:96], in_=Wuq4[:, k, :, 64:80]), reads=["Wuq", "Wrot"], writes=["Wrot"])
    op("sync", lambda e: e.dma_start(out=rqt[64:96], in_=rq), writes=["rqt"], dma=True)
    qi = 0
    for h in range(8):
        for (c0, cn) in XBLK:
            pa, pr_ = 2 * (qi % 2), 1 + 2 * (qi % 2)
            tb_ = tq[qi % 2]
            tk = ("tq", qi % 2)
            qi += 1
            for k in range(3):
                op("tensor", lambda e, k=k, h=h, c0=c0, cn=cn, pa=pa: e.matmul(bank(pa, cn, 96), lhsT=Wuq[:, k, h * 96:(h + 1) * 96], rhs=cqT[:, k, c0:c0 + cn], start=(k == 0), stop=(k == 2)),
                   reads=["Wuq", "cqT"], writes=[("ps", pa)])
            for k in range(3):
                op("tensor", lambda e, k=k, h=h, c0=c0, cn=cn, pr_=pr_: e.matmul(bank(pr_, cn, 96), lhsT=Wrot[:, k, h, :], rhs=cqT[:, k, c0:c0 + cn], start=(k == 0), stop=(k == 2)),
                   reads=["Wrot", "cqT"], writes=[("ps", pr_)])
            op("scalar", lambda e, h=h, c0=c0, cn=cn, pa=pa: e.copy(out=QBT[0:64, h, c0:c0 + cn], in_=bank(pa, cn, 64)), reads=[("ps", pa)], writes=["QBT"])
            op("vector", lambda e, c0=c0, cn=cn, pa=pa, tb_=tb_: e.tensor_tensor(out=tb_[64:96, 0:cn], in0=bank(pa, cn, 32, 64), in1=rqt[64:96, 0, c0:c0 + cn], op=ALU.mult),
               reads=[("ps", pa), "rqt"], writes=[tk])
            op("vector", lambda e, h=h, c0=c0, cn=cn, pr_=pr_: e.tensor_tensor(out=QBT[64:96, h, c0:c0 + cn], in0=bank(pr_, cn, 32, 64), in1=rqt[64:96, 1, c0:c0 + cn], op=ALU.mult),
               reads=[("ps", pr_), "rqt"], writes=["QBT"])
            op("vector", lambda e, h=h, c0=c0, cn=cn, tb_=tb_: e.tensor_tensor(out=QBT[64:96, h, c0:c0 + cn], in0=QBT[64:96, h, c0:c0 + cn], in1=tb_[64:96, 0:cn], op=ALU.add),
               reads=[tk, "QBT"], writes=["QBT"])
    dump("QBT", QBT, [96, 8, NX], BF16, ["QBT"])
    S.barrier()
    RT.off = RT.lo + 32832
    yb_tm = RT.alloc([128, 17, 512], F32)
    ynb_tmp = [RT.alloc([128, 512], BF16) for _ in range(2)]
    R1s = Region(R1s_lo, R1.hi)
    Vb = [R1s.alloc([128, 64, 65], BF16) for _ in range(2)]
    PT = [R1s.alloc([128, 2, 512], BF16) for _ in range(3)]
    OaccB = R1s.alloc([128, NX], F32)
    for b_ in range(2):
        op("gpsimd", lambda e, b_=b_: e.memset(Vb[b_][:, :, 64], 1.0), writes=[("Vb", b_)])
    scale_b = 96.0 ** -0.5
    MB = [(1 + 512 * i, 512) for i in range(4)]
    QG = [(0, 2), (2, 2)]
    sctr = [0]
    pctr = [0]

    def slot():
        s = 2 * (sctr[0] % 3)
        sctr[0] += 1
        return s

    def m_prep_v(h):
        vb = Vb[h % 2]
        for k8 in range(8):
            if k8 % 2 == 0:
                sb_ = slot()
            pb = sb_ + k8 % 2
            for i in range(8):
                kt = k8 * 8 + i
                for j in range(2):
                    op("tensor", lambda e, j=j, kt=kt, i=i: e.matmul(bank(pb)[:, i * 64:(i + 1) * 64], lhsT=ckvT[:, j, kt * 128:(kt + 1) * 128], rhs=Wukv[:, j, h * 128 + 64:h * 128 + 128], start=(j == 0), stop=(j == 1)),
                       reads=["Wukv", "ckvT"], writes=[("ps", pb)])
            op("vector", lambda e: e.tensor_copy(out=vb[:, k8 * 8:(k8 + 1) * 8, 0:64], in_=bank(pb).rearrange("p (t c) -> p t c", t=8)),
               reads=[("ps", pb)], writes=[("Vb", h % 2)])

    def m_prep_k(h):
        for nb_ in range(16):
            if nb_ % 2 == 0:
                sb_ = slot()
            pb = sb_ + nb_ % 2
            for j in range(2):
                op("tensor", lambda e, j=j: e.matmul(bank(pb, 512, 64), lhsT=Wukv[:, j, h * 128:h * 128 + 64], rhs=ckvT[:, j, nb_ * 512:(nb_ + 1) * 512], start=(j == 0), stop=(j == 1)),
                   reads=["Wukv", "ckvT"], writes=[("ps", pb)])
            if nb_ % 2 == 0:
                op("vector", lambda e: e.tensor_copy(out=KT[0:64, nb_ * 512:(nb_ + 1) * 512], in_=bank(pb, 512, 64)),
                   reads=[("ps", pb)], writes=[("grp", "KTn", nb_)])
            else:
                op("scalar", lambda e: e.copy(out=KT[0:64, nb_ * 512:(nb_ + 1) * 512], in_=bank(pb, 512, 64)),
                   reads=[("ps", pb)], writes=[("grp", "KTn", nb_)])

    m_prep_v(0)
    for h in range(8):
        vb = Vb[h % 2]
        m_prep_k(h)
        for gq, (b0, nbk) in enumerate(QG):
            ob = 6
            sl = {}

            def emit_S(kt):
                sbk = slot()
                sl[kt] = sbk
                for bb in range(nbk):
                    c0, cn = MB[b0 + bb]
                    op("tensor", lambda e, bb=bb, c0=c0, cn=cn: e.matmul(bank(sbk + bb, cn), lhsT=KT[0:96, kt * 128:(kt + 1) * 128], rhs=QBT[0:96, h, c0:c0 + cn], start=True, stop=True),
                       reads=[("grp", "KTn", kt // 4), "KTr", "QBT"], writes=[("ps", sbk + bb)])
            emit_S(0)
            emit_S(1)
            for kt in range(64):
                if kt + 2 < 64:
                    emit_S(kt + 2)
                sbk = sl[kt]
                pt = PT[pctr[0] % 3]
                ptk = ("PT", pctr[0] % 3)
                pctr[0] += 1
                sv = ps[:, sbk * 512:(sbk + nbk) * 512].rearrange("p (a b) -> p a b", a=nbk)
                op("scalar", lambda e: e.activation(out=pt[:, 0:nbk, :], in_=sv, func=AF.Exp, scale=scale_b),
                   reads=[("ps", sbk + bb) for bb in range(nbk)], writes=[ptk])
                for bb in range(nbk):
                    op("tensor", lambda e, bb=bb: e.matmul(bank(ob + bb, 512, 65), lhsT=vb[:, kt, :], rhs=pt[:, bb, :], start=(kt == 0), stop=(kt == 63)),
                       reads=[ptk, ("Vb", h % 2)], writes=[("ps", ob + bb)])
            if gq == 0 and h + 1 < 8:
                m_prep_v(h + 1)
            ov = ps[0:65, ob * 512:(ob + nbk) * 512]
            c0 = MB[b0][0]
            op("vector", lambda e: e.tensor_copy(out=OaccB[0:65, c0:c0 + nbk * 512], in_=ov),
               reads=[("ps", ob + bb) for bb in range(nbk)], writes=["OaccB"])
        sbk = slot()
        pt = PT[pctr[0] % 3]
        ptk = ("PT", pctr[0] % 3)
        pctr[0] += 1
        qh = QBT[0:96, h, 0:NX:NX - 1]
        for kt in range(64):
            op("tensor", lambda e, kt=kt: e.matmul(bank(sbk)[:, 2 * kt:2 * kt + 2], lhsT=KT[0:96, kt * 128:(kt + 1) * 128], rhs=qh, start=True, stop=True),
               reads=["KTn", "KTr", "QBT"], writes=[("ps", sbk)])
        op("scalar", lambda e: e.activation(out=pt[:, 0, 0:128], in_=bank(sbk, 128), func=AF.Exp, scale=scale_b), reads=[("ps", sbk)], writes=[ptk])
        for kt in range(64):
            op("tensor", lambda e, kt=kt: e.matmul(bank(6, 2, 65), lhsT=vb[:, kt, :], rhs=pt[:, 0, 2 * kt:2 * kt + 2], start=(kt == 0), stop=(kt == 63)),
               reads=[ptk, ("Vb", h % 2)], writes=[("ps", 6)])
        op("vector", lambda e: e.tensor_copy(out=OaccB[0:65, 0:NX:NX - 1], in_=bank(6, 2, 65)), reads=[("ps", 6)], writes=["OaccB"])
        oacc_to_tm(h, yb_tm, "OaccB", OaccB)
    tm_to_ynT(yb_tm, ynTb, "ynTb", ynb_tmp)
    dump("ynTb", ynTb, [128, 4, NX], BF16, ["ynTb"])
    dump("yb_tm", yb_tm, [128, 17, 512], F32, ["tm"])
    S.barrier()

    RT.off = RT.lo
    hTf = RT.alloc([128, 8, NX], BF16)
    A = Region(R1.lo, RC.hi)
    Wo = A.alloc([128, 8, D], BF16)
    stage4 = A.alloc([128, 8, 256], F32)
    gpo = A.alloc([128, 2, D], F32)
    xt_ = [A.alloc([128, D], F32) for _ in range(4)]
    xm_ = [A.alloc([128, D], F32) for _ in range(3)]
    hn_ = [A.alloc([128, D], BF16) for _ in range(2)]
    for c in range(4):
        load_weight(Wo[:, :, c * 256:(c + 1) * 256], w_o[:, c * 256:(c + 1) * 256], 8, 256, 13, stage4, "Wo")
    op("sync", lambda e: e.dma_start(out=gpo, in_=gpost), writes=["gpo"], dma=True)

    def f0M(t):
        return 128 if t < 16 else 2

    def f0s0(t):
        xt = xt_[t % 4]
        xk = ("xt", t % 4)
        if t < 16:
            op("sync", lambda e: e.dma_start(out=xt, in_=xw[OWN0 + 128 * t:OWN0 + 128 * (t + 1), :]), writes=[xk], dma=True)
        else:
            op("sync", lambda e: e.dma_start(out=xt[0:1, :], in_=xw[OWN0 - 1:OWN0, :]), writes=[xk], dma=True)
            op("sync", lambda e: e.dma_start(out=xt[1:2, :], in_=xw[OWN0 + OWN:OWN0 + OWN + 1, :]), writes=[xk], dma=True)

    def f0s1(t):
        M = f0M(t)
        pb = 2 * (t % 3)
        for n2 in range(2):
            for k in range(8):
                if t < 16:
                    la = (ynTa if k < 4 else ynTb)[:, k % 4, 1 + 128 * t:1 + 128 * (t + 1)]
                else:
                    la = (ynTa if k < 4 else ynTb)[:, k % 4, 0:NX:NX - 1]
                op("tensor", lambda e, k=k, n2=n2, la=la: e.matmul(bank(pb + n2, 512, M), lhsT=la, rhs=Wo[:, k, n2 * 512:(n2 + 1) * 512], start=(k == 0), stop=(k == 7)),
                   reads=["ynTa", "ynTb", "Wo"], writes=[("ps", pb + n2)])

    def f0s2(t):
        M = f0M(t)
        pb = 2 * (t % 3)
        yv = ps[0:M, pb * 512:(pb + 2) * 512]
        sc = 192 + 2 * (t % 4)
        sk = ("stt5", t % 4)
        op("scalar", lambda e: e.activation(out=junk[0:M, :], in_=yv, func=AF.Square, accum_out=stt[0:M, sc:sc + 1]),
           reads=[("ps", pb), ("ps", pb + 1)], writes=["junk", sk])
        op("scalar", lambda e: e.activation(out=stt[0:M, sc:sc + 1], in_=stt[0:M, sc:sc + 1], func=AF.Sqrt, bias=EPS, scale=1.0 / D), reads=[sk], writes=[sk])

    def f0s3(t):
        M = f0M(t)
        pb = 2 * (t % 3)
        yv = ps[0:M, pb * 512:(pb + 2) * 512]
        sc = 192 + 2 * (t % 4)
        sk = ("stt5", t % 4)
        xm, xt = xm_[t % 3], xt_[t % 4]
        mkk, xk = ("xm", t % 3), ("xt", t % 4)
        op("vector", lambda e: e.reciprocal(out=stt[0:M, sc + 1:sc + 2], in_=stt[0:M, sc:sc + 1]), reads=[sk], writes=[("stt5r", t % 4)])
        op("vector", lambda e: e.scalar_tensor_tensor(out=xm[0:M, :], in0=yv, scalar=stt[0:M, sc + 1:sc + 2], in1=gpo[0:M, 0, :], op0=ALU.mult, op1=ALU.mult),
           reads=[("ps", pb), ("ps", pb + 1), ("stt5r", t % 4), "gpo"], writes=[mkk])
        op("vector", lambda e: e.tensor_tensor(out=xm[0:M, :], in0=xm[0:M, :], in1=xt[0:M, :], op=ALU.add), reads=[mkk, xk], writes=[mkk])

    def f0s4(t):
        M = f0M(t)
        xm = xm_[t % 3]
        mkk = ("xm", t % 3)
        if t < 16:
            op("sync", lambda e: e.dma_start(out=xmid[128 * t:128 * (t + 1), :], in_=xm), reads=[mkk], writes=["xmid"], dma=True)
        sc2 = 200 + 2 * (t % 4)
        sk2 = ("stt6", t % 4)
        op("scalar", lambda e: e.activation(out=junk[0:M, :], in_=xm[0:M, :], func=AF.Square, accum_out=stt[0:M, sc2:sc2 + 1]),
           reads=[mkk], writes=["junk", sk2])
        op("scalar", lambda e: e.activation(out=stt[0:M, sc2:sc2 + 1], in_=stt[0:M, sc2:sc2 + 1], func=AF.Sqrt, bias=EPS, scale=1.0 / D), reads=[sk2], writes=[sk2])

    def f0s5(t):
        M = f0M(t)
        xm, hn = xm_[t % 3], hn_[t % 2]
        sc2 = 200 + 2 * (t % 4)
        op("vector", lambda e: e.reciprocal(out=stt[0:M, sc2 + 1:sc2 + 2], in_=stt[0:M, sc2:sc2 + 1]), reads=[("stt6", t % 4)], writes=[("stt6r", t % 4)])
        op("vector", lambda e: e.tensor_scalar(out=hn[0:M, :], in0=xm[0:M, :], scalar1=stt[0:M, sc2 + 1:sc2 + 2], scalar2=None, op0=ALU.mult),
           reads=[("xm", t % 3), ("stt6r", t % 4)], writes=[("hn", t % 2)])

    def f0s6(t):
        M = f0M(t)
        hn = hn_[t % 2]
        pb2 = 6 + t % 2
        for k in range(8):
            op("tensor", lambda e, k=k: e.transpose(out=bankb(pb2)[:, k * 128:k * 128 + M], in_=hn[0:M, k * 128:(k + 1) * 128], identity=idb[0:M, 0:M]),
               reads=[("hn", t % 2), "idb"], writes=[("ps", pb2)])

    def f0s7(t):
        M = f0M(t)
        pb2 = 6 + t % 2
        srcv = bankb(pb2).rearrange("p (k t) -> p k t", k=8)[:, :, 0:M]
        dstv = hTf[:, :, 1 + 128 * t:1 + 128 * (t + 1)] if t < 16 else hTf[:, :, 0:NX:NX - 1]
        if t % 2 == 0:
            op("vector", lambda e: e.tensor_copy(out=dstv, in_=srcv), reads=[("ps", pb2)], writes=[("grp", "hTf", t)])
        else:
            op("scalar", lambda e: e.copy(out=dstv, in_=srcv), reads=[("ps", pb2)], writes=[("grp", "hTf", t)])

    f0st = [f0s0, f0s1, f0s2, f0s3, f0s4, f0s5, f0s6, f0s7]
    for it in range(17 + len(f0st) - 1):
        for k in range(len(f0st) - 1, -1, -1):
            tt_ = it - k
            if 0 <= tt_ < 17:
                f0st[k](tt_)
    dump("hTf", hTf, [128, 8, NX], BF16, ["hTf"])
    S.barrier()

    aT = Region(R1.lo, RC.hi).alloc([128, 22, OWN], BF16)
    A = Region(RY.lo, RY.hi)
    stg = [A.alloc([128, 8, 256], F32) for _ in range(2)]
    Wub = [A.alloc([128, 8, 256], BF16) for _ in range(2)]
    cwt = A.alloc([128, 44, 4], F32)
    ufl = A.alloc([128, 2], F32)
    A = Region(RT.lo + 32832, SB_BYTES)
    cgb = [A.alloc([128, OWN], F32) for _ in range(2)]
    cvb = [A.alloc([128, OWN], F32) for _ in range(2)]
    op("sync", lambda e: e.dma_start(out=cwt, in_=cwb), writes=["cwt"], dma=True)
    op("sync", lambda e: e.dma_start(out=ufl, in_=uflag), writes=["ufl"], dma=True)
    op("vector", lambda e: e.tensor_tensor(out=hTf[:, :, 0:NX:NX - 1], in0=hTf[:, :, 0:NX:NX - 1], in1=ufl.unsqueeze(1).to_broadcast([128, 8, 2]), op=ALU.mult),
       reads=["hTf", "ufl"], writes=["hTf"])
    OB = [(410 * i, min(410, OWN - 410 * i)) for i in range(5)]
    pbi = 0

    def f1_weights(j):
        bi = j % 2
        sg, wb = stg[bi], Wub[bi]
        sgk, wbk = ("stg", bi), ("Wub", bi)
        op("gpsimd", lambda e: e.dma_start(out=sg[:, :, 0:128], in_=w_up[:, j * 128:(j + 1) * 128].rearrange("(k p) n -> p k n", p=128)), writes=[sgk], dma=True)
        op("gpsimd", lambda e: e.dma_start(out=sg[:, :, 128:256], in_=w_up[:, DFF + j * 128:DFF + (j + 1) * 128].rearrange("(k p) n -> p k n", p=128)), writes=[sgk], dma=True)
        for k in range(8):
            if k % 2 == 0:
                op("vector", lambda e, k=k: e.tensor_scalar(out=wb[:, k, :], in0=sg[:, k, :], scalar1=gpt[:, 21 + k:22 + k], scalar2=None, op0=ALU.mult),
                   reads=[sgk, "gpt"], writes=[("grp", wbk, k)])
            else:
                op("scalar", lambda e, k=k: e.activation(out=wb[:, k, :], in_=sg[:, k, :], func=AF.Identity, scale=gpt[:, 21 + k:22 + k]),
                   reads=[sgk, "gpt"], writes=[("grp", wbk, k)])

    def f1_tail(j):
        bi = j % 2
        cg, cv = cgb[bi], cvb[bi]
        cgk = [("cg", bi, bx) for bx in range(5)]
        cvk = [("cv", bi, bx) for bx in range(5)]
        op("scalar", lambda e: e.activation(out=cg, in_=cg, func=AF.Gelu_apprx_tanh), reads=cgk, writes=cgk)
        op("vector", lambda e: e.tensor_tensor(out=aT[:, j, :], in0=cg, in1=cv, op=ALU.mult), reads=cgk + cvk, writes=[("aT", j)])

    f1_weights(0)
    for j in range(22):
        bi = j % 2
        wb = Wub[bi]
        wbk = ("Wub", bi)
        if j + 1 < 22:
            f1_weights(j + 1)
        for half in range(2):
            cb = (cgb if half == 0 else cvb)[bi]
            ff = j + 22 * half
            for bx, (o0, n) in enumerate(OB):
                ck = ("cg" if half == 0 else "cv", bi, bx)
                pb = pbi % 8
                pbi += 1
                for k in range(8):
                    op("tensor", lambda e, k=k: e.matmul(bank(pb, n + 2), lhsT=wb[:, k, half * 128:(half + 1) * 128], rhs=hTf[:, k, o0:o0 + n + 2], start=(k == 0), stop=(k == 7)),
                       reads=[wbk, "hTf"], writes=[("ps", pb)])
                op("scalar", lambda e: e.activation(out=cb[:, o0:o0 + n], in_=bank(pb, n + 2)[:, 1:n + 1], func=AF.Identity, bias=cwt[:, ff, 3:4], scale=cwt[:, ff, 1:2]),
                   reads=[("ps", pb), "cwt"], writes=[ck])
                op("vector", lambda e: e.scalar_tensor_tensor(out=cb[:, o0:o0 + n], in0=bank(pb, n + 2)[:, 0:n], scalar=cwt[:, ff, 0:1], in1=cb[:, o0:o0 + n], op0=ALU.mult, op1=ALU.add),
                   reads=[("ps", pb), "cwt", ck], writes=[ck])
                op("vector", lambda e: e.scalar_tensor_tensor(out=cb[:, o0:o0 + n], in0=bank(pb, n + 2)[:, 2:n + 2], scalar=cwt[:, ff, 2:3], in1=cb[:, o0:o0 + n], op0=ALU.mult, op1=ALU.add),
                   reads=[("ps", pb), "cwt", ck], writes=[ck])
                if half == 0 and bx == 2 and j > 0:
                    f1_tail(j - 1)
    f1_tail(21)
    dump("aT", aT, [128, 22, OWN], BF16, [("aT", j) for j in range(22)])
    S.barrier()

    A = Region(RY.lo, SB_BYTES)
    Wdn = A.alloc([128, 22, D], BF16)
    stage5 = [A.alloc([128, 2, D], F32) for _ in range(2)]
    gpo2 = A.alloc([128, D], F32)
    xm2 = [A.alloc([128, D], F32) for _ in range(2)]
    ot = [A.alloc([128, D], F32) for _ in range(2)]
    for c in range(11):
        load_weight(Wdn[:, 2 * c:2 * c + 2, :], w_down[256 * c:256 * (c + 1), :], 2, D, None, stage5[c % 2], "Wdn")
    op("sync", lambda e: e.dma_start(out=gpo2, in_=gpost[:, 1, :]), writes=["gpo2"], dma=True)
    for t in range(16):
        bi = t % 2
        xk, ok = ("xm2", bi), ("ot", bi)
        op("sync", lambda e, t=t, bi=bi: e.dma_start(out=xm2[bi], in_=xmid[128 * t:128 * (t + 1), :]), reads=["xmid"], writes=[xk], dma=True)
        pb = 2 * (t % 2)
        for n2 in range(2):
            for j in range(22):
                op("tensor", lambda e, j=j, n2=n2, t=t, pb=pb: e.matmul(bank(pb + n2), lhsT=aT[:, j, 128 * t:128 * (t + 1)], rhs=Wdn[:, j, n2 * 512:(n2 + 1) * 512], start=(j == 0), stop=(j == 21)),
                   reads=[("aT", j), "Wdn"], writes=[("ps", pb + n2)])
        yv = ps[:, pb * 512:(pb + 2) * 512]
        sc = 144 + 2 * (t % 4)
        sk = ("stt7", t % 4)
        op("scalar", lambda e, yv=yv, sc=sc: e.activation(out=junk, in_=yv, func=AF.Square, accum_out=stt[:, sc:sc + 1]),
           reads=[("ps", pb), ("ps", pb + 1)], writes=["junk", sk])
        op("scalar", lambda e, sc=sc: e.activation(out=stt[:, sc:sc + 1], in_=stt[:, sc:sc + 1], func=AF.Sqrt, bias=EPS, scale=1.0 / D), reads=[sk], writes=[sk])
        op("vector", lambda e, sc=sc: e.reciprocal(out=stt[:, sc + 1:sc + 2], in_=stt[:, sc:sc + 1]), reads=[sk], writes=[sk])
        op("vector", lambda e, yv=yv, sc=sc, bi=bi: e.scalar_tensor_tensor(out=ot[bi], in0=yv, scalar=stt[:, sc + 1:sc + 2], in1=gpo2, op0=ALU.mult, op1=ALU.mult),
           reads=[("ps", pb), ("ps", pb + 1), sk, "gpo2"], writes=[ok])
        op("vector", lambda e, bi=bi: e.tensor_tensor(out=ot[bi], in0=ot[bi], in1=xm2[bi], op=ALU.add), reads=[ok, xk], writes=[ok])
        op("sync", lambda e, t=t, bi=bi: e.dma_start(out=yout[128 * t:128 * (t + 1), :], in_=ot[bi]), reads=[ok], dma=True)
    S.emit()
    return nc


_CACHE = {}


def _consts():
    if "c" in _CACHE:
        return _CACHE["c"]
    slopes = np.exp2(-8.0 * np.arange(1, 9, dtype=np.float32) / 8).astype(np.float32)
    k = np.arange(128)[:, None]
    q = np.arange(128)[None, :]
    masks = np.zeros((8, 128, 6, 128), np.float32)
    for h in range(8):
        for ri, r in enumerate((1, 4, 16)):
            d0 = k - 64 - q
            d1 = k + 64 - q
            masks[h, :, 2 * ri, :] = np.where(k >= q, np.exp(-slopes[h] * (np.abs(d0) * r).astype(np.float32)), 0.0)
            masks[h, :, 2 * ri + 1, :] = np.where(k <= q, np.exp(-slopes[h] * (np.abs(d1) * r).astype(np.float32)), 0.0)
    inv_freq = np.exp(-np.log(10000.0) * np.arange(0, 32, 2, dtype=np.float32) / 32).astype(np.float32)
    pos = np.arange(S_LEN, dtype=np.float32)
    ang = pos[:, None] * inv_freq[None, :]
    cosk = np.cos(ang).astype(np.float32).reshape(64, 128, 16).transpose(1, 0, 2)
    sink = np.sin(ang).astype(np.float32).reshape(64, 128, 16).transpose(1, 0, 2)
    rk = np.ascontiguousarray(np.stack([cosk, sink], axis=1))
    c = dict(masks=masks, inv_freq=inv_freq, rk=rk, ident=np.eye(128, dtype=np.float32))
    _CACHE["c"] = c
    return c


def _core_inputs(c, x, shared):
    cst = _consts()
    b, qc = c // 4, c % 4
    T0 = qc * OWN
    pos_w = T0 - OWN0 + np.arange(NW)
    valid = (pos_w >= 0) & (pos_w < S_LEN)
    xw = np.zeros((NW, D), np.float32)
    xw[valid] = x[b, pos_w[valid]]
    kvf = np.zeros((128, NVT), np.float32)
    for i, (r, es, nk) in enumerate(VLIST):
        kvf[:nk, i] = valid[es + r * np.arange(nk)].astype(np.float32)
    posq = (T0 - 1 + np.arange(NX)).astype(np.float32)
    ang = posq[None, :] * cst["inv_freq"][:, None]
    cq, sq = np.cos(ang).astype(np.float32), np.sin(ang).astype(np.float32)
    rq = np.ascontiguousarray(np.stack([np.concatenate([cq, cq], 0), np.concatenate([sq, sq], 0)], axis=1))
    uflag = np.zeros((128, 2), np.float32)
    uflag[:, 0] = 1.0 if T0 > 0 else 0.0
    uflag[:, 1] = 1.0 if T0 + OWN < S_LEN else 0.0
    d = dict(shared)
    d.update(xw=xw, xf=np.ascontiguousarray(x[b]), kvf=kvf, rq=rq, uflag=uflag, masks=cst["masks"], rk=cst["rk"], ident=cst["ident"])
    return d


def kernel(x, norm_mix_pre, w_in, q_lat_norm, w_uq, kv_lat_norm, w_ukv, out_norm_a, out_norm_b, w_o,
           norm_mix_post, norm_ffn_pre, w_up, conv_w, conv_b, w_down, norm_ffn_post):
    f = lambda a: np.ascontiguousarray(np.asarray(a, dtype=np.float32))
    x = f(x)
    gp = np.zeros((128, 32), np.float32)
    gp[:, 0:8] = f(norm_mix_pre)[0].reshape(8, 128).T
    gp[:, 8:11] = f(q_lat_norm)[0].reshape(3, 128).T
    gp[:, 11:13] = f(kv_lat_norm)[0].reshape(2, 128).T
    gp[:, 13:21] = np.concatenate([f(out_norm_a)[0], f(out_norm_b)[0]]).reshape(8, 128).T
    gp[:, 21:29] = f(norm_ffn_pre)[0].reshape(8, 128).T
    gpost = np.ascontiguousarray(np.broadcast_to(np.stack([f(norm_mix_post)[0], f(norm_ffn_post)[0]])[None], (128, 2, D)))
    cwb = np.zeros((128, 44, 4), np.float32)
    cwb[:, :, 0:3] = f(conv_w)[0].T.reshape(44, 128, 3).transpose(1, 0, 2)
    cwb[:, :, 3] = f(conv_b)[0].reshape(44, 128).T
    shared = dict(w_in=f(w_in)[0], w_uq=f(w_uq)[0], w_ukv=f(w_ukv)[0], w_o=f(w_o)[0], w_up=f(w_up)[0], w_down=f(w_down)[0],
                  gp=gp, gpost=gpost, cwb=cwb)
    if "nc" not in _CACHE:
        _CACHE["nc"] = build()
    nc = _CACHE["nc"]
    in_maps = [_core_inputs(c, x, shared) for c in range(8)]
    res = run_bass_kernel_spmd(nc, in_maps, core_ids=list(range(8)))
    out = np.zeros((2, S_LEN, D), np.float32)
    for c in range(8):
        b, qc = c // 4, c % 4
        out[b, qc * OWN:(qc + 1) * OWN] = res.results[c]["y"]
    return out
```
